# Optimizing a Trainium2 kernel written in Bass

```python
import jax, jax.numpy as jnp
from jax import lax
import numpy as np

D_MODEL = 2048
BATCH = 2
SEQ = 8192
DEPTH = 1

ATT_HEADS = 8
ATT_HEAD_DIM = 128
ATT_WIDTH = ATT_HEADS * ATT_HEAD_DIM
DILATION_PATTERNS = ((128, 1), (512, 4), (2048, 16))
BLOCK = 128
MLSTM_HEADS = 4
MLSTM_HEAD_DIM = 256
MLSTM_WIDTH = MLSTM_HEADS * MLSTM_HEAD_DIM
MLSTM_CHUNK = 64
CONV_WIDTH = 4
NORM_EPS = 1e-6
IN_SPLITS = (ATT_WIDTH,) * 4 + (MLSTM_WIDTH,) * 5 + (2 * MLSTM_HEADS, D_MODEL, D_MODEL)
IN_COLS = sum(IN_SPLITS)

kernel_name = 'hybrid_dilated_attn_mlstm_block'


def rmsnorm(x, g):
    xf = x.astype(jnp.float32)
    y = xf * lax.rsqrt(jnp.mean(xf * xf, axis=-1, keepdims=True) + NORM_EPS)
    return (y * g.astype(jnp.float32)).astype(x.dtype)


def alibi_slopes(n_heads):
    return jnp.asarray(2.0 ** (-8.0 * np.arange(1, n_heads + 1) / n_heads), dtype=jnp.float32)


def dilated_band_attention(q, k, v, window, dilation, slopes):
    B, S, H, E = q.shape
    d = dilation
    L = S // d
    nb = -(-L // BLOCK)
    Lp = nb * BLOCK
    w_sub = window // d

    def to_sub(t):
        t = t.reshape(B, L, d, H, E).transpose(0, 2, 1, 3, 4)
        return jnp.pad(t, ((0, 0), (0, 0), (0, Lp - L), (0, 0), (0, 0)))

    def band(t):
        t = jnp.pad(to_sub(t), ((0, 0), (0, 0), (BLOCK, 0), (0, 0), (0, 0)))
        t = t.reshape(B, d, nb + 1, BLOCK, H, E)
        return jnp.concatenate([t[:, :, :-1], t[:, :, 1:]], axis=3)

    qs = to_sub(q).reshape(B, d, nb, BLOCK, H, E)
    kb, vb = band(k), band(v)
    qi = jnp.arange(BLOCK)
    kj = jnp.arange(2 * BLOCK)
    delta = qi[:, None] - kj[None, :] + BLOCK
    key_pos = jnp.arange(nb)[:, None] * BLOCK + kj[None, :] - BLOCK
    valid = (delta >= 0)[None] & (delta <= w_sub)[None] & (key_pos >= 0)[:, None, :]
    alibi = -slopes[:, None, None] * (delta * d).astype(jnp.float32)[None]
    s = jnp.einsum('bdnqhe,bdnkhe->bdnhqk', qs, kb) * (E ** -0.5) + alibi
    s = jnp.where(valid[:, None], s, -jnp.inf)
    lse = jax.nn.logsumexp(s, axis=-1)
    p = jnp.exp(s - lse[..., None])
    o = jnp.einsum('bdnhqk,bdnkhe->bdnqhe', p, vb).reshape(B, d, Lp, H, E)[:, :, :L]
    o = o.transpose(0, 2, 1, 3, 4).reshape(B, S, H, E)
    lse = lse.transpose(0, 1, 2, 4, 3).reshape(B, d, Lp, H)[:, :, :L]
    lse = lse.transpose(0, 2, 1, 3).reshape(B, S, H)
    return o, lse


def dilated_mixture_attention(q, k, v):
    B, S, _ = q.shape
    shp = (B, S, ATT_HEADS, ATT_HEAD_DIM)
    qf, kf, vf = (t.astype(jnp.float32).reshape(shp) for t in (q, k, v))
    slopes = alibi_slopes(ATT_HEADS)
    outs, lses = [], []
    for window, dilation in DILATION_PATTERNS:
        o, lse = dilated_band_attention(qf, kf, vf, window, dilation, slopes)
        outs.append(o)
        lses.append(lse)
    wts = jax.nn.softmax(jnp.stack(lses, axis=-1), axis=-1)
    o = jnp.einsum('bshp,pbshe->bshe', wts, jnp.stack(outs, axis=0))
    return o.reshape(B, S, ATT_WIDTH).astype(q.dtype)


def causal_depthwise_conv(x, w, b):
    K = w.shape[0]
    S = x.shape[1]
    xp = jnp.pad(x, ((0, 0), (K - 1, 0), (0, 0)))
    y = b
    for j in range(K):
        y = y + w[j] * xp[:, j:j + S]
    return y


def mlstm_chunkwise(q, k, v, i_pre, f_pre):
    B, S, H, E = q.shape
    L = MLSTM_CHUNK
    nc = S // L
    q = q.astype(jnp.float32)
    k = k.astype(jnp.float32) * (E ** -0.5)
    v = v.astype(jnp.float32)
    ig = i_pre.astype(jnp.float32)
    logf = jax.nn.log_sigmoid(f_pre.astype(jnp.float32))

    def chunks(t):
        return t.reshape((B, nc, L) + t.shape[2:]).swapaxes(0, 1).swapaxes(2, 3)

    tril = jnp.tril(jnp.ones((L, L), dtype=bool))

    def step(carry, xs):
        C, n, m = carry
        qc, kc, vc, ic, fc = xs
        b = jnp.cumsum(fc, axis=-1)
        dmat = jnp.where(tril, b[..., :, None] - b[..., None, :] + ic[..., None, :], -jnp.inf)
        inter = b + m[..., None]
        m_t = jnp.maximum(inter, jnp.max(dmat, axis=-1))
        w_intra = jnp.exp(dmat - m_t[..., None])
        w_inter = jnp.exp(inter - m_t)
        s = jnp.einsum('bhte,bhse->bhts', qc, kc) * w_intra
        num = jnp.einsum('bhts,bhse->bhte', s, vc) + w_inter[..., None] * jnp.einsum('bhte,bhef->bhtf', qc, C)
        den = jnp.sum(s, axis=-1) + w_inter * jnp.einsum('bhte,bhe->bht', qc, n)
        h = num / jnp.maximum(jnp.abs(den), jnp.exp(-m_t))[..., None]
        b_last = b[..., -1]
        w_log = b_last[..., None] - b + ic
        m_new = jnp.maximum(b_last + m, jnp.max(w_log, axis=-1))
        w_k = jnp.exp(w_log - m_new[..., None])
        decay = jnp.exp(b_last + m - m_new)
        C = decay[..., None, None] * C + jnp.einsum('bhs,bhse,bhsf->bhef', w_k, kc, vc)
        n = decay[..., None] * n + jnp.einsum('bhs,bhse->bhe', w_k, kc)
        return (C, n, m_new), h

    init = (jnp.zeros((B, H, E, E), jnp.float32), jnp.zeros((B, H, E), jnp.float32),
            jnp.zeros((B, H), jnp.float32))
    _, h = lax.scan(step, init, (chunks(q), chunks(k), chunks(v), chunks(ig), chunks(logf)))
    return h.swapaxes(2, 3).swapaxes(0, 1).reshape(B, S, H, E)


def head_layernorm(h, g):
    mu = jnp.mean(h, axis=-1, keepdims=True)
    var = jnp.mean(jnp.square(h - mu), axis=-1, keepdims=True)
    y = (h - mu) * lax.rsqrt(var + NORM_EPS)
    B, S, H, E = h.shape
    return y.reshape(B, S, H * E) * g.astype(jnp.float32)


def setup_inputs(seed: int = 0) -> dict:
    key = jax.random.key(seed)
    ks = jax.random.split(key, 16)
    nrm = jax.random.normal
    f32 = jnp.float32
    x = nrm(ks[0], (BATCH, SEQ, D_MODEL), f32)
    c = nrm(ks[1], (BATCH, D_MODEL), f32)
    norm_gain = 1.0 + 0.02 * nrm(ks[2], (DEPTH, D_MODEL), f32)
    w_ada = nrm(ks[3], (DEPTH, D_MODEL, 3 * D_MODEL), f32) * (0.5 * D_MODEL ** -0.5)
    b_ada = 0.02 * nrm(ks[4], (DEPTH, 3 * D_MODEL), f32)
    w_in = nrm(ks[5], (DEPTH, D_MODEL, IN_COLS), f32) * (D_MODEL ** -0.5)
    b_i = 0.1 * nrm(ks[6], (DEPTH, MLSTM_HEADS), f32)
    b_f = jnp.linspace(3.0, 6.0, MLSTM_HEADS, dtype=f32)[None] + 0.1 * nrm(ks[7], (DEPTH, MLSTM_HEADS), f32)
    b_gate_if = jnp.concatenate([b_i, b_f], axis=-1)
    conv_w = nrm(ks[8], (DEPTH, CONV_WIDTH, 2 * MLSTM_WIDTH), f32) * (CONV_WIDTH ** -0.5)
    conv_b = 0.02 * nrm(ks[9], (DEPTH, 2 * MLSTM_WIDTH), f32)
    mlstm_norm_gain = 1.0 + 0.02 * nrm(ks[10], (DEPTH, MLSTM_WIDTH), f32)
    w_proj_attn = nrm(ks[11], (DEPTH, ATT_WIDTH, D_MODEL), f32) * (ATT_WIDTH ** -0.5)
    w_proj_mlstm = nrm(ks[12], (DEPTH, MLSTM_WIDTH, D_MODEL), f32) * (MLSTM_WIDTH ** -0.5)
    w_out = nrm(ks[13], (DEPTH, D_MODEL, D_MODEL), f32) * (D_MODEL ** -0.5)
    final_gain = 1.0 + 0.02 * nrm(ks[14], (D_MODEL,), f32)
    return {'x': x, 'c': c, 'norm_gain': norm_gain, 'w_ada': w_ada, 'b_ada': b_ada, 'w_in': w_in,
            'b_gate_if': b_gate_if, 'conv_w': conv_w, 'conv_b': conv_b,
            'mlstm_norm_gain': mlstm_norm_gain, 'w_proj_attn': w_proj_attn,
            'w_proj_mlstm': w_proj_mlstm, 'w_out': w_out, 'final_gain': final_gain}


def reference(x, c, norm_gain, w_ada, b_ada, w_in, b_gate_if, conv_w, conv_b, mlstm_norm_gain,
              w_proj_attn, w_proj_mlstm, w_out, final_gain):
    offsets = [int(o) for o in np.cumsum(IN_SPLITS)[:-1]]
    for l in range(DEPTH):
        mod = c @ w_ada[l] + b_ada[l]
        shift, scale, gate = jnp.split(mod, 3, axis=-1)
        h = rmsnorm(x, norm_gain[l]) * (1.0 + scale[:, None]) + shift[:, None]
        proj = h @ w_in[l]
        qa, ka, va, za, qm, km, vm, om, zm, ifg, ga, gb = jnp.split(proj, offsets, axis=-1)
        attn = dilated_mixture_attention(qa, ka, va)
        ya = (attn * jax.nn.silu(za)) @ w_proj_attn[l]
        qk = jax.nn.silu(causal_depthwise_conv(jnp.concatenate([qm, km], axis=-1), conv_w[l], conv_b[l]))
        qm_c, km_c = jnp.split(qk, 2, axis=-1)
        ifg = ifg + b_gate_if[l]
        i_pre, f_pre = ifg[..., :MLSTM_HEADS], ifg[..., MLSTM_HEADS:]
        B, S, _ = x.shape
        shp = (B, S, MLSTM_HEADS, MLSTM_HEAD_DIM)
        h_tilde = mlstm_chunkwise(qm_c.reshape(shp), km_c.reshape(shp), vm.reshape(shp), i_pre, f_pre)
        h_cell = jax.nn.sigmoid(om.astype(jnp.float32)).reshape(shp) * h_tilde
        hm = head_layernorm(h_cell, mlstm_norm_gain[l]).astype(x.dtype)
        yb = (hm * jax.nn.silu(zm)) @ w_proj_mlstm[l]
        merged = jax.nn.sigmoid(ga) * ya + jax.nn.sigmoid(gb) * yb
        out = merged @ w_out[l]
        x = x + gate[:, None] * out
    return rmsnorm(x, final_gain)
```

```python
import numpy as np
import concourse.bass as bass
import concourse.mybir as mybir
from concourse.bass_utils import run_bass_kernel_spmd

F32 = mybir.dt.float32
BF16 = mybir.dt.bfloat16
I32 = mybir.dt.int32
AF = mybir.ActivationFunctionType
ALU = mybir.AluOpType

D = 2048
KC = 16
EPS = 1e-6
SEQ = 8192
NEG = -30000.0
LN16 = 2.772588722239781
O_QA, O_KA, O_VA, O_ZA, O_QM, O_KM, O_VM, O_OM, O_ZM, O_IF, O_GA, O_GB = (
    0, 1024, 2048, 3072, 4096, 5120, 6144, 7168, 8192, 9216, 9224, 11272)


class Buf:
    __slots__ = ("name", "w", "r", "excl")

    def __init__(self, name, excl=False):
        self.name = name
        self.w = None
        self.r = {}
        self.excl = excl


class Sched:
    ENG = ("pe", "act", "dve", "pool", "sp")

    def __init__(self, nc):
        self.nc = nc
        self.prog = {e: [] for e in self.ENG}
        self.cnt = {}
        self.known = {e: {} for e in self.ENG}
        self.sems = {}

    def op(self, eng, fn, reads=(), writes=(), dkey=None, dinc=16):
        deps = {}

        def add(tok):
            if tok is None:
                return
            k, v = tok
            if deps.get(k, 0) < v:
                deps[k] = v

        ex = [b for b in reads if b.excl]
        if ex:
            writes = list(writes) + ex
        for b in reads:
            add(b.w)
        for b in writes:
            add(b.w)
            for k, v in b.r.items():
                add((k, v))
        waits = []
        kn = self.known[eng]
        for k, v in deps.items():
            if eng == "pe" and k == "pe":
                continue
            if kn.get(k, 0) >= v:
                continue
            kn[k] = v
            waits.append((k, v))
        if dkey is None:
            key, inc = eng, 1
        else:
            key, inc = dkey, dinc
        self.cnt[key] = self.cnt.get(key, 0) + inc
        tok = (key, self.cnt[key])
        self.prog[eng].append((waits, fn, key, inc))
        for b in reads:
            if b.r.get(key, 0) < tok[1]:
                b.r[key] = tok[1]
        for b in writes:
            b.w = tok
            b.r = {}
        return tok

    def barrier(self):
        for e in self.ENG:
            waits = []
            for k, v in self.cnt.items():
                if k == e and e == "pe":
                    continue
                if self.known[e].get(k, 0) >= v:
                    continue
                self.known[e][k] = v
                waits.append((k, v))
            if waits:
                self.prog[e].append((waits, None, None, 0))

    def emit(self):
        nc = self.nc
        keys = list(self.cnt.keys())
        for k in keys:
            self.sems[k] = nc.alloc_semaphore("s_" + k)
        engmap = {"pe": "tensor", "act": "scalar", "dve": "vector", "pool": "gpsimd", "sp": "sync"}
        with nc.Block() as block:
            for e in self.ENG:
                prog = self.prog[e]

                def body(eng, prog=prog):
                    for waits, fn, key, inc in prog:
                        for k, v in waits:
                            eng.wait_ge(self.sems[k], v)
                        if fn is not None:
                            ins = fn(eng)
                            ins.then_inc(self.sems[key], inc)

                getattr(block, engmap[e])(body)


def build(S=SEQ, stop=None, mode=None, sub=99):
    assert S % 2048 == 0
    NT = S // 512
    NS = S // 128
    NU = S // 2048
    SQ = S // 4
    QT = SQ // 512
    nc = bass.Bass("TRN2", target_bir_lowering=False)
    sch = Sched(nc)

    def din(name, shape, dt=F32):
        if mode in ("pa_only", "pb_only", "ex_only") and name not in ("w1", "cst", "abias", "roff", "hT_in", "ccol", "ng", "bif", "cw", "cb", "mg"):
            shape = [1, 1]
        return nc.dram_tensor(name, list(shape), dt, kind="ExternalInput")

    xT = din("xT", [NS, 128, KC * 128])
    xtok = din("xtok", [SQ, D])
    ccol = din("ccol", [128, KC])
    w_ada = din("w_ada", [D, 3 * D])
    b_row = din("b_row", [1, 3 * D])
    ng = din("ng", [128, KC])
    w1 = din("w1", [D, 2306])
    w_g = din("w_g", [D, 2 * D])
    bif = din("bif", [128, 2])
    cw = din("cw", [128, 16])
    cb = din("cb", [128, 4])
    mg = din("mg", [128, 256])
    w_pa = din("w_pa", [1024, D])
    w_pm = din("w_pm", [1024, D])
    w_out = din("w_out", [D, D])
    fg = din("fg", [128, D])
    cst = din("cst", [128, 384])
    abias_d = din("abias", [128, 6 * 256])
    roff = din("roff", [1, 4 + QT], I32)
    out_d = nc.dram_tensor("out", [SQ, D], F32, kind="ExternalOutput")

    hT_d = nc.dram_tensor("hT_s", [NT, 128, KC * 512], BF16)
    hTo_d = nc.dram_tensor("hTo_s", [QT, 128, KC * 512], BF16)
    ybuf_d = nc.dram_tensor("ybuf_s", [4 * 512, SQ], BF16)
    yall_d = nc.dram_tensor("yall_s", [4 * 4 * 512, SQ], BF16)
    ymine_d = nc.dram_tensor("ymine_s", [4 * 512, SQ], BF16)
    maT_d = nc.dram_tensor("maT_s", [QT, 128, KC * 512], BF16)
    mgT_d = nc.dram_tensor("mgT_s", [QT, 128, KC * 512], BF16)
    dbg_out = {}

    SB_LO = 16512
    SB_HI = 229344
    state = {"off": SB_LO, "n": 0}

    def alloc(shape, dt, name=None):
        nbytes = int(np.prod(shape[1:])) * (4 if dt in (F32, I32) else 2)
        off = (state["off"] + 63) // 64 * 64
        assert off + nbytes <= SB_HI, ("SBUF overflow", name, off, nbytes)
        state["off"] = off + nbytes
        state["n"] += 1
        return nc.alloc_sbuf_tensor_at("%s_%d" % (name or "t", state["n"]), list(shape), dt, offset=off)

    def mark():
        return state["off"]

    def release(m):
        state["off"] = m

    psb = [nc.alloc_psum_tensor("ps%d" % i, [128, 512], F32) for i in range(8)]
    PB_ = [Buf("psum%d" % i, excl=True) for i in range(8)]

    def MM(out, lhsT, rhs, st, sp, R, W):
        return sch.op("pe", lambda e: e.matmul(out, lhsT=lhsT, rhs=rhs, start=st, stop=sp), R, W)

    def TR(out, in_, ident, R, W):
        return sch.op("pe", lambda e: e.transpose(out=out, in_=in_, identity=ident), R, W)

    def ACT(out, in_, func, R, W, scale=None, bias=None):
        def f(e):
            kw = {}
            if scale is not None:
                kw["scale"] = scale
            if bias is not None:
                kw["bias"] = bias
            return e.activation(out=out, in_=in_, func=func, **kw)
        return sch.op("act", f, R, W)

    def TT(eng, out, in0, in1, op, R, W):
        return sch.op(eng, lambda e: e.tensor_tensor(out=out, in0=in0, in1=in1, op=op), R, W)

    def TS(eng, out, in0, s1, s2, op0, op1, R, W):
        if op1 is None:
            return sch.op(eng, lambda e: e.tensor_scalar(out=out, in0=in0, scalar1=s1, scalar2=None, op0=op0), R, W)
        return sch.op(eng, lambda e: e.tensor_scalar(out=out, in0=in0, scalar1=s1, scalar2=s2, op0=op0, op1=op1), R, W)

    def STT(out, in0, scalar, in1, op0, op1, R, W):
        return sch.op("dve", lambda e: e.scalar_tensor_tensor(out=out, in0=in0, scalar=scalar, in1=in1, op0=op0, op1=op1), R, W)

    def CP(eng, out, in_, R, W):
        if eng == "act":
            return sch.op("act", lambda e: e.copy(out=out, in_=in_), R, W)
        return sch.op(eng, lambda e: e.tensor_copy(out=out, in_=in_), R, W)

    def RECIP(out, in_, R, W):
        return sch.op("dve", lambda e: e.reciprocal(out=out, in_=in_), R, W)

    def MEMSET(eng, ap, val, W):
        return sch.op(eng, lambda e: e.memset(ap, val), (), W)

    def DMA(out, in_, R, W, key, eng="sp", **kw):
        return sch.op(eng, lambda e: e.dma_start(out=out, in_=in_, **kw), R, W, dkey=key)

    def bc_mid(t, n, reps, off=0):
        a = t[:, off:off + n]
        return bass.AP(a.tensor, a.offset, [list(a.ap[0]), [0, reps], [1, n]])

    def finish(dumps):
        for name, src_ap, shape, dt in dumps:
            o = nc.dram_tensor("dbg_" + name, list(shape), dt, kind="ExternalOutput")
            DMA(o.ap(), src_ap, [], [Buf("dbg")], "dbg_" + name)
        sch.barrier()
        sch.emit()
        return nc, None

    ident_f = alloc([128, 128], F32, "identf")
    tri_f = alloc([128, 128], F32, "trif")
    ones_f = alloc([128, 128], F32, "onesf")
    ident_b = alloc([128, 128], BF16, "identb")
    ones_b = alloc([128, 128], BF16, "onesb")
    cst_sb = alloc([128, 384], F32, "cst")
    gate_bc = alloc([128, D], F32, "gatebc")
    c_f = alloc([128, KC], F32, "cf")
    c_b = alloc([128, KC], BF16, "cb16")
    ng_sb = alloc([128, KC], F32, "ng")
    A_col = alloc([128, KC], F32, "Acol")
    sh_col = alloc([128, KC], F32, "shcol")
    ri = alloc([1, 4 + QT], I32, "ri")
    B_cst = Buf("cst")
    B_mod = Buf("modrow")
    B_col = Buf("cols")
    B_ri = Buf("ri")

    DMA(cst_sb[:], cst.ap(), (), [B_cst], "ld_cst")
    DMA(c_f[:], ccol.ap(), (), [B_cst], "ld_cst")
    DMA(ng_sb[:], ng.ap(), (), [B_cst], "ld_cst")
    DMA(ri[:], roff.ap(), (), [B_ri], "ld_ri")
    CP("dve", ident_f[:], cst_sb[:, 0:128], [B_cst], [B_cst])
    CP("dve", tri_f[:], cst_sb[:, 128:256], [B_cst], [B_cst])
    CP("dve", ones_f[:], cst_sb[:, 256:384], [B_cst], [B_cst])
    CP("dve", ident_b[:], cst_sb[:, 0:128], [B_cst], [B_cst])
    CP("dve", ones_b[:], cst_sb[:, 256:384], [B_cst], [B_cst])
    CP("dve", c_b[:], c_f[:], [B_cst], [B_cst])

    m_persist = mark()

    if mode not in ('pa_only', 'pb_only', 'ex_only'):
        modrow = alloc([1, 3 * D], F32, "modrow")
        brow = alloc([1, 3 * D], F32, "brow")
        DMA(brow[:], b_row.ap(), (), [B_cst], "ld_cst")
        wa = [alloc([128, 2048], BF16, "wa") for _ in range(2)]
        B_wa = [Buf("wa0"), Buf("wa1")]
        n_wa = 0
        for r in range(3):
            for k in range(KC):
                b = n_wa % 2
                n_wa += 1
                DMA(wa[b][:], w_ada[k * 128:(k + 1) * 128, r * 2048:(r + 1) * 2048], (), [B_wa[b]],
                    "ld_wa%d" % b, eng="pool")
                for n in range(4):
                    MM(psb[n][0:1, :], c_b[:, k:k + 1], wa[b][:, n * 512:(n + 1) * 512], k == 0, k == KC - 1,
                       [B_wa[b], B_cst], [PB_[n]])
            for n in range(4):
                TT("dve", modrow[0:1, r * 2048 + n * 512: r * 2048 + (n + 1) * 512], psb[n][0:1, :],
                   brow[0:1, r * 2048 + n * 512: r * 2048 + (n + 1) * 512], ALU.add, [PB_[n], B_cst], [B_mod])
        for j in range(32):
            TR(psb[4][:, j:j + 1], modrow[0:1, j * 128:(j + 1) * 128], ident_f[0:1, 0:1], [B_mod, B_cst], [PB_[4]])
        CP("dve", sh_col[:], psb[4][:, 0:16], [PB_[4]], [B_col])
        STT(A_col[:], psb[4][:, 16:32], 1.0, ng_sb[:], ALU.add, ALU.mult, [PB_[4], B_cst], [B_col])
        B_gate = Buf("gate")
        for n in range(4):
            MM(psb[n][:, :], ones_f[0:1, :], modrow[0:1, 2 * D + n * 512: 2 * D + (n + 1) * 512], True, True,
               [B_mod, B_cst], [PB_[n]])
            CP("act", gate_bc[:, n * 512:(n + 1) * 512], psb[n][:, :], [PB_[n]], [B_gate])
        release(m_persist)
        sch.barrier()
        if stop == "mod":
            return finish([("A", A_col[:], [128, KC], F32), ("sh", sh_col[:], [128, KC], F32), ("gate", gate_bc[:], [128, D], F32)])

        A_bc = alloc([128, KC * 128], F32, "Abc")
        sh_bc = alloc([128, KC * 128], F32, "shbc")
        B_abc = Buf("abc")
        for k in range(KC):
            TS("pool", A_bc[:, k * 128:(k + 1) * 128], ones_f[:], A_col[:, k:k + 1], None, ALU.mult, None, [B_col, B_cst], [B_abc])
            TS("pool", sh_bc[:, k * 128:(k + 1) * 128], ones_f[:], sh_col[:, k:k + 1], None, ALU.mult, None, [B_col, B_cst], [B_abc])
        xt = [alloc([128, KC * 128], F32, "xt") for _ in range(2)]
        sq = [alloc([128, KC * 128], BF16, "sq") for _ in range(2)]
        sd = [alloc([128, 128], F32, "sd") for _ in range(2)]
        rs = [alloc([128, 128], F32, "rs") for _ in range(2)]
        t2 = [alloc([128, KC * 128], F32, "t2") for _ in range(2)]
        hts = [alloc([128, KC * 512], BF16, "hts") for _ in range(2)]
        B_xt = [Buf("xt0"), Buf("xt1")]
        B_sq = [Buf("sq0"), Buf("sq1")]
        B_sd = [Buf("sd0"), Buf("sd1")]
        B_rs = [Buf("rs0"), Buf("rs1")]
        B_t2 = [Buf("t20"), Buf("t21")]
        B_hts = [Buf("hts0"), Buf("hts1")]
        B_hT = [Buf("hT%d" % i) for i in range(NT)]

        def p0_load(i):
            DMA(xt[i % 2][:], xT[i], (), [B_xt[i % 2]], "ld_xt%d" % (i % 2))

        p0_load(0)
        for i in range(NS):
            b = i % 2
            if i + 1 < NS:
                p0_load(i + 1)
            hb = (i // 4) % 2
            sub = i % 4
            ACT(sq[b][:], xt[b][:], AF.Square, [B_xt[b]], [B_sq[b]])
            for k in range(KC):
                MM(psb[b][:, 0:128], ones_b[:], sq[b][:, k * 128:(k + 1) * 128], k == 0, k == KC - 1, [B_sq[b], B_cst], [PB_[b]])
            ACT(sd[b][:], psb[b][:, 0:128], AF.Sqrt, [PB_[b]], [B_sd[b]], scale=1.0 / D, bias=EPS)
            RECIP(rs[b][:], sd[b][:], [B_sd[b]], [B_rs[b]])
            x3 = xt[b][:].rearrange("p (k t) -> p k t", k=KC)
            TT("dve", x3, x3, bc_mid(rs[b], 128, KC), ALU.mult, [B_xt[b], B_rs[b]], [B_xt[b]])
            TT("pool", t2[b][:], xt[b][:], A_bc[:], ALU.mult, [B_xt[b], B_abc], [B_t2[b]])
            h3 = hts[hb][:].rearrange("p (k t) -> p k t", k=KC)[:, :, sub * 128:(sub + 1) * 128]
            TT("pool", h3, t2[b][:].rearrange("p (k t) -> p k t", k=KC), sh_bc[:].rearrange("p (k t) -> p k t", k=KC),
               ALU.add, [B_t2[b], B_abc], [B_hts[hb]])
            if sub == 3:
                DMA(hT_d[i // 4], hts[hb][:], [B_hts[hb]], [B_hT[i // 4]], "st_hts%d" % hb)
        release(m_persist)
        sch.barrier()
        if stop == "p0":
            return finish([("hT", hT_d.ap(), [NT, 128, KC * 512], BF16)])

    else:
        B_hT = [Buf('hT%d' % i) for i in range(NT)]
        hT_d = din('hT_in', [NT, 128, KC * 512], BF16)
    B_ybuf = [[Buf("ybuf%d_%d" % (q, f)) for f in range(4)] for q in range(4)]
    if mode not in ('pb_only', 'ex_only'):
        wA = alloc([128, KC * 1024], BF16, "wA")
        B_wA = Buf("wA")
        for k in range(KC):
            DMA(wA[:, k * 1024:(k + 1) * 1024], w1[k * 128:(k + 1) * 128, 0:1024], (), [B_wA], "ld_wA", eng="pool")
        abias = alloc([128, 6 * 256], F32, "abias")
        B_ab = Buf("abias")
        DMA(abias[:], abias_d.ap(), (), [B_ab], "ld_ab")
        ht = [alloc([128, KC * 512], BF16, "ht") for _ in range(2)]
        B_ht = [Buf("ht0"), Buf("ht1")]
        qT = alloc([128, 2 * 2048], BF16, "qT")
        kT = alloc([128, 2 * 4096], BF16, "kT")
        vT = alloc([128, 2 * 2048], BF16, "vT")
        zs = alloc([128, 2 * 2048], BF16, "zs")
        B_q = [[Buf("q") for _ in range(4)] for _ in range(2)]
        B_k = [[[Buf("k") for _ in range(4)] for _ in range(2)] for _ in range(2)]
        B_v = [[Buf("v") for _ in range(4)] for _ in range(2)]
        B_z = [[Buf("z") for _ in range(4)] for _ in range(2)]
        NSLOT = {1: 3, 4: 8, 16: 32}
        Vv = {d: alloc([128, 2 * NSLOT[d] * 128], BF16, "Vv%d" % d) for d in (1, 4, 16)}
        B_Vv = {d: [[Buf("vv") for _ in range(NSLOT[d])] for _ in range(2)] for d in (1, 4, 16)}
        accn = alloc([128, 2048], F32, "accn")
        accd = alloc([128, 2048], F32, "accd")
        B_an = [Buf("an") for _ in range(4)]
        B_ad = [Buf("ad") for _ in range(4)]
        sbs = [alloc([128, 256], F32, "sbs") for _ in range(2)]
        B_sbs = [Buf("sbs0"), Buf("sbs1")]
        pTs = [alloc([128, 256], BF16, "pT") for _ in range(3)]
        B_pT = [Buf("pT%d" % i) for i in range(3)]
        yst = [alloc([128, 2048], BF16, "yst") for _ in range(2)]
        B_yst = [Buf("yst0"), Buf("yst1")]

        def pa_load(i):
            DMA(ht[i % 2][:], hT_d[i], [B_hT[i]], [B_ht[i % 2]], "ld_ht%d" % (i % 2))

        psS = [psb[3][:, 0:256], psb[4][:, 0:256], psb[5][:, 0:256]]
        B_psS = [PB_[3], PB_[4], PB_[5]]
        psN = [psb[6][:, 0:128], psb[7][:, 0:128]]
        psD = [psb[6][:, 128:256], psb[7][:, 128:256]]
        B_psN = [PB_[6], PB_[7]]
        B_psD = [PB_[6], PB_[7]]
        psT = [psb[0][:].bitcast(BF16)[:, 0:128], psb[1][:].bitcast(BF16)[:, 0:128]]
        B_psT = [PB_[0], PB_[1]]
        cnt = {"ip": 0, "s": 0, "n": 0, "t": 0, "p": 0, "sb": 0}

        def pa_inproj(i):
            u, m = i // 4, i % 4
            slot = u % 2
            b = i % 2
            for c in range(8):
                pb = cnt["ip"] % 3
                cnt["ip"] += 1
                for k in range(KC):
                    MM(psb[pb][:, :], wA[:, k * 1024 + c * 128: k * 1024 + (c + 1) * 128], ht[b][:, k * 512:(k + 1) * 512],
                       k == 0, k == KC - 1, [B_wA, B_ht[b]], [PB_[pb]])
                h = c % 2
                kind = c // 2
                if kind == 0:
                    CP("dve", qT[:, h * 2048 + m * 512: h * 2048 + (m + 1) * 512], psb[pb][:, :], [PB_[pb]], [B_q[h][m]])
                elif kind == 1:
                    o = h * 4096 + slot * 2048 + m * 512
                    CP("dve", kT[:, o:o + 512], psb[pb][:, :], [PB_[pb]], [B_k[h][slot][m]])
                elif kind == 2:
                    CP("act", vT[:, h * 2048 + m * 512: h * 2048 + (m + 1) * 512], psb[pb][:, :], [PB_[pb]], [B_v[h][m]])
                else:
                    ACT(zs[:, h * 2048 + m * 512: h * 2048 + (m + 1) * 512], psb[pb][:, :], AF.Silu, [PB_[pb]], [B_z[h][m]])

        def blocks_for(u):
            L = []
            for mm in range(16):
                g = 16 * u + mm
                prev = None
                if g > 0:
                    prev = (u % 2, 128 * (mm - 1)) if mm > 0 else ((u - 1) % 2, 1920)
                L.append((0, 1, 128 * mm, 1, [mm // 4], prev, g % 3, (g - 1) % 3))
            for m4 in range(4):
                for r in range(4):
                    g = 4 * u + m4
                    prev = None
                    if g > 0:
                        prev = (u % 2, 512 * (m4 - 1) + r) if m4 > 0 else ((u - 1) % 2, 1536 + r)
                    L.append((1, 4, 512 * m4 + r, 4, [m4], prev, r * 2 + g % 2, r * 2 + (g - 1) % 2))
            for r in range(16):
                prev = ((u - 1) % 2, r) if u > 0 else None
                L.append((2, 16, r, 16, [0, 1, 2, 3], prev, r * 2 + u % 2, r * 2 + (u - 1) % 2))
            return L

        def pa_attn(u, h):
            if sub < 1:
                return
            slot = u % 2
            blks = blocks_for(u)
            pend = []

            def stage1(bi):
                p, d, c0, st, ms, prev, vc, vp = blks[bi]
                qcols = slice(h * 2048 + c0, h * 2048 + c0 + 127 * st + 1, st)
                si = cnt["s"] % 3
                cnt["s"] += 1
                ti = cnt["t"] % 2
                cnt["t"] += 1
                TR(psT[ti], vT[:, qcols], ident_b[:], [B_v[h][m] for m in ms] + [B_cst], [B_psT[ti]])
                ACT(Vv[d][:, (h * NSLOT[d] + vc) * 128:(h * NSLOT[d] + vc + 1) * 128], psT[ti], AF.Identity, [B_psT[ti]], [B_Vv[d][h][vc]])
                kc0 = h * 4096 + slot * 2048 + c0
                lo = 0
                if sub < 1.2:
                    return (bi, 0, 0)
                if prev is not None:
                    ps_, pc0 = prev
                    pk0 = h * 4096 + ps_ * 2048 + pc0
                    pm = sorted(set([(pc0 + j * st) // 512 for j in (0, 127)])) if st < 16 else [0, 1, 2, 3]
                    MM(psS[si][:, 0:128], kT[:, pk0: pk0 + 127 * st + 1: st], qT[:, qcols], True, True,
                       [B_k[h][ps_][m] for m in pm] + [B_q[h][m] for m in ms], [B_psS[si]])
                else:
                    lo = 128
                MM(psS[si][:, 128:256], kT[:, kc0: kc0 + 127 * st + 1: st], qT[:, qcols], True, True,
                   [B_k[h][slot][m] for m in ms] + [B_q[h][m] for m in ms], [B_psS[si]])
                if sub < 1.4:
                    return (bi, 0, 0)
                sbi = cnt["sb"] % 2
                cnt["sb"] += 1
                ab0 = (p * 2 + h) * 256
                STT(sbs[sbi][:, lo:256], psS[si][:, lo:256], 128.0 ** -0.5, abias[:, ab0 + lo: ab0 + 256], ALU.mult, ALU.add,
                    [B_psS[si], B_ab], [B_sbs[sbi]])
                if sub < 1.6:
                    return (bi, 0, 0)
                pi = cnt["p"] % 3
                cnt["p"] += 1
                ACT(pTs[pi][:, lo:256], sbs[sbi][:, lo:256], AF.Exp, [B_sbs[sbi]], [B_pT[pi]])
                return (bi, pi, lo)

            def stage2(info):
                bi, pi, lo = info
                p, d, c0, st, ms, prev, vc, vp = blks[bi]
                ni = cnt["n"] % 2
                cnt["n"] += 1
                vcur = Vv[d][:, (h * NSLOT[d] + vc) * 128:(h * NSLOT[d] + vc + 1) * 128]
                vprev = Vv[d][:, (h * NSLOT[d] + vp) * 128:(h * NSLOT[d] + vp + 1) * 128]
                if lo == 0:
                    MM(psN[ni], vprev, pTs[pi][:, 0:128], True, False, [B_Vv[d][h][vp], B_pT[pi]], [B_psN[ni]])
                    MM(psN[ni], vcur, pTs[pi][:, 128:256], False, True, [B_Vv[d][h][vc], B_pT[pi]], [B_psN[ni]])
                    MM(psD[ni], ones_b[:], pTs[pi][:, 0:128], True, False, [B_cst, B_pT[pi]], [B_psD[ni]])
                    MM(psD[ni], ones_b[:], pTs[pi][:, 128:256], False, True, [B_cst, B_pT[pi]], [B_psD[ni]])
                else:
                    MM(psN[ni], vcur, pTs[pi][:, 128:256], True, True, [B_Vv[d][h][vc], B_pT[pi]], [B_psN[ni]])
                    MM(psD[ni], ones_b[:], pTs[pi][:, 128:256], True, True, [B_cst, B_pT[pi]], [B_psD[ni]])
                ocols = slice(c0, c0 + 127 * st + 1, st)
                if p == 0:
                    CP("act", accn[:, ocols], psN[ni], [B_psN[ni]], [B_an[m] for m in ms])
                    CP("act", accd[:, ocols], psD[ni], [B_psD[ni]], [B_ad[m] for m in ms])
                else:
                    TT("dve", accn[:, ocols], psN[ni], accn[:, ocols], ALU.add, [B_psN[ni]] + [B_an[m] for m in ms], [B_an[m] for m in ms])
                    TT("dve", accd[:, ocols], psD[ni], accd[:, ocols], ALU.add, [B_psD[ni]] + [B_ad[m] for m in ms], [B_ad[m] for m in ms])

            for bi in range(len(blks)):
                pend.append(stage1(bi))
                if len(pend) > 1:
                    x_ = pend.pop(0)
                    if sub >= 2:
                        stage2(x_)
            while pend:
                x_ = pend.pop(0)
                if sub >= 2:
                    stage2(x_)
            if sub < 3:
                return
            yb = (u * 2 + h) % 2
            for m in range(4):
                cs = slice(m * 512, (m + 1) * 512)
                ACT(accd[:, cs], accd[:, cs], AF.Ln, [B_ad[m]], [B_ad[m]])
                ACT(accd[:, cs], accd[:, cs], AF.Exp, [B_ad[m]], [B_ad[m]], scale=-1.0)
                TT("pool", accn[:, cs], accn[:, cs], accd[:, cs], ALU.mult, [B_an[m], B_ad[m]], [B_an[m]])
                TT("pool", yst[yb][:, cs], accn[:, cs], zs[:, h * 2048 + m * 512: h * 2048 + (m + 1) * 512], ALU.mult,
                   [B_an[m], B_z[h][m]], [B_yst[yb]])
            t0 = u * 2048
            while t0 < (u + 1) * 2048:
                q = t0 // SQ
                n = min(SQ - (t0 % SQ), (u + 1) * 2048 - t0)
                DMA(ybuf_d[q * 512 + h * 128: q * 512 + (h + 1) * 128, (t0 % SQ):(t0 % SQ) + n],
                    yst[yb][:, t0 - u * 2048: t0 - u * 2048 + n], [B_yst[yb]], [B_ybuf[q][h]], "st_yst%d" % yb)
                t0 += n

        pa_load(0)
        for u in range(NU):
            for m in range(4):
                i = 4 * u + m
                if i + 1 < NT:
                    pa_load(i + 1)
                pa_inproj(i)
            for h in range(2):
                pa_attn(u, h)
        release(m_persist)
        sch.barrier()
        if stop == "pa":
            if mode == "pa_only":
                return finish([("ybuf", ybuf_d.ap(), [2048, SQ], BF16), ("qT", qT[:], [128, 4096], BF16), ("kT", kT[:], [128, 8192], BF16),
                               ("accn", accn[:], [128, 2048], F32), ("accd", accd[:], [128, 2048], F32), ("pT", pTs[0][:], [128, 256], BF16)])
            return finish([("ybuf", ybuf_d.ap(), [2048, SQ], BF16), ("hT", hT_d.ap(), [NT, 128, KC * 512], BF16)])

    if mode != 'ex_only':
        wB = alloc([128, KC * 1282], BF16, "wB")
        B_wB = Buf("wB")
        for k in range(KC):
            DMA(wB[:, k * 1282:(k + 1) * 1282], w1[k * 128:(k + 1) * 128, 1024:2306], (), [B_wB], "ld_wB", eng="pool")
        ht = [alloc([128, KC * 512], BF16, "htb") for _ in range(2)]
        B_ht = [Buf("htb0"), Buf("htb1")]
        smalls = alloc([128, 16 + 4 + 2 + 256], F32, "smalls")
        B_sm = Buf("smalls")
        DMA(smalls[:, 0:16], cw.ap(), (), [B_sm], "ld_sm")
        DMA(smalls[:, 16:20], cb.ap(), (), [B_sm], "ld_sm")
        DMA(smalls[:, 20:22], bif.ap(), (), [B_sm], "ld_sm")
        DMA(smalls[:, 22:278], mg.ap(), (), [B_sm], "ld_sm")
        xq = [alloc([128, 515], F32, "xq") for _ in range(4)]
        B_xq = [Buf("xq%d" % i) for i in range(4)]
        cacc = [alloc([128, 512], F32, "cacc") for _ in range(2)]
        B_cacc = [Buf("cacc0"), Buf("cacc1")]
        qk = [[alloc([128, 512], BF16, "qk") for _ in range(4)] for _ in range(2)]
        B_qk = [[Buf("qk") for _ in range(4)] for _ in range(2)]
        vaug = [alloc([128, 257], BF16, "vaug") for _ in range(2)]
        B_vaug = [Buf("vaug0"), Buf("vaug1")]
        gsm = [alloc([128, 16], F32, "gsm") for _ in range(2)]
        B_gsm = [Buf("gsm0"), Buf("gsm1")]
        sgo = [alloc([128, 256], F32, "sgo") for _ in range(2)]
        szm = [alloc([128, 256], BF16, "szm") for _ in range(2)]
        B_sgo = [Buf("sgo0"), Buf("sgo1")]
        B_szm = [Buf("szm0"), Buf("szm1")]
        pTm = [alloc([128, 128], BF16, "pTm") for _ in range(2)]
        B_pTm = [Buf("pTm0"), Buf("pTm1")]
        kw = [alloc([128, 256], BF16, "kw") for _ in range(2)]
        B_kw = [Buf("kw0"), Buf("kw1")]
        Cst = alloc([128, 2 * 257], F32, "Cst")
        Cb = [alloc([128, 2 * 257], BF16, "Cb") for _ in range(2)]
        B_C = Buf("C")
        B_Cb = [Buf("Cb0"), Buf("Cb1")]
        hh = [alloc([128, 256], F32, "hh") for _ in range(2)]
        B_hh = [Buf("hh0"), Buf("hh1")]
        lnst = [alloc([128, 16], F32, "lnst") for _ in range(2)]
        B_lnst = [Buf("lnst0"), Buf("lnst1")]
        ym = [alloc([128, 256], BF16, "ym") for _ in range(2)]
        B_ym = [Buf("ym0"), Buf("ym1")]
        ystm = [alloc([128, 2 * 512], BF16, "ystm") for _ in range(2)]
        B_ystm = [Buf("ystm0"), Buf("ystm1")]
        for i in range(4):
            MEMSET("pool", xq[i][:, 0:3], 0.0, [B_xq[i]])
        for i in range(2):
            MEMSET("pool", vaug[i][:, 256:257], 1.0, [B_vaug[i]])
        MEMSET("pool", Cst[:], 0.0, [B_C])

        def pb_load(i):
            DMA(ht[i % 2][:], hT_d[i], [B_hT[i]], [B_ht[i % 2]], "ld_htb%d" % (i % 2))

        psSm = psb[4][:, 0:128]
        psG = psb[4][:, 128:130]
        psUn = psb[4][:, 130:132]
        B_psSm, B_psG, B_psUn = PB_[4], PB_[4], PB_[4]
        psH = psb[5][:, 0:257]
        B_psH = PB_[5]
        psU = psb[6][:, 0:512]
        B_psU = PB_[6]
        ps7 = psb[7][:].bitcast(BF16)
        psK = ps7[:, 0:256]
        psY = ps7[:, 256:512]
        B_psK, B_psY = PB_[7], PB_[7]

        def pb_inproj_feat(i):
            b = i % 2
            for ch in range(4):
                pbk = ch % 2
                for k in range(KC):
                    MM(psb[pbk][:, :], wB[:, k * 1282 + ch * 128: k * 1282 + (ch + 1) * 128], ht[b][:, k * 512:(k + 1) * 512],
                       k == 0, k == KC - 1, [B_wB, B_ht[b]], [PB_[pbk]])
                if i > 0:
                    CP("pool", xq[ch][:, 0:3], xq[ch][:, 512:515], [B_xq[ch]], [B_xq[ch]])
                CP("act", xq[ch][:, 3:515], psb[pbk][:, :], [PB_[pbk]], [B_xq[ch]])
                ca = ch % 2
                TS("dve", cacc[ca][:], xq[ch][:, 3:515], smalls[:, ch * 4 + 3: ch * 4 + 4], smalls[:, 16 + ch:17 + ch], ALU.mult, ALU.add,
                   [B_xq[ch], B_sm], [B_cacc[ca]])
                for j in range(3):
                    STT(cacc[ca][:], xq[ch][:, j:j + 512], smalls[:, ch * 4 + j: ch * 4 + j + 1], cacc[ca][:], ALU.mult, ALU.add,
                        [B_xq[ch], B_sm, B_cacc[ca]], [B_cacc[ca]])
                ACT(qk[b][ch][:], cacc[ca][:], AF.Silu, [B_cacc[ca]], [B_qk[b][ch]])

        def pb_chunk(i, s):
            c = 4 * i + s
            b = i % 2
            cb_ = c % 2
            tok = slice(s * 128, (s + 1) * 128)
            for k in range(KC):
                MM(psb[2][:, 0:258], ht[b][:, k * 512 + s * 128: k * 512 + (s + 1) * 128], wB[:, k * 1282 + 512: k * 1282 + 770],
                   k == 0, k == KC - 1, [B_wB, B_ht[b]], [PB_[2]])
            for k in range(KC):
                MM(psb[3][:, :], ht[b][:, k * 512 + s * 128: k * 512 + (s + 1) * 128], wB[:, k * 1282 + 770: k * 1282 + 1282],
                   k == 0, k == KC - 1, [B_wB, B_ht[b]], [PB_[3]])
            CP("act", vaug[cb_][:, 0:256], psb[2][:, 0:256], [PB_[2]], [B_vaug[cb_]])
            g = gsm[cb_]
            Bg = B_gsm[cb_]
            TT("dve", g[:, 0:2], psb[2][:, 256:258], smalls[:, 20:22], ALU.add, [PB_[2], B_sm], [Bg])
            ACT(g[:, 2:3], g[:, 1:2], AF.Exp, [Bg], [Bg], scale=-1.0)
            ACT(g[:, 3:4], g[:, 2:3], AF.Ln, [Bg], [Bg], bias=1.0)
            MM(psG[:, 0:1], tri_f[:], g[:, 3:4], True, True, [Bg, B_cst], [B_psG])
            MM(psG[:, 1:2], ones_f[:], g[:, 3:4], True, True, [Bg, B_cst], [B_psG])
            TT("dve", g[:, 4:5], g[:, 0:1], psG[:, 0:1], ALU.add, [Bg, B_psG], [Bg])
            ACT(g[:, 5:6], g[:, 4:5], AF.Exp, [Bg], [Bg], bias=-LN16)
            ACT(g[:, 6:7], psG[:, 0:1], AF.Exp, [B_psG], [Bg], scale=-1.0)
            TT("dve", g[:, 7:8], g[:, 4:5], psG[:, 1:2], ALU.subtract, [Bg, B_psG], [Bg])
            ACT(g[:, 8:9], g[:, 7:8], AF.Exp, [Bg], [Bg], bias=-LN16)
            ACT(g[:, 9:10], psG[:, 1:2], AF.Exp, [B_psG], [Bg], scale=-1.0)
            ACT(sgo[cb_][:], psb[3][:, 0:256], AF.Sigmoid, [PB_[3]], [B_sgo[cb_]])
            ACT(szm[cb_][:], psb[3][:, 256:512], AF.Silu, [PB_[3]], [B_szm[cb_]])
            for e2 in range(2):
                MM(psSm, qk[b][2 + e2][:, tok], qk[b][e2][:, tok], e2 == 0, e2 == 1, [B_qk[b][2 + e2], B_qk[b][e2]], [B_psSm])
            STT(pTm[cb_][:], psSm, g[:, 5:6], tri_f[:], ALU.mult, ALU.mult, [B_psSm, Bg, B_cst], [B_pTm[cb_]])
            for e2 in range(2):
                TR(psK[:, e2 * 128:(e2 + 1) * 128], qk[b][2 + e2][:, tok], ident_b[:], [B_qk[b][2 + e2], B_cst], [B_psK])
            ACT(kw[cb_][:], psK, AF.Copy, [B_psK, Bg], [B_kw[cb_]], scale=g[:, 8:9])
            for e2 in range(2):
                MM(psU[:, e2 * 256:(e2 + 1) * 256], kw[cb_][:, e2 * 128:(e2 + 1) * 128], vaug[cb_][:, 0:256], True, True,
                   [B_kw[cb_], B_vaug[cb_]], [B_psU])
            for e2 in range(2):
                MM(psUn[:, e2:e2 + 1], kw[cb_][:, e2 * 128:(e2 + 1) * 128], vaug[cb_][:, 256:257], True, True,
                   [B_kw[cb_], B_vaug[cb_]], [B_psUn])
            cbi = c % 2
            MM(psH, pTm[cb_][:], vaug[cb_][:], True, c == 0, [B_pTm[cb_], B_vaug[cb_]], [B_psH])
            if c > 0:
                for e2 in range(2):
                    MM(psH, qk[b][e2][:, tok], Cb[cbi][:, e2 * 257:(e2 + 1) * 257], False, e2 == 1,
                       [B_qk[b][e2], B_Cb[cbi]], [B_psH])
            C3 = Cst[:].rearrange("p (e f) -> p e f", e=2)
            STT(C3[:, :, 0:256], C3[:, :, 0:256], g[:, 9:10], psU.rearrange("p (e f) -> p e f", e=2), ALU.mult, ALU.add,
                [B_C, Bg, B_psU], [B_C])
            STT(C3[:, :, 256], C3[:, :, 256], g[:, 9:10], psUn, ALU.mult, ALU.add, [B_C, Bg, B_psUn], [B_C])
            CP("pool", Cb[1 - cbi][:], Cst[:], [B_C], [B_Cb[1 - cbi]])
            ACT(g[:, 10:11], psH[:, 256:257], AF.Abs, [B_psH, Bg], [Bg], scale=g[:, 6:7])
            TS("dve", g[:, 11:12], g[:, 10:11], 1.0, None, ALU.max, None, [Bg], [Bg])
            RECIP(g[:, 12:13], g[:, 11:12], [Bg], [Bg])
            TT("dve", g[:, 12:13], g[:, 12:13], g[:, 6:7], ALU.mult, [Bg], [Bg])
            STT(hh[cb_][:], psH[:, 0:256], g[:, 12:13], sgo[cb_][:], ALU.mult, ALU.mult, [B_psH, Bg, B_sgo[cb_]], [B_hh[cb_]])
            ls = lnst[cb_]
            Bl = B_lnst[cb_]
            sch.op("dve", lambda e: e.bn_stats(out=ls[:, 0:6], in_=hh[cb_][:]), [B_hh[cb_]], [Bl])
            sch.op("dve", lambda e: e.bn_aggr(out=ls[:, 6:8], in_=ls[:, 0:6]), [Bl], [Bl])
            ACT(ls[:, 8:9], ls[:, 7:8], AF.Sqrt, [Bl], [Bl], bias=EPS)
            RECIP(ls[:, 9:10], ls[:, 8:9], [Bl], [Bl])
            TS("dve", hh[cb_][:], hh[cb_][:], ls[:, 6:7], ls[:, 9:10], ALU.subtract, ALU.mult, [B_hh[cb_], Bl], [B_hh[cb_]])
            TT("pool", hh[cb_][:], hh[cb_][:], smalls[:, 22:278], ALU.mult, [B_hh[cb_], B_sm], [B_hh[cb_]])
            TT("pool", ym[cb_][:], hh[cb_][:], szm[cb_][:], ALU.mult, [B_hh[cb_], B_szm[cb_]], [B_ym[cb_]])
            for f2 in range(2):
                TR(psY[:, f2 * 128:(f2 + 1) * 128], ym[cb_][:, f2 * 128:(f2 + 1) * 128], ident_b[:], [B_ym[cb_], B_cst], [B_psY])
            y3 = ystm[b][:].rearrange("p (f t) -> p f t", f=2)[:, :, s * 128:(s + 1) * 128]
            ACT(y3, psY.rearrange("p (f t) -> p f t", f=2), AF.Identity, [B_psY], [B_ystm[b]])
            if s == 3:
                t0 = i * 512
                q = t0 // SQ
                for f2 in range(2):
                    DMA(ybuf_d[q * 512 + 256 + f2 * 128: q * 512 + 256 + (f2 + 1) * 128, (t0 % SQ):(t0 % SQ) + 512],
                        ystm[b][:, f2 * 512:(f2 + 1) * 512], [B_ystm[b]], [B_ybuf[q][2 + f2]], "st_ystm%d" % b)

        pb_load(0)
        for i in range(NT):
            if i + 1 < NT:
                pb_load(i + 1)
            pb_inproj_feat(i)
            for s in range(4):
                pb_chunk(i, s)
        release(m_persist)
        sch.barrier()
        if stop == "pb":
            return finish([("ybuf", ybuf_d.ap(), [2048, SQ], BF16), ("C", Cst[:], [128, 514], F32), ("gsm", gsm[0][:], [128, 16], F32),
                           ("hh", hh[0][:], [128, 256], F32), ("qk", qk[1][0][:], [128, 512], BF16)])

    B_yall = Buf("yall")
    B_ymine = Buf("ymine")
    B_hTo = Buf("hTo")
    allyb = [B_ybuf[q][f] for q in range(4) for f in range(4)]
    groups = [[0, 1, 2, 3], [4, 5, 6, 7]]
    for q in range(4):
        for f4 in range(4):
            i_ = q * 4 + f4

            def emit_cc(e, q=q, f4=f4, i_=i_):
                return e.collective_compute("AllGather", ALU.bypass, replica_groups=groups,
                                            ins=[ybuf_d[q * 512 + f4 * 128: q * 512 + (f4 + 1) * 128, :]],
                                            outs=[yall_d[i_ * 512:(i_ + 1) * 512, :]])
            sch.op("pool", emit_cc, [B_ybuf[q][f4]], [B_yall], dkey="cc", dinc=1)
    for f4 in range(4):
        reg = nc.gpsimd.alloc_register("rry%d" % f4)

        def f(e, reg=reg, idx=f4, f4=f4):
            e.reg_load(reg, ri[0:1, idx:idx + 1])
            dst = ymine_d.ap().rearrange("(r f p) t -> f r p t", r=4, f=4)[f4]
            return e.dma_start(out=dst, in_=bass.AP(yall_d, reg, [[128 * SQ, 4], [SQ, 128], [1, SQ]]))
        sch.op("pool", f, [B_yall, B_ri], [B_ymine], dkey="cp_y")
    for t in range(QT):
        reg = nc.gpsimd.alloc_register("rrh%d" % t)

        def f(e, reg=reg, idx=4 + t, t=t):
            e.reg_load(reg, ri[0:1, idx:idx + 1])
            return e.dma_start(out=hTo_d[t], in_=bass.AP(hT_d, reg, [[KC * 512, 128], [1, KC * 512]]))
        sch.op("pool", f, [B_hT[i] for i in range(NT)] + [B_ri], [B_hTo], dkey="cp_h")
    sch.barrier()
    if stop == "ex":
        return finish([("ymine", ymine_d.ap(), [2048, SQ], BF16), ("hTo", hTo_d.ap(), [QT, 128, KC * 512], BF16)])

    B_mg = [Buf("merged%d" % t) for t in range(QT)]
    B_maT = [Buf("maT%d" % t) for t in range(QT)]
    m_p2 = mark()
    for ph in range(2):
        wG = alloc([128, KC * D], BF16, "wG")
        wP = alloc([128, 8 * D], BF16, "wP")
        B_wG, B_wP = Buf("wG"), Buf("wP")
        wp_d = w_pa if ph == 0 else w_pm
        for k in range(KC):
            DMA(wG[:, k * D:(k + 1) * D], w_g[k * 128:(k + 1) * 128, ph * D:(ph + 1) * D], (), [B_wG], "ld_wG%d" % ph, eng="pool")
        for k in range(8):
            DMA(wP[:, k * D:(k + 1) * D], wp_d[k * 128:(k + 1) * 128, :], (), [B_wP], "ld_wP%d" % ph, eng="pool")
        hto = [alloc([128, KC * 512], BF16, "hto") for _ in range(2)]
        yin = [alloc([128, 8 * 512], BF16, "yin") for _ in range(2)]
        mao = [alloc([128, KC * 512], BF16, "mao") for _ in range(2)]
        sg = [alloc([128, 512], F32, "sg") for _ in range(2)]
        B_hto = [Buf("hto0"), Buf("hto1")]
        B_yin = [Buf("yin0"), Buf("yin1")]
        B_mao = [Buf("mao0"), Buf("mao1")]
        B_sg = [Buf("sg0"), Buf("sg1")]

        def p2_load(t, ph=ph, hto=hto, yin=yin, mao=mao, B_hto=B_hto, B_yin=B_yin, B_mao=B_mao):
            b = t % 2
            DMA(hto[b][:], hTo_d[t], [B_hTo], [B_hto[b]], "ld_hto%d_%d" % (ph, b))
            for r in range(4):
                base = r * 512 + (0 if ph == 0 else 256)
                src = ymine_d[base: base + 256, t * 512:(t + 1) * 512].rearrange("(c p) t -> p c t", p=128)
                dst = yin[b][:].rearrange("p (c t) -> p c t", c=8)[:, 2 * r: 2 * r + 2, :]
                DMA(dst, src, [B_ymine], [B_yin[b]], "ld_yin%d_%d" % (ph, b))
            if ph == 1:
                DMA(mao[b][:], maT_d[t], [B_maT[t]], [B_mao[b]], "ld_mao%d" % b)

        p2_load(0)
        nj = 0
        for t in range(QT):
            b = t % 2
            if t + 1 < QT:
                p2_load(t + 1)
            for j in range(KC):
                pg = nj % 2
                py = 2 + nj % 2
                sgi = nj % 2
                nj += 1
                for k in range(KC):
                    MM(psb[pg][:, :], wG[:, k * D + j * 128: k * D + (j + 1) * 128], hto[b][:, k * 512:(k + 1) * 512],
                       k == 0, k == KC - 1, [B_wG, B_hto[b]], [PB_[pg]])
                for k in range(8):
                    MM(psb[py][:, :], wP[:, k * D + j * 128: k * D + (j + 1) * 128], yin[b][:, k * 512:(k + 1) * 512],
                       k == 0, k == 7, [B_wP, B_yin[b]], [PB_[py]])
                ACT(sg[sgi][:], psb[pg][:, :], AF.Sigmoid, [PB_[pg]], [B_sg[sgi]])
                if ph == 0:
                    TT("dve", mao[b][:, j * 512:(j + 1) * 512], psb[py][:, :], sg[sgi][:], ALU.mult, [PB_[py], B_sg[sgi]], [B_mao[b]])
                else:
                    TT("dve", sg[sgi][:], psb[py][:, :], sg[sgi][:], ALU.mult, [PB_[py], B_sg[sgi]], [B_sg[sgi]])
                    TT("pool", mao[b][:, j * 512:(j + 1) * 512], sg[sgi][:], mao[b][:, j * 512:(j + 1) * 512], ALU.add,
                       [B_sg[sgi], B_mao[b]], [B_mao[b]])
            if ph == 0:
                DMA(maT_d[t], mao[b][:], [B_mao[b]], [B_maT[t]], "st_mao%d" % b)
            else:
                DMA(mgT_d[t], mao[b][:], [B_mao[b]], [B_mg[t]], "st_mao%d" % b)
        release(m_p2)
        sch.barrier()

    if stop == "p2":
        return finish([("mgT", mgT_d.ap(), [QT, 128, KC * 512], BF16)])

    wO = alloc([128, KC * D], BF16, "wO")
    B_wO = Buf("wO")
    for k in range(KC):
        DMA(wO[:, k * D:(k + 1) * D], w_out[k * 128:(k + 1) * 128, :], (), [B_wO], "ld_wO", eng="pool")
    fgs = alloc([128, D], F32, "fgs")
    B_fg = Buf("fg")
    DMA(fgs[:], fg.ap(), (), [B_fg], "ld_fg")
    xin = [alloc([128, D], F32, "xin") for _ in range(2)]
    xn = [alloc([128, D], F32, "xn") for _ in range(2)]
    mti = [alloc([128, KC * 512], BF16, "mti") for _ in range(2)]
    B_mti = [Buf("mti0"), Buf("mti1")]
    st2 = [alloc([128, 40], F32, "st2") for _ in range(2)]
    B_xin = [Buf("xin0"), Buf("xin1")]
    B_xn = [Buf("xn0"), Buf("xn1")]
    B_st2 = [Buf("st20"), Buf("st21")]
    NTT = SQ // 128

    def p2c_load(tt):
        DMA(xin[tt % 2][:], xtok[tt * 128:(tt + 1) * 128, :], (), [B_xin[tt % 2]], "ld_xin%d" % (tt % 2))
        if tt % 4 == 0:
            t_ = tt // 4
            DMA(mti[t_ % 2][:], mgT_d[t_], [B_mg[t_]], [B_mti[t_ % 2]], "ld_mti%d" % (t_ % 2))

    p2c_load(0)
    for tt in range(NTT):
        b = tt % 2
        if tt + 1 < NTT:
            p2c_load(tt + 1)
        t, sub = tt // 4, tt % 4
        for n in range(4):
            pbk = (tt % 2) * 4 + n
            for j in range(KC):
                MM(psb[pbk][:, :], mti[t % 2][:, j * 512 + sub * 128: j * 512 + (sub + 1) * 128], wO[:, j * D + n * 512: j * D + (n + 1) * 512],
                   j == 0, j == KC - 1, [B_wO, B_mti[t % 2]], [PB_[pbk]])
            cs = slice(n * 512, (n + 1) * 512)
            TT("dve", xn[b][:, cs], psb[pbk][:, :], gate_bc[:, cs], ALU.mult, [PB_[pbk], B_gate], [B_xn[b]])
            TT("pool", xn[b][:, cs], xn[b][:, cs], xin[b][:, cs], ALU.add, [B_xn[b], B_xin[b]], [B_xn[b]])
        s2 = st2[b]
        for n in range(4):
            sch.op("dve", lambda e, b=b, s2=s2, n=n: e.bn_stats(out=s2[:, 8 + n * 6: 8 + (n + 1) * 6], in_=xn[b][:, n * 512:(n + 1) * 512]),
                   [B_xn[b]], [B_st2[b]])
        sch.op("dve", lambda e, s2=s2: e.bn_aggr(out=s2[:, 4:6], in_=s2[:, 8:32]), [B_st2[b]], [B_st2[b]])
        STT(s2[:, 0:1], s2[:, 4:5], s2[:, 4:5], s2[:, 5:6], ALU.mult, ALU.add, [B_st2[b]], [B_st2[b]])
        ACT(s2[:, 1:2], s2[:, 0:1], AF.Sqrt, [B_st2[b]], [B_st2[b]], bias=EPS)
        RECIP(s2[:, 2:3], s2[:, 1:2], [B_st2[b]], [B_st2[b]])
        STT(xn[b][:], xn[b][:], s2[:, 2:3], fgs[:], ALU.mult, ALU.mult, [B_xn[b], B_st2[b], B_fg], [B_xn[b]])
        DMA(out_d[tt * 128:(tt + 1) * 128, :], xn[b][:], [B_xn[b]], [Buf("o")], "st_out%d" % b)
    sch.barrier()
    sch.emit()
    return nc, dbg_out


def _regload(e, reg, ap):
    return e.reg_load(reg, ap)


def _dyn(t, reg, const_off, pattern):
    return bass.AP(t, reg + const_off, pattern)


def _prep_inputs(S, x, c, norm_gain, w_ada, b_ada, w_in, b_gate_if, conv_w, conv_b, mlstm_norm_gain,
                 w_proj_attn, w_proj_mlstm, w_out, final_gain):
    f = np.float32
    NS = S // 128
    SQ = S // 4
    x = np.asarray(x, f)
    w_in0 = np.asarray(w_in, f)[0]
    ident = np.eye(128, dtype=f)
    tri = np.triu(np.ones((128, 128), f))
    ones = np.ones((128, 128), f)
    cst = np.ascontiguousarray(np.concatenate([ident, tri, ones], axis=1))
    xTs = []
    for b in range(2):
        a = x[b].reshape(NS, 128, KC, 128).transpose(0, 3, 2, 1)
        xTs.append(np.ascontiguousarray(a).reshape(NS, 128, KC * 128))
    w_ada0 = np.ascontiguousarray(np.asarray(w_ada, f)[0])
    b_row = np.ascontiguousarray(np.asarray(b_ada, f)[0].reshape(1, -1))
    ngc = np.ascontiguousarray(np.asarray(norm_gain, f)[0].reshape(KC, 128).T)
    w_g = np.ascontiguousarray(w_in0[:, O_GA:O_GA + 2 * D])
    w_pa = np.ascontiguousarray(np.asarray(w_proj_attn, f)[0])
    w_pm = np.ascontiguousarray(np.asarray(w_proj_mlstm, f)[0])
    w_o = np.ascontiguousarray(np.asarray(w_out, f)[0])
    fgb = np.ascontiguousarray(np.broadcast_to(np.asarray(final_gain, f)[None, :], (128, D)))
    cwf = np.asarray(conv_w, f)[0]
    cbf = np.asarray(conv_b, f)[0]
    bg = np.asarray(b_gate_if, f)[0]
    mgf = np.asarray(mlstm_norm_gain, f)[0]
    ki = np.arange(128)[:, None]
    qi = np.arange(128)[None, :]
    in_maps = []
    for core in range(8):
        b, g = core // 4, core % 4
        cols = []
        for off in (O_QA, O_KA, O_VA, O_ZA):
            for lh in range(2):
                h = 2 * g + lh
                cols.append(np.arange(off + h * 128, off + (h + 1) * 128))
        cols.append(np.arange(O_QM + g * 256, O_QM + (g + 1) * 256))
        cols.append(np.arange(O_KM + g * 256, O_KM + (g + 1) * 256))
        cols.append(np.arange(O_VM + g * 256, O_VM + (g + 1) * 256))
        cols.append(np.array([O_IF + g, O_IF + 4 + g]))
        cols.append(np.arange(O_OM + g * 256, O_OM + (g + 1) * 256))
        cols.append(np.arange(O_ZM + g * 256, O_ZM + (g + 1) * 256))
        cols = np.concatenate(cols)
        assert cols.size == 2306
        w1 = np.ascontiguousarray(w_in0[:, cols])
        cwc = np.zeros((128, 16), f)
        cbc = np.zeros((128, 4), f)
        for ch in range(4):
            base = (0 if ch < 2 else 1024) + g * 256 + (ch % 2) * 128
            cwc[:, ch * 4:(ch + 1) * 4] = cwf[:, base:base + 128].T
            cbc[:, ch] = cbf[base:base + 128]
        bifc = np.ascontiguousarray(np.broadcast_to(np.array([bg[g], bg[4 + g]], f)[None, :], (128, 2)))
        mgc = np.ascontiguousarray(np.broadcast_to(mgf[g * 256:(g + 1) * 256][None, :], (128, 256)))
        ab = np.zeros((128, 6, 256), f)
        for p, d in enumerate((1, 4, 16)):
            for lh in range(2):
                slope = 2.0 ** (-(2 * g + lh + 1))
                for half, shift in ((0, 128), (1, 0)):
                    delta = qi - ki + shift
                    valid = (delta >= 0) & (delta <= 128)
                    ab[:, p * 2 + lh, half * 128:(half + 1) * 128] = np.where(valid, -slope * d * delta, NEG)
        in_maps.append({
            "xT": xTs[b],
            "xtok": np.ascontiguousarray(x[b, g * SQ:(g + 1) * SQ, :]),
            "ccol": np.ascontiguousarray(np.asarray(c, f)[b].reshape(KC, 128).T),
            "w_ada": w_ada0, "b_row": b_row, "ng": ngc, "w1": w1, "w_g": w_g,
            "bif": bifc, "cw": cwc, "cb": cbc, "mg": mgc,
            "w_pa": w_pa, "w_pm": w_pm, "w_out": w_o, "fg": fgb, "cst": cst,
            "abias": np.ascontiguousarray(ab.reshape(128, 6 * 256)),
            "roff": np.array([[(g * 4 + f4) * 512 * SQ for f4 in range(4)]
                              + [(g * (SQ // 512) + t) * 128 * KC * 512 for t in range(SQ // 512)]], dtype=np.int32),
        })
    return in_maps


_STOP = None


def kernel(x, c, norm_gain, w_ada, b_ada, w_in, b_gate_if, conv_w, conv_b, mlstm_norm_gain,
           w_proj_attn, w_proj_mlstm, w_out, final_gain):
    S = int(np.asarray(x).shape[1])
    in_maps = _prep_inputs(S, x, c, norm_gain, w_ada, b_ada, w_in, b_gate_if, conv_w, conv_b, mlstm_norm_gain,
                           w_proj_attn, w_proj_mlstm, w_out, final_gain)
    nc, _ = build(S, stop=_STOP)
    res = run_bass_kernel_spmd(nc, in_maps, core_ids=list(range(8)))
    if _STOP is not None:
        return res.results
    SQ = S // 4
    out = np.zeros((2, S, D), np.float32)
    for core in range(8):
        b, g = core // 4, core % 4
        out[b, g * SQ:(g + 1) * SQ, :] = res.results[core]["out"]
    return out
```

```python
import numpy as np
import concourse.bass as bass
import concourse.mybir as mybir
from concourse.bass_utils import run_bass_kernel_spmd

F32 = mybir.dt.float32
BF16 = mybir.dt.bfloat16
I32 = mybir.dt.int32
AF = mybir.ActivationFunctionType
ALU = mybir.AluOpType

D = 2048
KC = 16
EPS = 1e-6
SEQ = 8192
NEG = -30000.0
LN16 = 2.772588722239781
O_QA, O_KA, O_VA, O_ZA, O_QM, O_KM, O_VM, O_OM, O_ZM, O_IF, O_GA, O_GB = (
    0, 1024, 2048, 3072, 4096, 5120, 6144, 7168, 8192, 9216, 9224, 11272)


class Buf:
    __slots__ = ("name", "w", "r", "excl")

    def __init__(self, name, excl=False):
        self.name = name
        self.w = None
        self.r = {}
        self.excl = excl


class Sched:
    ENG = ("pe", "act", "dve", "pool", "sp")

    def __init__(self, nc):
        self.nc = nc
        self.prog = {e: [] for e in self.ENG}
        self.cnt = {}
        self.known = {e: {} for e in self.ENG}
        self.sems = {}

    def op(self, eng, fn, reads=(), writes=(), dkey=None, dinc=16):
        deps = {}

        def add(tok):
            if tok is None:
                return
            k, v = tok
            if deps.get(k, 0) < v:
                deps[k] = v

        ex = [b for b in reads if b.excl]
        if ex:
            writes = list(writes) + ex
        for b in reads:
            add(b.w)
        for b in writes:
            add(b.w)
            for k, v in b.r.items():
                add((k, v))
        waits = []
        kn = self.known[eng]
        for k, v in deps.items():
            if eng == "pe" and k == "pe":
                continue
            if kn.get(k, 0) >= v:
                continue
            kn[k] = v
            waits.append((k, v))
        if dkey is None:
            key, inc = eng, 1
        else:
            key, inc = dkey, dinc
        self.cnt[key] = self.cnt.get(key, 0) + inc
        tok = (key, self.cnt[key])
        self.prog[eng].append((waits, fn, key, inc))
        for b in reads:
            if b.r.get(key, 0) < tok[1]:
                b.r[key] = tok[1]
        for b in writes:
            b.w = tok
            b.r = {}
        return tok

    def barrier(self):
        for e in self.ENG:
            waits = []
            for k, v in self.cnt.items():
                if (k == e and e == "pe") or k == "cc":
                    continue
                if self.known[e].get(k, 0) >= v:
                    continue
                self.known[e][k] = v
                waits.append((k, v))
            if waits:
                self.prog[e].append((waits, None, None, 0))

    def emit(self):
        nc = self.nc
        keys = list(self.cnt.keys())
        for k in keys:
            self.sems[k] = nc.alloc_semaphore("s_" + k)
        engmap = {"pe": "tensor", "act": "scalar", "dve": "vector", "pool": "gpsimd", "sp": "sync"}
        with nc.Block() as block:
            for e in self.ENG:
                prog = self.prog[e]

                def body(eng, prog=prog):
                    for waits, fn, key, inc in prog:
                        for k, v in waits:
                            eng.wait_ge(self.sems[k], v)
                        if fn is not None:
                            ins = fn(eng)
                            ins.then_inc(self.sems[key], inc)

                getattr(block, engmap[e])(body)


def build(S=SEQ, stop=None, mode=None, sub=99):
    assert S % 2048 == 0
    NT = S // 512
    NS = S // 128
    NU = S // 2048
    SQ = S // 4
    QT = SQ // 512
    nc = bass.Bass("TRN2", target_bir_lowering=False)
    sch = Sched(nc)

    def din(name, shape, dt=F32):
        if mode in ("pa_only", "pb_only", "ex_only") and name not in ("w1", "cst", "abias", "roff", "hT_in", "ccol", "ng", "bif", "cw", "cb", "mg"):
            shape = [1, 1]
        return nc.dram_tensor(name, list(shape), dt, kind="ExternalInput")

    xT = din("xT", [NT, 128, KC * 512])
    xtok = din("xtok", [SQ, D])
    ccol = din("ccol", [128, KC])
    w_ada = din("w_ada", [D, 3 * D])
    b_row = din("b_row", [1, 3 * D])
    ng = din("ng", [128, KC])
    w1 = din("w1", [D, 2306])
    w_g = din("w_g", [D, 2 * D])
    bif = din("bif", [128, 2])
    cw = din("cw", [128, 16])
    cb = din("cb", [128, 4])
    mg = din("mg", [128, 256])
    w_pa = din("w_pa", [1024, D])
    w_pm = din("w_pm", [1024, D])
    w_out = din("w_out", [D, D])
    fg = din("fg", [128, D])
    cst = din("cst", [128, 384])
    abias_d = din("abias", [128, 6 * 256])
    roff = din("roff", [1, 4 + QT], I32)
    out_d = nc.dram_tensor("out", [SQ, D], F32, kind="ExternalOutput")

    hT_d = nc.dram_tensor("hT_s", [NT, 128, KC * 512], BF16)
    hTo_d = nc.dram_tensor("hTo_s", [QT, 128, KC * 512], BF16)
    ybuf_d = nc.dram_tensor("ybuf_s", [4 * 512, SQ], BF16)
    yall_d = nc.dram_tensor("yall_s", [4 * 4 * 512, SQ], BF16)
    ymine_d = nc.dram_tensor("ymine_s", [4 * 512, SQ], BF16)
    maT_d = nc.dram_tensor("maT_s", [QT, 128, KC * 512], BF16)
    mgT_d = nc.dram_tensor("mgT_s", [QT, 128, KC * 512], BF16)
    dbg_out = {}

    SB_LO = 16512
    SB_HI = 229344
    state = {"off": SB_LO, "n": 0}

    def alloc(shape, dt, name=None):
        nbytes = int(np.prod(shape[1:])) * (4 if dt in (F32, I32) else 2)
        off = (state["off"] + 63) // 64 * 64
        assert off + nbytes <= SB_HI, ("SBUF overflow", name, off, nbytes)
        state["off"] = off + nbytes
        state["n"] += 1
        return nc.alloc_sbuf_tensor_at("%s_%d" % (name or "t", state["n"]), list(shape), dt, offset=off)

    def mark():
        return state["off"]

    def release(m):
        state["off"] = m

    psb = [nc.alloc_psum_tensor("ps%d" % i, [128, 512], F32) for i in range(8)]
    PB_ = [Buf("psum%d" % i, excl=True) for i in range(8)]

    def MM(out, lhsT, rhs, st, sp, R, W):
        return sch.op("pe", lambda e: e.matmul(out, lhsT=lhsT, rhs=rhs, start=st, stop=sp), R, W)

    def TR(out, in_, ident, R, W):
        return sch.op("pe", lambda e: e.transpose(out=out, in_=in_, identity=ident), R, W)

    def ACT(out, in_, func, R, W, scale=None, bias=None):
        def f(e):
            kw = {}
            if scale is not None:
                kw["scale"] = scale
            if bias is not None:
                kw["bias"] = bias
            return e.activation(out=out, in_=in_, func=func, **kw)
        return sch.op("act", f, R, W)

    def TT(eng, out, in0, in1, op, R, W):
        return sch.op(eng, lambda e: e.tensor_tensor(out=out, in0=in0, in1=in1, op=op), R, W)

    def TS(eng, out, in0, s1, s2, op0, op1, R, W):
        if op1 is None:
            return sch.op(eng, lambda e: e.tensor_scalar(out=out, in0=in0, scalar1=s1, scalar2=None, op0=op0), R, W)
        return sch.op(eng, lambda e: e.tensor_scalar(out=out, in0=in0, scalar1=s1, scalar2=s2, op0=op0, op1=op1), R, W)

    def STT(out, in0, scalar, in1, op0, op1, R, W):
        return sch.op("dve", lambda e: e.scalar_tensor_tensor(out=out, in0=in0, scalar=scalar, in1=in1, op0=op0, op1=op1), R, W)

    def CP(eng, out, in_, R, W):
        if eng == "act":
            return sch.op("act", lambda e: e.copy(out=out, in_=in_), R, W)
        return sch.op(eng, lambda e: e.tensor_copy(out=out, in_=in_), R, W)

    def RECIP(out, in_, R, W):
        return sch.op("dve", lambda e: e.reciprocal(out=out, in_=in_), R, W)

    def MEMSET(eng, ap, val, W):
        return sch.op(eng, lambda e: e.memset(ap, val), (), W)

    def DMA(out, in_, R, W, key, eng="sp", **kw):
        return sch.op(eng, lambda e: e.dma_start(out=out, in_=in_, **kw), R, W, dkey=key)

    def bc_mid(t, n, reps, off=0):
        a = t[:, off:off + n]
        return bass.AP(a.tensor, a.offset, [list(a.ap[0]), [0, reps], [1, n]])

    def finish(dumps):
        for name, src_ap, shape, dt in dumps:
            o = nc.dram_tensor("dbg_" + name, list(shape), dt, kind="ExternalOutput")
            DMA(o.ap(), src_ap, [], [Buf("dbg")], "dbg_" + name)
        sch.barrier()
        sch.emit()
        return nc, None

    ident_f = alloc([128, 128], F32, "identf")
    tri_f = alloc([128, 128], F32, "trif")
    ones_f = alloc([128, 128], F32, "onesf")
    ident_b = alloc([128, 128], BF16, "identb")
    ones_b = alloc([128, 128], BF16, "onesb")
    cst_sb = alloc([128, 384], F32, "cst")
    gate_bc = alloc([128, D], F32, "gatebc")
    c_f = alloc([128, KC], F32, "cf")
    c_b = alloc([128, KC], BF16, "cb16")
    ng_sb = alloc([128, KC], F32, "ng")
    A_col = alloc([128, KC], F32, "Acol")
    sh_col = alloc([128, KC], F32, "shcol")
    ri = alloc([1, 4 + QT], I32, "ri")
    B_cst = Buf("cst")
    B_mod = Buf("modrow")
    B_col = Buf("cols")
    B_ri = Buf("ri")

    DMA(cst_sb[:], cst.ap(), (), [B_cst], "ld_cst")
    DMA(c_f[:], ccol.ap(), (), [B_cst], "ld_cst")
    DMA(ng_sb[:], ng.ap(), (), [B_cst], "ld_cst")
    DMA(ri[:], roff.ap(), (), [B_ri], "ld_ri")
    CP("dve", ident_f[:], cst_sb[:, 0:128], [B_cst], [B_cst])
    CP("dve", tri_f[:], cst_sb[:, 128:256], [B_cst], [B_cst])
    CP("dve", ones_f[:], cst_sb[:, 256:384], [B_cst], [B_cst])
    CP("dve", ident_b[:], cst_sb[:, 0:128], [B_cst], [B_cst])
    CP("dve", ones_b[:], cst_sb[:, 256:384], [B_cst], [B_cst])
    CP("dve", c_b[:], c_f[:], [B_cst], [B_cst])

    m_persist = mark()

    if mode not in ('pa_only', 'pb_only', 'ex_only'):
        modrow = alloc([1, 3 * D], F32, "modrow")
        brow = alloc([1, 3 * D], F32, "brow")
        DMA(brow[:], b_row.ap(), (), [B_cst], "ld_cst")
        wa = [alloc([128, 2048], BF16, "wa") for _ in range(2)]
        B_wa = [Buf("wa0"), Buf("wa1")]
        n_wa = 0
        for r in range(3):
            for k in range(KC):
                b = n_wa % 2
                n_wa += 1
                DMA(wa[b][:], w_ada[k * 128:(k + 1) * 128, r * 2048:(r + 1) * 2048], (), [B_wa[b]],
                    "ld_wa%d" % b, eng="pool")
                for n in range(4):
                    MM(psb[n][0:1, :], c_b[:, k:k + 1], wa[b][:, n * 512:(n + 1) * 512], k == 0, k == KC - 1,
                       [B_wa[b], B_cst], [PB_[n]])
            for n in range(4):
                TT("dve", modrow[0:1, r * 2048 + n * 512: r * 2048 + (n + 1) * 512], psb[n][0:1, :],
                   brow[0:1, r * 2048 + n * 512: r * 2048 + (n + 1) * 512], ALU.add, [PB_[n], B_cst], [B_mod])
        for j in range(32):
            TR(psb[4][:, j:j + 1], modrow[0:1, j * 128:(j + 1) * 128], ident_f[0:1, 0:1], [B_mod, B_cst], [PB_[4]])
        CP("dve", sh_col[:], psb[4][:, 0:16], [PB_[4]], [B_col])
        STT(A_col[:], psb[4][:, 16:32], 1.0, ng_sb[:], ALU.add, ALU.mult, [PB_[4], B_cst], [B_col])
        B_gate = Buf("gate")
        for n in range(4):
            MM(psb[n][:, :], ones_f[0:1, :], modrow[0:1, 2 * D + n * 512: 2 * D + (n + 1) * 512], True, True,
               [B_mod, B_cst], [PB_[n]])
            CP("act", gate_bc[:, n * 512:(n + 1) * 512], psb[n][:, :], [PB_[n]], [B_gate])
        release(m_persist)
        sch.barrier()
        if stop == "mod":
            return finish([("A", A_col[:], [128, KC], F32), ("sh", sh_col[:], [128, KC], F32), ("gate", gate_bc[:], [128, D], F32)])

        xt = [alloc([128, KC * 512], F32, "xt") for _ in range(2)]
        sq = [alloc([128, KC * 512], BF16, "sq") for _ in range(2)]
        sd = [alloc([128, 512], F32, "sd") for _ in range(2)]
        rs = [alloc([128, 512], F32, "rs") for _ in range(2)]
        hts = [alloc([128, KC * 512], BF16, "hts") for _ in range(2)]
        B_xt = [Buf("xt0"), Buf("xt1")]
        B_sq = [Buf("sq0"), Buf("sq1")]
        B_sd = [Buf("sd0"), Buf("sd1")]
        B_rs = [Buf("rs0"), Buf("rs1")]
        B_hts = [Buf("hts0"), Buf("hts1")]
        B_hT = [Buf("hT%d" % i) for i in range(NT)]

        def p0_load(i):
            DMA(xt[i % 2][:], xT[i], (), [B_xt[i % 2]], "ld_xt%d" % (i % 2))

        p0_load(0)
        for i in range(NT):
            b = i % 2
            if i + 1 < NT:
                p0_load(i + 1)
            for hf in range(2):
                cs = slice(hf * 8 * 512, (hf + 1) * 8 * 512)
                ACT(sq[b][:, cs], xt[b][:, cs], AF.Square, [B_xt[b]], [B_sq[b]])
            for k in range(KC):
                MM(psb[b][:, :], ones_b[:], sq[b][:, k * 512:(k + 1) * 512], k == 0, k == KC - 1, [B_sq[b], B_cst], [PB_[b]])
            ACT(sd[b][:], psb[b][:, :], AF.Sqrt, [PB_[b]], [B_sd[b]], scale=1.0 / D, bias=EPS)
            RECIP(rs[b][:], sd[b][:], [B_sd[b]], [B_rs[b]])
            x3 = xt[b][:].rearrange("p (k t) -> p k t", k=KC)
            TT("dve", x3, x3, bc_mid(rs[b], 512, KC), ALU.mult, [B_xt[b], B_rs[b]], [B_xt[b]])
            for k in range(KC):
                ACT(hts[b][:, k * 512:(k + 1) * 512], xt[b][:, k * 512:(k + 1) * 512], AF.Identity, [B_xt[b], B_col], [B_hts[b]],
                    scale=A_col[:, k:k + 1], bias=sh_col[:, k:k + 1])
            DMA(hT_d[i], hts[b][:], [B_hts[b]], [B_hT[i]], "st_hts%d" % b)
        release(m_persist)
        sch.barrier()
        if stop == "p0":
            return finish([("hT", hT_d.ap(), [NT, 128, KC * 512], BF16)])

    else:
        B_hT = [Buf('hT%d' % i) for i in range(NT)]
        hT_d = din('hT_in', [NT, 128, KC * 512], BF16)
    B_ybuf = [[Buf("ybuf%d_%d" % (q, f)) for f in range(4)] for q in range(4)]
    B_yallp = [Buf("yall%d" % i) for i in range(16)]
    groups = [[0, 1, 2, 3], [4, 5, 6, 7]]

    def emit_gather(q, f4):
        i_ = q * 4 + f4

        def emit_cc(e):
            return e.collective_compute("AllGather", ALU.bypass, replica_groups=groups,
                                        ins=[ybuf_d[q * 512 + f4 * 128: q * 512 + (f4 + 1) * 128, :]],
                                        outs=[yall_d[i_ * 512:(i_ + 1) * 512, :]])
        sch.op("pool", emit_cc, [B_ybuf[q][f4]], [B_yallp[i_]], dkey="cc", dinc=1)
    if mode not in ('pb_only', 'ex_only'):
        wA = alloc([128, KC * 1024], BF16, "wA")
        B_wA = Buf("wA")
        for k in range(KC):
            DMA(wA[:, k * 1024:(k + 1) * 1024], w1[k * 128:(k + 1) * 128, 0:1024], (), [B_wA], "ld_wA", eng="pool")
        abias = alloc([128, 6 * 256], F32, "abias")
        B_ab = Buf("abias")
        DMA(abias[:], abias_d.ap(), (), [B_ab], "ld_ab")
        ht = [alloc([128, KC * 512], BF16, "ht") for _ in range(2)]
        B_ht = [Buf("ht0"), Buf("ht1")]
        qT = alloc([128, 2 * 2048], BF16, "qT")
        kT = alloc([128, 2 * 4096], BF16, "kT")
        vT = alloc([128, 2 * 2048], BF16, "vT")
        zs = alloc([128, 2 * 2048], BF16, "zs")
        B_q = [[Buf("q") for _ in range(4)] for _ in range(2)]
        B_k = [[[Buf("k") for _ in range(4)] for _ in range(2)] for _ in range(2)]
        B_v = [[Buf("v") for _ in range(4)] for _ in range(2)]
        B_z = [[Buf("z") for _ in range(4)] for _ in range(2)]
        NSLOT = {1: 3, 4: 8, 16: 32}
        Vv = {d: alloc([128, 2 * NSLOT[d] * 128], BF16, "Vv%d" % d) for d in (1, 4, 16)}
        B_Vv = {d: [[Buf("vv") for _ in range(NSLOT[d])] for _ in range(2)] for d in (1, 4, 16)}
        accn = alloc([128, 2048], F32, "accn")
        accd = alloc([128, 2048], F32, "accd")
        B_an = [Buf("an") for _ in range(4)]
        B_ad = [Buf("ad") for _ in range(4)]
        sbs = [alloc([128, 256], F32, "sbs") for _ in range(2)]
        B_sbs = [Buf("sbs0"), Buf("sbs1")]
        pTs = [alloc([128, 256], BF16, "pT") for _ in range(3)]
        B_pT = [Buf("pT%d" % i) for i in range(3)]
        yst = [alloc([128, 2048], BF16, "yst") for _ in range(2)]
        B_yst = [Buf("yst0"), Buf("yst1")]

        def pa_load(i):
            DMA(ht[i % 2][:], hT_d[i], [B_hT[i]], [B_ht[i % 2]], "ld_ht%d" % (i % 2))

        psS = [psb[3][:, 0:256], psb[4][:, 0:256], psb[5][:, 0:256]]
        B_psS = [PB_[3], PB_[4], PB_[5]]
        psN = [psb[6][:, 0:128], psb[7][:, 0:128]]
        psD = [psb[6][:, 128:256], psb[7][:, 128:256]]
        B_psN = [PB_[6], PB_[7]]
        B_psD = [PB_[6], PB_[7]]
        psT = [psb[0][:].bitcast(BF16)[:, 0:128], psb[1][:].bitcast(BF16)[:, 0:128]]
        B_psT = [PB_[0], PB_[1]]
        cnt = {"ip": 0, "s": 0, "n": 0, "t": 0, "p": 0, "sb": 0}

        def pa_inproj(i):
            u, m = i // 4, i % 4
            slot = u % 2
            b = i % 2
            for c in range(8):
                pb = cnt["ip"] % 3
                cnt["ip"] += 1
                for k in range(KC):
                    MM(psb[pb][:, :], wA[:, k * 1024 + c * 128: k * 1024 + (c + 1) * 128], ht[b][:, k * 512:(k + 1) * 512],
                       k == 0, k == KC - 1, [B_wA, B_ht[b]], [PB_[pb]])
                h = c % 2
                kind = c // 2
                if kind == 0:
                    CP("dve", qT[:, h * 2048 + m * 512: h * 2048 + (m + 1) * 512], psb[pb][:, :], [PB_[pb]], [B_q[h][m]])
                elif kind == 1:
                    o = h * 4096 + slot * 2048 + m * 512
                    CP("dve", kT[:, o:o + 512], psb[pb][:, :], [PB_[pb]], [B_k[h][slot][m]])
                elif kind == 2:
                    CP("act", vT[:, h * 2048 + m * 512: h * 2048 + (m + 1) * 512], psb[pb][:, :], [PB_[pb]], [B_v[h][m]])
                else:
                    ACT(zs[:, h * 2048 + m * 512: h * 2048 + (m + 1) * 512], psb[pb][:, :], AF.Silu, [PB_[pb]], [B_z[h][m]])

        def blocks_for(u):
            L = []
            for mm in range(16):
                g = 16 * u + mm
                prev = None
                if g > 0:
                    prev = (u % 2, 128 * (mm - 1)) if mm > 0 else ((u - 1) % 2, 1920)
                L.append((0, 1, 128 * mm, 1, [mm // 4], prev, g % 3, (g - 1) % 3))
            for m4 in range(4):
                for r in range(4):
                    g = 4 * u + m4
                    prev = None
                    if g > 0:
                        prev = (u % 2, 512 * (m4 - 1) + r) if m4 > 0 else ((u - 1) % 2, 1536 + r)
                    L.append((1, 4, 512 * m4 + r, 4, [m4], prev, r * 2 + g % 2, r * 2 + (g - 1) % 2))
            for r in range(16):
                prev = ((u - 1) % 2, r) if u > 0 else None
                L.append((2, 16, r, 16, [0, 1, 2, 3], prev, r * 2 + u % 2, r * 2 + (u - 1) % 2))
            return L

        def pa_attn(u, h):
            if sub < 1:
                return
            slot = u % 2
            blks = blocks_for(u)
            pend = []

            def stage1(bi):
                p, d, c0, st, ms, prev, vc, vp = blks[bi]
                qcols = slice(h * 2048 + c0, h * 2048 + c0 + 127 * st + 1, st)
                si = cnt["s"] % 3
                cnt["s"] += 1
                ti = cnt["t"] % 2
                cnt["t"] += 1
                TR(psT[ti], vT[:, qcols], ident_b[:], [B_v[h][m] for m in ms] + [B_cst], [B_psT[ti]])
                ACT(Vv[d][:, (h * NSLOT[d] + vc) * 128:(h * NSLOT[d] + vc + 1) * 128], psT[ti], AF.Identity, [B_psT[ti]], [B_Vv[d][h][vc]])
                kc0 = h * 4096 + slot * 2048 + c0
                lo = 0
                if sub < 1.2:
                    return (bi, 0, 0)
                if prev is not None:
                    ps_, pc0 = prev
                    pk0 = h * 4096 + ps_ * 2048 + pc0
                    pm = sorted(set([(pc0 + j * st) // 512 for j in (0, 127)])) if st < 16 else [0, 1, 2, 3]
                    MM(psS[si][:, 0:128], kT[:, pk0: pk0 + 127 * st + 1: st], qT[:, qcols], True, True,
                       [B_k[h][ps_][m] for m in pm] + [B_q[h][m] for m in ms], [B_psS[si]])
                else:
                    lo = 128
                MM(psS[si][:, 128:256], kT[:, kc0: kc0 + 127 * st + 1: st], qT[:, qcols], True, True,
                   [B_k[h][slot][m] for m in ms] + [B_q[h][m] for m in ms], [B_psS[si]])
                if sub < 1.4:
                    return (bi, 0, 0)
                sbi = cnt["sb"] % 2
                cnt["sb"] += 1
                ab0 = (p * 2 + h) * 256
                STT(sbs[sbi][:, lo:256], psS[si][:, lo:256], 128.0 ** -0.5, abias[:, ab0 + lo: ab0 + 256], ALU.mult, ALU.add,
                    [B_psS[si], B_ab], [B_sbs[sbi]])
                if sub < 1.6:
                    return (bi, 0, 0)
                pi = cnt["p"] % 3
                cnt["p"] += 1
                ACT(pTs[pi][:, lo:256], sbs[sbi][:, lo:256], AF.Exp, [B_sbs[sbi]], [B_pT[pi]])
                return (bi, pi, lo)

            def stage2(info):
                bi, pi, lo = info
                p, d, c0, st, ms, prev, vc, vp = blks[bi]
                ni = cnt["n"] % 2
                cnt["n"] += 1
                vcur = Vv[d][:, (h * NSLOT[d] + vc) * 128:(h * NSLOT[d] + vc + 1) * 128]
                vprev = Vv[d][:, (h * NSLOT[d] + vp) * 128:(h * NSLOT[d] + vp + 1) * 128]
                if lo == 0:
                    MM(psN[ni], vprev, pTs[pi][:, 0:128], True, False, [B_Vv[d][h][vp], B_pT[pi]], [B_psN[ni]])
                    MM(psN[ni], vcur, pTs[pi][:, 128:256], False, True, [B_Vv[d][h][vc], B_pT[pi]], [B_psN[ni]])
                    MM(psD[ni], ones_b[:], pTs[pi][:, 0:128], True, False, [B_cst, B_pT[pi]], [B_psD[ni]])
                    MM(psD[ni], ones_b[:], pTs[pi][:, 128:256], False, True, [B_cst, B_pT[pi]], [B_psD[ni]])
                else:
                    MM(psN[ni], vcur, pTs[pi][:, 128:256], True, True, [B_Vv[d][h][vc], B_pT[pi]], [B_psN[ni]])
                    MM(psD[ni], ones_b[:], pTs[pi][:, 128:256], True, True, [B_cst, B_pT[pi]], [B_psD[ni]])
                ocols = slice(c0, c0 + 127 * st + 1, st)
                if p == 0:
                    CP("act", accn[:, ocols], psN[ni], [B_psN[ni]], [B_an[m] for m in ms])
                    CP("act", accd[:, ocols], psD[ni], [B_psD[ni]], [B_ad[m] for m in ms])
                else:
                    TT("dve", accn[:, ocols], psN[ni], accn[:, ocols], ALU.add, [B_psN[ni]] + [B_an[m] for m in ms], [B_an[m] for m in ms])
                    TT("dve", accd[:, ocols], psD[ni], accd[:, ocols], ALU.add, [B_psD[ni]] + [B_ad[m] for m in ms], [B_ad[m] for m in ms])

            for bi in range(len(blks)):
                pend.append(stage1(bi))
                if len(pend) > 1:
                    x_ = pend.pop(0)
                    if sub >= 2:
                        stage2(x_)
            while pend:
                x_ = pend.pop(0)
                if sub >= 2:
                    stage2(x_)
            if sub < 3:
                return
            yb = (u * 2 + h) % 2
            for m in range(4):
                cs = slice(m * 512, (m + 1) * 512)
                ACT(accd[:, cs], accd[:, cs], AF.Ln, [B_ad[m]], [B_ad[m]])
                ACT(accd[:, cs], accd[:, cs], AF.Exp, [B_ad[m]], [B_ad[m]], scale=-1.0)
                TT("pool", accn[:, cs], accn[:, cs], accd[:, cs], ALU.mult, [B_an[m], B_ad[m]], [B_an[m]])
                TT("pool", yst[yb][:, cs], accn[:, cs], zs[:, h * 2048 + m * 512: h * 2048 + (m + 1) * 512], ALU.mult,
                   [B_an[m], B_z[h][m]], [B_yst[yb]])
            t0 = u * 2048
            while t0 < (u + 1) * 2048:
                q = t0 // SQ
                n = min(SQ - (t0 % SQ), (u + 1) * 2048 - t0)
                DMA(ybuf_d[q * 512 + h * 128: q * 512 + (h + 1) * 128, (t0 % SQ):(t0 % SQ) + n],
                    yst[yb][:, t0 - u * 2048: t0 - u * 2048 + n], [B_yst[yb]], [B_ybuf[q][h]], "st_yst%d" % yb)
                t0 += n

        pa_load(0)
        for u in range(NU):
            for m in range(4):
                i = 4 * u + m
                if i + 1 < NT:
                    pa_load(i + 1)
                pa_inproj(i)
            for h in range(2):
                pa_attn(u, h)
        release(m_persist)
        sch.barrier()
        for q in range(4):
            for f4 in range(2):
                emit_gather(q, f4)
        if stop == "pa":
            if mode == "pa_only":
                return finish([("ybuf", ybuf_d.ap(), [2048, SQ], BF16), ("qT", qT[:], [128, 4096], BF16), ("kT", kT[:], [128, 8192], BF16),
                               ("accn", accn[:], [128, 2048], F32), ("accd", accd[:], [128, 2048], F32), ("pT", pTs[0][:], [128, 256], BF16)])
            return finish([("ybuf", ybuf_d.ap(), [2048, SQ], BF16), ("hT", hT_d.ap(), [NT, 128, KC * 512], BF16)])

    if mode != 'ex_only':
        wB = alloc([128, KC * 1282], BF16, "wB")
        B_wB = Buf("wB")
        for k in range(KC):
            DMA(wB[:, k * 1282:(k + 1) * 1282], w1[k * 128:(k + 1) * 128, 1024:2306], (), [B_wB], "ld_wB", eng="pool")
        ht = [alloc([128, KC * 512], BF16, "htb") for _ in range(2)]
        B_ht = [Buf("htb0"), Buf("htb1")]
        smalls = alloc([128, 16 + 4 + 2 + 256], F32, "smalls")
        B_sm = Buf("smalls")
        DMA(smalls[:, 0:16], cw.ap(), (), [B_sm], "ld_sm")
        DMA(smalls[:, 16:20], cb.ap(), (), [B_sm], "ld_sm")
        DMA(smalls[:, 20:22], bif.ap(), (), [B_sm], "ld_sm")
        DMA(smalls[:, 22:278], mg.ap(), (), [B_sm], "ld_sm")
        xq = [alloc([128, 515], F32, "xq") for _ in range(4)]
        B_xq = [Buf("xq%d" % i) for i in range(4)]
        cacc = [alloc([128, 512], F32, "cacc") for _ in range(2)]
        B_cacc = [Buf("cacc0"), Buf("cacc1")]
        qk = [[alloc([128, 512], BF16, "qk") for _ in range(4)] for _ in range(2)]
        B_qk = [[Buf("qk") for _ in range(4)] for _ in range(2)]
        vaug = [alloc([128, 257], BF16, "vaug") for _ in range(2)]
        B_vaug = [Buf("vaug0"), Buf("vaug1")]
        gsm = [alloc([128, 16], F32, "gsm") for _ in range(2)]
        B_gsm = [Buf("gsm0"), Buf("gsm1")]
        sgo = [alloc([128, 256], F32, "sgo") for _ in range(2)]
        szm = [alloc([128, 256], BF16, "szm") for _ in range(2)]
        B_sgo = [Buf("sgo0"), Buf("sgo1")]
        B_szm = [Buf("szm0"), Buf("szm1")]
        pTm = [alloc([128, 128], BF16, "pTm") for _ in range(2)]
        B_pTm = [Buf("pTm0"), Buf("pTm1")]
        kw = [alloc([128, 256], BF16, "kw") for _ in range(2)]
        B_kw = [Buf("kw0"), Buf("kw1")]
        Cst = alloc([128, 2 * 257], F32, "Cst")
        Cb = [alloc([128, 2 * 257], BF16, "Cb") for _ in range(2)]
        B_C = Buf("C")
        B_Cb = [Buf("Cb0"), Buf("Cb1")]
        hh = [alloc([128, 256], F32, "hh") for _ in range(2)]
        B_hh = [Buf("hh0"), Buf("hh1")]
        lnst = [alloc([128, 16], F32, "lnst") for _ in range(2)]
        B_lnst = [Buf("lnst0"), Buf("lnst1")]
        ym = [alloc([128, 256], BF16, "ym") for _ in range(2)]
        B_ym = [Buf("ym0"), Buf("ym1")]
        ystm = [alloc([128, 2 * 512], BF16, "ystm") for _ in range(2)]
        B_ystm = [Buf("ystm0"), Buf("ystm1")]
        for i in range(4):
            MEMSET("pool", xq[i][:, 0:3], 0.0, [B_xq[i]])
        for i in range(2):
            MEMSET("pool", vaug[i][:, 256:257], 1.0, [B_vaug[i]])
        MEMSET("pool", Cst[:], 0.0, [B_C])

        def pb_load(i):
            DMA(ht[i % 2][:], hT_d[i], [B_hT[i]], [B_ht[i % 2]], "ld_htb%d" % (i % 2))

        psSm = psb[4][:, 0:128]
        psG = psb[4][:, 128:130]
        psUn = psb[4][:, 130:132]
        B_psSm, B_psG, B_psUn = PB_[4], PB_[4], PB_[4]
        psH = psb[5][:, 0:257]
        B_psH = PB_[5]
        psU = psb[6][:, 0:512]
        B_psU = PB_[6]
        ps7 = psb[7][:].bitcast(BF16)
        psK = ps7[:, 0:256]
        psY = ps7[:, 256:512]
        B_psK, B_psY = PB_[7], PB_[7]

        def pb_inproj_feat(i):
            b = i % 2
            for ch in range(4):
                pbk = ch % 2
                for k in range(KC):
                    MM(psb[pbk][:, :], wB[:, k * 1282 + ch * 128: k * 1282 + (ch + 1) * 128], ht[b][:, k * 512:(k + 1) * 512],
                       k == 0, k == KC - 1, [B_wB, B_ht[b]], [PB_[pbk]])
                if i > 0:
                    CP("pool", xq[ch][:, 0:3], xq[ch][:, 512:515], [B_xq[ch]], [B_xq[ch]])
                CP("act", xq[ch][:, 3:515], psb[pbk][:, :], [PB_[pbk]], [B_xq[ch]])
                ca = ch % 2
                TS("dve", cacc[ca][:], xq[ch][:, 3:515], smalls[:, ch * 4 + 3: ch * 4 + 4], smalls[:, 16 + ch:17 + ch], ALU.mult, ALU.add,
                   [B_xq[ch], B_sm], [B_cacc[ca]])
                for j in range(3):
                    STT(cacc[ca][:], xq[ch][:, j:j + 512], smalls[:, ch * 4 + j: ch * 4 + j + 1], cacc[ca][:], ALU.mult, ALU.add,
                        [B_xq[ch], B_sm, B_cacc[ca]], [B_cacc[ca]])
                ACT(qk[b][ch][:], cacc[ca][:], AF.Silu, [B_cacc[ca]], [B_qk[b][ch]])

        def pb_chunk(i, s):
            c = 4 * i + s
            b = i % 2
            cb_ = c % 2
            tok = slice(s * 128, (s + 1) * 128)
            for k in range(KC):
                MM(psb[2][:, 0:258], ht[b][:, k * 512 + s * 128: k * 512 + (s + 1) * 128], wB[:, k * 1282 + 512: k * 1282 + 770],
                   k == 0, k == KC - 1, [B_wB, B_ht[b]], [PB_[2]])
            for k in range(KC):
                MM(psb[3][:, :], ht[b][:, k * 512 + s * 128: k * 512 + (s + 1) * 128], wB[:, k * 1282 + 770: k * 1282 + 1282],
                   k == 0, k == KC - 1, [B_wB, B_ht[b]], [PB_[3]])
            CP("act", vaug[cb_][:, 0:256], psb[2][:, 0:256], [PB_[2]], [B_vaug[cb_]])
            g = gsm[cb_]
            Bg = B_gsm[cb_]
            TT("dve", g[:, 0:2], psb[2][:, 256:258], smalls[:, 20:22], ALU.add, [PB_[2], B_sm], [Bg])
            ACT(g[:, 2:3], g[:, 1:2], AF.Exp, [Bg], [Bg], scale=-1.0)
            ACT(g[:, 3:4], g[:, 2:3], AF.Ln, [Bg], [Bg], bias=1.0)
            MM(psG[:, 0:1], tri_f[:], g[:, 3:4], True, True, [Bg, B_cst], [B_psG])
            MM(psG[:, 1:2], ones_f[:], g[:, 3:4], True, True, [Bg, B_cst], [B_psG])
            TT("dve", g[:, 4:5], g[:, 0:1], psG[:, 0:1], ALU.add, [Bg, B_psG], [Bg])
            ACT(g[:, 5:6], g[:, 4:5], AF.Exp, [Bg], [Bg], bias=-LN16)
            ACT(g[:, 6:7], psG[:, 0:1], AF.Exp, [B_psG], [Bg], scale=-1.0)
            TT("dve", g[:, 7:8], g[:, 4:5], psG[:, 1:2], ALU.subtract, [Bg, B_psG], [Bg])
            ACT(g[:, 8:9], g[:, 7:8], AF.Exp, [Bg], [Bg], bias=-LN16)
            ACT(g[:, 9:10], psG[:, 1:2], AF.Exp, [B_psG], [Bg], scale=-1.0)
            ACT(sgo[cb_][:], psb[3][:, 0:256], AF.Sigmoid, [PB_[3]], [B_sgo[cb_]])
            ACT(szm[cb_][:], psb[3][:, 256:512], AF.Silu, [PB_[3]], [B_szm[cb_]])
            for e2 in range(2):
                MM(psSm, qk[b][2 + e2][:, tok], qk[b][e2][:, tok], e2 == 0, e2 == 1, [B_qk[b][2 + e2], B_qk[b][e2]], [B_psSm])
            STT(pTm[cb_][:], psSm, g[:, 5:6], tri_f[:], ALU.mult, ALU.mult, [B_psSm, Bg, B_cst], [B_pTm[cb_]])
            for e2 in range(2):
                TR(psK[:, e2 * 128:(e2 + 1) * 128], qk[b][2 + e2][:, tok], ident_b[:], [B_qk[b][2 + e2], B_cst], [B_psK])
            ACT(kw[cb_][:], psK, AF.Copy, [B_psK, Bg], [B_kw[cb_]], scale=g[:, 8:9])
            for e2 in range(2):
                MM(psU[:, e2 * 256:(e2 + 1) * 256], kw[cb_][:, e2 * 128:(e2 + 1) * 128], vaug[cb_][:, 0:256], True, True,
                   [B_kw[cb_], B_vaug[cb_]], [B_psU])
            for e2 in range(2):
                MM(psUn[:, e2:e2 + 1], kw[cb_][:, e2 * 128:(e2 + 1) * 128], vaug[cb_][:, 256:257], True, True,
                   [B_kw[cb_], B_vaug[cb_]], [B_psUn])
            cbi = c % 2
            MM(psH, pTm[cb_][:], vaug[cb_][:], True, c == 0, [B_pTm[cb_], B_vaug[cb_]], [B_psH])
            if c > 0:
                for e2 in range(2):
                    MM(psH, qk[b][e2][:, tok], Cb[cbi][:, e2 * 257:(e2 + 1) * 257], False, e2 == 1,
                       [B_qk[b][e2], B_Cb[cbi]], [B_psH])
            C3 = Cst[:].rearrange("p (e f) -> p e f", e=2)
            STT(C3[:, :, 0:256], C3[:, :, 0:256], g[:, 9:10], psU.rearrange("p (e f) -> p e f", e=2), ALU.mult, ALU.add,
                [B_C, Bg, B_psU], [B_C])
            STT(C3[:, :, 256], C3[:, :, 256], g[:, 9:10], psUn, ALU.mult, ALU.add, [B_C, Bg, B_psUn], [B_C])
            CP("pool", Cb[1 - cbi][:], Cst[:], [B_C], [B_Cb[1 - cbi]])
            ACT(g[:, 10:11], psH[:, 256:257], AF.Abs, [B_psH, Bg], [Bg], scale=g[:, 6:7])
            TS("dve", g[:, 11:12], g[:, 10:11], 1.0, None, ALU.max, None, [Bg], [Bg])
            RECIP(g[:, 12:13], g[:, 11:12], [Bg], [Bg])
            TT("dve", g[:, 12:13], g[:, 12:13], g[:, 6:7], ALU.mult, [Bg], [Bg])
            STT(hh[cb_][:], psH[:, 0:256], g[:, 12:13], sgo[cb_][:], ALU.mult, ALU.mult, [B_psH, Bg, B_sgo[cb_]], [B_hh[cb_]])
            ls = lnst[cb_]
            Bl = B_lnst[cb_]
            sch.op("dve", lambda e: e.bn_stats(out=ls[:, 0:6], in_=hh[cb_][:]), [B_hh[cb_]], [Bl])
            sch.op("dve", lambda e: e.bn_aggr(out=ls[:, 6:8], in_=ls[:, 0:6]), [Bl], [Bl])
            ACT(ls[:, 8:9], ls[:, 7:8], AF.Sqrt, [Bl], [Bl], bias=EPS)
            RECIP(ls[:, 9:10], ls[:, 8:9], [Bl], [Bl])
            TS("dve", hh[cb_][:], hh[cb_][:], ls[:, 6:7], ls[:, 9:10], ALU.subtract, ALU.mult, [B_hh[cb_], Bl], [B_hh[cb_]])
            TT("pool", hh[cb_][:], hh[cb_][:], smalls[:, 22:278], ALU.mult, [B_hh[cb_], B_sm], [B_hh[cb_]])
            TT("pool", ym[cb_][:], hh[cb_][:], szm[cb_][:], ALU.mult, [B_hh[cb_], B_szm[cb_]], [B_ym[cb_]])
            for f2 in range(2):
                TR(psY[:, f2 * 128:(f2 + 1) * 128], ym[cb_][:, f2 * 128:(f2 + 1) * 128], ident_b[:], [B_ym[cb_], B_cst], [B_psY])
            y3 = ystm[b][:].rearrange("p (f t) -> p f t", f=2)[:, :, s * 128:(s + 1) * 128]
            ACT(y3, psY.rearrange("p (f t) -> p f t", f=2), AF.Identity, [B_psY], [B_ystm[b]])
            if s == 3:
                t0 = i * 512
                q = t0 // SQ
                for f2 in range(2):
                    DMA(ybuf_d[q * 512 + 256 + f2 * 128: q * 512 + 256 + (f2 + 1) * 128, (t0 % SQ):(t0 % SQ) + 512],
                        ystm[b][:, f2 * 512:(f2 + 1) * 512], [B_ystm[b]], [B_ybuf[q][2 + f2]], "st_ystm%d" % b)
                if (i + 1) % QT == 0:
                    pend_g.append(q)

        pend_g = []
        pb_load(0)
        for i in range(NT):
            if i + 1 < NT:
                pb_load(i + 1)
            pb_inproj_feat(i)
            for s in range(4):
                pb_chunk(i, s)
                if s == 1 and pend_g:
                    q_ = pend_g.pop(0)
                    emit_gather(q_, 2)
                    emit_gather(q_, 3)
        while pend_g:
            q_ = pend_g.pop(0)
            emit_gather(q_, 2)
            emit_gather(q_, 3)
        release(m_persist)
        sch.barrier()
        if stop == "pb":
            return finish([("ybuf", ybuf_d.ap(), [2048, SQ], BF16), ("C", Cst[:], [128, 514], F32), ("gsm", gsm[0][:], [128, 16], F32),
                           ("hh", hh[0][:], [128, 256], F32), ("qk", qk[1][0][:], [128, 512], BF16)])

    if mode == 'ex_only':
        for q in range(4):
            for f4 in range(4):
                emit_gather(q, f4)
    B_yall = Buf("yall")
    B_ymine = Buf("ymine")
    B_hTo = Buf("hTo")
    allyb = [B_ybuf[q][f] for q in range(4) for f in range(4)]
    groups = [[0, 1, 2, 3], [4, 5, 6, 7]]
    for f4 in range(4):
        reg = nc.gpsimd.alloc_register("rry%d" % f4)

        def f(e, reg=reg, idx=f4, f4=f4):
            e.reg_load(reg, ri[0:1, idx:idx + 1])
            dst = ymine_d.ap().rearrange("(r f p) t -> f r p t", r=4, f=4)[f4]
            return e.dma_start(out=dst, in_=bass.AP(yall_d, reg, [[128 * SQ, 4], [SQ, 128], [1, SQ]]))
        sch.op("pool", f, B_yallp + [B_ri], [B_ymine], dkey="cp_y")
    for t in range(QT):
        reg = nc.gpsimd.alloc_register("rrh%d" % t)

        def f(e, reg=reg, idx=4 + t, t=t):
            e.reg_load(reg, ri[0:1, idx:idx + 1])
            return e.dma_start(out=hTo_d[t], in_=bass.AP(hT_d, reg, [[KC * 512, 128], [1, KC * 512]]))
        sch.op("pool", f, [B_hT[i] for i in range(NT)] + [B_ri], [B_hTo], dkey="cp_h")
    sch.barrier()
    if stop == "ex":
        return finish([("ymine", ymine_d.ap(), [2048, SQ], BF16), ("hTo", hTo_d.ap(), [QT, 128, KC * 512], BF16)])

    B_mg = [Buf("merged%d" % t) for t in range(QT)]
    B_maT = [Buf("maT%d" % t) for t in range(QT)]
    m_p2 = mark()
    for ph in range(2):
        wG = alloc([128, KC * D], BF16, "wG")
        wP = alloc([128, 8 * D], BF16, "wP")
        B_wG, B_wP = Buf("wG"), Buf("wP")
        wp_d = w_pa if ph == 0 else w_pm
        for k in range(KC):
            DMA(wG[:, k * D:(k + 1) * D], w_g[k * 128:(k + 1) * 128, ph * D:(ph + 1) * D], (), [B_wG], "ld_wG%d" % ph, eng="pool")
        for k in range(8):
            DMA(wP[:, k * D:(k + 1) * D], wp_d[k * 128:(k + 1) * 128, :], (), [B_wP], "ld_wP%d" % ph, eng="pool")
        hto = [alloc([128, KC * 512], BF16, "hto") for _ in range(2)]
        yin = [alloc([128, 8 * 512], BF16, "yin") for _ in range(2)]
        mao = [alloc([128, KC * 512], BF16, "mao") for _ in range(2)]
        sg = [alloc([128, 512], F32, "sg") for _ in range(2)]
        B_hto = [Buf("hto0"), Buf("hto1")]
        B_yin = [Buf("yin0"), Buf("yin1")]
        B_mao = [Buf("mao0"), Buf("mao1")]
        B_sg = [Buf("sg0"), Buf("sg1")]

        def p2_load(t, ph=ph, hto=hto, yin=yin, mao=mao, B_hto=B_hto, B_yin=B_yin, B_mao=B_mao):
            b = t % 2
            DMA(hto[b][:], hTo_d[t], [B_hTo], [B_hto[b]], "ld_hto%d_%d" % (ph, b))
            for r in range(4):
                base = r * 512 + (0 if ph == 0 else 256)
                src = ymine_d[base: base + 256, t * 512:(t + 1) * 512].rearrange("(c p) t -> p c t", p=128)
                dst = yin[b][:].rearrange("p (c t) -> p c t", c=8)[:, 2 * r: 2 * r + 2, :]
                DMA(dst, src, [B_ymine], [B_yin[b]], "ld_yin%d_%d" % (ph, b))
            if ph == 1:
                DMA(mao[b][:], maT_d[t], [B_maT[t]], [B_mao[b]], "ld_mao%d" % b)

        p2_load(0)
        nj = 0
        for t in range(QT):
            b = t % 2
            if t + 1 < QT:
                p2_load(t + 1)
            for j in range(KC):
                pg = nj % 2
                py = 2 + nj % 2
                sgi = nj % 2
                nj += 1
                for k in range(KC):
                    MM(psb[pg][:, :], wG[:, k * D + j * 128: k * D + (j + 1) * 128], hto[b][:, k * 512:(k + 1) * 512],
                       k == 0, k == KC - 1, [B_wG, B_hto[b]], [PB_[pg]])
                for k in range(8):
                    MM(psb[py][:, :], wP[:, k * D + j * 128: k * D + (j + 1) * 128], yin[b][:, k * 512:(k + 1) * 512],
                       k == 0, k == 7, [B_wP, B_yin[b]], [PB_[py]])
                ACT(sg[sgi][:], psb[pg][:, :], AF.Sigmoid, [PB_[pg]], [B_sg[sgi]])
                if ph == 0:
                    TT("dve", mao[b][:, j * 512:(j + 1) * 512], psb[py][:, :], sg[sgi][:], ALU.mult, [PB_[py], B_sg[sgi]], [B_mao[b]])
                else:
                    TT("dve", sg[sgi][:], psb[py][:, :], sg[sgi][:], ALU.mult, [PB_[py], B_sg[sgi]], [B_sg[sgi]])
                    TT("pool", mao[b][:, j * 512:(j + 1) * 512], sg[sgi][:], mao[b][:, j * 512:(j + 1) * 512], ALU.add,
                       [B_sg[sgi], B_mao[b]], [B_mao[b]])
            if ph == 0:
                DMA(maT_d[t], mao[b][:], [B_mao[b]], [B_maT[t]], "st_mao%d" % b)
            else:
                DMA(mgT_d[t], mao[b][:], [B_mao[b]], [B_mg[t]], "st_mao%d" % b)
        release(m_p2)
        sch.barrier()

    if stop == "p2":
        return finish([("mgT", mgT_d.ap(), [QT, 128, KC * 512], BF16)])

    wO = alloc([128, KC * D], BF16, "wO")
    B_wO = Buf("wO")
    for k in range(KC):
        DMA(wO[:, k * D:(k + 1) * D], w_out[k * 128:(k + 1) * 128, :], (), [B_wO], "ld_wO", eng="pool")
    fgs = alloc([128, D], F32, "fgs")
    B_fg = Buf("fg")
    DMA(fgs[:], fg.ap(), (), [B_fg], "ld_fg")
    xin = [alloc([128, D], F32, "xin") for _ in range(2)]
    xn = [alloc([128, D], F32, "xn") for _ in range(2)]
    mti = [alloc([128, KC * 512], BF16, "mti") for _ in range(2)]
    B_mti = [Buf("mti0"), Buf("mti1")]
    st2 = [alloc([128, 40], F32, "st2") for _ in range(2)]
    B_xin = [Buf("xin0"), Buf("xin1")]
    B_xn = [Buf("xn0"), Buf("xn1")]
    B_st2 = [Buf("st20"), Buf("st21")]
    NTT = SQ // 128

    def p2c_load(tt):
        DMA(xin[tt % 2][:], xtok[tt * 128:(tt + 1) * 128, :], (), [B_xin[tt % 2]], "ld_xin%d" % (tt % 2))
        if tt % 4 == 0:
            t_ = tt // 4
            DMA(mti[t_ % 2][:], mgT_d[t_], [B_mg[t_]], [B_mti[t_ % 2]], "ld_mti%d" % (t_ % 2))

    p2c_load(0)
    for tt in range(NTT):
        b = tt % 2
        if tt + 1 < NTT:
            p2c_load(tt + 1)
        t, sub = tt // 4, tt % 4
        for n in range(4):
            pbk = (tt % 2) * 4 + n
            for j in range(KC):
                MM(psb[pbk][:, :], mti[t % 2][:, j * 512 + sub * 128: j * 512 + (sub + 1) * 128], wO[:, j * D + n * 512: j * D + (n + 1) * 512],
                   j == 0, j == KC - 1, [B_wO, B_mti[t % 2]], [PB_[pbk]])
            cs = slice(n * 512, (n + 1) * 512)
            TT("dve", xn[b][:, cs], psb[pbk][:, :], gate_bc[:, cs], ALU.mult, [PB_[pbk], B_gate], [B_xn[b]])
            TT("pool", xn[b][:, cs], xn[b][:, cs], xin[b][:, cs], ALU.add, [B_xn[b], B_xin[b]], [B_xn[b]])
        s2 = st2[b]
        for n in range(4):
            sch.op("dve", lambda e, b=b, s2=s2, n=n: e.bn_stats(out=s2[:, 8 + n * 6: 8 + (n + 1) * 6], in_=xn[b][:, n * 512:(n + 1) * 512]),
                   [B_xn[b]], [B_st2[b]])
        sch.op("dve", lambda e, s2=s2: e.bn_aggr(out=s2[:, 4:6], in_=s2[:, 8:32]), [B_st2[b]], [B_st2[b]])
        STT(s2[:, 0:1], s2[:, 4:5], s2[:, 4:5], s2[:, 5:6], ALU.mult, ALU.add, [B_st2[b]], [B_st2[b]])
        ACT(s2[:, 1:2], s2[:, 0:1], AF.Sqrt, [B_st2[b]], [B_st2[b]], bias=EPS)
        RECIP(s2[:, 2:3], s2[:, 1:2], [B_st2[b]], [B_st2[b]])
        STT(xn[b][:], xn[b][:], s2[:, 2:3], fgs[:], ALU.mult, ALU.mult, [B_xn[b], B_st2[b], B_fg], [B_xn[b]])
        DMA(out_d[tt * 128:(tt + 1) * 128, :], xn[b][:], [B_xn[b]], [Buf("o")], "st_out%d" % b)
    sch.barrier()
    sch.emit()
    return nc, dbg_out


def _regload(e, reg, ap):
    return e.reg_load(reg, ap)


def _dyn(t, reg, const_off, pattern):
    return bass.AP(t, reg + const_off, pattern)


def _prep_inputs(S, x, c, norm_gain, w_ada, b_ada, w_in, b_gate_if, conv_w, conv_b, mlstm_norm_gain,
                 w_proj_attn, w_proj_mlstm, w_out, final_gain):
    f = np.float32
    NS = S // 128
    SQ = S // 4
    x = np.asarray(x, f)
    w_in0 = np.asarray(w_in, f)[0]
    ident = np.eye(128, dtype=f)
    tri = np.triu(np.ones((128, 128), f))
    ones = np.ones((128, 128), f)
    cst = np.ascontiguousarray(np.concatenate([ident, tri, ones], axis=1))
    xTs = []
    for b in range(2):
        a = x[b].reshape(NS // 4, 512, KC, 128).transpose(0, 3, 2, 1)
        xTs.append(np.ascontiguousarray(a).reshape(NS // 4, 128, KC * 512))
    w_ada0 = np.ascontiguousarray(np.asarray(w_ada, f)[0])
    b_row = np.ascontiguousarray(np.asarray(b_ada, f)[0].reshape(1, -1))
    ngc = np.ascontiguousarray(np.asarray(norm_gain, f)[0].reshape(KC, 128).T)
    w_g = np.ascontiguousarray(w_in0[:, O_GA:O_GA + 2 * D])
    w_pa = np.ascontiguousarray(np.asarray(w_proj_attn, f)[0])
    w_pm = np.ascontiguousarray(np.asarray(w_proj_mlstm, f)[0])
    w_o = np.ascontiguousarray(np.asarray(w_out, f)[0])
    fgb = np.ascontiguousarray(np.broadcast_to(np.asarray(final_gain, f)[None, :], (128, D)))
    cwf = np.asarray(conv_w, f)[0]
    cbf = np.asarray(conv_b, f)[0]
    bg = np.asarray(b_gate_if, f)[0]
    mgf = np.asarray(mlstm_norm_gain, f)[0]
    ki = np.arange(128)[:, None]
    qi = np.arange(128)[None, :]
    in_maps = []
    for core in range(8):
        b, g = core // 4, core % 4
        cols = []
        for off in (O_QA, O_KA, O_VA, O_ZA):
            for lh in range(2):
                h = 2 * g + lh
                cols.append(np.arange(off + h * 128, off + (h + 1) * 128))
        cols.append(np.arange(O_QM + g * 256, O_QM + (g + 1) * 256))
        cols.append(np.arange(O_KM + g * 256, O_KM + (g + 1) * 256))
        cols.append(np.arange(O_VM + g * 256, O_VM + (g + 1) * 256))
        cols.append(np.array([O_IF + g, O_IF + 4 + g]))
        cols.append(np.arange(O_OM + g * 256, O_OM + (g + 1) * 256))
        cols.append(np.arange(O_ZM + g * 256, O_ZM + (g + 1) * 256))
        cols = np.concatenate(cols)
        assert cols.size == 2306
        w1 = np.ascontiguousarray(w_in0[:, cols])
        cwc = np.zeros((128, 16), f)
        cbc = np.zeros((128, 4), f)
        for ch in range(4):
            base = (0 if ch < 2 else 1024) + g * 256 + (ch % 2) * 128
            cwc[:, ch * 4:(ch + 1) * 4] = cwf[:, base:base + 128].T
            cbc[:, ch] = cbf[base:base + 128]
        bifc = np.ascontiguousarray(np.broadcast_to(np.array([bg[g], bg[4 + g]], f)[None, :], (128, 2)))
        mgc = np.ascontiguousarray(np.broadcast_to(mgf[g * 256:(g + 1) * 256][None, :], (128, 256)))
        ab = np.zeros((128, 6, 256), f)
        for p, d in enumerate((1, 4, 16)):
            for lh in range(2):
                slope = 2.0 ** (-(2 * g + lh + 1))
                for half, shift in ((0, 128), (1, 0)):
                    delta = qi - ki + shift
                    valid = (delta >= 0) & (delta <= 128)
                    ab[:, p * 2 + lh, half * 128:(half + 1) * 128] = np.where(valid, -slope * d * delta, NEG)
        in_maps.append({
            "xT": xTs[b],
            "xtok": np.ascontiguousarray(x[b, g * SQ:(g + 1) * SQ, :]),
            "ccol": np.ascontiguousarray(np.asarray(c, f)[b].reshape(KC, 128).T),
            "w_ada": w_ada0, "b_row": b_row, "ng": ngc, "w1": w1, "w_g": w_g,
            "bif": bifc, "cw": cwc, "cb": cbc, "mg": mgc,
            "w_pa": w_pa, "w_pm": w_pm, "w_out": w_o, "fg": fgb, "cst": cst,
            "abias": np.ascontiguousarray(ab.reshape(128, 6 * 256)),
            "roff": np.array([[(g * 4 + f4) * 512 * SQ for f4 in range(4)]
                              + [(g * (SQ // 512) + t) * 128 * KC * 512 for t in range(SQ // 512)]], dtype=np.int32),
        })
    return in_maps


_STOP = None


def kernel(x, c, norm_gain, w_ada, b_ada, w_in, b_gate_if, conv_w, conv_b, mlstm_norm_gain,
           w_proj_attn, w_proj_mlstm, w_out, final_gain):
    S = int(np.asarray(x).shape[1])
    in_maps = _prep_inputs(S, x, c, norm_gain, w_ada, b_ada, w_in, b_gate_if, conv_w, conv_b, mlstm_norm_gain,
                           w_proj_attn, w_proj_mlstm, w_out, final_gain)
    nc, _ = build(S, stop=_STOP)
    res = run_bass_kernel_spmd(nc, in_maps, core_ids=list(range(8)))
    if _STOP is not None:
        return res.results
    SQ = S // 4
    out = np.zeros((2, S, D), np.float32)
    for core in range(8):
        b, g = core // 4, core % 4
        out[b, g * SQ:(g + 1) * SQ, :] = res.results[core]["out"]
    return out
```

```python
import numpy as np
import concourse.bass as bass
import concourse.mybir as mybir
from concourse.bass_utils import run_bass_kernel_spmd

F32 = mybir.dt.float32
BF16 = mybir.dt.bfloat16
I32 = mybir.dt.int32
AF = mybir.ActivationFunctionType
ALU = mybir.AluOpType

D = 2048
KC = 16
EPS = 1e-6
SEQ = 8192
NEG = -30000.0
LN16 = 2.772588722239781
O_QA, O_KA, O_VA, O_ZA, O_QM, O_KM, O_VM, O_OM, O_ZM, O_IF, O_GA, O_GB = (
    0, 1024, 2048, 3072, 4096, 5120, 6144, 7168, 8192, 9216, 9224, 11272)


class Buf:
    __slots__ = ("name", "w", "r", "excl")

    def __init__(self, name, excl=False):
        self.name = name
        self.w = None
        self.r = {}
        self.excl = excl


class Sched:
    ENG = ("pe", "act", "dve", "pool", "sp")

    def __init__(self, nc):
        self.nc = nc
        self.prog = {e: [] for e in self.ENG}
        self.cnt = {}
        self.known = {e: {} for e in self.ENG}
        self.sems = {}

    def op(self, eng, fn, reads=(), writes=(), dkey=None, dinc=16):
        deps = {}

        def add(tok):
            if tok is None:
                return
            k, v = tok
            if deps.get(k, 0) < v:
                deps[k] = v

        ex = [b for b in reads if b.excl]
        if ex:
            writes = list(writes) + ex
        for b in reads:
            add(b.w)
        for b in writes:
            add(b.w)
            for k, v in b.r.items():
                add((k, v))
        waits = []
        kn = self.known[eng]
        for k, v in deps.items():
            if eng == "pe" and k == "pe":
                continue
            if kn.get(k, 0) >= v:
                continue
            kn[k] = v
            waits.append((k, v))
        if dkey is None:
            key, inc = eng, 1
        else:
            key, inc = dkey, dinc
        self.cnt[key] = self.cnt.get(key, 0) + inc
        tok = (key, self.cnt[key])
        self.prog[eng].append((waits, fn, key, inc))
        for b in reads:
            if b.r.get(key, 0) < tok[1]:
                b.r[key] = tok[1]
        for b in writes:
            b.w = tok
            b.r = {}
        return tok

    def barrier(self):
        for e in self.ENG:
            waits = []
            for k, v in self.cnt.items():
                if (k == e and e == "pe") or k == "cc":
                    continue
                if self.known[e].get(k, 0) >= v:
                    continue
                self.known[e][k] = v
                waits.append((k, v))
            if waits:
                self.prog[e].append((waits, None, None, 0))

    def emit(self):
        nc = self.nc
        keys = list(self.cnt.keys())
        for k in keys:
            self.sems[k] = nc.alloc_semaphore("s_" + k)
        engmap = {"pe": "tensor", "act": "scalar", "dve": "vector", "pool": "gpsimd", "sp": "sync"}
        with nc.Block() as block:
            for e in self.ENG:
                prog = self.prog[e]

                def body(eng, prog=prog):
                    for waits, fn, key, inc in prog:
                        for k, v in waits:
                            eng.wait_ge(self.sems[k], v)
                        if fn is not None:
                            ins = fn(eng)
                            ins.then_inc(self.sems[key], inc)

                getattr(block, engmap[e])(body)


def build(S=SEQ, stop=None, mode=None, sub=99):
    assert S % 2048 == 0
    NT = S // 512
    NS = S // 128
    NU = S // 2048
    SQ = S // 4
    QT = SQ // 512
    nc = bass.Bass("TRN2", target_bir_lowering=False)
    sch = Sched(nc)

    def din(name, shape, dt=F32):
        if mode in ("pa_only", "pb_only", "ex_only") and name not in ("w1", "cst", "abias", "roff", "hT_in", "ccol", "ng", "bif", "cw", "cb", "mg"):
            shape = [1, 1]
        return nc.dram_tensor(name, list(shape), dt, kind="ExternalInput")

    xT = din("xT", [NT, 128, KC * 512])
    xtok = din("xtok", [SQ, D])
    ccol = din("ccol", [128, KC])
    w_ada = din("w_ada", [D, 3 * D])
    b_row = din("b_row", [1, 3 * D])
    ng = din("ng", [128, KC])
    w1 = din("w1", [D, 2306])
    w_gj = din("w_gj", [KC, 128, 2 * KC * 128])
    w_pj = din("w_pj", [KC, 128, 2 * 8 * 128])
    bif = din("bif", [128, 2])
    cw = din("cw", [128, 16])
    cb = din("cb", [128, 4])
    mg = din("mg", [128, 256])
    w_out = din("w_out", [D, D])
    fg = din("fg", [128, D])
    cst = din("cst", [128, 384])
    abias_d = din("abias", [128, 6 * 256])
    roff = din("roff", [1, 4 + QT], I32)
    out_d = nc.dram_tensor("out", [SQ, D], F32, kind="ExternalOutput")

    hT_d = nc.dram_tensor("hT_s", [NT, 128, KC * 512], BF16)
    hTo_d = nc.dram_tensor("hTo_s", [QT, 128, KC * 512], BF16)
    ybuf_d = nc.dram_tensor("ybuf_s", [4 * 512, SQ], BF16)
    yall_d = nc.dram_tensor("yall_s", [4 * 4 * 512, SQ], BF16)
    ymine_d = nc.dram_tensor("ymine_s", [4 * 512, SQ], BF16)
    maT_d = nc.dram_tensor("maT_s", [QT, 128, KC * 512], BF16)
    mgT_d = nc.dram_tensor("mgT_s", [QT, 128, KC * 512], BF16)
    dbg_out = {}

    SB_LO = 16512
    SB_HI = 229344
    state = {"off": SB_LO, "n": 0}

    def alloc(shape, dt, name=None):
        nbytes = int(np.prod(shape[1:])) * (4 if dt in (F32, I32) else 2)
        off = (state["off"] + 63) // 64 * 64
        assert off + nbytes <= SB_HI, ("SBUF overflow", name, off, nbytes)
        state["off"] = off + nbytes
        state["n"] += 1
        return nc.alloc_sbuf_tensor_at("%s_%d" % (name or "t", state["n"]), list(shape), dt, offset=off)

    def mark():
        return state["off"]

    def release(m):
        state["off"] = m

    psb = [nc.alloc_psum_tensor("ps%d" % i, [128, 512], F32) for i in range(8)]
    PB_ = [Buf("psum%d" % i, excl=True) for i in range(8)]

    def MM(out, lhsT, rhs, st, sp, R, W):
        return sch.op("pe", lambda e: e.matmul(out, lhsT=lhsT, rhs=rhs, start=st, stop=sp), R, W)

    def TR(out, in_, ident, R, W):
        return sch.op("pe", lambda e: e.transpose(out=out, in_=in_, identity=ident), R, W)

    def ACT(out, in_, func, R, W, scale=None, bias=None):
        def f(e):
            kw = {}
            if scale is not None:
                kw["scale"] = scale
            if bias is not None:
                kw["bias"] = bias
            return e.activation(out=out, in_=in_, func=func, **kw)
        return sch.op("act", f, R, W)

    def TT(eng, out, in0, in1, op, R, W):
        return sch.op(eng, lambda e: e.tensor_tensor(out=out, in0=in0, in1=in1, op=op), R, W)

    def TS(eng, out, in0, s1, s2, op0, op1, R, W):
        if op1 is None:
            return sch.op(eng, lambda e: e.tensor_scalar(out=out, in0=in0, scalar1=s1, scalar2=None, op0=op0), R, W)
        return sch.op(eng, lambda e: e.tensor_scalar(out=out, in0=in0, scalar1=s1, scalar2=s2, op0=op0, op1=op1), R, W)

    def STT(out, in0, scalar, in1, op0, op1, R, W):
        return sch.op("dve", lambda e: e.scalar_tensor_tensor(out=out, in0=in0, scalar=scalar, in1=in1, op0=op0, op1=op1), R, W)

    def CP(eng, out, in_, R, W):
        if eng == "act":
            return sch.op("act", lambda e: e.copy(out=out, in_=in_), R, W)
        return sch.op(eng, lambda e: e.tensor_copy(out=out, in_=in_), R, W)

    def RECIP(out, in_, R, W):
        return sch.op("dve", lambda e: e.reciprocal(out=out, in_=in_), R, W)

    def MEMSET(eng, ap, val, W):
        return sch.op(eng, lambda e: e.memset(ap, val), (), W)

    def DMA(out, in_, R, W, key, eng="sp", **kw):
        return sch.op(eng, lambda e: e.dma_start(out=out, in_=in_, **kw), R, W, dkey=key)

    def bc_mid(t, n, reps, off=0):
        a = t[:, off:off + n]
        return bass.AP(a.tensor, a.offset, [list(a.ap[0]), [0, reps], [1, n]])

    def finish(dumps):
        for name, src_ap, shape, dt in dumps:
            o = nc.dram_tensor("dbg_" + name, list(shape), dt, kind="ExternalOutput")
            DMA(o.ap(), src_ap, [], [Buf("dbg")], "dbg_" + name)
        sch.barrier()
        sch.emit()
        return nc, None

    ident_f = alloc([128, 128], F32, "identf")
    tri_f = alloc([128, 128], F32, "trif")
    ones_f = alloc([128, 128], F32, "onesf")
    ident_b = alloc([128, 128], BF16, "identb")
    ones_b = alloc([128, 128], BF16, "onesb")
    cst_sb = alloc([128, 384], F32, "cst")
    gate_bc = alloc([128, D], F32, "gatebc")
    c_f = alloc([128, KC], F32, "cf")
    c_b = alloc([128, KC], BF16, "cb16")
    ng_sb = alloc([128, KC], F32, "ng")
    A_col = alloc([128, KC], F32, "Acol")
    sh_col = alloc([128, KC], F32, "shcol")
    ri = alloc([1, 4 + QT], I32, "ri")
    B_cst = Buf("cst")
    B_mod = Buf("modrow")
    B_col = Buf("cols")
    B_ri = Buf("ri")

    DMA(cst_sb[:], cst.ap(), (), [B_cst], "ld_cst")
    DMA(c_f[:], ccol.ap(), (), [B_cst], "ld_cst")
    DMA(ng_sb[:], ng.ap(), (), [B_cst], "ld_cst")
    DMA(ri[:], roff.ap(), (), [B_ri], "ld_ri")
    CP("dve", ident_f[:], cst_sb[:, 0:128], [B_cst], [B_cst])
    CP("dve", tri_f[:], cst_sb[:, 128:256], [B_cst], [B_cst])
    CP("dve", ones_f[:], cst_sb[:, 256:384], [B_cst], [B_cst])
    CP("dve", ident_b[:], cst_sb[:, 0:128], [B_cst], [B_cst])
    CP("dve", ones_b[:], cst_sb[:, 256:384], [B_cst], [B_cst])
    CP("dve", c_b[:], c_f[:], [B_cst], [B_cst])

    m_persist = mark()

    if mode not in ('pa_only', 'pb_only', 'ex_only'):
        modrow = alloc([1, 3 * D], F32, "modrow")
        brow = alloc([1, 3 * D], F32, "brow")
        B_brow = Buf("brow")
        DMA(brow[:], b_row.ap(), (), [B_brow], "ld_brow")
        wa = [alloc([128, 2048], BF16, "wa") for _ in range(2)]
        B_wa = [Buf("wa0"), Buf("wa1")]
        n_wa = 0
        for r in range(3):
            for k in range(KC):
                b = n_wa % 2
                n_wa += 1
                DMA(wa[b][:], w_ada[k * 128:(k + 1) * 128, r * 2048:(r + 1) * 2048], (), [B_wa[b]],
                    "ld_wa%d" % b, eng="pool")
                for n in range(4):
                    MM(psb[n][0:1, :], c_b[:, k:k + 1], wa[b][:, n * 512:(n + 1) * 512], k == 0, k == KC - 1,
                       [B_wa[b], B_cst], [PB_[n]])
            for n in range(4):
                TT("dve", modrow[0:1, r * 2048 + n * 512: r * 2048 + (n + 1) * 512], psb[n][0:1, :],
                   brow[0:1, r * 2048 + n * 512: r * 2048 + (n + 1) * 512], ALU.add, [PB_[n], B_brow], [B_mod])
        for j in range(32):
            TR(psb[4][:, j:j + 1], modrow[0:1, j * 128:(j + 1) * 128], ident_f[0:1, 0:1], [B_mod, B_cst], [PB_[4]])
        CP("dve", sh_col[:], psb[4][:, 0:16], [PB_[4]], [B_col])
        STT(A_col[:], psb[4][:, 16:32], 1.0, ng_sb[:], ALU.add, ALU.mult, [PB_[4], B_cst], [B_col])
        B_gate = Buf("gate")
        for n in range(4):
            MM(psb[n][:, :], ones_f[0:1, :], modrow[0:1, 2 * D + n * 512: 2 * D + (n + 1) * 512], True, True,
               [B_mod, B_cst], [PB_[n]])
            CP("act", gate_bc[:, n * 512:(n + 1) * 512], psb[n][:, :], [PB_[n]], [B_gate])
        release(m_persist)
        sch.barrier()
        if stop == "mod":
            return finish([("A", A_col[:], [128, KC], F32), ("sh", sh_col[:], [128, KC], F32), ("gate", gate_bc[:], [128, D], F32)])

        xt = [alloc([128, KC * 512], F32, "xt") for _ in range(2)]
        sq = [alloc([128, KC * 512], BF16, "sq") for _ in range(2)]
        sd = [alloc([128, 512], F32, "sd") for _ in range(2)]
        rs = [alloc([128, 512], F32, "rs") for _ in range(2)]
        hts = [alloc([128, KC * 512], BF16, "hts") for _ in range(2)]
        B_xt = [Buf("xt0"), Buf("xt1")]
        B_sq = [Buf("sq0"), Buf("sq1")]
        B_sd = [Buf("sd0"), Buf("sd1")]
        B_rs = [Buf("rs0"), Buf("rs1")]
        B_hts = [Buf("hts0"), Buf("hts1")]
        B_hT = [Buf("hT%d" % i) for i in range(NT)]

        def p0_load(i):
            DMA(xt[i % 2][:], xT[i], (), [B_xt[i % 2]], "ld_xt%d" % (i % 2))

        def p0_a(i):
            b = i % 2
            for hf in range(2):
                cs = slice(hf * 8 * 512, (hf + 1) * 8 * 512)
                ACT(sq[b][:, cs], xt[b][:, cs], AF.Square, [B_xt[b]], [B_sq[b]])
            for k in range(KC):
                MM(psb[b][:, :], ones_b[:], sq[b][:, k * 512:(k + 1) * 512], k == 0, k == KC - 1, [B_sq[b], B_cst], [PB_[b]])

        def p0_a2(i):
            b = i % 2
            ACT(sd[b][:], psb[b][:, :], AF.Sqrt, [PB_[b]], [B_sd[b]], scale=1.0 / D, bias=EPS)
            RECIP(rs[b][:], sd[b][:], [B_sd[b]], [B_rs[b]])
            x3 = xt[b][:].rearrange("p (k t) -> p k t", k=KC)
            TT("dve", x3, x3, bc_mid(rs[b], 512, KC), ALU.mult, [B_xt[b], B_rs[b]], [B_xt[b]])

        def p0_b(i):
            b = i % 2
            for k in range(KC):
                ACT(hts[b][:, k * 512:(k + 1) * 512], xt[b][:, k * 512:(k + 1) * 512], AF.Identity, [B_xt[b], B_col], [B_hts[b]],
                    scale=A_col[:, k:k + 1], bias=sh_col[:, k:k + 1])
            DMA(hT_d[i], hts[b][:], [B_hts[b]], [B_hT[i]], "st_hts%d" % b)

        p0_load(0)
        if NT > 1:
            p0_load(1)
        p0_a(0)
        p0_a2(0)
        for i in range(NT):
            if i + 1 < NT:
                p0_a(i + 1)
            p0_b(i)
            if i + 1 < NT:
                p0_a2(i + 1)
            if i + 2 < NT:
                p0_load(i + 2)
        release(m_persist)
        sch.barrier()
        if stop == "p0":
            return finish([("hT", hT_d.ap(), [NT, 128, KC * 512], BF16)])

    else:
        B_hT = [Buf('hT%d' % i) for i in range(NT)]
        hT_d = din('hT_in', [NT, 128, KC * 512], BF16)
    B_hTo = Buf("hTo")
    for t in range(QT):
        reg = nc.gpsimd.alloc_register("rrh%d" % t)

        def f(e, reg=reg, idx=4 + t, t=t):
            e.reg_load(reg, ri[0:1, idx:idx + 1])
            return e.dma_start(out=hTo_d[t], in_=bass.AP(hT_d, reg, [[KC * 512, 128], [1, KC * 512]]))
        sch.op("pool", f, [B_hT[i] for i in range(NT)] + [B_ri], [B_hTo], dkey="cp_h")
    B_ybuf = [[Buf("ybuf%d_%d" % (q, f)) for f in range(4)] for q in range(4)]
    B_yallp = [Buf("yall%d" % i) for i in range(16)]
    groups = [[0, 1, 2, 3], [4, 5, 6, 7]]

    def emit_gather(q, f4):
        i_ = q * 4 + f4

        def emit_cc(e):
            return e.collective_compute("AllGather", ALU.bypass, replica_groups=groups,
                                        ins=[ybuf_d[q * 512 + f4 * 128: q * 512 + (f4 + 1) * 128, :]],
                                        outs=[yall_d[i_ * 512:(i_ + 1) * 512, :]])
        sch.op("pool", emit_cc, [B_ybuf[q][f4]], [B_yallp[i_]], dkey="cc", dinc=1)
    if mode not in ('pb_only', 'ex_only'):
        wA = alloc([128, KC * 1024], BF16, "wA")
        B_wA = Buf("wA")
        for k in range(KC):
            DMA(wA[:, k * 1024:(k + 1) * 1024], w1[k * 128:(k + 1) * 128, 0:1024], (), [B_wA], "ld_wA", eng="pool")
        abias = alloc([128, 6 * 256], F32, "abias")
        B_ab = Buf("abias")
        DMA(abias[:], abias_d.ap(), (), [B_ab], "ld_ab")
        ht = [alloc([128, KC * 512], BF16, "ht") for _ in range(2)]
        B_ht = [Buf("ht0"), Buf("ht1")]
        qT = alloc([128, 2 * 2048], BF16, "qT")
        kT = alloc([128, 2 * 4096], BF16, "kT")
        vT = alloc([128, 2 * 2048], BF16, "vT")
        zs = alloc([128, 2 * 2048], BF16, "zs")
        B_q = [[Buf("q") for _ in range(4)] for _ in range(2)]
        B_k = [[[Buf("k") for _ in range(4)] for _ in range(2)] for _ in range(2)]
        B_v = [[Buf("v") for _ in range(4)] for _ in range(2)]
        B_z = [[Buf("z") for _ in range(4)] for _ in range(2)]
        NSLOT = {1: 3, 4: 8, 16: 32}
        Vv = {d: alloc([128, 2 * NSLOT[d] * 128], BF16, "Vv%d" % d) for d in (1, 4, 16)}
        B_Vv = {d: [[Buf("vv") for _ in range(NSLOT[d])] for _ in range(2)] for d in (1, 4, 16)}
        accn = alloc([128, 2048], F32, "accn")
        accd = alloc([128, 2048], F32, "accd")
        B_an = [Buf("an") for _ in range(4)]
        B_ad = [Buf("ad") for _ in range(4)]
        sbs = [alloc([128, 256], F32, "sbs") for _ in range(2)]
        B_sbs = [Buf("sbs0"), Buf("sbs1")]
        pTs = [alloc([128, 256], BF16, "pT") for _ in range(3)]
        B_pT = [Buf("pT%d" % i) for i in range(3)]
        yst = [alloc([128, 2048], BF16, "yst") for _ in range(2)]
        B_yst = [Buf("yst0"), Buf("yst1")]

        def pa_load(i):
            DMA(ht[i % 2][:], hT_d[i], [B_hT[i]], [B_ht[i % 2]], "ld_ht%d" % (i % 2))

        psS = [psb[3][:, 0:256], psb[4][:, 0:256], psb[5][:, 0:256]]
        B_psS = [PB_[3], PB_[4], PB_[5]]
        psN = [psb[6][:, 0:128], psb[7][:, 0:128]]
        psD = [psb[6][:, 128:256], psb[7][:, 128:256]]
        B_psN = [PB_[6], PB_[7]]
        B_psD = [PB_[6], PB_[7]]
        psT = [psb[0][:].bitcast(BF16)[:, 0:128], psb[1][:].bitcast(BF16)[:, 0:128]]
        B_psT = [PB_[0], PB_[1]]
        cnt = {"ip": 0, "s": 0, "n": 0, "t": 0, "p": 0, "sb": 0}

        def pa_inproj(i):
            u, m = i // 4, i % 4
            slot = u % 2
            b = i % 2
            for c in range(8):
                pb = cnt["ip"] % 3
                cnt["ip"] += 1
                for k in range(KC):
                    MM(psb[pb][:, :], wA[:, k * 1024 + c * 128: k * 1024 + (c + 1) * 128], ht[b][:, k * 512:(k + 1) * 512],
                       k == 0, k == KC - 1, [B_wA, B_ht[b]], [PB_[pb]])
                h = c % 2
                kind = c // 2
                if kind == 0:
                    CP("dve", qT[:, h * 2048 + m * 512: h * 2048 + (m + 1) * 512], psb[pb][:, :], [PB_[pb]], [B_q[h][m]])
                elif kind == 1:
                    o = h * 4096 + slot * 2048 + m * 512
                    CP("dve", kT[:, o:o + 512], psb[pb][:, :], [PB_[pb]], [B_k[h][slot][m]])
                elif kind == 2:
                    CP("act", vT[:, h * 2048 + m * 512: h * 2048 + (m + 1) * 512], psb[pb][:, :], [PB_[pb]], [B_v[h][m]])
                else:
                    ACT(zs[:, h * 2048 + m * 512: h * 2048 + (m + 1) * 512], psb[pb][:, :], AF.Silu, [PB_[pb]], [B_z[h][m]])

        def blocks_for(u):
            L = []
            for mm in range(16):
                g = 16 * u + mm
                prev = None
                if g > 0:
                    prev = (u % 2, 128 * (mm - 1)) if mm > 0 else ((u - 1) % 2, 1920)
                L.append((0, 1, 128 * mm, 1, [mm // 4], prev, g % 3, (g - 1) % 3))
            for m4 in range(4):
                for r in range(4):
                    g = 4 * u + m4
                    prev = None
                    if g > 0:
                        prev = (u % 2, 512 * (m4 - 1) + r) if m4 > 0 else ((u - 1) % 2, 1536 + r)
                    L.append((1, 4, 512 * m4 + r, 4, [m4], prev, r * 2 + g % 2, r * 2 + (g - 1) % 2))
            for r in range(16):
                prev = ((u - 1) % 2, r) if u > 0 else None
                L.append((2, 16, r, 16, [0, 1, 2, 3], prev, r * 2 + u % 2, r * 2 + (u - 1) % 2))
            return L

        def pa_attn(u, h):
            if sub < 1:
                return
            slot = u % 2
            blks = blocks_for(u)
            pend = []

            def stage1(bi):
                p, d, c0, st, ms, prev, vc, vp = blks[bi]
                qcols = slice(h * 2048 + c0, h * 2048 + c0 + 127 * st + 1, st)
                si = cnt["s"] % 3
                cnt["s"] += 1
                ti = cnt["t"] % 2
                cnt["t"] += 1
                TR(psT[ti], vT[:, qcols], ident_b[:], [B_v[h][m] for m in ms] + [B_cst], [B_psT[ti]])
                ACT(Vv[d][:, (h * NSLOT[d] + vc) * 128:(h * NSLOT[d] + vc + 1) * 128], psT[ti], AF.Identity, [B_psT[ti]], [B_Vv[d][h][vc]])
                kc0 = h * 4096 + slot * 2048 + c0
                lo = 0
                if sub < 1.2:
                    return (bi, 0, 0)
                if prev is not None:
                    ps_, pc0 = prev
                    pk0 = h * 4096 + ps_ * 2048 + pc0
                    pm = sorted(set([(pc0 + j * st) // 512 for j in (0, 127)])) if st < 16 else [0, 1, 2, 3]
                    MM(psS[si][:, 0:128], kT[:, pk0: pk0 + 127 * st + 1: st], qT[:, qcols], True, True,
                       [B_k[h][ps_][m] for m in pm] + [B_q[h][m] for m in ms], [B_psS[si]])
                else:
                    lo = 128
                MM(psS[si][:, 128:256], kT[:, kc0: kc0 + 127 * st + 1: st], qT[:, qcols], True, True,
                   [B_k[h][slot][m] for m in ms] + [B_q[h][m] for m in ms], [B_psS[si]])
                if sub < 1.4:
                    return (bi, 0, 0)
                sbi = cnt["sb"] % 2
                cnt["sb"] += 1
                ab0 = (p * 2 + h) * 256
                STT(sbs[sbi][:, lo:256], psS[si][:, lo:256], 128.0 ** -0.5, abias[:, ab0 + lo: ab0 + 256], ALU.mult, ALU.add,
                    [B_psS[si], B_ab], [B_sbs[sbi]])
                if sub < 1.6:
                    return (bi, 0, 0)
                pi = cnt["p"] % 3
                cnt["p"] += 1
                ACT(pTs[pi][:, lo:256], sbs[sbi][:, lo:256], AF.Exp, [B_sbs[sbi]], [B_pT[pi]])
                return (bi, pi, lo)

            def stage2(info):
                bi, pi, lo = info
                p, d, c0, st, ms, prev, vc, vp = blks[bi]
                ni = cnt["n"] % 2
                cnt["n"] += 1
                vcur = Vv[d][:, (h * NSLOT[d] + vc) * 128:(h * NSLOT[d] + vc + 1) * 128]
                vprev = Vv[d][:, (h * NSLOT[d] + vp) * 128:(h * NSLOT[d] + vp + 1) * 128]
                if lo == 0:
                    MM(psN[ni], vprev, pTs[pi][:, 0:128], True, False, [B_Vv[d][h][vp], B_pT[pi]], [B_psN[ni]])
                    MM(psN[ni], vcur, pTs[pi][:, 128:256], False, True, [B_Vv[d][h][vc], B_pT[pi]], [B_psN[ni]])
                    MM(psD[ni], ones_b[:], pTs[pi][:, 0:128], True, False, [B_cst, B_pT[pi]], [B_psD[ni]])
                    MM(psD[ni], ones_b[:], pTs[pi][:, 128:256], False, True, [B_cst, B_pT[pi]], [B_psD[ni]])
                else:
                    MM(psN[ni], vcur, pTs[pi][:, 128:256], True, True, [B_Vv[d][h][vc], B_pT[pi]], [B_psN[ni]])
                    MM(psD[ni], ones_b[:], pTs[pi][:, 128:256], True, True, [B_cst, B_pT[pi]], [B_psD[ni]])
                ocols = slice(c0, c0 + 127 * st + 1, st)
                if p == 0:
                    CP("act", accn[:, ocols], psN[ni], [B_psN[ni]], [B_an[m] for m in ms])
                    CP("act", accd[:, ocols], psD[ni], [B_psD[ni]], [B_ad[m] for m in ms])
                else:
                    TT("dve", accn[:, ocols], psN[ni], accn[:, ocols], ALU.add, [B_psN[ni]] + [B_an[m] for m in ms], [B_an[m] for m in ms])
                    TT("dve", accd[:, ocols], psD[ni], accd[:, ocols], ALU.add, [B_psD[ni]] + [B_ad[m] for m in ms], [B_ad[m] for m in ms])

            for bi in range(len(blks)):
                pend.append(stage1(bi))
                if len(pend) > 1:
                    x_ = pend.pop(0)
                    if sub >= 2:
                        stage2(x_)
            while pend:
                x_ = pend.pop(0)
                if sub >= 2:
                    stage2(x_)
            if sub < 3:
                return
            yb = (u * 2 + h) % 2
            for m in range(4):
                cs = slice(m * 512, (m + 1) * 512)
                ACT(accd[:, cs], accd[:, cs], AF.Ln, [B_ad[m]], [B_ad[m]])
                ACT(accd[:, cs], accd[:, cs], AF.Exp, [B_ad[m]], [B_ad[m]], scale=-1.0)
                TT("pool", accn[:, cs], accn[:, cs], accd[:, cs], ALU.mult, [B_an[m], B_ad[m]], [B_an[m]])
                TT("pool", yst[yb][:, cs], accn[:, cs], zs[:, h * 2048 + m * 512: h * 2048 + (m + 1) * 512], ALU.mult,
                   [B_an[m], B_z[h][m]], [B_yst[yb]])
            t0 = u * 2048
            while t0 < (u + 1) * 2048:
                q = t0 // SQ
                n = min(SQ - (t0 % SQ), (u + 1) * 2048 - t0)
                DMA(ybuf_d[q * 512 + h * 128: q * 512 + (h + 1) * 128, (t0 % SQ):(t0 % SQ) + n],
                    yst[yb][:, t0 - u * 2048: t0 - u * 2048 + n], [B_yst[yb]], [B_ybuf[q][h]], "st_yst%d" % yb)
                t0 += n

        pa_load(0)
        for u in range(NU):
            for m in range(4):
                i = 4 * u + m
                if i + 1 < NT:
                    pa_load(i + 1)
                pa_inproj(i)
            for h in range(2):
                pa_attn(u, h)
        release(m_persist)
        sch.barrier()
        for q in range(4):
            for f4 in range(2):
                emit_gather(q, f4)
        if stop == "pa":
            if mode == "pa_only":
                return finish([("ybuf", ybuf_d.ap(), [2048, SQ], BF16), ("qT", qT[:], [128, 4096], BF16), ("kT", kT[:], [128, 8192], BF16),
                               ("accn", accn[:], [128, 2048], F32), ("accd", accd[:], [128, 2048], F32), ("pT", pTs[0][:], [128, 256], BF16)])
            return finish([("ybuf", ybuf_d.ap(), [2048, SQ], BF16), ("hT", hT_d.ap(), [NT, 128, KC * 512], BF16)])

    if mode != 'ex_only':
        wB = alloc([128, KC * 1282], BF16, "wB")
        B_wB = Buf("wB")
        for k in range(KC):
            DMA(wB[:, k * 1282:(k + 1) * 1282], w1[k * 128:(k + 1) * 128, 1024:2306], (), [B_wB], "ld_wB", eng="pool")
        ht = [alloc([128, KC * 512], BF16, "htb") for _ in range(2)]
        B_ht = [Buf("htb0"), Buf("htb1")]
        smalls = alloc([128, 16 + 4 + 2 + 256], F32, "smalls")
        B_sm = Buf("smalls")
        DMA(smalls[:, 0:16], cw.ap(), (), [B_sm], "ld_sm")
        DMA(smalls[:, 16:20], cb.ap(), (), [B_sm], "ld_sm")
        DMA(smalls[:, 20:22], bif.ap(), (), [B_sm], "ld_sm")
        DMA(smalls[:, 22:278], mg.ap(), (), [B_sm], "ld_sm")
        xq = [alloc([128, 515], F32, "xq") for _ in range(4)]
        B_xq = [Buf("xq%d" % i) for i in range(4)]
        cacc = [alloc([128, 512], F32, "cacc") for _ in range(2)]
        B_cacc = [Buf("cacc0"), Buf("cacc1")]
        qk = [[alloc([128, 512], BF16, "qk") for _ in range(4)] for _ in range(2)]
        B_qk = [[Buf("qk") for _ in range(4)] for _ in range(2)]
        vaug = [alloc([128, 257], BF16, "vaug") for _ in range(2)]
        B_vaug = [Buf("vaug0"), Buf("vaug1")]
        gsm = [alloc([128, 16], F32, "gsm") for _ in range(2)]
        B_gsm = [Buf("gsm0"), Buf("gsm1")]
        sgo = [alloc([128, 256], F32, "sgo") for _ in range(2)]
        szm = [alloc([128, 256], BF16, "szm") for _ in range(2)]
        B_sgo = [Buf("sgo0"), Buf("sgo1")]
        B_szm = [Buf("szm0"), Buf("szm1")]
        pTm = [alloc([128, 128], BF16, "pTm") for _ in range(2)]
        B_pTm = [Buf("pTm0"), Buf("pTm1")]
        kw = [alloc([128, 256], BF16, "kw") for _ in range(2)]
        B_kw = [Buf("kw0"), Buf("kw1")]
        Cst = alloc([128, 2 * 257], F32, "Cst")
        Cb = [alloc([128, 2 * 257], BF16, "Cb") for _ in range(2)]
        B_C = Buf("C")
        B_Cb = [Buf("Cb0"), Buf("Cb1")]
        hh = [alloc([128, 256], F32, "hh") for _ in range(2)]
        B_hh = [Buf("hh0"), Buf("hh1")]
        lnst = [alloc([128, 16], F32, "lnst") for _ in range(2)]
        B_lnst = [Buf("lnst0"), Buf("lnst1")]
        ym = [alloc([128, 256], BF16, "ym") for _ in range(2)]
        B_ym = [Buf("ym0"), Buf("ym1")]
        ystm = [alloc([128, 2 * 512], BF16, "ystm") for _ in range(2)]
        B_ystm = [Buf("ystm0"), Buf("ystm1")]
        for i in range(4):
            MEMSET("pool", xq[i][:, 0:3], 0.0, [B_xq[i]])
        for i in range(2):
            MEMSET("pool", vaug[i][:, 256:257], 1.0, [B_vaug[i]])
        MEMSET("pool", Cst[:], 0.0, [B_C])

        def pb_load(i):
            DMA(ht[i % 2][:], hT_d[i], [B_hT[i]], [B_ht[i % 2]], "ld_htb%d" % (i % 2))

        psSm = psb[4][:, 0:128]
        psG = psb[4][:, 128:130]
        psUn = psb[4][:, 130:132]
        B_psSm, B_psG, B_psUn = PB_[4], PB_[4], PB_[4]
        psH = psb[5][:, 0:257]
        B_psH = PB_[5]
        psU = psb[6][:, 0:512]
        B_psU = PB_[6]
        ps7 = psb[7][:].bitcast(BF16)
        psK = ps7[:, 0:256]
        psY = ps7[:, 256:512]
        B_psK, B_psY = PB_[7], PB_[7]

        def pb_inproj_feat(i):
            b = i % 2
            for ch in range(4):
                pbk = ch % 2
                for k in range(KC):
                    MM(psb[pbk][:, :], wB[:, k * 1282 + ch * 128: k * 1282 + (ch + 1) * 128], ht[b][:, k * 512:(k + 1) * 512],
                       k == 0, k == KC - 1, [B_wB, B_ht[b]], [PB_[pbk]])
                if i > 0:
                    CP("pool", xq[ch][:, 0:3], xq[ch][:, 512:515], [B_xq[ch]], [B_xq[ch]])
                CP("act", xq[ch][:, 3:515], psb[pbk][:, :], [PB_[pbk]], [B_xq[ch]])
                ca = ch % 2
                TS("dve", cacc[ca][:], xq[ch][:, 3:515], smalls[:, ch * 4 + 3: ch * 4 + 4], smalls[:, 16 + ch:17 + ch], ALU.mult, ALU.add,
                   [B_xq[ch], B_sm], [B_cacc[ca]])
                for j in range(3):
                    STT(cacc[ca][:], xq[ch][:, j:j + 512], smalls[:, ch * 4 + j: ch * 4 + j + 1], cacc[ca][:], ALU.mult, ALU.add,
                        [B_xq[ch], B_sm, B_cacc[ca]], [B_cacc[ca]])
                ACT(qk[b][ch][:], cacc[ca][:], AF.Silu, [B_cacc[ca]], [B_qk[b][ch]])

        def pb_chunk(i, s):
            c = 4 * i + s
            b = i % 2
            cb_ = c % 2
            tok = slice(s * 128, (s + 1) * 128)
            for k in range(KC):
                MM(psb[2][:, 0:258], ht[b][:, k * 512 + s * 128: k * 512 + (s + 1) * 128], wB[:, k * 1282 + 512: k * 1282 + 770],
                   k == 0, k == KC - 1, [B_wB, B_ht[b]], [PB_[2]])
            for k in range(KC):
                MM(psb[3][:, :], ht[b][:, k * 512 + s * 128: k * 512 + (s + 1) * 128], wB[:, k * 1282 + 770: k * 1282 + 1282],
                   k == 0, k == KC - 1, [B_wB, B_ht[b]], [PB_[3]])
            CP("act", vaug[cb_][:, 0:256], psb[2][:, 0:256], [PB_[2]], [B_vaug[cb_]])
            g = gsm[cb_]
            Bg = B_gsm[cb_]
            TT("dve", g[:, 0:2], psb[2][:, 256:258], smalls[:, 20:22], ALU.add, [PB_[2], B_sm], [Bg])
            ACT(g[:, 2:3], g[:, 1:2], AF.Exp, [Bg], [Bg], scale=-1.0)
            ACT(g[:, 3:4], g[:, 2:3], AF.Ln, [Bg], [Bg], bias=1.0)
            MM(psG[:, 0:1], tri_f[:], g[:, 3:4], True, True, [Bg, B_cst], [B_psG])
            MM(psG[:, 1:2], ones_f[:], g[:, 3:4], True, True, [Bg, B_cst], [B_psG])
            TT("dve", g[:, 4:5], g[:, 0:1], psG[:, 0:1], ALU.add, [Bg, B_psG], [Bg])
            ACT(g[:, 5:6], g[:, 4:5], AF.Exp, [Bg], [Bg], bias=-LN16)
            ACT(g[:, 6:7], psG[:, 0:1], AF.Exp, [B_psG], [Bg], scale=-1.0)
            TT("dve", g[:, 7:8], g[:, 4:5], psG[:, 1:2], ALU.subtract, [Bg, B_psG], [Bg])
            ACT(g[:, 8:9], g[:, 7:8], AF.Exp, [Bg], [Bg], bias=-LN16)
            ACT(g[:, 9:10], psG[:, 1:2], AF.Exp, [B_psG], [Bg], scale=-1.0)
            ACT(sgo[cb_][:], psb[3][:, 0:256], AF.Sigmoid, [PB_[3]], [B_sgo[cb_]])
            ACT(szm[cb_][:], psb[3][:, 256:512], AF.Silu, [PB_[3]], [B_szm[cb_]])
            for e2 in range(2):
                MM(psSm, qk[b][2 + e2][:, tok], qk[b][e2][:, tok], e2 == 0, e2 == 1, [B_qk[b][2 + e2], B_qk[b][e2]], [B_psSm])
            STT(pTm[cb_][:], psSm, g[:, 5:6], tri_f[:], ALU.mult, ALU.mult, [B_psSm, Bg, B_cst], [B_pTm[cb_]])
            for e2 in range(2):
                TR(psK[:, e2 * 128:(e2 + 1) * 128], qk[b][2 + e2][:, tok], ident_b[:], [B_qk[b][2 + e2], B_cst], [B_psK])
            ACT(kw[cb_][:], psK, AF.Copy, [B_psK, Bg], [B_kw[cb_]], scale=g[:, 8:9])
            for e2 in range(2):
                MM(psU[:, e2 * 256:(e2 + 1) * 256], kw[cb_][:, e2 * 128:(e2 + 1) * 128], vaug[cb_][:, 0:256], True, True,
                   [B_kw[cb_], B_vaug[cb_]], [B_psU])
            for e2 in range(2):
                MM(psUn[:, e2:e2 + 1], kw[cb_][:, e2 * 128:(e2 + 1) * 128], vaug[cb_][:, 256:257], True, True,
                   [B_kw[cb_], B_vaug[cb_]], [B_psUn])
            cbi = c % 2
            MM(psH, pTm[cb_][:], vaug[cb_][:], True, c == 0, [B_pTm[cb_], B_vaug[cb_]], [B_psH])
            if c > 0:
                for e2 in range(2):
                    MM(psH, qk[b][e2][:, tok], Cb[cbi][:, e2 * 257:(e2 + 1) * 257], False, e2 == 1,
                       [B_qk[b][e2], B_Cb[cbi]], [B_psH])
            C3 = Cst[:].rearrange("p (e f) -> p e f", e=2)
            STT(C3[:, :, 0:256], C3[:, :, 0:256], g[:, 9:10], psU.rearrange("p (e f) -> p e f", e=2), ALU.mult, ALU.add,
                [B_C, Bg, B_psU], [B_C])
            STT(C3[:, :, 256], C3[:, :, 256], g[:, 9:10], psUn, ALU.mult, ALU.add, [B_C, Bg, B_psUn], [B_C])
            CP("pool", Cb[1 - cbi][:], Cst[:], [B_C], [B_Cb[1 - cbi]])
            ACT(g[:, 10:11], psH[:, 256:257], AF.Abs, [B_psH, Bg], [Bg], scale=g[:, 6:7])
            TS("dve", g[:, 11:12], g[:, 10:11], 1.0, None, ALU.max, None, [Bg], [Bg])
            RECIP(g[:, 12:13], g[:, 11:12], [Bg], [Bg])
            TT("dve", g[:, 12:13], g[:, 12:13], g[:, 6:7], ALU.mult, [Bg], [Bg])
            STT(hh[cb_][:], psH[:, 0:256], g[:, 12:13], sgo[cb_][:], ALU.mult, ALU.mult, [B_psH, Bg, B_sgo[cb_]], [B_hh[cb_]])
            ls = lnst[cb_]
            Bl = B_lnst[cb_]
            sch.op("dve", lambda e: e.bn_stats(out=ls[:, 0:6], in_=hh[cb_][:]), [B_hh[cb_]], [Bl])
            sch.op("dve", lambda e: e.bn_aggr(out=ls[:, 6:8], in_=ls[:, 0:6]), [Bl], [Bl])
            ACT(ls[:, 8:9], ls[:, 7:8], AF.Sqrt, [Bl], [Bl], bias=EPS)
            RECIP(ls[:, 9:10], ls[:, 8:9], [Bl], [Bl])
            TS("dve", hh[cb_][:], hh[cb_][:], ls[:, 6:7], ls[:, 9:10], ALU.subtract, ALU.mult, [B_hh[cb_], Bl], [B_hh[cb_]])
            TT("pool", hh[cb_][:], hh[cb_][:], smalls[:, 22:278], ALU.mult, [B_hh[cb_], B_sm], [B_hh[cb_]])
            TT("pool", ym[cb_][:], hh[cb_][:], szm[cb_][:], ALU.mult, [B_hh[cb_], B_szm[cb_]], [B_ym[cb_]])
            for f2 in range(2):
                TR(psY[:, f2 * 128:(f2 + 1) * 128], ym[cb_][:, f2 * 128:(f2 + 1) * 128], ident_b[:], [B_ym[cb_], B_cst], [B_psY])
            y3 = ystm[b][:].rearrange("p (f t) -> p f t", f=2)[:, :, s * 128:(s + 1) * 128]
            ACT(y3, psY.rearrange("p (f t) -> p f t", f=2), AF.Identity, [B_psY], [B_ystm[b]])
            if s == 3:
                t0 = i * 512
                q = t0 // SQ
                for f2 in range(2):
                    DMA(ybuf_d[q * 512 + 256 + f2 * 128: q * 512 + 256 + (f2 + 1) * 128, (t0 % SQ):(t0 % SQ) + 512],
                        ystm[b][:, f2 * 512:(f2 + 1) * 512], [B_ystm[b]], [B_ybuf[q][2 + f2]], "st_ystm%d_%d" % (b, f2))
                if (i + 1) % QT == 0:
                    pend_g.append(q)

        pend_g = []
        pb_load(0)
        for i in range(NT):
            if i + 1 < NT:
                pb_load(i + 1)
            pb_inproj_feat(i)
            for s in range(4):
                pb_chunk(i, s)
                if s == 1 and pend_g:
                    q_ = pend_g.pop(0)
                    emit_gather(q_, 2)
                    emit_gather(q_, 3)
        while pend_g:
            q_ = pend_g.pop(0)
            emit_gather(q_, 2)
            emit_gather(q_, 3)
        release(m_persist)
        sch.barrier()
        if stop == "pb":
            return finish([("ybuf", ybuf_d.ap(), [2048, SQ], BF16), ("C", Cst[:], [128, 514], F32), ("gsm", gsm[0][:], [128, 16], F32),
                           ("hh", hh[0][:], [128, 256], F32), ("qk", qk[1][0][:], [128, 512], BF16)])

    if mode == 'ex_only':
        for q in range(4):
            for f4 in range(4):
                emit_gather(q, f4)
    B_yall = Buf("yall")
    B_ymine = Buf("ymine")
    allyb = [B_ybuf[q][f] for q in range(4) for f in range(4)]
    groups = [[0, 1, 2, 3], [4, 5, 6, 7]]
    for f4 in range(4):
        reg = nc.gpsimd.alloc_register("rry%d" % f4)

        def f(e, reg=reg, idx=f4, f4=f4):
            e.reg_load(reg, ri[0:1, idx:idx + 1])
            dst = ymine_d.ap().rearrange("(r f p) t -> f r p t", r=4, f=4)[f4]
            return e.dma_start(out=dst, in_=bass.AP(yall_d, reg, [[128 * SQ, 4], [SQ, 128], [1, SQ]]))
        sch.op("pool", f, B_yallp + [B_ri], [B_ymine], dkey="cp_y")
    sch.barrier()
    if stop == "ex":
        return finish([("ymine", ymine_d.ap(), [2048, SQ], BF16), ("hTo", hTo_d.ap(), [QT, 128, KC * 512], BF16)])

    B_mg = [[Buf("mg%d_%d" % (t, sl)) for sl in range(2)] for t in range(QT)]
    hto_all = alloc([128, QT * KC * 512], BF16, "htoall")
    yin_all = alloc([128, 16 * SQ], BF16, "yinall")
    B_htoa = [Buf("htoa%d" % t) for t in range(QT)]
    B_yina = Buf("yina")
    for t in range(QT):
        DMA(hto_all[:, t * KC * 512:(t + 1) * KC * 512], hTo_d[t], [B_hTo], [B_htoa[t]], "ld_htoa%d" % t)
    for a in range(2):
        for r in range(4):
            base = r * 512 + a * 256
            src = ymine_d[base: base + 256, :].rearrange("(c p) t -> p c t", p=128)
            dst = yin_all[:].rearrange("p (c t) -> p c t", c=16)[:, a * 8 + 2 * r: a * 8 + 2 * r + 2, :]
            DMA(dst, src, [B_ymine], [B_yina], "ld_yina")
    wgj = [alloc([128, 2 * KC * 128], BF16, "wgj") for _ in range(2)]
    wpj = [alloc([128, 2 * 8 * 128], BF16, "wpj") for _ in range(2)]
    B_wgj = [Buf("wgj0"), Buf("wgj1")]
    B_wpj = [Buf("wpj0"), Buf("wpj1")]
    sga = [alloc([128, 512], F32, "sga") for _ in range(2)]
    sgb = [alloc([128, 512], F32, "sgb") for _ in range(2)]
    mo = [alloc([128, 512], BF16, "mo") for _ in range(2)]
    B_sga = [Buf("sga0"), Buf("sga1")]
    B_sgb = [Buf("sgb0"), Buf("sgb1")]
    B_mo = [Buf("mo0"), Buf("mo1")]

    def p2_wload(j):
        b = j % 2
        for a in range(2):
            DMA(wgj[b][:, a * KC * 128:(a + 1) * KC * 128], w_gj[j, :, a * KC * 128:(a + 1) * KC * 128], (), [B_wgj[b]],
                "ld_wgj%d" % b, eng="pool")
        DMA(wpj[b][:], w_pj[j], (), [B_wpj[b]], "ld_wpj%d" % b, eng="pool")

    p2_wload(0)
    it = 0
    for j in range(KC):
        b = j % 2
        if j + 1 < KC:
            p2_wload(j + 1)
        for t in range(QT):
            pb0 = (it % 2) * 4
            si = it % 2
            it += 1
            for a in range(2):
                for k in range(KC):
                    MM(psb[pb0 + a][:, :], wgj[b][:, (a * KC + k) * 128:(a * KC + k + 1) * 128],
                       hto_all[:, (t * KC + k) * 512:(t * KC + k + 1) * 512], k == 0, k == KC - 1, [B_wgj[b], B_htoa[t]], [PB_[pb0 + a]])
            for a in range(2):
                for k in range(8):
                    MM(psb[pb0 + 2 + a][:, :], wpj[b][:, (a * 8 + k) * 128:(a * 8 + k + 1) * 128],
                       yin_all[:, (a * 8 + k) * SQ + t * 512:(a * 8 + k) * SQ + (t + 1) * 512], k == 0, k == 7, [B_wpj[b], B_yina], [PB_[pb0 + 2 + a]])
            ACT(sga[si][:], psb[pb0][:, :], AF.Sigmoid, [PB_[pb0]], [B_sga[si]])
            ACT(sgb[si][:], psb[pb0 + 1][:, :], AF.Sigmoid, [PB_[pb0 + 1]], [B_sgb[si]])
            TT("dve", sga[si][:], psb[pb0 + 2][:, :], sga[si][:], ALU.mult, [PB_[pb0 + 2], B_sga[si]], [B_sga[si]])
            TT("dve", sgb[si][:], psb[pb0 + 3][:, :], sgb[si][:], ALU.mult, [PB_[pb0 + 3], B_sgb[si]], [B_sgb[si]])
            TT("pool", mo[si][:], sga[si][:], sgb[si][:], ALU.add, [B_sga[si], B_sgb[si]], [B_mo[si]])
            DMA(mgT_d[t][:, j * 512:(j + 1) * 512], mo[si][:], [B_mo[si]], [B_mg[t][si]], "st_mo%d" % si)
    release(m_persist)
    sch.barrier()
    if stop == "p2":
        return finish([("mgT", mgT_d.ap(), [QT, 128, KC * 512], BF16)])

    wO = alloc([128, KC * D], BF16, "wO")
    B_wO = Buf("wO")
    for k in range(KC):
        DMA(wO[:, k * D:(k + 1) * D], w_out[k * 128:(k + 1) * 128, :], (), [B_wO], "ld_wO", eng="pool")
    fgs = alloc([128, D], F32, "fgs")
    B_fg = Buf("fg")
    DMA(fgs[:], fg.ap(), (), [B_fg], "ld_fg")
    xin = [alloc([128, D], F32, "xin") for _ in range(2)]
    xn = [alloc([128, D], F32, "xn") for _ in range(2)]
    mti = [alloc([128, KC * 512], BF16, "mti") for _ in range(2)]
    B_mti = [Buf("mti0"), Buf("mti1")]
    st2 = [alloc([128, 40], F32, "st2") for _ in range(2)]
    B_xin = [Buf("xin0"), Buf("xin1")]
    B_xn = [Buf("xn0"), Buf("xn1")]
    B_st2 = [Buf("st20"), Buf("st21")]
    NTT = SQ // 128

    def p2c_load(tt):
        DMA(xin[tt % 2][:], xtok[tt * 128:(tt + 1) * 128, :], (), [B_xin[tt % 2]], "ld_xin%d" % (tt % 2))
        if tt % 4 == 0:
            t_ = tt // 4
            DMA(mti[t_ % 2][:], mgT_d[t_], B_mg[t_], [B_mti[t_ % 2]], "ld_mti%d" % (t_ % 2))

    p2c_load(0)
    for tt in range(NTT):
        b = tt % 2
        if tt + 1 < NTT:
            p2c_load(tt + 1)
        t, sub = tt // 4, tt % 4
        for n in range(4):
            pbk = (tt % 2) * 4 + n
            for j in range(KC):
                MM(psb[pbk][:, :], mti[t % 2][:, j * 512 + sub * 128: j * 512 + (sub + 1) * 128], wO[:, j * D + n * 512: j * D + (n + 1) * 512],
                   j == 0, j == KC - 1, [B_wO, B_mti[t % 2]], [PB_[pbk]])
            cs = slice(n * 512, (n + 1) * 512)
            TT("dve", xn[b][:, cs], psb[pbk][:, :], gate_bc[:, cs], ALU.mult, [PB_[pbk], B_gate], [B_xn[b]])
            TT("pool", xn[b][:, cs], xn[b][:, cs], xin[b][:, cs], ALU.add, [B_xn[b], B_xin[b]], [B_xn[b]])
        s2 = st2[b]
        for n in range(4):
            sch.op("dve", lambda e, b=b, s2=s2, n=n: e.bn_stats(out=s2[:, 8 + n * 6: 8 + (n + 1) * 6], in_=xn[b][:, n * 512:(n + 1) * 512]),
                   [B_xn[b]], [B_st2[b]])
        sch.op("dve", lambda e, s2=s2: e.bn_aggr(out=s2[:, 4:6], in_=s2[:, 8:32]), [B_st2[b]], [B_st2[b]])
        STT(s2[:, 0:1], s2[:, 4:5], s2[:, 4:5], s2[:, 5:6], ALU.mult, ALU.add, [B_st2[b]], [B_st2[b]])
        ACT(s2[:, 1:2], s2[:, 0:1], AF.Sqrt, [B_st2[b]], [B_st2[b]], bias=EPS)
        RECIP(s2[:, 2:3], s2[:, 1:2], [B_st2[b]], [B_st2[b]])
        STT(xn[b][:], xn[b][:], s2[:, 2:3], fgs[:], ALU.mult, ALU.mult, [B_xn[b], B_st2[b], B_fg], [B_xn[b]])
        DMA(out_d[tt * 128:(tt + 1) * 128, :], xn[b][:], [B_xn[b]], [Buf("o")], "st_out%d" % b)
    sch.barrier()
    sch.emit()
    return nc, dbg_out


def _regload(e, reg, ap):
    return e.reg_load(reg, ap)


def _dyn(t, reg, const_off, pattern):
    return bass.AP(t, reg + const_off, pattern)


def _prep_inputs(S, x, c, norm_gain, w_ada, b_ada, w_in, b_gate_if, conv_w, conv_b, mlstm_norm_gain,
                 w_proj_attn, w_proj_mlstm, w_out, final_gain):
    f = np.float32
    NS = S // 128
    SQ = S // 4
    x = np.asarray(x, f)
    w_in0 = np.asarray(w_in, f)[0]
    ident = np.eye(128, dtype=f)
    tri = np.triu(np.ones((128, 128), f))
    ones = np.ones((128, 128), f)
    cst = np.ascontiguousarray(np.concatenate([ident, tri, ones], axis=1))
    xTs = []
    for b in range(2):
        a = x[b].reshape(NS // 4, 512, KC, 128).transpose(0, 3, 2, 1)
        xTs.append(np.ascontiguousarray(a).reshape(NS // 4, 128, KC * 512))
    w_ada0 = np.ascontiguousarray(np.asarray(w_ada, f)[0])
    b_row = np.ascontiguousarray(np.asarray(b_ada, f)[0].reshape(1, -1))
    ngc = np.ascontiguousarray(np.asarray(norm_gain, f)[0].reshape(KC, 128).T)
    wg4 = w_in0[:, O_GA:O_GA + 2 * D].reshape(KC, 128, 2, KC, 128)
    w_gj = np.ascontiguousarray(wg4.transpose(3, 1, 2, 0, 4)).reshape(KC, 128, 2 * KC * 128)
    wp4 = np.stack([np.asarray(w_proj_attn, f)[0], np.asarray(w_proj_mlstm, f)[0]], axis=0).reshape(2, 8, 128, KC, 128)
    w_pj = np.ascontiguousarray(wp4.transpose(3, 2, 0, 1, 4)).reshape(KC, 128, 2 * 8 * 128)
    w_o = np.ascontiguousarray(np.asarray(w_out, f)[0])
    fgb = np.ascontiguousarray(np.broadcast_to(np.asarray(final_gain, f)[None, :], (128, D)))
    cwf = np.asarray(conv_w, f)[0]
    cbf = np.asarray(conv_b, f)[0]
    bg = np.asarray(b_gate_if, f)[0]
    mgf = np.asarray(mlstm_norm_gain, f)[0]
    ki = np.arange(128)[:, None]
    qi = np.arange(128)[None, :]
    in_maps = []
    for core in range(8):
        b, g = core // 4, core % 4
        cols = []
        for off in (O_QA, O_KA, O_VA, O_ZA):
            for lh in range(2):
                h = 2 * g + lh
                cols.append(np.arange(off + h * 128, off + (h + 1) * 128))
        cols.append(np.arange(O_QM + g * 256, O_QM + (g + 1) * 256))
        cols.append(np.arange(O_KM + g * 256, O_KM + (g + 1) * 256))
        cols.append(np.arange(O_VM + g * 256, O_VM + (g + 1) * 256))
        cols.append(np.array([O_IF + g, O_IF + 4 + g]))
        cols.append(np.arange(O_OM + g * 256, O_OM + (g + 1) * 256))
        cols.append(np.arange(O_ZM + g * 256, O_ZM + (g + 1) * 256))
        cols = np.concatenate(cols)
        assert cols.size == 2306
        w1 = np.ascontiguousarray(w_in0[:, cols])
        cwc = np.zeros((128, 16), f)
        cbc = np.zeros((128, 4), f)
        for ch in range(4):
            base = (0 if ch < 2 else 1024) + g * 256 + (ch % 2) * 128
            cwc[:, ch * 4:(ch + 1) * 4] = cwf[:, base:base + 128].T
            cbc[:, ch] = cbf[base:base + 128]
        bifc = np.ascontiguousarray(np.broadcast_to(np.array([bg[g], bg[4 + g]], f)[None, :], (128, 2)))
        mgc = np.ascontiguousarray(np.broadcast_to(mgf[g * 256:(g + 1) * 256][None, :], (128, 256)))
        ab = np.zeros((128, 6, 256), f)
        for p, d in enumerate((1, 4, 16)):
            for lh in range(2):
                slope = 2.0 ** (-(2 * g + lh + 1))
                for half, shift in ((0, 128), (1, 0)):
                    delta = qi - ki + shift
                    valid = (delta >= 0) & (delta <= 128)
                    ab[:, p * 2 + lh, half * 128:(half + 1) * 128] = np.where(valid, -slope * d * delta, NEG)
        in_maps.append({
            "xT": xTs[b],
            "xtok": np.ascontiguousarray(x[b, g * SQ:(g + 1) * SQ, :]),
            "ccol": np.ascontiguousarray(np.asarray(c, f)[b].reshape(KC, 128).T),
            "w_ada": w_ada0, "b_row": b_row, "ng": ngc, "w1": w1, "w_gj": w_gj, "w_pj": w_pj,
            "bif": bifc, "cw": cwc, "cb": cbc, "mg": mgc,
            "w_out": w_o, "fg": fgb, "cst": cst,
            "abias": np.ascontiguousarray(ab.reshape(128, 6 * 256)),
            "roff": np.array([[(g * 4 + f4) * 512 * SQ for f4 in range(4)]
                              + [(g * (SQ // 512) + t) * 128 * KC * 512 for t in range(SQ // 512)]], dtype=np.int32),
        })
    return in_maps


_STOP = None


def kernel(x, c, norm_gain, w_ada, b_ada, w_in, b_gate_if, conv_w, conv_b, mlstm_norm_gain,
           w_proj_attn, w_proj_mlstm, w_out, final_gain):
    S = int(np.asarray(x).shape[1])
    in_maps = _prep_inputs(S, x, c, norm_gain, w_ada, b_ada, w_in, b_gate_if, conv_w, conv_b, mlstm_norm_gain,
                           w_proj_attn, w_proj_mlstm, w_out, final_gain)
    nc, _ = build(S, stop=_STOP)
    res = run_bass_kernel_spmd(nc, in_maps, core_ids=list(range(8)))
    if _STOP is not None:
        return res.results
    SQ = S // 4
    out = np.zeros((2, S, D), np.float32)
    for core in range(8):
        b, g = core // 4, core % 4
        out[b, g * SQ:(g + 1) * SQ, :] = res.results[core]["out"]
    return out
```

```python
import numpy as np
import concourse.bass as bass
import concourse.mybir as mybir
from concourse.bass_utils import run_bass_kernel_spmd

F32 = mybir.dt.float32
BF16 = mybir.dt.bfloat16
I32 = mybir.dt.int32
AF = mybir.ActivationFunctionType
ALU = mybir.AluOpType

D = 2048
KC = 16
EPS = 1e-6
SEQ = 8192
import os as _os
OPT_CONV = _os.environ.get("OPT_CONV", "1") == "1"
OPT_SIG = _os.environ.get("OPT_SIG", "1") == "1"
OPT_LN = _os.environ.get("OPT_LN", "1") == "1"
NEG = -30000.0
LN16 = 2.772588722239781
O_QA, O_KA, O_VA, O_ZA, O_QM, O_KM, O_VM, O_OM, O_ZM, O_IF, O_GA, O_GB = (
    0, 1024, 2048, 3072, 4096, 5120, 6144, 7168, 8192, 9216, 9224, 11272)


class Buf:
    __slots__ = ("name", "w", "r", "excl")

    def __init__(self, name, excl=False):
        self.name = name
        self.w = None
        self.r = {}
        self.excl = excl


class Sched:
    ENG = ("pe", "act", "dve", "pool", "sp")

    def __init__(self, nc):
        self.nc = nc
        self.prog = {e: [] for e in self.ENG}
        self.cnt = {}
        self.known = {e: {} for e in self.ENG}
        self.sems = {}

    def op(self, eng, fn, reads=(), writes=(), dkey=None, dinc=16):
        deps = {}

        def add(tok):
            if tok is None:
                return
            k, v = tok
            if deps.get(k, 0) < v:
                deps[k] = v

        ex = [b for b in reads if b.excl]
        if ex:
            writes = list(writes) + ex
        for b in reads:
            add(b.w)
        for b in writes:
            add(b.w)
            for k, v in b.r.items():
                add((k, v))
        waits = []
        kn = self.known[eng]
        for k, v in deps.items():
            if eng == "pe" and k == "pe":
                continue
            if kn.get(k, 0) >= v:
                continue
            kn[k] = v
            waits.append((k, v))
        if dkey is None:
            key, inc = eng, 1
        else:
            key, inc = dkey, dinc
        self.cnt[key] = self.cnt.get(key, 0) + inc
        tok = (key, self.cnt[key])
        self.prog[eng].append((waits, fn, key, inc))
        for b in reads:
            if b.r.get(key, 0) < tok[1]:
                b.r[key] = tok[1]
        for b in writes:
            b.w = tok
            b.r = {}
        return tok

    def barrier(self):
        for e in self.ENG:
            waits = []
            for k, v in self.cnt.items():
                if (k == e and e == "pe") or k == "cc":
                    continue
                if self.known[e].get(k, 0) >= v:
                    continue
                self.known[e][k] = v
                waits.append((k, v))
            if waits:
                self.prog[e].append((waits, None, None, 0))

    def emit(self):
        nc = self.nc
        keys = list(self.cnt.keys())
        for k in keys:
            self.sems[k] = nc.alloc_semaphore("s_" + k)
        engmap = {"pe": "tensor", "act": "scalar", "dve": "vector", "pool": "gpsimd", "sp": "sync"}
        with nc.Block() as block:
            for e in self.ENG:
                prog = self.prog[e]

                def body(eng, prog=prog):
                    for waits, fn, key, inc in prog:
                        for k, v in waits:
                            eng.wait_ge(self.sems[k], v)
                        if fn is not None:
                            ins = fn(eng)
                            ins.then_inc(self.sems[key], inc)

                getattr(block, engmap[e])(body)


def build(S=SEQ, stop=None, mode=None, sub=99):
    assert S % 2048 == 0
    NT = S // 512
    NS = S // 128
    NU = S // 2048
    SQ = S // 4
    QT = SQ // 512
    nc = bass.Bass("TRN2", target_bir_lowering=False)
    sch = Sched(nc)

    def din(name, shape, dt=F32):
        if mode in ("pa_only", "pb_only", "ex_only") and name not in ("w1", "cst", "abias", "roff", "hT_in", "ccol", "ng", "bif", "cw", "cb", "mg"):
            shape = [1, 1]
        return nc.dram_tensor(name, list(shape), dt, kind="ExternalInput")

    xT = din("xT", [NT, 128, KC * 512])
    xtok = din("xtok", [SQ, D])
    ccol = din("ccol", [128, KC])
    w_ada = din("w_ada", [D, 3 * D])
    b_row = din("b_row", [1, 3 * D])
    ng = din("ng", [128, KC])
    w1 = din("w1", [D, 2306])
    w_gj = din("w_gj", [KC, 128, 2 * KC * 128])
    w_pj = din("w_pj", [KC, 128, 2 * 8 * 128])
    bif = din("bif", [128, 2])
    cw = din("cw", [128, 16])
    cb = din("cb", [128, 4])
    mg = din("mg", [128, 256])
    w_out = din("w_out", [D, D])
    fg = din("fg", [128, D])
    cst = din("cst", [128, 384])
    abias_d = din("abias", [128, 6 * 256])
    roff = din("roff", [1, 4 + QT], I32)
    out_d = nc.dram_tensor("out", [SQ, D], F32, kind="ExternalOutput")

    hT_d = nc.dram_tensor("hT_s", [NT, 128, KC * 512], BF16)
    hTo_d = nc.dram_tensor("hTo_s", [QT, 128, KC * 512], BF16)
    ybuf_d = nc.dram_tensor("ybuf_s", [4 * 512, SQ], BF16)
    yall_d = nc.dram_tensor("yall_s", [4 * 4 * 512, SQ], BF16)
    ymine_d = nc.dram_tensor("ymine_s", [4 * 512, SQ], BF16)
    maT_d = nc.dram_tensor("maT_s", [QT, 128, KC * 512], BF16)
    mgT_d = nc.dram_tensor("mgT_s", [QT, 128, KC * 512], BF16)
    dbg_out = {}

    SB_LO = 16512
    SB_HI = 229344
    state = {"off": SB_LO, "n": 0}

    def alloc(shape, dt, name=None):
        nbytes = int(np.prod(shape[1:])) * (4 if dt in (F32, I32) else 2)
        off = (state["off"] + 63) // 64 * 64
        assert off + nbytes <= SB_HI, ("SBUF overflow", name, off, nbytes)
        state["off"] = off + nbytes
        state["n"] += 1
        return nc.alloc_sbuf_tensor_at("%s_%d" % (name or "t", state["n"]), list(shape), dt, offset=off)

    def alloc_top(shape, dt, name):
        nbytes = int(np.prod(shape[1:])) * (4 if dt in (F32, I32) else 2)
        off = (SB_HI - nbytes) // 64 * 64
        state["n"] += 1
        return nc.alloc_sbuf_tensor_at("%s_%d" % (name, state["n"]), list(shape), dt, offset=off), off

    def mark():
        return state["off"]

    def release(m):
        state["off"] = m

    psb = [nc.alloc_psum_tensor("ps%d" % i, [128, 512], F32) for i in range(8)]
    PB_ = [Buf("psum%d" % i, excl=True) for i in range(8)]

    def MM(out, lhsT, rhs, st, sp, R, W):
        return sch.op("pe", lambda e: e.matmul(out, lhsT=lhsT, rhs=rhs, start=st, stop=sp), R, W)

    def TR(out, in_, ident, R, W):
        return sch.op("pe", lambda e: e.transpose(out=out, in_=in_, identity=ident), R, W)

    def ACT(out, in_, func, R, W, scale=None, bias=None):
        def f(e):
            kw = {}
            if scale is not None:
                kw["scale"] = scale
            if bias is not None:
                kw["bias"] = bias
            return e.activation(out=out, in_=in_, func=func, **kw)
        return sch.op("act", f, R, W)

    def TT(eng, out, in0, in1, op, R, W):
        return sch.op(eng, lambda e: e.tensor_tensor(out=out, in0=in0, in1=in1, op=op), R, W)

    def TS(eng, out, in0, s1, s2, op0, op1, R, W):
        if op1 is None:
            return sch.op(eng, lambda e: e.tensor_scalar(out=out, in0=in0, scalar1=s1, scalar2=None, op0=op0), R, W)
        return sch.op(eng, lambda e: e.tensor_scalar(out=out, in0=in0, scalar1=s1, scalar2=s2, op0=op0, op1=op1), R, W)

    def STT(out, in0, scalar, in1, op0, op1, R, W):
        return sch.op("dve", lambda e: e.scalar_tensor_tensor(out=out, in0=in0, scalar=scalar, in1=in1, op0=op0, op1=op1), R, W)

    def CP(eng, out, in_, R, W):
        if eng == "act":
            return sch.op("act", lambda e: e.copy(out=out, in_=in_), R, W)
        return sch.op(eng, lambda e: e.tensor_copy(out=out, in_=in_), R, W)

    def RECIP(out, in_, R, W):
        return sch.op("dve", lambda e: e.reciprocal(out=out, in_=in_), R, W)

    def MEMSET(eng, ap, val, W):
        return sch.op(eng, lambda e: e.memset(ap, val), (), W)

    def DMA(out, in_, R, W, key, eng="sp", **kw):
        return sch.op(eng, lambda e: e.dma_start(out=out, in_=in_, **kw), R, W, dkey=key)

    def bc_mid(t, n, reps, off=0):
        a = t[:, off:off + n]
        return bass.AP(a.tensor, a.offset, [list(a.ap[0]), [0, reps], [1, n]])

    def finish(dumps):
        for name, src_ap, shape, dt in dumps:
            o = nc.dram_tensor("dbg_" + name, list(shape), dt, kind="ExternalOutput")
            DMA(o.ap(), src_ap, [], [Buf("dbg")], "dbg_" + name)
        sch.barrier()
        sch.emit()
        return nc, None

    ident_f = alloc([128, 128], F32, "identf")
    tri_f = alloc([128, 128], F32, "trif")
    ones_f = alloc([128, 128], F32, "onesf")
    ident_b = alloc([128, 128], BF16, "identb")
    ones_b = alloc([128, 128], BF16, "onesb")
    cst_sb = alloc([128, 384], F32, "cst")
    gate_bc = alloc([128, D], F32, "gatebc")
    c_f = alloc([128, KC], F32, "cf")
    c_b = alloc([128, KC], BF16, "cb16")
    ng_sb = alloc([128, KC], F32, "ng")
    A_col = alloc([128, KC], F32, "Acol")
    sh_col = alloc([128, KC], F32, "shcol")
    ri = alloc([1, 4 + QT], I32, "ri")
    B_cst = Buf("cst")
    B_mod = Buf("modrow")
    B_col = Buf("cols")
    B_ri = Buf("ri")

    DMA(cst_sb[:], cst.ap(), (), [B_cst], "ld_cst")
    DMA(c_f[:], ccol.ap(), (), [B_cst], "ld_cst")
    DMA(ng_sb[:], ng.ap(), (), [B_cst], "ld_cst")
    DMA(ri[:], roff.ap(), (), [B_ri], "ld_ri")
    CP("dve", ident_f[:], cst_sb[:, 0:128], [B_cst], [B_cst])
    CP("dve", tri_f[:], cst_sb[:, 128:256], [B_cst], [B_cst])
    CP("dve", ones_f[:], cst_sb[:, 256:384], [B_cst], [B_cst])
    CP("dve", ident_b[:], cst_sb[:, 0:128], [B_cst], [B_cst])
    CP("dve", ones_b[:], cst_sb[:, 256:384], [B_cst], [B_cst])
    CP("dve", c_b[:], c_f[:], [B_cst], [B_cst])

    m_persist = mark()

    if mode not in ('pa_only', 'pb_only', 'ex_only'):
        modrow = alloc([1, 3 * D], F32, "modrow")
        brow = alloc([1, 3 * D], F32, "brow")
        B_brow = Buf("brow")
        DMA(brow[:], b_row.ap(), (), [B_brow], "ld_brow")
        wa = [alloc([128, 2048], BF16, "wa") for _ in range(2)]
        B_wa = [Buf("wa0"), Buf("wa1")]
        n_wa = 0
        for r in range(3):
            for k in range(KC):
                b = n_wa % 2
                n_wa += 1
                DMA(wa[b][:], w_ada[k * 128:(k + 1) * 128, r * 2048:(r + 1) * 2048], (), [B_wa[b]],
                    "ld_wa%d" % b, eng="pool")
                for n in range(4):
                    MM(psb[n][0:1, :], c_b[:, k:k + 1], wa[b][:, n * 512:(n + 1) * 512], k == 0, k == KC - 1,
                       [B_wa[b], B_cst], [PB_[n]])
            for n in range(4):
                TT("dve", modrow[0:1, r * 2048 + n * 512: r * 2048 + (n + 1) * 512], psb[n][0:1, :],
                   brow[0:1, r * 2048 + n * 512: r * 2048 + (n + 1) * 512], ALU.add, [PB_[n], B_brow], [B_mod])
        for j in range(32):
            TR(psb[4][:, j:j + 1], modrow[0:1, j * 128:(j + 1) * 128], ident_f[0:1, 0:1], [B_mod, B_cst], [PB_[4]])
        CP("dve", sh_col[:], psb[4][:, 0:16], [PB_[4]], [B_col])
        STT(A_col[:], psb[4][:, 16:32], 1.0, ng_sb[:], ALU.add, ALU.mult, [PB_[4], B_cst], [B_col])
        B_gate = Buf("gate")
        for n in range(4):
            MM(psb[n][:, :], ones_f[0:1, :], modrow[0:1, 2 * D + n * 512: 2 * D + (n + 1) * 512], True, True,
               [B_mod, B_cst], [PB_[n]])
            CP("act", gate_bc[:, n * 512:(n + 1) * 512], psb[n][:, :], [PB_[n]], [B_gate])
        release(m_persist)
        sch.barrier()
        if stop == "mod":
            return finish([("A", A_col[:], [128, KC], F32), ("sh", sh_col[:], [128, KC], F32), ("gate", gate_bc[:], [128, D], F32)])

        xt = [alloc([128, KC * 512], F32, "xt") for _ in range(3)]
        sq = [alloc([128, KC * 512], BF16, "sq") for _ in range(2)]
        sd = [alloc([128, 512], F32, "sd") for _ in range(2)]
        rs = [alloc([128, 512], F32, "rs") for _ in range(2)]
        hts = [alloc([128, KC * 512], BF16, "hts") for _ in range(2)]
        B_xt = [Buf("xt0"), Buf("xt1"), Buf("xt2")]
        B_sq = [Buf("sq0"), Buf("sq1")]
        B_sd = [Buf("sd0"), Buf("sd1")]
        B_rs = [Buf("rs0"), Buf("rs1")]
        B_hts = [Buf("hts0"), Buf("hts1")]
        B_hT = [Buf("hT%d" % i) for i in range(NT)]

        def p0_load(i):
            DMA(xt[i % 3][:], xT[i], (), [B_xt[i % 3]], "ld_xt%d" % (i % 3))

        def p0_a(i):
            b = i % 2
            for hf in range(2):
                cs = slice(hf * 8 * 512, (hf + 1) * 8 * 512)
                ACT(sq[b][:, cs], xt[i % 3][:, cs], AF.Square, [B_xt[i % 3]], [B_sq[b]])
            for k in range(KC):
                MM(psb[b][:, :], ones_b[:], sq[b][:, k * 512:(k + 1) * 512], k == 0, k == KC - 1, [B_sq[b], B_cst], [PB_[b]])

        def p0_a2(i):
            b = i % 2
            ACT(sd[b][:], psb[b][:, :], AF.Sqrt, [PB_[b]], [B_sd[b]], scale=1.0 / D, bias=EPS)
            RECIP(rs[b][:], sd[b][:], [B_sd[b]], [B_rs[b]])
            x3 = xt[i % 3][:].rearrange("p (k t) -> p k t", k=KC)
            TT("dve", x3, x3, bc_mid(rs[b], 512, KC), ALU.mult, [B_xt[i % 3], B_rs[b]], [B_xt[i % 3]])

        def p0_b(i):
            b = i % 2
            for k in range(KC):
                ACT(hts[b][:, k * 512:(k + 1) * 512], xt[i % 3][:, k * 512:(k + 1) * 512], AF.Identity, [B_xt[i % 3], B_col], [B_hts[b]],
                    scale=A_col[:, k:k + 1], bias=sh_col[:, k:k + 1])
            DMA(hT_d[i], hts[b][:], [B_hts[b]], [B_hT[i]], "st_hts%d" % b)

        p0_load(0)
        p0_load(1)
        p0_load(2)
        p0_a(0)
        p0_a2(0)
        for i in range(NT):
            if i + 1 < NT:
                p0_a(i + 1)
            p0_b(i)
            if i + 1 < NT:
                p0_a2(i + 1)
            if i + 3 < NT:
                p0_load(i + 3)
        release(m_persist)
        sch.barrier()
        if stop == "p0":
            return finish([("hT", hT_d.ap(), [NT, 128, KC * 512], BF16)])

    else:
        B_hT = [Buf('hT%d' % i) for i in range(NT)]
        hT_d = din('hT_in', [NT, 128, KC * 512], BF16)
    B_hTo = Buf("hTo")
    for t in range(QT):
        reg = nc.gpsimd.alloc_register("rrh%d" % t)

        def f(e, reg=reg, idx=4 + t, t=t):
            e.reg_load(reg, ri[0:1, idx:idx + 1])
            return e.dma_start(out=hTo_d[t], in_=bass.AP(hT_d, reg, [[KC * 512, 128], [1, KC * 512]]))
        sch.op("pool", f, [B_hT[i] for i in range(NT)] + [B_ri], [B_hTo], dkey="cp_h")
    hto_all, hto_off = alloc_top([128, QT * KC * 512], BF16, "htoall")
    B_htoa = [Buf("htoa%d" % t) for t in range(QT)]
    B_ybuf = [[Buf("ybuf%d_%d" % (q, f)) for f in range(4)] for q in range(4)]
    B_yallp = [Buf("yall%d" % i) for i in range(16)]
    groups = [[0, 1, 2, 3], [4, 5, 6, 7]]

    def emit_gather(q, f4):
        i_ = q * 4 + f4

        def emit_cc(e):
            return e.collective_compute("AllGather", ALU.bypass, replica_groups=groups,
                                        ins=[ybuf_d[q * 512 + f4 * 128: q * 512 + (f4 + 1) * 128, :]],
                                        outs=[yall_d[i_ * 512:(i_ + 1) * 512, :]])
        sch.op("pool", emit_cc, [B_ybuf[q][f4]], [B_yallp[i_]], dkey="cc", dinc=1)
    if mode not in ('pb_only', 'ex_only'):
        wA = alloc([128, KC * 1024], BF16, "wA")
        B_wA = Buf("wA")
        for k in range(KC):
            DMA(wA[:, k * 1024:(k + 1) * 1024], w1[k * 128:(k + 1) * 128, 0:1024], (), [B_wA], "ld_wA", eng="pool")
        abias = alloc([128, 6 * 256], F32, "abias")
        B_ab = Buf("abias")
        DMA(abias[:], abias_d.ap(), (), [B_ab], "ld_ab")
        ht = [alloc([128, KC * 512], BF16, "ht") for _ in range(2)]
        B_ht = [Buf("ht0"), Buf("ht1")]
        qT = alloc([128, 2 * 2048], BF16, "qT")
        kT = alloc([128, 2 * 4096], BF16, "kT")
        vT = alloc([128, 2 * 2048], BF16, "vT")
        zs = alloc([128, 2 * 2048], BF16, "zs")
        B_q = [[Buf("q") for _ in range(4)] for _ in range(2)]
        B_k = [[[Buf("k") for _ in range(4)] for _ in range(2)] for _ in range(2)]
        B_v = [[Buf("v") for _ in range(4)] for _ in range(2)]
        B_z = [[Buf("z") for _ in range(4)] for _ in range(2)]
        NSLOT = {1: 3, 4: 8, 16: 32}
        Vv = {d: alloc([128, 2 * NSLOT[d] * 128], BF16, "Vv%d" % d) for d in (1, 4, 16)}
        B_Vv = {d: [[Buf("vv") for _ in range(NSLOT[d])] for _ in range(2)] for d in (1, 4, 16)}
        accn = alloc([128, 2048], F32, "accn")
        accd = alloc([128, 2048], F32, "accd")
        B_an = [Buf("an") for _ in range(4)]
        B_ad = [Buf("ad") for _ in range(4)]
        sbs = [alloc([128, 256], F32, "sbs") for _ in range(2)]
        B_sbs = [Buf("sbs0"), Buf("sbs1")]
        pTs = [alloc([128, 256], BF16, "pT") for _ in range(3)]
        B_pT = [Buf("pT%d" % i) for i in range(3)]
        yst = [alloc([128, 2048], BF16, "yst") for _ in range(2)]
        B_yst = [Buf("yst0"), Buf("yst1")]

        def pa_load(i):
            DMA(ht[i % 2][:], hT_d[i], [B_hT[i]], [B_ht[i % 2]], "ld_ht%d" % (i % 2))

        psS = [psb[3][:, 0:256], psb[4][:, 0:256], psb[5][:, 0:256]]
        B_psS = [PB_[3], PB_[4], PB_[5]]
        psN = [psb[6][:, 0:128], psb[7][:, 0:128]]
        psD = [psb[6][:, 128:256], psb[7][:, 128:256]]
        B_psN = [PB_[6], PB_[7]]
        B_psD = [PB_[6], PB_[7]]
        psT = [psb[0][:].bitcast(BF16)[:, 0:128], psb[1][:].bitcast(BF16)[:, 0:128]]
        B_psT = [PB_[0], PB_[1]]
        cnt = {"ip": 0, "s": 0, "n": 0, "t": 0, "p": 0, "sb": 0}

        def pa_inproj(i):
            u, m = i // 4, i % 4
            slot = u % 2
            b = i % 2
            for c in range(8):
                pb = cnt["ip"] % 3
                cnt["ip"] += 1
                for k in range(KC):
                    MM(psb[pb][:, :], wA[:, k * 1024 + c * 128: k * 1024 + (c + 1) * 128], ht[b][:, k * 512:(k + 1) * 512],
                       k == 0, k == KC - 1, [B_wA, B_ht[b]], [PB_[pb]])
                h = c % 2
                kind = c // 2
                if kind == 0:
                    CP("dve", qT[:, h * 2048 + m * 512: h * 2048 + (m + 1) * 512], psb[pb][:, :], [PB_[pb]], [B_q[h][m]])
                elif kind == 1:
                    o = h * 4096 + slot * 2048 + m * 512
                    CP("dve", kT[:, o:o + 512], psb[pb][:, :], [PB_[pb]], [B_k[h][slot][m]])
                elif kind == 2:
                    CP("act", vT[:, h * 2048 + m * 512: h * 2048 + (m + 1) * 512], psb[pb][:, :], [PB_[pb]], [B_v[h][m]])
                else:
                    ACT(zs[:, h * 2048 + m * 512: h * 2048 + (m + 1) * 512], psb[pb][:, :], AF.Silu, [PB_[pb]], [B_z[h][m]])

        def blocks_for(u):
            L = []
            for mm in range(16):
                g = 16 * u + mm
                prev = None
                if g > 0:
                    prev = (u % 2, 128 * (mm - 1)) if mm > 0 else ((u - 1) % 2, 1920)
                L.append((0, 1, 128 * mm, 1, [mm // 4], prev, g % 3, (g - 1) % 3))
            for m4 in range(4):
                for r in range(4):
                    g = 4 * u + m4
                    prev = None
                    if g > 0:
                        prev = (u % 2, 512 * (m4 - 1) + r) if m4 > 0 else ((u - 1) % 2, 1536 + r)
                    L.append((1, 4, 512 * m4 + r, 4, [m4], prev, r * 2 + g % 2, r * 2 + (g - 1) % 2))
            for r in range(16):
                prev = ((u - 1) % 2, r) if u > 0 else None
                L.append((2, 16, r, 16, [0, 1, 2, 3], prev, r * 2 + u % 2, r * 2 + (u - 1) % 2))
            return L

        def pa_attn(u, h):
            if sub < 1:
                return
            slot = u % 2
            blks = blocks_for(u)
            pend = []

            def stage1(bi):
                p, d, c0, st, ms, prev, vc, vp = blks[bi]
                qcols = slice(h * 2048 + c0, h * 2048 + c0 + 127 * st + 1, st)
                si = cnt["s"] % 3
                cnt["s"] += 1
                ti = cnt["t"] % 2
                cnt["t"] += 1
                TR(psT[ti], vT[:, qcols], ident_b[:], [B_v[h][m] for m in ms] + [B_cst], [B_psT[ti]])
                ACT(Vv[d][:, (h * NSLOT[d] + vc) * 128:(h * NSLOT[d] + vc + 1) * 128], psT[ti], AF.Identity, [B_psT[ti]], [B_Vv[d][h][vc]])
                kc0 = h * 4096 + slot * 2048 + c0
                lo = 0
                if sub < 1.2:
                    return (bi, 0, 0)
                if prev is not None:
                    ps_, pc0 = prev
                    pk0 = h * 4096 + ps_ * 2048 + pc0
                    pm = sorted(set([(pc0 + j * st) // 512 for j in (0, 127)])) if st < 16 else [0, 1, 2, 3]
                    MM(psS[si][:, 0:128], kT[:, pk0: pk0 + 127 * st + 1: st], qT[:, qcols], True, True,
                       [B_k[h][ps_][m] for m in pm] + [B_q[h][m] for m in ms], [B_psS[si]])
                else:
                    lo = 128
                MM(psS[si][:, 128:256], kT[:, kc0: kc0 + 127 * st + 1: st], qT[:, qcols], True, True,
                   [B_k[h][slot][m] for m in ms] + [B_q[h][m] for m in ms], [B_psS[si]])
                if sub < 1.4:
                    return (bi, 0, 0)
                sbi = cnt["sb"] % 2
                cnt["sb"] += 1
                ab0 = (p * 2 + h) * 256
                STT(sbs[sbi][:, lo:256], psS[si][:, lo:256], 128.0 ** -0.5, abias[:, ab0 + lo: ab0 + 256], ALU.mult, ALU.add,
                    [B_psS[si], B_ab], [B_sbs[sbi]])
                if sub < 1.6:
                    return (bi, 0, 0)
                pi = cnt["p"] % 3
                cnt["p"] += 1
                ACT(pTs[pi][:, lo:256], sbs[sbi][:, lo:256], AF.Exp, [B_sbs[sbi]], [B_pT[pi]])
                return (bi, pi, lo)

            def stage2(info):
                bi, pi, lo = info
                p, d, c0, st, ms, prev, vc, vp = blks[bi]
                ni = cnt["n"] % 2
                cnt["n"] += 1
                vcur = Vv[d][:, (h * NSLOT[d] + vc) * 128:(h * NSLOT[d] + vc + 1) * 128]
                vprev = Vv[d][:, (h * NSLOT[d] + vp) * 128:(h * NSLOT[d] + vp + 1) * 128]
                if lo == 0:
                    MM(psN[ni], vprev, pTs[pi][:, 0:128], True, False, [B_Vv[d][h][vp], B_pT[pi]], [B_psN[ni]])
                    MM(psN[ni], vcur, pTs[pi][:, 128:256], False, True, [B_Vv[d][h][vc], B_pT[pi]], [B_psN[ni]])
                    MM(psD[ni], ones_b[:], pTs[pi][:, 0:128], True, False, [B_cst, B_pT[pi]], [B_psD[ni]])
                    MM(psD[ni], ones_b[:], pTs[pi][:, 128:256], False, True, [B_cst, B_pT[pi]], [B_psD[ni]])
                else:
                    MM(psN[ni], vcur, pTs[pi][:, 128:256], True, True, [B_Vv[d][h][vc], B_pT[pi]], [B_psN[ni]])
                    MM(psD[ni], ones_b[:], pTs[pi][:, 128:256], True, True, [B_cst, B_pT[pi]], [B_psD[ni]])
                ocols = slice(c0, c0 + 127 * st + 1, st)
                if p == 0:
                    CP("act", accn[:, ocols], psN[ni], [B_psN[ni]], [B_an[m] for m in ms])
                    CP("act", accd[:, ocols], psD[ni], [B_psD[ni]], [B_ad[m] for m in ms])
                else:
                    TT("dve", accn[:, ocols], psN[ni], accn[:, ocols], ALU.add, [B_psN[ni]] + [B_an[m] for m in ms], [B_an[m] for m in ms])
                    TT("dve", accd[:, ocols], psD[ni], accd[:, ocols], ALU.add, [B_psD[ni]] + [B_ad[m] for m in ms], [B_ad[m] for m in ms])

            for bi in range(len(blks)):
                pend.append(stage1(bi))
                if len(pend) > 1:
                    x_ = pend.pop(0)
                    if sub >= 2:
                        stage2(x_)
            while pend:
                x_ = pend.pop(0)
                if sub >= 2:
                    stage2(x_)
            if sub < 3:
                return
            yb = (u * 2 + h) % 2
            for m in range(4):
                cs = slice(m * 512, (m + 1) * 512)
                ACT(accd[:, cs], accd[:, cs], AF.Ln, [B_ad[m]], [B_ad[m]])
                ACT(accd[:, cs], accd[:, cs], AF.Exp, [B_ad[m]], [B_ad[m]], scale=-1.0)
                TT("pool", accn[:, cs], accn[:, cs], accd[:, cs], ALU.mult, [B_an[m], B_ad[m]], [B_an[m]])
                TT("pool", yst[yb][:, cs], accn[:, cs], zs[:, h * 2048 + m * 512: h * 2048 + (m + 1) * 512], ALU.mult,
                   [B_an[m], B_z[h][m]], [B_yst[yb]])
            t0 = u * 2048
            while t0 < (u + 1) * 2048:
                q = t0 // SQ
                n = min(SQ - (t0 % SQ), (u + 1) * 2048 - t0)
                DMA(ybuf_d[q * 512 + h * 128: q * 512 + (h + 1) * 128, (t0 % SQ):(t0 % SQ) + n],
                    yst[yb][:, t0 - u * 2048: t0 - u * 2048 + n], [B_yst[yb]], [B_ybuf[q][h]], "st_yst%d" % yb)
                t0 += n

        pa_load(0)
        for u in range(NU):
            for m in range(4):
                i = 4 * u + m
                if i + 1 < NT:
                    pa_load(i + 1)
                pa_inproj(i)
            for h in range(2):
                pa_attn(u, h)
        release(m_persist)
        sch.barrier()
        for q in range(4):
            for f4 in range(2):
                emit_gather(q, f4)
        if stop == "pa":
            if mode == "pa_only":
                return finish([("ybuf", ybuf_d.ap(), [2048, SQ], BF16), ("qT", qT[:], [128, 4096], BF16), ("kT", kT[:], [128, 8192], BF16),
                               ("accn", accn[:], [128, 2048], F32), ("accd", accd[:], [128, 2048], F32), ("pT", pTs[0][:], [128, 256], BF16)])
            return finish([("ybuf", ybuf_d.ap(), [2048, SQ], BF16), ("hT", hT_d.ap(), [NT, 128, KC * 512], BF16)])

    if mode != 'ex_only':
        wB = alloc([128, KC * 1282], BF16, "wB")
        B_wB = Buf("wB")
        for k in range(KC):
            DMA(wB[:, k * 1282:(k + 1) * 1282], w1[k * 128:(k + 1) * 128, 1024:2306], (), [B_wB], "ld_wB", eng="pool")
        ht = [alloc([128, KC * 512], BF16, "htb") for _ in range(2)]
        B_ht = [Buf("htb0"), Buf("htb1")]
        for t in range(QT):
            DMA(hto_all[:, t * KC * 512:(t + 1) * KC * 512], hTo_d[t], [B_hTo], [B_htoa[t]], "ld_htoa%d" % t)
        smalls = alloc([128, 16 + 4 + 2 + 256], F32, "smalls")
        B_sm = Buf("smalls")
        DMA(smalls[:, 0:16], cw.ap(), (), [B_sm], "ld_sm")
        DMA(smalls[:, 16:20], cb.ap(), (), [B_sm], "ld_sm")
        DMA(smalls[:, 20:22], bif.ap(), (), [B_sm], "ld_sm")
        DMA(smalls[:, 22:278], mg.ap(), (), [B_sm], "ld_sm")
        xq = [alloc([128, 515], F32, "xq") for _ in range(4)]
        B_xq = [Buf("xq%d" % i) for i in range(4)]
        cacc = [alloc([128, 512], F32, "cacc") for _ in range(2)]
        B_cacc = [Buf("cacc0"), Buf("cacc1")]
        csig = [alloc([128, 512], F32, "csig") for _ in range(2)]
        B_csig = [Buf("csig0"), Buf("csig1")]
        qk = [[alloc([128, 512], BF16, "qk") for _ in range(4)] for _ in range(2)]
        B_qk = [[Buf("qk") for _ in range(4)] for _ in range(2)]
        vaug = [alloc([128, 257], BF16, "vaug") for _ in range(2)]
        B_vaug = [Buf("vaug0"), Buf("vaug1")]
        gsm = [alloc([128, 16], F32, "gsm") for _ in range(2)]
        B_gsm = [Buf("gsm0"), Buf("gsm1")]
        sgo = [alloc([128, 512], F32, "sgo") for _ in range(2)]
        szm = [alloc([128, 256], BF16, "szm") for _ in range(2)]
        B_sgo = [Buf("sgo0"), Buf("sgo1")]
        B_szm = [Buf("szm0"), Buf("szm1")]
        pTm = [alloc([128, 128], BF16, "pTm") for _ in range(2)]
        B_pTm = [Buf("pTm0"), Buf("pTm1")]
        kw = [alloc([128, 256], BF16, "kw") for _ in range(2)]
        B_kw = [Buf("kw0"), Buf("kw1")]
        Cst = alloc([128, 2 * 257], F32, "Cst")
        Cb = [alloc([128, 2 * 257], BF16, "Cb") for _ in range(2)]
        B_C = Buf("C")
        B_Cb = [Buf("Cb0"), Buf("Cb1")]
        hh = [alloc([128, 256], F32, "hh") for _ in range(2)]
        B_hh = [Buf("hh0"), Buf("hh1")]
        lnst = [alloc([128, 16], F32, "lnst") for _ in range(2)]
        B_lnst = [Buf("lnst0"), Buf("lnst1")]
        ym = [alloc([128, 256], BF16, "ym") for _ in range(2)]
        B_ym = [Buf("ym0"), Buf("ym1")]
        ystm = [alloc([128, 2 * 512], BF16, "ystm") for _ in range(2)]
        B_ystm = [Buf("ystm0"), Buf("ystm1")]
        for i in range(4):
            MEMSET("pool", xq[i][:, 0:3], 0.0, [B_xq[i]])
        for i in range(2):
            MEMSET("pool", vaug[i][:, 256:257], 1.0, [B_vaug[i]])
        MEMSET("pool", Cst[:], 0.0, [B_C])

        assert state["off"] < hto_off, (state["off"], hto_off)

        def pb_load(i):
            DMA(ht[i % 2][:], hT_d[i], [B_hT[i]], [B_ht[i % 2]], "ld_htb%d" % (i % 2))

        psSm = psb[4][:, 0:128]
        psG = psb[4][:, 128:130]
        psUn = psb[4][:, 130:132]
        B_psSm, B_psG, B_psUn = PB_[4], PB_[4], PB_[4]
        psH = psb[5][:, 0:257]
        B_psH = PB_[5]
        psU = psb[6][:, 0:512]
        B_psU = PB_[6]
        ps7 = psb[7][:].bitcast(BF16)
        psK = ps7[:, 0:256]
        psY = ps7[:, 256:512]
        B_psK, B_psY = PB_[7], PB_[7]

        def pb_inproj_feat(i):
            b = i % 2
            for ch in range(4):
                pbk = ch % 2
                for k in range(KC):
                    MM(psb[pbk][:, :], wB[:, k * 1282 + ch * 128: k * 1282 + (ch + 1) * 128], ht[b][:, k * 512:(k + 1) * 512],
                       k == 0, k == KC - 1, [B_wB, B_ht[b]], [PB_[pbk]])
                if i > 0:
                    CP("pool", xq[ch][:, 0:3], xq[ch][:, 512:515], [B_xq[ch]], [B_xq[ch]])
                CP("act", xq[ch][:, 3:515], psb[pbk][:, :], [PB_[pbk]], [B_xq[ch]])
                ca = ch % 2
                TS("dve", cacc[ca][:], xq[ch][:, 3:515], smalls[:, ch * 4 + 3: ch * 4 + 4], smalls[:, 16 + ch:17 + ch], ALU.mult, ALU.add,
                   [B_xq[ch], B_sm], [B_cacc[ca]])
                for j in range(3):
                    STT(cacc[ca][:], xq[ch][:, j:j + 512], smalls[:, ch * 4 + j: ch * 4 + j + 1], cacc[ca][:], ALU.mult, ALU.add,
                        [B_xq[ch], B_sm, B_cacc[ca]], [B_cacc[ca]])
                if OPT_CONV:
                    ACT(csig[ca][:], cacc[ca][:], AF.Sigmoid, [B_cacc[ca]], [B_csig[ca]])
                    TT("pool", qk[b][ch][:], cacc[ca][:], csig[ca][:], ALU.mult, [B_cacc[ca], B_csig[ca]], [B_qk[b][ch]])
                else:
                    ACT(qk[b][ch][:], cacc[ca][:], AF.Silu, [B_cacc[ca]], [B_qk[b][ch]])

        def pb_chunk(i, s):
            c = 4 * i + s
            b = i % 2
            cb_ = c % 2
            tok = slice(s * 128, (s + 1) * 128)
            for k in range(KC):
                MM(psb[2][:, 0:258], ht[b][:, k * 512 + s * 128: k * 512 + (s + 1) * 128], wB[:, k * 1282 + 512: k * 1282 + 770],
                   k == 0, k == KC - 1, [B_wB, B_ht[b]], [PB_[2]])
            for k in range(KC):
                MM(psb[3][:, :], ht[b][:, k * 512 + s * 128: k * 512 + (s + 1) * 128], wB[:, k * 1282 + 770: k * 1282 + 1282],
                   k == 0, k == KC - 1, [B_wB, B_ht[b]], [PB_[3]])
            CP("act", vaug[cb_][:, 0:256], psb[2][:, 0:256], [PB_[2]], [B_vaug[cb_]])
            g = gsm[cb_]
            Bg = B_gsm[cb_]
            TT("dve", g[:, 0:2], psb[2][:, 256:258], smalls[:, 20:22], ALU.add, [PB_[2], B_sm], [Bg])
            ACT(g[:, 2:3], g[:, 1:2], AF.Exp, [Bg], [Bg], scale=-1.0)
            ACT(g[:, 3:4], g[:, 2:3], AF.Ln, [Bg], [Bg], bias=1.0)
            MM(psG[:, 0:1], tri_f[:], g[:, 3:4], True, True, [Bg, B_cst], [B_psG])
            MM(psG[:, 1:2], ones_f[:], g[:, 3:4], True, True, [Bg, B_cst], [B_psG])
            TT("dve", g[:, 4:5], g[:, 0:1], psG[:, 0:1], ALU.add, [Bg, B_psG], [Bg])
            ACT(g[:, 5:6], g[:, 4:5], AF.Exp, [Bg], [Bg], bias=-LN16)
            ACT(g[:, 6:7], psG[:, 0:1], AF.Exp, [B_psG], [Bg], scale=-1.0)
            TT("dve", g[:, 7:8], g[:, 4:5], psG[:, 1:2], ALU.subtract, [Bg, B_psG], [Bg])
            ACT(g[:, 8:9], g[:, 7:8], AF.Exp, [Bg], [Bg], bias=-LN16)
            ACT(g[:, 9:10], psG[:, 1:2], AF.Exp, [B_psG], [Bg], scale=-1.0)
            if OPT_SIG:
                ACT(sgo[cb_][:], psb[3][:, 0:512], AF.Sigmoid, [PB_[3]], [B_sgo[cb_]])
                TT("dve", szm[cb_][:], psb[3][:, 256:512], sgo[cb_][:, 256:512], ALU.mult, [PB_[3], B_sgo[cb_]], [B_szm[cb_]])
            else:
                ACT(sgo[cb_][:, 0:256], psb[3][:, 0:256], AF.Sigmoid, [PB_[3]], [B_sgo[cb_]])
                ACT(szm[cb_][:], psb[3][:, 256:512], AF.Silu, [PB_[3]], [B_szm[cb_]])
            for e2 in range(2):
                MM(psSm, qk[b][2 + e2][:, tok], qk[b][e2][:, tok], e2 == 0, e2 == 1, [B_qk[b][2 + e2], B_qk[b][e2]], [B_psSm])
            STT(pTm[cb_][:], psSm, g[:, 5:6], tri_f[:], ALU.mult, ALU.mult, [B_psSm, Bg, B_cst], [B_pTm[cb_]])
            for e2 in range(2):
                TR(psK[:, e2 * 128:(e2 + 1) * 128], qk[b][2 + e2][:, tok], ident_b[:], [B_qk[b][2 + e2], B_cst], [B_psK])
            ACT(kw[cb_][:], psK, AF.Copy, [B_psK, Bg], [B_kw[cb_]], scale=g[:, 8:9])
            for e2 in range(2):
                MM(psU[:, e2 * 256:(e2 + 1) * 256], kw[cb_][:, e2 * 128:(e2 + 1) * 128], vaug[cb_][:, 0:256], True, True,
                   [B_kw[cb_], B_vaug[cb_]], [B_psU])
            for e2 in range(2):
                MM(psUn[:, e2:e2 + 1], kw[cb_][:, e2 * 128:(e2 + 1) * 128], vaug[cb_][:, 256:257], True, True,
                   [B_kw[cb_], B_vaug[cb_]], [B_psUn])
            cbi = c % 2
            MM(psH, pTm[cb_][:], vaug[cb_][:], True, c == 0, [B_pTm[cb_], B_vaug[cb_]], [B_psH])
            if c > 0:
                for e2 in range(2):
                    MM(psH, qk[b][e2][:, tok], Cb[cbi][:, e2 * 257:(e2 + 1) * 257], False, e2 == 1,
                       [B_qk[b][e2], B_Cb[cbi]], [B_psH])
            C3 = Cst[:].rearrange("p (e f) -> p e f", e=2)
            STT(C3[:, :, 0:256], C3[:, :, 0:256], g[:, 9:10], psU.rearrange("p (e f) -> p e f", e=2), ALU.mult, ALU.add,
                [B_C, Bg, B_psU], [B_C])
            STT(C3[:, :, 256], C3[:, :, 256], g[:, 9:10], psUn, ALU.mult, ALU.add, [B_C, Bg, B_psUn], [B_C])
            CP("pool", Cb[1 - cbi][:], Cst[:], [B_C], [B_Cb[1 - cbi]])
            ACT(g[:, 10:11], psH[:, 256:257], AF.Abs, [B_psH, Bg], [Bg], scale=g[:, 6:7])
            TS("dve", g[:, 11:12], g[:, 10:11], 1.0, None, ALU.max, None, [Bg], [Bg])
            RECIP(g[:, 12:13], g[:, 11:12], [Bg], [Bg])
            TT("dve", g[:, 12:13], g[:, 12:13], g[:, 6:7], ALU.mult, [Bg], [Bg])
            STT(hh[cb_][:], psH[:, 0:256], g[:, 12:13], sgo[cb_][:, 0:256], ALU.mult, ALU.mult, [B_psH, Bg, B_sgo[cb_]], [B_hh[cb_]])
            ls = lnst[cb_]
            Bl = B_lnst[cb_]
            sch.op("dve", lambda e: e.bn_stats(out=ls[:, 0:6], in_=hh[cb_][:]), [B_hh[cb_]], [Bl])
            sch.op("dve", lambda e: e.bn_aggr(out=ls[:, 6:8], in_=ls[:, 0:6]), [Bl], [Bl])
            if OPT_LN:
                ACT(ls[:, 8:9], ls[:, 7:8], AF.Ln, [Bl], [Bl], bias=EPS)
                ACT(ls[:, 9:10], ls[:, 8:9], AF.Exp, [Bl], [Bl], scale=-0.5)
            else:
                ACT(ls[:, 8:9], ls[:, 7:8], AF.Sqrt, [Bl], [Bl], bias=EPS)
                RECIP(ls[:, 9:10], ls[:, 8:9], [Bl], [Bl])
            TS("dve", hh[cb_][:], hh[cb_][:], ls[:, 6:7], ls[:, 9:10], ALU.subtract, ALU.mult, [B_hh[cb_], Bl], [B_hh[cb_]])
            TT("pool", hh[cb_][:], hh[cb_][:], smalls[:, 22:278], ALU.mult, [B_hh[cb_], B_sm], [B_hh[cb_]])
            TT("pool", ym[cb_][:], hh[cb_][:], szm[cb_][:], ALU.mult, [B_hh[cb_], B_szm[cb_]], [B_ym[cb_]])
            for f2 in range(2):
                TR(psY[:, f2 * 128:(f2 + 1) * 128], ym[cb_][:, f2 * 128:(f2 + 1) * 128], ident_b[:], [B_ym[cb_], B_cst], [B_psY])
            y3 = ystm[b][:].rearrange("p (f t) -> p f t", f=2)[:, :, s * 128:(s + 1) * 128]
            ACT(y3, psY.rearrange("p (f t) -> p f t", f=2), AF.Identity, [B_psY], [B_ystm[b]])
            if s == 3:
                t0 = i * 512
                q = t0 // SQ
                for f2 in range(2):
                    DMA(ybuf_d[q * 512 + 256 + f2 * 128: q * 512 + 256 + (f2 + 1) * 128, (t0 % SQ):(t0 % SQ) + 512],
                        ystm[b][:, f2 * 512:(f2 + 1) * 512], [B_ystm[b]], [B_ybuf[q][2 + f2]], "st_ystm%d_%d" % (b, f2))
                if (i + 1) % QT == 0:
                    pend_g.append(q)

        pend_g = []
        pb_load(0)
        for i in range(NT):
            if i + 1 < NT:
                pb_load(i + 1)
            pb_inproj_feat(i)
            for s in range(4):
                pb_chunk(i, s)
                if s == 1 and pend_g:
                    q_ = pend_g.pop(0)
                    emit_gather(q_, 2)
                    emit_gather(q_, 3)
        while pend_g:
            q_ = pend_g.pop(0)
            emit_gather(q_, 2)
            emit_gather(q_, 3)
        release(m_persist)
        sch.barrier()
        if stop == "pb":
            return finish([("ybuf", ybuf_d.ap(), [2048, SQ], BF16), ("C", Cst[:], [128, 514], F32), ("gsm", gsm[0][:], [128, 16], F32),
                           ("hh", hh[0][:], [128, 256], F32), ("qk", qk[1][0][:], [128, 512], BF16)])

    if mode == 'ex_only':
        for q in range(4):
            for f4 in range(4):
                emit_gather(q, f4)
    B_yall = Buf("yall")
    B_ymine = Buf("ymine")
    allyb = [B_ybuf[q][f] for q in range(4) for f in range(4)]
    groups = [[0, 1, 2, 3], [4, 5, 6, 7]]
    yregs = [nc.gpsimd.alloc_register("rry%d" % f4) for f4 in range(4)]
    sch.barrier()
    if stop == "ex":
        return finish([("hTo", hTo_d.ap(), [QT, 128, KC * 512], BF16)])

    B_mg = [[Buf("mg%d_%d" % (t, sl)) for sl in range(2)] for t in range(QT)]
    yin_all = alloc([128, 16 * SQ], BF16, "yinall")
    B_yina = Buf("yina")
    assert state["off"] + 40000 < hto_off
    for f4 in range(4):
        a_, c_ = f4 // 2, f4 % 2

        def f(e, reg=yregs[f4], idx=f4, a_=a_, c_=c_):
            e.reg_load(reg, ri[0:1, idx:idx + 1])
            y3 = yin_all[:].rearrange("p (c t) -> p c t", c=16)
            dst = y3[:, a_ * 8 + c_: a_ * 8 + c_ + 7: 2, :]
            return e.dma_start(out=dst, in_=bass.AP(yall_d, reg, [[SQ, 128], [128 * SQ, 4], [1, SQ]]))
        sch.op("pool", f, B_yallp + [B_ri], [B_yina], dkey="ld_yina")
    wgj = [alloc([128, 2 * KC * 128], BF16, "wgj") for _ in range(2)]
    wpj = [alloc([128, 2 * 8 * 128], BF16, "wpj") for _ in range(2)]
    B_wgj = [Buf("wgj0"), Buf("wgj1")]
    B_wpj = [Buf("wpj0"), Buf("wpj1")]
    sga = [alloc([128, 512], F32, "sga") for _ in range(2)]
    sgb = [alloc([128, 512], F32, "sgb") for _ in range(2)]
    mo = [alloc([128, 512], BF16, "mo") for _ in range(2)]
    B_sga = [Buf("sga0"), Buf("sga1")]
    B_sgb = [Buf("sgb0"), Buf("sgb1")]
    B_mo = [Buf("mo0"), Buf("mo1")]

    def p2_wload(j):
        b = j % 2
        for a in range(2):
            DMA(wgj[b][:, a * KC * 128:(a + 1) * KC * 128], w_gj[j, :, a * KC * 128:(a + 1) * KC * 128], (), [B_wgj[b]],
                "ld_wgj%d" % b, eng="pool")
        DMA(wpj[b][:], w_pj[j], (), [B_wpj[b]], "ld_wpj%d" % b, eng="pool")

    p2_wload(0)
    it = 0
    for j in range(KC):
        b = j % 2
        if j + 1 < KC:
            p2_wload(j + 1)
        for t in range(QT):
            pb0 = (it % 2) * 4
            si = it % 2
            it += 1
            for a in range(2):
                for k in range(KC):
                    MM(psb[pb0 + a][:, :], wgj[b][:, (a * KC + k) * 128:(a * KC + k + 1) * 128],
                       hto_all[:, (t * KC + k) * 512:(t * KC + k + 1) * 512], k == 0, k == KC - 1, [B_wgj[b], B_htoa[t]], [PB_[pb0 + a]])
            for a in range(2):
                for k in range(8):
                    MM(psb[pb0 + 2 + a][:, :], wpj[b][:, (a * 8 + k) * 128:(a * 8 + k + 1) * 128],
                       yin_all[:, (a * 8 + k) * SQ + t * 512:(a * 8 + k) * SQ + (t + 1) * 512], k == 0, k == 7, [B_wpj[b], B_yina], [PB_[pb0 + 2 + a]])
            ACT(sga[si][:], psb[pb0][:, :], AF.Sigmoid, [PB_[pb0]], [B_sga[si]])
            ACT(sgb[si][:], psb[pb0 + 1][:, :], AF.Sigmoid, [PB_[pb0 + 1]], [B_sgb[si]])
            TT("dve", sga[si][:], psb[pb0 + 2][:, :], sga[si][:], ALU.mult, [PB_[pb0 + 2], B_sga[si]], [B_sga[si]])
            TT("dve", sgb[si][:], psb[pb0 + 3][:, :], sgb[si][:], ALU.mult, [PB_[pb0 + 3], B_sgb[si]], [B_sgb[si]])
            TT("pool", mo[si][:], sga[si][:], sgb[si][:], ALU.add, [B_sga[si], B_sgb[si]], [B_mo[si]])
            DMA(mgT_d[t][:, j * 512:(j + 1) * 512], mo[si][:], [B_mo[si]], [B_mg[t][si]], "st_mo%d" % si)
    release(m_persist)
    sch.barrier()
    if stop == "p2":
        return finish([("mgT", mgT_d.ap(), [QT, 128, KC * 512], BF16)])

    wO = alloc([128, KC * D], BF16, "wO")
    B_wO = Buf("wO")
    for k in range(KC):
        DMA(wO[:, k * D:(k + 1) * D], w_out[k * 128:(k + 1) * 128, :], (), [B_wO], "ld_wO", eng="pool")
    fgs = alloc([128, D], F32, "fgs")
    B_fg = Buf("fg")
    DMA(fgs[:], fg.ap(), (), [B_fg], "ld_fg")
    xin = [alloc([128, D], F32, "xin") for _ in range(2)]
    xn = [alloc([128, D], F32, "xn") for _ in range(2)]
    mti = [alloc([128, KC * 512], BF16, "mti") for _ in range(2)]
    B_mti = [Buf("mti0"), Buf("mti1")]
    st2 = [alloc([128, 40], F32, "st2") for _ in range(2)]
    B_xin = [Buf("xin0"), Buf("xin1")]
    B_xn = [Buf("xn0"), Buf("xn1")]
    B_st2 = [Buf("st20"), Buf("st21")]
    NTT = SQ // 128

    def p2c_load(tt):
        DMA(xin[tt % 2][:], xtok[tt * 128:(tt + 1) * 128, :], (), [B_xin[tt % 2]], "ld_xin%d" % (tt % 2))
        if tt % 4 == 0:
            t_ = tt // 4
            DMA(mti[t_ % 2][:], mgT_d[t_], B_mg[t_], [B_mti[t_ % 2]], "ld_mti%d" % (t_ % 2))

    p2c_load(0)
    for tt in range(NTT):
        b = tt % 2
        if tt + 1 < NTT:
            p2c_load(tt + 1)
        t, sub = tt // 4, tt % 4
        for n in range(4):
            pbk = (tt % 2) * 4 + n
            for j in range(KC):
                MM(psb[pbk][:, :], mti[t % 2][:, j * 512 + sub * 128: j * 512 + (sub + 1) * 128], wO[:, j * D + n * 512: j * D + (n + 1) * 512],
                   j == 0, j == KC - 1, [B_wO, B_mti[t % 2]], [PB_[pbk]])
            cs = slice(n * 512, (n + 1) * 512)
            TT("dve", xn[b][:, cs], psb[pbk][:, :], gate_bc[:, cs], ALU.mult, [PB_[pbk], B_gate], [B_xn[b]])
            TT("pool", xn[b][:, cs], xn[b][:, cs], xin[b][:, cs], ALU.add, [B_xn[b], B_xin[b]], [B_xn[b]])
        s2 = st2[b]
        for n in range(4):
            sch.op("dve", lambda e, b=b, s2=s2, n=n: e.bn_stats(out=s2[:, 8 + n * 6: 8 + (n + 1) * 6], in_=xn[b][:, n * 512:(n + 1) * 512]),
                   [B_xn[b]], [B_st2[b]])
        sch.op("dve", lambda e, s2=s2: e.bn_aggr(out=s2[:, 4:6], in_=s2[:, 8:32]), [B_st2[b]], [B_st2[b]])
        STT(s2[:, 0:1], s2[:, 4:5], s2[:, 4:5], s2[:, 5:6], ALU.mult, ALU.add, [B_st2[b]], [B_st2[b]])
        ACT(s2[:, 1:2], s2[:, 0:1], AF.Sqrt, [B_st2[b]], [B_st2[b]], bias=EPS)
        RECIP(s2[:, 2:3], s2[:, 1:2], [B_st2[b]], [B_st2[b]])
        STT(xn[b][:], xn[b][:], s2[:, 2:3], fgs[:], ALU.mult, ALU.mult, [B_xn[b], B_st2[b], B_fg], [B_xn[b]])
        DMA(out_d[tt * 128:(tt + 1) * 128, :], xn[b][:], [B_xn[b]], [Buf("o")], "st_out%d" % b)
    sch.barrier()
    sch.emit()
    return nc, dbg_out


def _regload(e, reg, ap):
    return e.reg_load(reg, ap)


def _dyn(t, reg, const_off, pattern):
    return bass.AP(t, reg + const_off, pattern)


def _prep_inputs(S, x, c, norm_gain, w_ada, b_ada, w_in, b_gate_if, conv_w, conv_b, mlstm_norm_gain,
                 w_proj_attn, w_proj_mlstm, w_out, final_gain):
    f = np.float32
    NS = S // 128
    SQ = S // 4
    x = np.asarray(x, f)
    w_in0 = np.asarray(w_in, f)[0]
    ident = np.eye(128, dtype=f)
    tri = np.triu(np.ones((128, 128), f))
    ones = np.ones((128, 128), f)
    cst = np.ascontiguousarray(np.concatenate([ident, tri, ones], axis=1))
    xTs = []
    for b in range(2):
        a = x[b].reshape(NS // 4, 512, KC, 128).transpose(0, 3, 2, 1)
        xTs.append(np.ascontiguousarray(a).reshape(NS // 4, 128, KC * 512))
    w_ada0 = np.ascontiguousarray(np.asarray(w_ada, f)[0])
    b_row = np.ascontiguousarray(np.asarray(b_ada, f)[0].reshape(1, -1))
    ngc = np.ascontiguousarray(np.asarray(norm_gain, f)[0].reshape(KC, 128).T)
    wg4 = w_in0[:, O_GA:O_GA + 2 * D].reshape(KC, 128, 2, KC, 128)
    w_gj = np.ascontiguousarray(wg4.transpose(3, 1, 2, 0, 4)).reshape(KC, 128, 2 * KC * 128)
    wp4 = np.stack([np.asarray(w_proj_attn, f)[0], np.asarray(w_proj_mlstm, f)[0]], axis=0).reshape(2, 8, 128, KC, 128)
    w_pj = np.ascontiguousarray(wp4.transpose(3, 2, 0, 1, 4)).reshape(KC, 128, 2 * 8 * 128)
    w_o = np.ascontiguousarray(np.asarray(w_out, f)[0])
    fgb = np.ascontiguousarray(np.broadcast_to(np.asarray(final_gain, f)[None, :], (128, D)))
    cwf = np.asarray(conv_w, f)[0]
    cbf = np.asarray(conv_b, f)[0]
    bg = np.asarray(b_gate_if, f)[0]
    mgf = np.asarray(mlstm_norm_gain, f)[0]
    ki = np.arange(128)[:, None]
    qi = np.arange(128)[None, :]
    in_maps = []
    for core in range(8):
        b, g = core // 4, core % 4
        cols = []
        for off in (O_QA, O_KA, O_VA, O_ZA):
            for lh in range(2):
                h = 2 * g + lh
                cols.append(np.arange(off + h * 128, off + (h + 1) * 128))
        cols.append(np.arange(O_QM + g * 256, O_QM + (g + 1) * 256))
        cols.append(np.arange(O_KM + g * 256, O_KM + (g + 1) * 256))
        cols.append(np.arange(O_VM + g * 256, O_VM + (g + 1) * 256))
        cols.append(np.array([O_IF + g, O_IF + 4 + g]))
        cols.append(np.arange(O_OM + g * 256, O_OM + (g + 1) * 256))
        cols.append(np.arange(O_ZM + g * 256, O_ZM + (g + 1) * 256))
        cols = np.concatenate(cols)
        assert cols.size == 2306
        w1 = np.ascontiguousarray(w_in0[:, cols])
        cwc = np.zeros((128, 16), f)
        cbc = np.zeros((128, 4), f)
        for ch in range(4):
            base = (0 if ch < 2 else 1024) + g * 256 + (ch % 2) * 128
            cwc[:, ch * 4:(ch + 1) * 4] = cwf[:, base:base + 128].T
            cbc[:, ch] = cbf[base:base + 128]
        bifc = np.ascontiguousarray(np.broadcast_to(np.array([bg[g], bg[4 + g]], f)[None, :], (128, 2)))
        mgc = np.ascontiguousarray(np.broadcast_to(mgf[g * 256:(g + 1) * 256][None, :], (128, 256)))
        ab = np.zeros((128, 6, 256), f)
        for p, d in enumerate((1, 4, 16)):
            for lh in range(2):
                slope = 2.0 ** (-(2 * g + lh + 1))
                for half, shift in ((0, 128), (1, 0)):
                    delta = qi - ki + shift
                    valid = (delta >= 0) & (delta <= 128)
                    ab[:, p * 2 + lh, half * 128:(half + 1) * 128] = np.where(valid, -slope * d * delta, NEG)
        in_maps.append({
            "xT": xTs[b],
            "xtok": np.ascontiguousarray(x[b, g * SQ:(g + 1) * SQ, :]),
            "ccol": np.ascontiguousarray(np.asarray(c, f)[b].reshape(KC, 128).T),
            "w_ada": w_ada0, "b_row": b_row, "ng": ngc, "w1": w1, "w_gj": w_gj, "w_pj": w_pj,
            "bif": bifc, "cw": cwc, "cb": cbc, "mg": mgc,
            "w_out": w_o, "fg": fgb, "cst": cst,
            "abias": np.ascontiguousarray(ab.reshape(128, 6 * 256)),
            "roff": np.array([[(g * 4 + f4) * 512 * SQ for f4 in range(4)]
                              + [(g * (SQ // 512) + t) * 128 * KC * 512 for t in range(SQ // 512)]], dtype=np.int32),
        })
    return in_maps


_STOP = None


def kernel(x, c, norm_gain, w_ada, b_ada, w_in, b_gate_if, conv_w, conv_b, mlstm_norm_gain,
           w_proj_attn, w_proj_mlstm, w_out, final_gain):
    S = int(np.asarray(x).shape[1])
    in_maps = _prep_inputs(S, x, c, norm_gain, w_ada, b_ada, w_in, b_gate_if, conv_w, conv_b, mlstm_norm_gain,
                           w_proj_attn, w_proj_mlstm, w_out, final_gain)
    nc, _ = build(S, stop=_STOP)
    res = run_bass_kernel_spmd(nc, in_maps, core_ids=list(range(8)))
    if _STOP is not None:
        return res.results
    SQ = S // 4
    out = np.zeros((2, S, D), np.float32)
    for core in range(8):
        b, g = core // 4, core % 4
        out[b, g * SQ:(g + 1) * SQ, :] = res.results[core]["out"]
    return out
```

```python
import numpy as np
import concourse.bass as bass
import concourse.mybir as mybir
from concourse.bass_utils import run_bass_kernel_spmd

F32 = mybir.dt.float32
BF16 = mybir.dt.bfloat16
I32 = mybir.dt.int32
AF = mybir.ActivationFunctionType
ALU = mybir.AluOpType

D = 2048
KC = 16
EPS = 1e-6
SEQ = 8192
import os as _os
OPT_CONV = _os.environ.get("OPT_CONV", "0") == "1"
OPT_SIG = _os.environ.get("OPT_SIG", "1") == "1"
OPT_LN = _os.environ.get("OPT_LN", "1") == "1"
NEG = -30000.0
LN16 = 2.772588722239781
O_QA, O_KA, O_VA, O_ZA, O_QM, O_KM, O_VM, O_OM, O_ZM, O_IF, O_GA, O_GB = (
    0, 1024, 2048, 3072, 4096, 5120, 6144, 7168, 8192, 9216, 9224, 11272)


class Buf:
    __slots__ = ("name", "w", "r", "excl")

    def __init__(self, name, excl=False):
        self.name = name
        self.w = None
        self.r = {}
        self.excl = excl


class Sched:
    ENG = ("pe", "act", "dve", "pool", "sp")

    def __init__(self, nc):
        self.nc = nc
        self.prog = {e: [] for e in self.ENG}
        self.cnt = {}
        self.known = {e: {} for e in self.ENG}
        self.sems = {}

    def op(self, eng, fn, reads=(), writes=(), dkey=None, dinc=16):
        deps = {}

        def add(tok):
            if tok is None:
                return
            k, v = tok
            if deps.get(k, 0) < v:
                deps[k] = v

        ex = [b for b in reads if b.excl]
        if ex:
            writes = list(writes) + ex
        for b in reads:
            add(b.w)
        for b in writes:
            add(b.w)
            for k, v in b.r.items():
                add((k, v))
        waits = []
        kn = self.known[eng]
        for k, v in deps.items():
            if eng == "pe" and k == "pe":
                continue
            if kn.get(k, 0) >= v:
                continue
            kn[k] = v
            waits.append((k, v))
        if dkey is None:
            key, inc = eng, 1
        else:
            key, inc = dkey, dinc
        self.cnt[key] = self.cnt.get(key, 0) + inc
        tok = (key, self.cnt[key])
        self.prog[eng].append((waits, fn, key, inc))
        for b in reads:
            if b.r.get(key, 0) < tok[1]:
                b.r[key] = tok[1]
        for b in writes:
            b.w = tok
            b.r = {}
        return tok

    def barrier(self):
        for e in self.ENG:
            waits = []
            for k, v in self.cnt.items():
                if (k == e and e == "pe") or k == "cc":
                    continue
                if self.known[e].get(k, 0) >= v:
                    continue
                self.known[e][k] = v
                waits.append((k, v))
            if waits:
                self.prog[e].append((waits, None, None, 0))

    def emit(self):
        nc = self.nc
        keys = list(self.cnt.keys())
        for k in keys:
            self.sems[k] = nc.alloc_semaphore("s_" + k)
        engmap = {"pe": "tensor", "act": "scalar", "dve": "vector", "pool": "gpsimd", "sp": "sync"}
        with nc.Block() as block:
            for e in self.ENG:
                prog = self.prog[e]

                def body(eng, prog=prog):
                    for waits, fn, key, inc in prog:
                        for k, v in waits:
                            eng.wait_ge(self.sems[k], v)
                        if fn is not None:
                            ins = fn(eng)
                            ins.then_inc(self.sems[key], inc)

                getattr(block, engmap[e])(body)


def build(S=SEQ, stop=None, mode=None, sub=99):
    assert S % 2048 == 0
    NT = S // 512
    NS = S // 128
    NU = S // 2048
    SQ = S // 4
    QT = SQ // 512
    nc = bass.Bass("TRN2", target_bir_lowering=False)
    sch = Sched(nc)

    def din(name, shape, dt=F32):
        if mode in ("pa_only", "pb_only", "ex_only") and name not in ("w1", "cst", "abias", "roff", "hT_in", "ccol", "ng", "bif", "cw", "cb", "mg"):
            shape = [1, 1]
        return nc.dram_tensor(name, list(shape), dt, kind="ExternalInput")

    xT = din("xT", [NT, 128, KC * 512])
    xtok = din("xtok", [SQ, D])
    ccol = din("ccol", [128, KC])
    w_ada = din("w_ada", [D, 3 * D])
    b_row = din("b_row", [1, 3 * D])
    ng = din("ng", [128, KC])
    w1 = din("w1", [D, 2306])
    w_gj = din("w_gj", [KC, 128, 2 * KC * 128])
    w_pj = din("w_pj", [KC, 128, 2 * 8 * 128])
    bif = din("bif", [128, 2])
    cw = din("cw", [128, 16])
    cb = din("cb", [128, 4])
    mg = din("mg", [128, 256])
    w_out = din("w_out", [D, D])
    fg = din("fg", [128, D])
    cst = din("cst", [128, 384])
    abias_d = din("abias", [128, 6 * 256])
    roff = din("roff", [1, 4 + QT], I32)
    out_d = nc.dram_tensor("out", [SQ, D], F32, kind="ExternalOutput")

    hT_d = nc.dram_tensor("hT_s", [NT, 128, KC * 512], BF16)
    hTo_d = nc.dram_tensor("hTo_s", [QT, 128, KC * 512], BF16)
    ybuf_d = nc.dram_tensor("ybuf_s", [4 * 512, SQ], BF16)
    yall_d = nc.dram_tensor("yall_s", [4 * 4 * 512, SQ], BF16)
    ymine_d = nc.dram_tensor("ymine_s", [4 * 512, SQ], BF16)
    maT_d = nc.dram_tensor("maT_s", [QT, 128, KC * 512], BF16)
    mgT_d = nc.dram_tensor("mgT_s", [QT, 128, KC * 512], BF16)
    dbg_out = {}

    SB_LO = 16512
    SB_HI = 229344
    state = {"off": SB_LO, "n": 0}

    def alloc(shape, dt, name=None):
        nbytes = int(np.prod(shape[1:])) * (4 if dt in (F32, I32) else 2)
        off = (state["off"] + 63) // 64 * 64
        assert off + nbytes <= SB_HI, ("SBUF overflow", name, off, nbytes)
        state["off"] = off + nbytes
        state["n"] += 1
        return nc.alloc_sbuf_tensor_at("%s_%d" % (name or "t", state["n"]), list(shape), dt, offset=off)

    def alloc_top(shape, dt, name):
        nbytes = int(np.prod(shape[1:])) * (4 if dt in (F32, I32) else 2)
        off = (SB_HI - nbytes) // 64 * 64
        state["n"] += 1
        return nc.alloc_sbuf_tensor_at("%s_%d" % (name, state["n"]), list(shape), dt, offset=off), off

    def mark():
        return state["off"]

    def release(m):
        state["off"] = m

    psb = [nc.alloc_psum_tensor("ps%d" % i, [128, 512], F32) for i in range(8)]
    PB_ = [Buf("psum%d" % i, excl=True) for i in range(8)]

    def MM(out, lhsT, rhs, st, sp, R, W):
        return sch.op("pe", lambda e: e.matmul(out, lhsT=lhsT, rhs=rhs, start=st, stop=sp), R, W)

    def TR(out, in_, ident, R, W):
        return sch.op("pe", lambda e: e.transpose(out=out, in_=in_, identity=ident), R, W)

    def ACT(out, in_, func, R, W, scale=None, bias=None):
        def f(e):
            kw = {}
            if scale is not None:
                kw["scale"] = scale
            if bias is not None:
                kw["bias"] = bias
            return e.activation(out=out, in_=in_, func=func, **kw)
        return sch.op("act", f, R, W)

    def TT(eng, out, in0, in1, op, R, W):
        return sch.op(eng, lambda e: e.tensor_tensor(out=out, in0=in0, in1=in1, op=op), R, W)

    def TS(eng, out, in0, s1, s2, op0, op1, R, W):
        if op1 is None:
            return sch.op(eng, lambda e: e.tensor_scalar(out=out, in0=in0, scalar1=s1, scalar2=None, op0=op0), R, W)
        return sch.op(eng, lambda e: e.tensor_scalar(out=out, in0=in0, scalar1=s1, scalar2=s2, op0=op0, op1=op1), R, W)

    def STT(out, in0, scalar, in1, op0, op1, R, W):
        return sch.op("dve", lambda e: e.scalar_tensor_tensor(out=out, in0=in0, scalar=scalar, in1=in1, op0=op0, op1=op1), R, W)

    def CP(eng, out, in_, R, W):
        if eng == "act":
            return sch.op("act", lambda e: e.copy(out=out, in_=in_), R, W)
        return sch.op(eng, lambda e: e.tensor_copy(out=out, in_=in_), R, W)

    def RECIP(out, in_, R, W):
        return sch.op("dve", lambda e: e.reciprocal(out=out, in_=in_), R, W)

    def MEMSET(eng, ap, val, W):
        return sch.op(eng, lambda e: e.memset(ap, val), (), W)

    def DMA(out, in_, R, W, key, eng="sp", **kw):
        return sch.op(eng, lambda e: e.dma_start(out=out, in_=in_, **kw), R, W, dkey=key)

    def bc_mid(t, n, reps, off=0):
        a = t[:, off:off + n]
        return bass.AP(a.tensor, a.offset, [list(a.ap[0]), [0, reps], [1, n]])

    def finish(dumps):
        for name, src_ap, shape, dt in dumps:
            o = nc.dram_tensor("dbg_" + name, list(shape), dt, kind="ExternalOutput")
            DMA(o.ap(), src_ap, [], [Buf("dbg")], "dbg_" + name)
        sch.barrier()
        sch.emit()
        return nc, None

    ident_f = alloc([128, 128], F32, "identf")
    tri_f = alloc([128, 128], F32, "trif")
    ones_f = alloc([128, 128], F32, "onesf")
    ident_b = alloc([128, 128], BF16, "identb")
    ones_b = alloc([128, 128], BF16, "onesb")
    cst_sb = alloc([128, 384], F32, "cst")
    gate_bc = alloc([128, D], F32, "gatebc")
    c_f = alloc([128, KC], F32, "cf")
    c_b = alloc([128, KC], BF16, "cb16")
    ng_sb = alloc([128, KC], F32, "ng")
    A_col = alloc([128, KC], F32, "Acol")
    sh_col = alloc([128, KC], F32, "shcol")
    ri = alloc([1, 4 + QT], I32, "ri")
    B_cst = Buf("cst")
    B_mod = Buf("modrow")
    B_col = Buf("cols")
    B_ri = Buf("ri")

    DMA(cst_sb[:], cst.ap(), (), [B_cst], "ld_cst")
    DMA(c_f[:], ccol.ap(), (), [B_cst], "ld_cst")
    DMA(ng_sb[:], ng.ap(), (), [B_cst], "ld_cst")
    DMA(ri[:], roff.ap(), (), [B_ri], "ld_ri")
    CP("dve", ident_f[:], cst_sb[:, 0:128], [B_cst], [B_cst])
    CP("dve", tri_f[:], cst_sb[:, 128:256], [B_cst], [B_cst])
    CP("dve", ones_f[:], cst_sb[:, 256:384], [B_cst], [B_cst])
    CP("dve", ident_b[:], cst_sb[:, 0:128], [B_cst], [B_cst])
    CP("dve", ones_b[:], cst_sb[:, 256:384], [B_cst], [B_cst])
    CP("dve", c_b[:], c_f[:], [B_cst], [B_cst])

    m_persist = mark()

    if mode not in ('pa_only', 'pb_only', 'ex_only'):
        modrow = alloc([1, 3 * D], F32, "modrow")
        brow = alloc([1, 3 * D], F32, "brow")
        B_brow = Buf("brow")
        DMA(brow[:], b_row.ap(), (), [B_brow], "ld_brow")
        wa = [alloc([128, 2048], BF16, "wa") for _ in range(2)]
        B_wa = [Buf("wa0"), Buf("wa1")]
        n_wa = 0
        for r in range(3):
            for k in range(KC):
                b = n_wa % 2
                n_wa += 1
                DMA(wa[b][:], w_ada[k * 128:(k + 1) * 128, r * 2048:(r + 1) * 2048], (), [B_wa[b]],
                    "ld_wa%d" % b, eng="pool")
                for n in range(4):
                    MM(psb[n][0:1, :], c_b[:, k:k + 1], wa[b][:, n * 512:(n + 1) * 512], k == 0, k == KC - 1,
                       [B_wa[b], B_cst], [PB_[n]])
            for n in range(4):
                TT("dve", modrow[0:1, r * 2048 + n * 512: r * 2048 + (n + 1) * 512], psb[n][0:1, :],
                   brow[0:1, r * 2048 + n * 512: r * 2048 + (n + 1) * 512], ALU.add, [PB_[n], B_brow], [B_mod])
        for j in range(32):
            TR(psb[4][:, j:j + 1], modrow[0:1, j * 128:(j + 1) * 128], ident_f[0:1, 0:1], [B_mod, B_cst], [PB_[4]])
        CP("dve", sh_col[:], psb[4][:, 0:16], [PB_[4]], [B_col])
        STT(A_col[:], psb[4][:, 16:32], 1.0, ng_sb[:], ALU.add, ALU.mult, [PB_[4], B_cst], [B_col])
        B_gate = Buf("gate")
        for n in range(4):
            MM(psb[n][:, :], ones_f[0:1, :], modrow[0:1, 2 * D + n * 512: 2 * D + (n + 1) * 512], True, True,
               [B_mod, B_cst], [PB_[n]])
            CP("act", gate_bc[:, n * 512:(n + 1) * 512], psb[n][:, :], [PB_[n]], [B_gate])
        release(m_persist)
        sch.barrier()
        if stop == "mod":
            return finish([("A", A_col[:], [128, KC], F32), ("sh", sh_col[:], [128, KC], F32), ("gate", gate_bc[:], [128, D], F32)])

        xt = [alloc([128, KC * 512], F32, "xt") for _ in range(3)]
        sq = [alloc([128, KC * 512], BF16, "sq") for _ in range(2)]
        sd = [alloc([128, 512], F32, "sd") for _ in range(2)]
        rs = [alloc([128, 512], F32, "rs") for _ in range(2)]
        hts = [alloc([128, KC * 512], BF16, "hts") for _ in range(2)]
        B_xt = [Buf("xt0"), Buf("xt1"), Buf("xt2")]
        B_sq = [Buf("sq0"), Buf("sq1")]
        B_sd = [Buf("sd0"), Buf("sd1")]
        B_rs = [Buf("rs0"), Buf("rs1")]
        B_hts = [Buf("hts0"), Buf("hts1")]
        B_hT = [Buf("hT%d" % i) for i in range(NT)]

        def p0_load(i):
            DMA(xt[i % 3][:], xT[i], (), [B_xt[i % 3]], "ld_xt%d" % (i % 3))

        def p0_a(i):
            b = i % 2
            for hf in range(2):
                cs = slice(hf * 8 * 512, (hf + 1) * 8 * 512)
                ACT(sq[b][:, cs], xt[i % 3][:, cs], AF.Square, [B_xt[i % 3]], [B_sq[b]])
            for k in range(KC):
                MM(psb[b][:, :], ones_b[:], sq[b][:, k * 512:(k + 1) * 512], k == 0, k == KC - 1, [B_sq[b], B_cst], [PB_[b]])

        def p0_a2(i):
            b = i % 2
            ACT(sd[b][:], psb[b][:, :], AF.Sqrt, [PB_[b]], [B_sd[b]], scale=1.0 / D, bias=EPS)
            RECIP(rs[b][:], sd[b][:], [B_sd[b]], [B_rs[b]])
            x3 = xt[i % 3][:].rearrange("p (k t) -> p k t", k=KC)
            TT("dve", x3, x3, bc_mid(rs[b], 512, KC), ALU.mult, [B_xt[i % 3], B_rs[b]], [B_xt[i % 3]])

        def p0_b(i):
            b = i % 2
            for k in range(KC):
                if k % 8 in (2, 5, 7):
                    TS("dve", hts[b][:, k * 512:(k + 1) * 512], xt[i % 3][:, k * 512:(k + 1) * 512], A_col[:, k:k + 1], sh_col[:, k:k + 1],
                       ALU.mult, ALU.add, [B_xt[i % 3], B_col], [B_hts[b]])
                else:
                    ACT(hts[b][:, k * 512:(k + 1) * 512], xt[i % 3][:, k * 512:(k + 1) * 512], AF.Identity, [B_xt[i % 3], B_col], [B_hts[b]],
                        scale=A_col[:, k:k + 1], bias=sh_col[:, k:k + 1])
            DMA(hT_d[i], hts[b][:], [B_hts[b]], [B_hT[i]], "st_hts%d" % b)

        p0_load(0)
        p0_load(1)
        p0_load(2)
        p0_a(0)
        p0_a2(0)
        for i in range(NT):
            if i + 1 < NT:
                p0_a(i + 1)
            p0_b(i)
            if i + 1 < NT:
                p0_a2(i + 1)
            if i + 3 < NT:
                p0_load(i + 3)
        release(m_persist)
        sch.barrier()
        if stop == "p0":
            return finish([("hT", hT_d.ap(), [NT, 128, KC * 512], BF16)])

    else:
        B_hT = [Buf('hT%d' % i) for i in range(NT)]
        hT_d = din('hT_in', [NT, 128, KC * 512], BF16)
    B_hTo = Buf("hTo")
    for t in range(QT):
        reg = nc.gpsimd.alloc_register("rrh%d" % t)

        def f(e, reg=reg, idx=4 + t, t=t):
            e.reg_load(reg, ri[0:1, idx:idx + 1])
            return e.dma_start(out=hTo_d[t], in_=bass.AP(hT_d, reg, [[KC * 512, 128], [1, KC * 512]]))
        sch.op("pool", f, [B_hT[i] for i in range(NT)] + [B_ri], [B_hTo], dkey="cp_h")
    hto_all, hto_off = alloc_top([128, QT * KC * 512], BF16, "htoall")
    B_htoa = [Buf("htoa%d" % t) for t in range(QT)]
    B_ybuf = [[Buf("ybuf%d_%d" % (q, f)) for f in range(4)] for q in range(4)]
    B_yallp = [Buf("yall%d" % i) for i in range(16)]
    groups = [[0, 1, 2, 3], [4, 5, 6, 7]]

    def emit_gather(q, f4):
        i_ = q * 4 + f4

        def emit_cc(e):
            return e.collective_compute("AllGather", ALU.bypass, replica_groups=groups,
                                        ins=[ybuf_d[q * 512 + f4 * 128: q * 512 + (f4 + 1) * 128, :]],
                                        outs=[yall_d[i_ * 512:(i_ + 1) * 512, :]])
        sch.op("pool", emit_cc, [B_ybuf[q][f4]], [B_yallp[i_]], dkey="cc", dinc=1)
    if mode not in ('pb_only', 'ex_only'):
        wA = alloc([128, KC * 1024], BF16, "wA")
        B_wA = Buf("wA")
        for k in range(KC):
            DMA(wA[:, k * 1024:(k + 1) * 1024], w1[k * 128:(k + 1) * 128, 0:1024], (), [B_wA], "ld_wA", eng="pool")
        abias = alloc([128, 6 * 256], F32, "abias")
        B_ab = Buf("abias")
        DMA(abias[:], abias_d.ap(), (), [B_ab], "ld_ab")
        ht = [alloc([128, KC * 512], BF16, "ht") for _ in range(2)]
        B_ht = [Buf("ht0"), Buf("ht1")]
        qT = alloc([128, 2 * 2048], BF16, "qT")
        kT = alloc([128, 2 * 4096], BF16, "kT")
        vT = alloc([128, 2 * 2048], BF16, "vT")
        zs = alloc([128, 2 * 2048], BF16, "zs")
        B_q = [[Buf("q") for _ in range(4)] for _ in range(2)]
        B_k = [[[Buf("k") for _ in range(4)] for _ in range(2)] for _ in range(2)]
        B_v = [[Buf("v") for _ in range(4)] for _ in range(2)]
        B_z = [[Buf("z") for _ in range(4)] for _ in range(2)]
        NSLOT = {1: 3, 4: 8, 16: 32}
        Vv = {d: alloc([128, 2 * NSLOT[d] * 128], BF16, "Vv%d" % d) for d in (1, 4, 16)}
        B_Vv = {d: [[Buf("vv") for _ in range(NSLOT[d])] for _ in range(2)] for d in (1, 4, 16)}
        accn = alloc([128, 2048], F32, "accn")
        accd = alloc([128, 2048], F32, "accd")
        B_an = [Buf("an") for _ in range(4)]
        B_ad = [Buf("ad") for _ in range(4)]
        sbs = [alloc([128, 256], F32, "sbs") for _ in range(2)]
        B_sbs = [Buf("sbs0"), Buf("sbs1")]
        pTs = [alloc([128, 256], BF16, "pT") for _ in range(3)]
        B_pT = [Buf("pT%d" % i) for i in range(3)]
        yst = [alloc([128, 2048], BF16, "yst") for _ in range(2)]
        B_yst = [Buf("yst0"), Buf("yst1")]

        def pa_load(i):
            DMA(ht[i % 2][:], hT_d[i], [B_hT[i]], [B_ht[i % 2]], "ld_ht%d" % (i % 2))

        psS = [psb[3][:, 0:256], psb[4][:, 0:256], psb[5][:, 0:256]]
        B_psS = [PB_[3], PB_[4], PB_[5]]
        psN = [psb[6][:, 0:128], psb[7][:, 0:128]]
        psD = [psb[6][:, 128:256], psb[7][:, 128:256]]
        B_psN = [PB_[6], PB_[7]]
        B_psD = [PB_[6], PB_[7]]
        psT = [psb[0][:].bitcast(BF16)[:, 0:128], psb[1][:].bitcast(BF16)[:, 0:128]]
        B_psT = [PB_[0], PB_[1]]
        cnt = {"ip": 0, "s": 0, "n": 0, "t": 0, "p": 0, "sb": 0}

        def pa_inproj(i):
            u, m = i // 4, i % 4
            slot = u % 2
            b = i % 2
            for c in range(8):
                pb = cnt["ip"] % 3
                cnt["ip"] += 1
                for k in range(KC):
                    MM(psb[pb][:, :], wA[:, k * 1024 + c * 128: k * 1024 + (c + 1) * 128], ht[b][:, k * 512:(k + 1) * 512],
                       k == 0, k == KC - 1, [B_wA, B_ht[b]], [PB_[pb]])
                h = c % 2
                kind = c // 2
                if kind == 0:
                    CP("dve", qT[:, h * 2048 + m * 512: h * 2048 + (m + 1) * 512], psb[pb][:, :], [PB_[pb]], [B_q[h][m]])
                elif kind == 1:
                    o = h * 4096 + slot * 2048 + m * 512
                    CP("dve", kT[:, o:o + 512], psb[pb][:, :], [PB_[pb]], [B_k[h][slot][m]])
                elif kind == 2:
                    CP("act", vT[:, h * 2048 + m * 512: h * 2048 + (m + 1) * 512], psb[pb][:, :], [PB_[pb]], [B_v[h][m]])
                else:
                    ACT(zs[:, h * 2048 + m * 512: h * 2048 + (m + 1) * 512], psb[pb][:, :], AF.Silu, [PB_[pb]], [B_z[h][m]])

        def blocks_for(u):
            L = []
            for mm in range(16):
                g = 16 * u + mm
                prev = None
                if g > 0:
                    prev = (u % 2, 128 * (mm - 1)) if mm > 0 else ((u - 1) % 2, 1920)
                L.append((0, 1, 128 * mm, 1, [mm // 4], prev, g % 3, (g - 1) % 3))
            for m4 in range(4):
                for r in range(4):
                    g = 4 * u + m4
                    prev = None
                    if g > 0:
                        prev = (u % 2, 512 * (m4 - 1) + r) if m4 > 0 else ((u - 1) % 2, 1536 + r)
                    L.append((1, 4, 512 * m4 + r, 4, [m4], prev, r * 2 + g % 2, r * 2 + (g - 1) % 2))
            for r in range(16):
                prev = ((u - 1) % 2, r) if u > 0 else None
                L.append((2, 16, r, 16, [0, 1, 2, 3], prev, r * 2 + u % 2, r * 2 + (u - 1) % 2))
            return L

        def pa_attn(u, h):
            if sub < 1:
                return
            slot = u % 2
            blks = blocks_for(u)
            pend = []

            def stage1(bi):
                p, d, c0, st, ms, prev, vc, vp = blks[bi]
                qcols = slice(h * 2048 + c0, h * 2048 + c0 + 127 * st + 1, st)
                si = cnt["s"] % 3
                cnt["s"] += 1
                ti = cnt["t"] % 2
                cnt["t"] += 1
                TR(psT[ti], vT[:, qcols], ident_b[:], [B_v[h][m] for m in ms] + [B_cst], [B_psT[ti]])
                ACT(Vv[d][:, (h * NSLOT[d] + vc) * 128:(h * NSLOT[d] + vc + 1) * 128], psT[ti], AF.Identity, [B_psT[ti]], [B_Vv[d][h][vc]])
                kc0 = h * 4096 + slot * 2048 + c0
                lo = 0
                if sub < 1.2:
                    return (bi, 0, 0)
                if prev is not None:
                    ps_, pc0 = prev
                    pk0 = h * 4096 + ps_ * 2048 + pc0
                    pm = sorted(set([(pc0 + j * st) // 512 for j in (0, 127)])) if st < 16 else [0, 1, 2, 3]
                    MM(psS[si][:, 0:128], kT[:, pk0: pk0 + 127 * st + 1: st], qT[:, qcols], True, True,
                       [B_k[h][ps_][m] for m in pm] + [B_q[h][m] for m in ms], [B_psS[si]])
                else:
                    lo = 128
                MM(psS[si][:, 128:256], kT[:, kc0: kc0 + 127 * st + 1: st], qT[:, qcols], True, True,
                   [B_k[h][slot][m] for m in ms] + [B_q[h][m] for m in ms], [B_psS[si]])
                if sub < 1.4:
                    return (bi, 0, 0)
                sbi = cnt["sb"] % 2
                cnt["sb"] += 1
                ab0 = (p * 2 + h) * 256
                STT(sbs[sbi][:, lo:256], psS[si][:, lo:256], 128.0 ** -0.5, abias[:, ab0 + lo: ab0 + 256], ALU.mult, ALU.add,
                    [B_psS[si], B_ab], [B_sbs[sbi]])
                if sub < 1.6:
                    return (bi, 0, 0)
                pi = cnt["p"] % 3
                cnt["p"] += 1
                ACT(pTs[pi][:, lo:256], sbs[sbi][:, lo:256], AF.Exp, [B_sbs[sbi]], [B_pT[pi]])
                return (bi, pi, lo)

            def stage2(info):
                bi, pi, lo = info
                p, d, c0, st, ms, prev, vc, vp = blks[bi]
                ni = cnt["n"] % 2
                cnt["n"] += 1
                vcur = Vv[d][:, (h * NSLOT[d] + vc) * 128:(h * NSLOT[d] + vc + 1) * 128]
                vprev = Vv[d][:, (h * NSLOT[d] + vp) * 128:(h * NSLOT[d] + vp + 1) * 128]
                if lo == 0:
                    MM(psN[ni], vprev, pTs[pi][:, 0:128], True, False, [B_Vv[d][h][vp], B_pT[pi]], [B_psN[ni]])
                    MM(psN[ni], vcur, pTs[pi][:, 128:256], False, True, [B_Vv[d][h][vc], B_pT[pi]], [B_psN[ni]])
                    MM(psD[ni], ones_b[:], pTs[pi][:, 0:128], True, False, [B_cst, B_pT[pi]], [B_psD[ni]])
                    MM(psD[ni], ones_b[:], pTs[pi][:, 128:256], False, True, [B_cst, B_pT[pi]], [B_psD[ni]])
                else:
                    MM(psN[ni], vcur, pTs[pi][:, 128:256], True, True, [B_Vv[d][h][vc], B_pT[pi]], [B_psN[ni]])
                    MM(psD[ni], ones_b[:], pTs[pi][:, 128:256], True, True, [B_cst, B_pT[pi]], [B_psD[ni]])
                ocols = slice(c0, c0 + 127 * st + 1, st)
                if p == 0:
                    CP("act", accn[:, ocols], psN[ni], [B_psN[ni]], [B_an[m] for m in ms])
                    CP("act", accd[:, ocols], psD[ni], [B_psD[ni]], [B_ad[m] for m in ms])
                else:
                    TT("dve", accn[:, ocols], psN[ni], accn[:, ocols], ALU.add, [B_psN[ni]] + [B_an[m] for m in ms], [B_an[m] for m in ms])
                    TT("dve", accd[:, ocols], psD[ni], accd[:, ocols], ALU.add, [B_psD[ni]] + [B_ad[m] for m in ms], [B_ad[m] for m in ms])

            for bi in range(len(blks)):
                pend.append(stage1(bi))
                if len(pend) > 1:
                    x_ = pend.pop(0)
                    if sub >= 2:
                        stage2(x_)
            while pend:
                x_ = pend.pop(0)
                if sub >= 2:
                    stage2(x_)
            if sub < 3:
                return
            yb = (u * 2 + h) % 2
            for m in range(4):
                cs = slice(m * 512, (m + 1) * 512)
                ACT(accd[:, cs], accd[:, cs], AF.Ln, [B_ad[m]], [B_ad[m]])
                ACT(accd[:, cs], accd[:, cs], AF.Exp, [B_ad[m]], [B_ad[m]], scale=-1.0)
                TT("pool", accn[:, cs], accn[:, cs], accd[:, cs], ALU.mult, [B_an[m], B_ad[m]], [B_an[m]])
                TT("pool", yst[yb][:, cs], accn[:, cs], zs[:, h * 2048 + m * 512: h * 2048 + (m + 1) * 512], ALU.mult,
                   [B_an[m], B_z[h][m]], [B_yst[yb]])
            t0 = u * 2048
            while t0 < (u + 1) * 2048:
                q = t0 // SQ
                n = min(SQ - (t0 % SQ), (u + 1) * 2048 - t0)
                DMA(ybuf_d[q * 512 + h * 128: q * 512 + (h + 1) * 128, (t0 % SQ):(t0 % SQ) + n],
                    yst[yb][:, t0 - u * 2048: t0 - u * 2048 + n], [B_yst[yb]], [B_ybuf[q][h]], "st_yst%d" % yb)
                t0 += n

        pa_load(0)
        for u in range(NU):
            for m in range(4):
                i = 4 * u + m
                if i + 1 < NT:
                    pa_load(i + 1)
                pa_inproj(i)
            for h in range(2):
                pa_attn(u, h)
        release(m_persist)
        sch.barrier()
        if stop == "pa":
            if mode == "pa_only":
                return finish([("ybuf", ybuf_d.ap(), [2048, SQ], BF16), ("qT", qT[:], [128, 4096], BF16), ("kT", kT[:], [128, 8192], BF16),
                               ("accn", accn[:], [128, 2048], F32), ("accd", accd[:], [128, 2048], F32), ("pT", pTs[0][:], [128, 256], BF16)])
            return finish([("ybuf", ybuf_d.ap(), [2048, SQ], BF16), ("hT", hT_d.ap(), [NT, 128, KC * 512], BF16)])

    if mode != 'ex_only':
        wB = alloc([128, KC * 1282], BF16, "wB")
        B_wB = Buf("wB")
        for k in range(KC):
            DMA(wB[:, k * 1282:(k + 1) * 1282], w1[k * 128:(k + 1) * 128, 1024:2306], (), [B_wB], "ld_wB", eng="pool")
        ht = [alloc([128, KC * 512], BF16, "htb") for _ in range(2)]
        B_ht = [Buf("htb0"), Buf("htb1")]
        for t in range(QT):
            DMA(hto_all[:, t * KC * 512:(t + 1) * KC * 512], hTo_d[t], [B_hTo], [B_htoa[t]], "ld_htoa%d" % t)
        smalls = alloc([128, 16 + 4 + 2 + 256], F32, "smalls")
        B_sm = Buf("smalls")
        DMA(smalls[:, 0:16], cw.ap(), (), [B_sm], "ld_sm")
        DMA(smalls[:, 16:20], cb.ap(), (), [B_sm], "ld_sm")
        DMA(smalls[:, 20:22], bif.ap(), (), [B_sm], "ld_sm")
        DMA(smalls[:, 22:278], mg.ap(), (), [B_sm], "ld_sm")
        xq = [alloc([128, 515], F32, "xq") for _ in range(4)]
        B_xq = [Buf("xq%d" % i) for i in range(4)]
        cacc = [alloc([128, 512], F32, "cacc") for _ in range(2)]
        B_cacc = [Buf("cacc0"), Buf("cacc1")]
        csig = [alloc([128, 512], F32, "csig") for _ in range(2)]
        B_csig = [Buf("csig0"), Buf("csig1")]
        qk = [[alloc([128, 512], BF16, "qk") for _ in range(4)] for _ in range(2)]
        B_qk = [[Buf("qk") for _ in range(4)] for _ in range(2)]
        vaug = [alloc([128, 257], BF16, "vaug") for _ in range(2)]
        B_vaug = [Buf("vaug0"), Buf("vaug1")]
        gsm = [alloc([128, 16], F32, "gsm") for _ in range(2)]
        B_gsm = [Buf("gsm0"), Buf("gsm1")]
        sgo = [alloc([128, 512], F32, "sgo") for _ in range(2)]
        szm = [alloc([128, 256], BF16, "szm") for _ in range(2)]
        B_sgo = [Buf("sgo0"), Buf("sgo1")]
        B_szm = [Buf("szm0"), Buf("szm1")]
        pTm = [alloc([128, 128], BF16, "pTm") for _ in range(2)]
        B_pTm = [Buf("pTm0"), Buf("pTm1")]
        kw = [alloc([128, 256], BF16, "kw") for _ in range(2)]
        B_kw = [Buf("kw0"), Buf("kw1")]
        Cst = alloc([128, 2 * 257], F32, "Cst")
        Cb = [alloc([128, 2 * 257], BF16, "Cb") for _ in range(2)]
        B_C = Buf("C")
        B_Cb = [Buf("Cb0"), Buf("Cb1")]
        hh = [alloc([128, 256], F32, "hh") for _ in range(2)]
        B_hh = [Buf("hh0"), Buf("hh1")]
        lnst = [alloc([128, 16], F32, "lnst") for _ in range(2)]
        B_lnst = [Buf("lnst0"), Buf("lnst1")]
        ym = [alloc([128, 256], BF16, "ym") for _ in range(2)]
        B_ym = [Buf("ym0"), Buf("ym1")]
        ystm = [alloc([128, 2 * 512], BF16, "ystm") for _ in range(2)]
        B_ystm = [Buf("ystm0"), Buf("ystm1")]
        for i in range(4):
            MEMSET("dve", xq[i][:, 0:3], 0.0, [B_xq[i]])
        for i in range(2):
            MEMSET("dve", vaug[i][:, 256:257], 1.0, [B_vaug[i]])
        MEMSET("dve", Cst[:], 0.0, [B_C])

        assert state["off"] < hto_off, (state["off"], hto_off)

        def pb_load(i):
            DMA(ht[i % 2][:], hT_d[i], [B_hT[i]], [B_ht[i % 2]], "ld_htb%d" % (i % 2))

        psSm = psb[4][:, 0:128]
        psG = psb[2][:, 300:302]
        psUn = psb[4][:, 130:132]
        B_psSm, B_psG, B_psUn = PB_[4], PB_[2], PB_[4]
        psH = psb[5][:, 0:257]
        B_psH = PB_[5]
        psU = psb[6][:, 0:512]
        B_psU = PB_[6]
        ps7 = psb[7][:].bitcast(BF16)
        psK = ps7[:, 0:256]
        psY = ps7[:, 256:512]
        B_psK, B_psY = PB_[7], PB_[7]

        def pb_inproj_feat(i):
            b = i % 2
            for ch in range(4):
                pbk = ch % 2
                for k in range(KC):
                    MM(psb[pbk][:, :], wB[:, k * 1282 + ch * 128: k * 1282 + (ch + 1) * 128], ht[b][:, k * 512:(k + 1) * 512],
                       k == 0, k == KC - 1, [B_wB, B_ht[b]], [PB_[pbk]])
                if i > 0:
                    CP("dve", xq[ch][:, 0:3], xq[ch][:, 512:515], [B_xq[ch]], [B_xq[ch]])
                CP("act", xq[ch][:, 3:515], psb[pbk][:, :], [PB_[pbk]], [B_xq[ch]])
                ca = ch % 2
                TS("dve", cacc[ca][:], xq[ch][:, 3:515], smalls[:, ch * 4 + 3: ch * 4 + 4], smalls[:, 16 + ch:17 + ch], ALU.mult, ALU.add,
                   [B_xq[ch], B_sm], [B_cacc[ca]])
                for j in range(3):
                    STT(cacc[ca][:], xq[ch][:, j:j + 512], smalls[:, ch * 4 + j: ch * 4 + j + 1], cacc[ca][:], ALU.mult, ALU.add,
                        [B_xq[ch], B_sm, B_cacc[ca]], [B_cacc[ca]])
                if OPT_CONV:
                    ACT(csig[ca][:], cacc[ca][:], AF.Sigmoid, [B_cacc[ca]], [B_csig[ca]])
                    TT("pool", qk[b][ch][:], cacc[ca][:], csig[ca][:], ALU.mult, [B_cacc[ca], B_csig[ca]], [B_qk[b][ch]])
                else:
                    ACT(qk[b][ch][:], cacc[ca][:], AF.Silu, [B_cacc[ca]], [B_qk[b][ch]])

        def pb_T(i, s):
            c = 4 * i + s
            b = i % 2
            cb_ = c % 2
            tok = slice(s * 128, (s + 1) * 128)
            for k in range(KC):
                MM(psb[2][:, 0:258], ht[b][:, k * 512 + s * 128: k * 512 + (s + 1) * 128], wB[:, k * 1282 + 512: k * 1282 + 770],
                   k == 0, k == KC - 1, [B_wB, B_ht[b]], [PB_[2]])
            for k in range(KC):
                MM(psb[3][:, :], ht[b][:, k * 512 + s * 128: k * 512 + (s + 1) * 128], wB[:, k * 1282 + 770: k * 1282 + 1282],
                   k == 0, k == KC - 1, [B_wB, B_ht[b]], [PB_[3]])
            CP("act", vaug[cb_][:, 0:256], psb[2][:, 0:256], [PB_[2]], [B_vaug[cb_]])
            g = gsm[cb_]
            Bg = B_gsm[cb_]
            TT("dve", g[:, 0:2], psb[2][:, 256:258], smalls[:, 20:22], ALU.add, [PB_[2], B_sm], [Bg])
            ACT(g[:, 2:3], g[:, 1:2], AF.Exp, [Bg], [Bg], scale=-1.0)
            ACT(g[:, 3:4], g[:, 2:3], AF.Ln, [Bg], [Bg], bias=1.0)
            MM(psG[:, 0:1], tri_f[:], g[:, 3:4], True, True, [Bg, B_cst], [B_psG])
            MM(psG[:, 1:2], ones_f[:], g[:, 3:4], True, True, [Bg, B_cst], [B_psG])
            TT("dve", g[:, 4:5], g[:, 0:1], psG[:, 0:1], ALU.add, [Bg, B_psG], [Bg])
            ACT(g[:, 5:6], g[:, 4:5], AF.Exp, [Bg], [Bg], bias=-LN16)
            ACT(g[:, 6:7], psG[:, 0:1], AF.Exp, [B_psG], [Bg], scale=-1.0)
            TT("dve", g[:, 7:8], g[:, 4:5], psG[:, 1:2], ALU.subtract, [Bg, B_psG], [Bg])
            ACT(g[:, 8:9], g[:, 7:8], AF.Exp, [Bg], [Bg], bias=-LN16)
            ACT(g[:, 9:10], psG[:, 1:2], AF.Exp, [B_psG], [Bg], scale=-1.0)
            if OPT_SIG:
                ACT(sgo[cb_][:], psb[3][:, 0:512], AF.Sigmoid, [PB_[3]], [B_sgo[cb_]])
                TT("dve", szm[cb_][:], psb[3][:, 256:512], sgo[cb_][:, 256:512], ALU.mult, [PB_[3], B_sgo[cb_]], [B_szm[cb_]])
            else:
                ACT(sgo[cb_][:, 0:256], psb[3][:, 0:256], AF.Sigmoid, [PB_[3]], [B_sgo[cb_]])
                ACT(szm[cb_][:], psb[3][:, 256:512], AF.Silu, [PB_[3]], [B_szm[cb_]])

        def pb_M(i, s):
            c = 4 * i + s
            b = i % 2
            cb_ = c % 2
            tok = slice(s * 128, (s + 1) * 128)
            g = gsm[cb_]
            Bg = B_gsm[cb_]
            for e2 in range(2):
                MM(psSm, qk[b][2 + e2][:, tok], qk[b][e2][:, tok], e2 == 0, e2 == 1, [B_qk[b][2 + e2], B_qk[b][e2]], [B_psSm])
            STT(pTm[cb_][:], psSm, g[:, 5:6], tri_f[:], ALU.mult, ALU.mult, [B_psSm, Bg, B_cst], [B_pTm[cb_]])
            for e2 in range(2):
                TR(psK[:, e2 * 128:(e2 + 1) * 128], qk[b][2 + e2][:, tok], ident_b[:], [B_qk[b][2 + e2], B_cst], [B_psK])
            ACT(kw[cb_][:], psK, AF.Copy, [B_psK, Bg], [B_kw[cb_]], scale=g[:, 8:9])
            for e2 in range(2):
                MM(psU[:, e2 * 256:(e2 + 1) * 256], kw[cb_][:, e2 * 128:(e2 + 1) * 128], vaug[cb_][:, 0:256], True, True,
                   [B_kw[cb_], B_vaug[cb_]], [B_psU])
            for e2 in range(2):
                MM(psUn[:, e2:e2 + 1], kw[cb_][:, e2 * 128:(e2 + 1) * 128], vaug[cb_][:, 256:257], True, True,
                   [B_kw[cb_], B_vaug[cb_]], [B_psUn])
            cbi = c % 2
            MM(psH, pTm[cb_][:], vaug[cb_][:], True, c == 0, [B_pTm[cb_], B_vaug[cb_]], [B_psH])
            if c > 0:
                for e2 in range(2):
                    MM(psH, qk[b][e2][:, tok], Cb[cbi][:, e2 * 257:(e2 + 1) * 257], False, e2 == 1,
                       [B_qk[b][e2], B_Cb[cbi]], [B_psH])
            C3 = Cst[:].rearrange("p (e f) -> p e f", e=2)
            STT(C3[:, :, 0:256], C3[:, :, 0:256], g[:, 9:10], psU.rearrange("p (e f) -> p e f", e=2), ALU.mult, ALU.add,
                [B_C, Bg, B_psU], [B_C])
            STT(C3[:, :, 256], C3[:, :, 256], g[:, 9:10], psUn, ALU.mult, ALU.add, [B_C, Bg, B_psUn], [B_C])
            ACT(Cb[1 - cbi][:], Cst[:], AF.Identity, [B_C], [B_Cb[1 - cbi]])
            ACT(g[:, 10:11], psH[:, 256:257], AF.Abs, [B_psH, Bg], [Bg], scale=g[:, 6:7])
            TS("dve", g[:, 11:12], g[:, 10:11], 1.0, None, ALU.max, None, [Bg], [Bg])
            RECIP(g[:, 12:13], g[:, 11:12], [Bg], [Bg])
            TT("dve", g[:, 12:13], g[:, 12:13], g[:, 6:7], ALU.mult, [Bg], [Bg])
            STT(hh[cb_][:], psH[:, 0:256], g[:, 12:13], sgo[cb_][:, 0:256], ALU.mult, ALU.mult, [B_psH, Bg, B_sgo[cb_]], [B_hh[cb_]])
            ls = lnst[cb_]
            Bl = B_lnst[cb_]
            sch.op("dve", lambda e: e.bn_stats(out=ls[:, 0:6], in_=hh[cb_][:]), [B_hh[cb_]], [Bl])
            sch.op("dve", lambda e: e.bn_aggr(out=ls[:, 6:8], in_=ls[:, 0:6]), [Bl], [Bl])
            if OPT_LN:
                ACT(ls[:, 8:9], ls[:, 7:8], AF.Ln, [Bl], [Bl], bias=EPS)
                ACT(ls[:, 9:10], ls[:, 8:9], AF.Exp, [Bl], [Bl], scale=-0.5)
            else:
                ACT(ls[:, 8:9], ls[:, 7:8], AF.Sqrt, [Bl], [Bl], bias=EPS)
                RECIP(ls[:, 9:10], ls[:, 8:9], [Bl], [Bl])
            TS("dve", hh[cb_][:], hh[cb_][:], ls[:, 6:7], ls[:, 9:10], ALU.subtract, ALU.mult, [B_hh[cb_], Bl], [B_hh[cb_]])
            TT("dve", hh[cb_][:], hh[cb_][:], smalls[:, 22:278], ALU.mult, [B_hh[cb_], B_sm], [B_hh[cb_]])
            TT("dve", ym[cb_][:], hh[cb_][:], szm[cb_][:], ALU.mult, [B_hh[cb_], B_szm[cb_]], [B_ym[cb_]])
            for f2 in range(2):
                TR(psY[:, f2 * 128:(f2 + 1) * 128], ym[cb_][:, f2 * 128:(f2 + 1) * 128], ident_b[:], [B_ym[cb_], B_cst], [B_psY])
            y3 = ystm[b][:].rearrange("p (f t) -> p f t", f=2)[:, :, s * 128:(s + 1) * 128]
            ACT(y3, psY.rearrange("p (f t) -> p f t", f=2), AF.Identity, [B_psY], [B_ystm[b]])
            if s == 3:
                t0 = i * 512
                q = t0 // SQ
                for f2 in range(2):
                    DMA(ybuf_d[q * 512 + 256 + f2 * 128: q * 512 + 256 + (f2 + 1) * 128, (t0 % SQ):(t0 % SQ) + 512],
                        ystm[b][:, f2 * 512:(f2 + 1) * 512], [B_ystm[b]], [B_ybuf[q][2 + f2]], "st_ystm%d_%d" % (b, f2))
                if (i + 1) % QT == 0:
                    pend_g.append(q)

        pend_g = []
        pb_load(0)
        for q in range(4):
            for f4 in range(2):
                emit_gather(q, f4)
        if NT > 1:
            pb_load(1)
        chunks = [(i, s) for i in range(NT) for s in range(4)]
        pb_inproj_feat(0)
        pb_T(0, 0)
        for ci, (i, s) in enumerate(chunks):
            if ci + 1 < len(chunks):
                i2, s2 = chunks[ci + 1]
                if s2 == 0:
                    pb_inproj_feat(i2)
                pb_T(i2, s2)
                if s2 == 3 and i2 + 2 < NT:
                    pb_load(i2 + 2)
            pb_M(i, s)
            if s == 1 and pend_g:
                q_ = pend_g.pop(0)
                emit_gather(q_, 2)
                emit_gather(q_, 3)
        while pend_g:
            q_ = pend_g.pop(0)
            emit_gather(q_, 2)
            emit_gather(q_, 3)
        release(m_persist)
        sch.barrier()
        if stop == "pb":
            return finish([("ybuf", ybuf_d.ap(), [2048, SQ], BF16), ("C", Cst[:], [128, 514], F32), ("gsm", gsm[0][:], [128, 16], F32),
                           ("hh", hh[0][:], [128, 256], F32), ("qk", qk[1][0][:], [128, 512], BF16)])

    if mode == 'ex_only':
        for q in range(4):
            for f4 in range(4):
                emit_gather(q, f4)
    B_yall = Buf("yall")
    B_ymine = Buf("ymine")
    allyb = [B_ybuf[q][f] for q in range(4) for f in range(4)]
    groups = [[0, 1, 2, 3], [4, 5, 6, 7]]
    yregs = [nc.gpsimd.alloc_register("rry%d" % f4) for f4 in range(4)]
    sch.barrier()
    if stop == "ex":
        return finish([("hTo", hTo_d.ap(), [QT, 128, KC * 512], BF16)])

    B_mg = [[Buf("mg%d_%d" % (t, sl)) for sl in range(2)] for t in range(QT)]
    yin_all = alloc([128, 16 * SQ], BF16, "yinall")
    B_yina = Buf("yina")
    assert state["off"] + 40000 < hto_off
    for f4 in range(4):
        a_, c_ = f4 // 2, f4 % 2

        def f(e, reg=yregs[f4], idx=f4, a_=a_, c_=c_):
            e.reg_load(reg, ri[0:1, idx:idx + 1])
            y3 = yin_all[:].rearrange("p (c t) -> p c t", c=16)
            dst = y3[:, a_ * 8 + c_: a_ * 8 + c_ + 7: 2, :]
            return e.dma_start(out=dst, in_=bass.AP(yall_d, reg, [[SQ, 128], [128 * SQ, 4], [1, SQ]]))
        sch.op("pool", f, B_yallp + [B_ri], [B_yina], dkey="ld_yina")
    wgj = [alloc([128, 2 * KC * 128], BF16, "wgj") for _ in range(2)]
    wpj = [alloc([128, 2 * 8 * 128], BF16, "wpj") for _ in range(2)]
    B_wgj = [Buf("wgj0"), Buf("wgj1")]
    B_wpj = [Buf("wpj0"), Buf("wpj1")]
    sga = [alloc([128, 512], F32, "sga") for _ in range(2)]
    sgb = [alloc([128, 512], F32, "sgb") for _ in range(2)]
    mo = [alloc([128, 512], BF16, "mo") for _ in range(2)]
    B_sga = [Buf("sga0"), Buf("sga1")]
    B_sgb = [Buf("sgb0"), Buf("sgb1")]
    B_mo = [Buf("mo0"), Buf("mo1")]

    def p2_wload(j):
        b = j % 2
        for a in range(2):
            DMA(wgj[b][:, a * KC * 128:(a + 1) * KC * 128], w_gj[j, :, a * KC * 128:(a + 1) * KC * 128], (), [B_wgj[b]],
                "ld_wgj%d" % b, eng="pool")
        DMA(wpj[b][:], w_pj[j], (), [B_wpj[b]], "ld_wpj%d" % b, eng="pool")

    p2_wload(0)
    it = 0
    for j in range(KC):
        b = j % 2
        if j + 1 < KC:
            p2_wload(j + 1)
        for t in range(QT):
            pb0 = (it % 2) * 4
            si = it % 2
            it += 1
            for a in range(2):
                for k in range(KC):
                    MM(psb[pb0 + a][:, :], wgj[b][:, (a * KC + k) * 128:(a * KC + k + 1) * 128],
                       hto_all[:, (t * KC + k) * 512:(t * KC + k + 1) * 512], k == 0, k == KC - 1, [B_wgj[b], B_htoa[t]], [PB_[pb0 + a]])
            for a in range(2):
                for k in range(8):
                    MM(psb[pb0 + 2 + a][:, :], wpj[b][:, (a * 8 + k) * 128:(a * 8 + k + 1) * 128],
                       yin_all[:, (a * 8 + k) * SQ + t * 512:(a * 8 + k) * SQ + (t + 1) * 512], k == 0, k == 7, [B_wpj[b], B_yina], [PB_[pb0 + 2 + a]])
            ACT(sga[si][:], psb[pb0][:, :], AF.Sigmoid, [PB_[pb0]], [B_sga[si]])
            ACT(sgb[si][:], psb[pb0 + 1][:, :], AF.Sigmoid, [PB_[pb0 + 1]], [B_sgb[si]])
            TT("dve", sga[si][:], psb[pb0 + 2][:, :], sga[si][:], ALU.mult, [PB_[pb0 + 2], B_sga[si]], [B_sga[si]])
            TT("dve", sgb[si][:], psb[pb0 + 3][:, :], sgb[si][:], ALU.mult, [PB_[pb0 + 3], B_sgb[si]], [B_sgb[si]])
            TT("pool", mo[si][:], sga[si][:], sgb[si][:], ALU.add, [B_sga[si], B_sgb[si]], [B_mo[si]])
            DMA(mgT_d[t][:, j * 512:(j + 1) * 512], mo[si][:], [B_mo[si]], [B_mg[t][si]], "st_mo%d" % si)
    release(m_persist)
    sch.barrier()
    if stop == "p2":
        return finish([("mgT", mgT_d.ap(), [QT, 128, KC * 512], BF16)])

    wO = alloc([128, KC * D], BF16, "wO")
    B_wO = Buf("wO")
    for k in range(KC):
        DMA(wO[:, k * D:(k + 1) * D], w_out[k * 128:(k + 1) * 128, :], (), [B_wO], "ld_wO", eng="pool")
    fgs = alloc([128, D], F32, "fgs")
    B_fg = Buf("fg")
    DMA(fgs[:], fg.ap(), (), [B_fg], "ld_fg")
    xin = [alloc([128, D], F32, "xin") for _ in range(2)]
    xn = [alloc([128, D], F32, "xn") for _ in range(2)]
    mti = [alloc([128, KC * 512], BF16, "mti") for _ in range(2)]
    B_mti = [Buf("mti0"), Buf("mti1")]
    st2 = [alloc([128, 40], F32, "st2") for _ in range(2)]
    B_xin = [Buf("xin0"), Buf("xin1")]
    B_xn = [Buf("xn0"), Buf("xn1")]
    B_st2 = [Buf("st20"), Buf("st21")]
    NTT = SQ // 128

    def p2c_load(tt):
        DMA(xin[tt % 2][:], xtok[tt * 128:(tt + 1) * 128, :], (), [B_xin[tt % 2]], "ld_xin%d" % (tt % 2))
        if tt % 4 == 0:
            t_ = tt // 4
            DMA(mti[t_ % 2][:], mgT_d[t_], B_mg[t_], [B_mti[t_ % 2]], "ld_mti%d" % (t_ % 2))

    p2c_load(0)
    for tt in range(NTT):
        b = tt % 2
        if tt + 1 < NTT:
            p2c_load(tt + 1)
        t, sub = tt // 4, tt % 4
        for n in range(4):
            pbk = (tt % 2) * 4 + n
            for j in range(KC):
                MM(psb[pbk][:, :], mti[t % 2][:, j * 512 + sub * 128: j * 512 + (sub + 1) * 128], wO[:, j * D + n * 512: j * D + (n + 1) * 512],
                   j == 0, j == KC - 1, [B_wO, B_mti[t % 2]], [PB_[pbk]])
            cs = slice(n * 512, (n + 1) * 512)
            TT("dve", xn[b][:, cs], psb[pbk][:, :], gate_bc[:, cs], ALU.mult, [PB_[pbk], B_gate], [B_xn[b]])
            TT("pool", xn[b][:, cs], xn[b][:, cs], xin[b][:, cs], ALU.add, [B_xn[b], B_xin[b]], [B_xn[b]])
        s2 = st2[b]
        for n in range(4):
            sch.op("dve", lambda e, b=b, s2=s2, n=n: e.bn_stats(out=s2[:, 8 + n * 6: 8 + (n + 1) * 6], in_=xn[b][:, n * 512:(n + 1) * 512]),
                   [B_xn[b]], [B_st2[b]])
        sch.op("dve", lambda e, s2=s2: e.bn_aggr(out=s2[:, 4:6], in_=s2[:, 8:32]), [B_st2[b]], [B_st2[b]])
        STT(s2[:, 0:1], s2[:, 4:5], s2[:, 4:5], s2[:, 5:6], ALU.mult, ALU.add, [B_st2[b]], [B_st2[b]])
        ACT(s2[:, 1:2], s2[:, 0:1], AF.Sqrt, [B_st2[b]], [B_st2[b]], bias=EPS)
        RECIP(s2[:, 2:3], s2[:, 1:2], [B_st2[b]], [B_st2[b]])
        STT(xn[b][:], xn[b][:], s2[:, 2:3], fgs[:], ALU.mult, ALU.mult, [B_xn[b], B_st2[b], B_fg], [B_xn[b]])
        DMA(out_d[tt * 128:(tt + 1) * 128, :], xn[b][:], [B_xn[b]], [Buf("o")], "st_out%d" % b)
    sch.barrier()
    sch.emit()
    return nc, dbg_out


def _regload(e, reg, ap):
    return e.reg_load(reg, ap)


def _dyn(t, reg, const_off, pattern):
    return bass.AP(t, reg + const_off, pattern)


def _prep_inputs(S, x, c, norm_gain, w_ada, b_ada, w_in, b_gate_if, conv_w, conv_b, mlstm_norm_gain,
                 w_proj_attn, w_proj_mlstm, w_out, final_gain):
    f = np.float32
    NS = S // 128
    SQ = S // 4
    x = np.asarray(x, f)
    w_in0 = np.asarray(w_in, f)[0]
    ident = np.eye(128, dtype=f)
    tri = np.triu(np.ones((128, 128), f))
    ones = np.ones((128, 128), f)
    cst = np.ascontiguousarray(np.concatenate([ident, tri, ones], axis=1))
    xTs = []
    for b in range(2):
        a = x[b].reshape(NS // 4, 512, KC, 128).transpose(0, 3, 2, 1)
        xTs.append(np.ascontiguousarray(a).reshape(NS // 4, 128, KC * 512))
    w_ada0 = np.ascontiguousarray(np.asarray(w_ada, f)[0])
    b_row = np.ascontiguousarray(np.asarray(b_ada, f)[0].reshape(1, -1))
    ngc = np.ascontiguousarray(np.asarray(norm_gain, f)[0].reshape(KC, 128).T)
    wg4 = w_in0[:, O_GA:O_GA + 2 * D].reshape(KC, 128, 2, KC, 128)
    w_gj = np.ascontiguousarray(wg4.transpose(3, 1, 2, 0, 4)).reshape(KC, 128, 2 * KC * 128)
    wp4 = np.stack([np.asarray(w_proj_attn, f)[0], np.asarray(w_proj_mlstm, f)[0]], axis=0).reshape(2, 8, 128, KC, 128)
    w_pj = np.ascontiguousarray(wp4.transpose(3, 2, 0, 1, 4)).reshape(KC, 128, 2 * 8 * 128)
    w_o = np.ascontiguousarray(np.asarray(w_out, f)[0])
    fgb = np.ascontiguousarray(np.broadcast_to(np.asarray(final_gain, f)[None, :], (128, D)))
    cwf = np.asarray(conv_w, f)[0]
    cbf = np.asarray(conv_b, f)[0]
    bg = np.asarray(b_gate_if, f)[0]
    mgf = np.asarray(mlstm_norm_gain, f)[0]
    ki = np.arange(128)[:, None]
    qi = np.arange(128)[None, :]
    in_maps = []
    for core in range(8):
        b, g = core // 4, core % 4
        cols = []
        for off in (O_QA, O_KA, O_VA, O_ZA):
            for lh in range(2):
                h = 2 * g + lh
                cols.append(np.arange(off + h * 128, off + (h + 1) * 128))
        cols.append(np.arange(O_QM + g * 256, O_QM + (g + 1) * 256))
        cols.append(np.arange(O_KM + g * 256, O_KM + (g + 1) * 256))
        cols.append(np.arange(O_VM + g * 256, O_VM + (g + 1) * 256))
        cols.append(np.array([O_IF + g, O_IF + 4 + g]))
        cols.append(np.arange(O_OM + g * 256, O_OM + (g + 1) * 256))
        cols.append(np.arange(O_ZM + g * 256, O_ZM + (g + 1) * 256))
        cols = np.concatenate(cols)
        assert cols.size == 2306
        w1 = np.ascontiguousarray(w_in0[:, cols])
        cwc = np.zeros((128, 16), f)
        cbc = np.zeros((128, 4), f)
        for ch in range(4):
            base = (0 if ch < 2 else 1024) + g * 256 + (ch % 2) * 128
            cwc[:, ch * 4:(ch + 1) * 4] = cwf[:, base:base + 128].T
            cbc[:, ch] = cbf[base:base + 128]
        bifc = np.ascontiguousarray(np.broadcast_to(np.array([bg[g], bg[4 + g]], f)[None, :], (128, 2)))
        mgc = np.ascontiguousarray(np.broadcast_to(mgf[g * 256:(g + 1) * 256][None, :], (128, 256)))
        ab = np.zeros((128, 6, 256), f)
        for p, d in enumerate((1, 4, 16)):
            for lh in range(2):
                slope = 2.0 ** (-(2 * g + lh + 1))
                for half, shift in ((0, 128), (1, 0)):
                    delta = qi - ki + shift
                    valid = (delta >= 0) & (delta <= 128)
                    ab[:, p * 2 + lh, half * 128:(half + 1) * 128] = np.where(valid, -slope * d * delta, NEG)
        in_maps.append({
            "xT": xTs[b],
            "xtok": np.ascontiguousarray(x[b, g * SQ:(g + 1) * SQ, :]),
            "ccol": np.ascontiguousarray(np.asarray(c, f)[b].reshape(KC, 128).T),
            "w_ada": w_ada0, "b_row": b_row, "ng": ngc, "w1": w1, "w_gj": w_gj, "w_pj": w_pj,
            "bif": bifc, "cw": cwc, "cb": cbc, "mg": mgc,
            "w_out": w_o, "fg": fgb, "cst": cst,
            "abias": np.ascontiguousarray(ab.reshape(128, 6 * 256)),
            "roff": np.array([[(g * 4 + f4) * 512 * SQ for f4 in range(4)]
                              + [(g * (SQ // 512) + t) * 128 * KC * 512 for t in range(SQ // 512)]], dtype=np.int32),
        })
    return in_maps


_STOP = None


def kernel(x, c, norm_gain, w_ada, b_ada, w_in, b_gate_if, conv_w, conv_b, mlstm_norm_gain,
           w_proj_attn, w_proj_mlstm, w_out, final_gain):
    S = int(np.asarray(x).shape[1])
    in_maps = _prep_inputs(S, x, c, norm_gain, w_ada, b_ada, w_in, b_gate_if, conv_w, conv_b, mlstm_norm_gain,
                           w_proj_attn, w_proj_mlstm, w_out, final_gain)
    nc, _ = build(S, stop=_STOP)
    res = run_bass_kernel_spmd(nc, in_maps, core_ids=list(range(8)))
    if _STOP is not None:
        return res.results
    SQ = S // 4
    out = np.zeros((2, S, D), np.float32)
    for core in range(8):
        b, g = core // 4, core % 4
        out[b, g * SQ:(g + 1) * SQ, :] = res.results[core]["out"]
    return out
```

```python
import numpy as np
import concourse.bass as bass
import concourse.mybir as mybir
from concourse.bass_utils import run_bass_kernel_spmd

F32 = mybir.dt.float32
BF16 = mybir.dt.bfloat16
I32 = mybir.dt.int32
AF = mybir.ActivationFunctionType
ALU = mybir.AluOpType

D = 2048
KC = 16
EPS = 1e-6
SEQ = 8192
import os as _os
OPT_CONV = _os.environ.get("OPT_CONV", "0") == "1"
OPT_SIG = _os.environ.get("OPT_SIG", "1") == "1"
OPT_LN = _os.environ.get("OPT_LN", "1") == "1"
NEG = -30000.0
LN16 = 2.772588722239781
O_QA, O_KA, O_VA, O_ZA, O_QM, O_KM, O_VM, O_OM, O_ZM, O_IF, O_GA, O_GB = (
    0, 1024, 2048, 3072, 4096, 5120, 6144, 7168, 8192, 9216, 9224, 11272)


class Buf:
    __slots__ = ("name", "w", "r", "excl")

    def __init__(self, name, excl=False):
        self.name = name
        self.w = None
        self.r = {}
        self.excl = excl


class Sched:
    ENG = ("pe", "act", "dve", "pool", "sp")

    def __init__(self, nc):
        self.nc = nc
        self.prog = {e: [] for e in self.ENG}
        self.cnt = {}
        self.known = {e: {} for e in self.ENG}
        self.sems = {}

    def op(self, eng, fn, reads=(), writes=(), dkey=None, dinc=16):
        deps = {}

        def add(tok):
            if tok is None:
                return
            k, v = tok
            if deps.get(k, 0) < v:
                deps[k] = v

        ex = [b for b in reads if b.excl]
        if ex:
            writes = list(writes) + ex
        for b in reads:
            add(b.w)
        for b in writes:
            add(b.w)
            for k, v in b.r.items():
                add((k, v))
        waits = []
        kn = self.known[eng]
        for k, v in deps.items():
            if eng == "pe" and k == "pe":
                continue
            if kn.get(k, 0) >= v:
                continue
            kn[k] = v
            waits.append((k, v))
        if dkey is None:
            key, inc = eng, 1
        else:
            key, inc = dkey, dinc
        self.cnt[key] = self.cnt.get(key, 0) + inc
        tok = (key, self.cnt[key])
        self.prog[eng].append((waits, fn, key, inc))
        for b in reads:
            if b.r.get(key, 0) < tok[1]:
                b.r[key] = tok[1]
        for b in writes:
            b.w = tok
            b.r = {}
        return tok

    def barrier(self):
        for e in self.ENG:
            waits = []
            for k, v in self.cnt.items():
                if (k == e and e == "pe") or k == "cc":
                    continue
                if self.known[e].get(k, 0) >= v:
                    continue
                self.known[e][k] = v
                waits.append((k, v))
            if waits:
                self.prog[e].append((waits, None, None, 0))

    def emit(self):
        nc = self.nc
        keys = list(self.cnt.keys())
        for k in keys:
            self.sems[k] = nc.alloc_semaphore("s_" + k)
        engmap = {"pe": "tensor", "act": "scalar", "dve": "vector", "pool": "gpsimd", "sp": "sync"}
        with nc.Block() as block:
            for e in self.ENG:
                prog = self.prog[e]

                def body(eng, prog=prog):
                    for waits, fn, key, inc in prog:
                        for k, v in waits:
                            eng.wait_ge(self.sems[k], v)
                        if fn is not None:
                            ins = fn(eng)
                            ins.then_inc(self.sems[key], inc)

                getattr(block, engmap[e])(body)


def build(S=SEQ, stop=None, mode=None, sub=99):
    assert S % 2048 == 0
    NT = S // 512
    NS = S // 128
    NU = S // 2048
    SQ = S // 4
    QT = SQ // 512
    nc = bass.Bass("TRN2", target_bir_lowering=False)
    sch = Sched(nc)

    def din(name, shape, dt=F32):
        if mode in ("pa_only", "pb_only", "ex_only") and name not in ("w1", "cst", "abias", "roff", "hT_in", "ccol", "ng", "bif", "cw", "cb", "mg"):
            shape = [1, 1]
        return nc.dram_tensor(name, list(shape), dt, kind="ExternalInput")

    xT = din("xT", [NT, 128, KC * 512])
    xtok = din("xtok", [SQ, D])
    ccol = din("ccol", [128, KC])
    w_ada = din("w_ada", [D, 3 * D])
    b_row = din("b_row", [1, 3 * D])
    ng = din("ng", [128, KC])
    w1 = din("w1", [D, 2306])
    w_gj = din("w_gj", [KC, 128, 2 * KC * 128])
    w_pj = din("w_pj", [KC, 128, 2 * 8 * 128])
    bif = din("bif", [128, 2])
    cw = din("cw", [128, 16])
    cb = din("cb", [128, 4])
    mg = din("mg", [128, 256])
    w_out = din("w_out", [D, D])
    fg = din("fg", [128, D])
    cst = din("cst", [128, 384])
    abias_d = din("abias", [128, 6 * 256])
    roff = din("roff", [1, 4 + QT], I32)
    out_d = nc.dram_tensor("out", [SQ, D], F32, kind="ExternalOutput")

    hT_d = nc.dram_tensor("hT_s", [NT, 128, KC * 512], BF16)
    hTo_d = nc.dram_tensor("hTo_s", [QT, 128, KC * 512], BF16)
    ybuf_d = nc.dram_tensor("ybuf_s", [4 * 512, SQ], BF16)
    yall_d = nc.dram_tensor("yall_s", [4 * 4 * 512, SQ], BF16)
    ymine_d = nc.dram_tensor("ymine_s", [4 * 512, SQ], BF16)
    maT_d = nc.dram_tensor("maT_s", [QT, 128, KC * 512], BF16)
    mgT_d = nc.dram_tensor("mgT_s", [QT, 128, KC * 512], BF16)
    dbg_out = {}

    SB_LO = 16512
    SB_HI = 229344
    state = {"off": SB_LO, "n": 0}

    def alloc(shape, dt, name=None):
        nbytes = int(np.prod(shape[1:])) * (4 if dt in (F32, I32) else 2)
        off = (state["off"] + 63) // 64 * 64
        assert off + nbytes <= SB_HI, ("SBUF overflow", name, off, nbytes)
        state["off"] = off + nbytes
        state["n"] += 1
        return nc.alloc_sbuf_tensor_at("%s_%d" % (name or "t", state["n"]), list(shape), dt, offset=off)

    def alloc_top(shape, dt, name):
        nbytes = int(np.prod(shape[1:])) * (4 if dt in (F32, I32) else 2)
        off = (SB_HI - nbytes) // 64 * 64
        state["n"] += 1
        return nc.alloc_sbuf_tensor_at("%s_%d" % (name, state["n"]), list(shape), dt, offset=off), off

    def mark():
        return state["off"]

    def release(m):
        state["off"] = m

    psb = [nc.alloc_psum_tensor("ps%d" % i, [128, 512], F32) for i in range(8)]
    PB_ = [Buf("psum%d" % i, excl=True) for i in range(8)]

    def MM(out, lhsT, rhs, st, sp, R, W):
        return sch.op("pe", lambda e: e.matmul(out, lhsT=lhsT, rhs=rhs, start=st, stop=sp), R, W)

    def TR(out, in_, ident, R, W):
        return sch.op("pe", lambda e: e.transpose(out=out, in_=in_, identity=ident), R, W)

    def ACT(out, in_, func, R, W, scale=None, bias=None):
        def f(e):
            kw = {}
            if scale is not None:
                kw["scale"] = scale
            if bias is not None:
                kw["bias"] = bias
            return e.activation(out=out, in_=in_, func=func, **kw)
        return sch.op("act", f, R, W)

    def TT(eng, out, in0, in1, op, R, W):
        return sch.op(eng, lambda e: e.tensor_tensor(out=out, in0=in0, in1=in1, op=op), R, W)

    def TS(eng, out, in0, s1, s2, op0, op1, R, W):
        if op1 is None:
            return sch.op(eng, lambda e: e.tensor_scalar(out=out, in0=in0, scalar1=s1, scalar2=None, op0=op0), R, W)
        return sch.op(eng, lambda e: e.tensor_scalar(out=out, in0=in0, scalar1=s1, scalar2=s2, op0=op0, op1=op1), R, W)

    def STT(out, in0, scalar, in1, op0, op1, R, W):
        return sch.op("dve", lambda e: e.scalar_tensor_tensor(out=out, in0=in0, scalar=scalar, in1=in1, op0=op0, op1=op1), R, W)

    def CP(eng, out, in_, R, W):
        if eng == "act":
            return sch.op("act", lambda e: e.copy(out=out, in_=in_), R, W)
        return sch.op(eng, lambda e: e.tensor_copy(out=out, in_=in_), R, W)

    def RECIP(out, in_, R, W):
        return sch.op("dve", lambda e: e.reciprocal(out=out, in_=in_), R, W)

    def MEMSET(eng, ap, val, W):
        return sch.op(eng, lambda e: e.memset(ap, val), (), W)

    def DMA(out, in_, R, W, key, eng="sp", **kw):
        return sch.op(eng, lambda e: e.dma_start(out=out, in_=in_, **kw), R, W, dkey=key)

    def bc_mid(t, n, reps, off=0):
        a = t[:, off:off + n]
        return bass.AP(a.tensor, a.offset, [list(a.ap[0]), [0, reps], [1, n]])

    def finish(dumps):
        for name, src_ap, shape, dt in dumps:
            o = nc.dram_tensor("dbg_" + name, list(shape), dt, kind="ExternalOutput")
            DMA(o.ap(), src_ap, [], [Buf("dbg")], "dbg_" + name)
        sch.barrier()
        sch.emit()
        return nc, None

    ident_f = alloc([128, 128], F32, "identf")
    tri_f = alloc([128, 128], F32, "trif")
    ones_f = alloc([128, 128], F32, "onesf")
    ident_b = alloc([128, 128], BF16, "identb")
    ones_b = alloc([128, 128], BF16, "onesb")
    cst_sb = alloc([128, 384], F32, "cst")
    gate_bc = alloc([128, D], F32, "gatebc")
    c_f = alloc([128, KC], F32, "cf")
    c_b = alloc([128, KC], BF16, "cb16")
    ng_sb = alloc([128, KC], F32, "ng")
    A_col = alloc([128, KC], F32, "Acol")
    sh_col = alloc([128, KC], F32, "shcol")
    ri = alloc([1, 4 + QT], I32, "ri")
    B_cst = Buf("cst")
    B_mod = Buf("modrow")
    B_col = Buf("cols")
    B_ri = Buf("ri")

    DMA(cst_sb[:], cst.ap(), (), [B_cst], "ld_cst")
    DMA(c_f[:], ccol.ap(), (), [B_cst], "ld_cst")
    DMA(ng_sb[:], ng.ap(), (), [B_cst], "ld_cst")
    DMA(ri[:], roff.ap(), (), [B_ri], "ld_ri")
    CP("dve", ident_f[:], cst_sb[:, 0:128], [B_cst], [B_cst])
    CP("dve", tri_f[:], cst_sb[:, 128:256], [B_cst], [B_cst])
    CP("dve", ones_f[:], cst_sb[:, 256:384], [B_cst], [B_cst])
    CP("dve", ident_b[:], cst_sb[:, 0:128], [B_cst], [B_cst])
    CP("dve", ones_b[:], cst_sb[:, 256:384], [B_cst], [B_cst])
    CP("dve", c_b[:], c_f[:], [B_cst], [B_cst])

    m_persist = mark()

    if mode not in ('pa_only', 'pb_only', 'ex_only'):
        modrow = alloc([1, 3 * D], F32, "modrow")
        brow = alloc([1, 3 * D], F32, "brow")
        B_brow = Buf("brow")
        DMA(brow[:], b_row.ap(), (), [B_brow], "ld_brow")
        wa = [alloc([128, 2048], BF16, "wa") for _ in range(2)]
        B_wa = [Buf("wa0"), Buf("wa1")]
        n_wa = 0
        for r in range(3):
            for k in range(KC):
                b = n_wa % 2
                n_wa += 1
                DMA(wa[b][:], w_ada[k * 128:(k + 1) * 128, r * 2048:(r + 1) * 2048], (), [B_wa[b]],
                    "ld_wa%d" % b, eng="pool")
                for n in range(4):
                    MM(psb[n][0:1, :], c_b[:, k:k + 1], wa[b][:, n * 512:(n + 1) * 512], k == 0, k == KC - 1,
                       [B_wa[b], B_cst], [PB_[n]])
            for n in range(4):
                TT("dve", modrow[0:1, r * 2048 + n * 512: r * 2048 + (n + 1) * 512], psb[n][0:1, :],
                   brow[0:1, r * 2048 + n * 512: r * 2048 + (n + 1) * 512], ALU.add, [PB_[n], B_brow], [B_mod])
        for j in range(32):
            TR(psb[4][:, j:j + 1], modrow[0:1, j * 128:(j + 1) * 128], ident_f[0:1, 0:1], [B_mod, B_cst], [PB_[4]])
        CP("dve", sh_col[:], psb[4][:, 0:16], [PB_[4]], [B_col])
        STT(A_col[:], psb[4][:, 16:32], 1.0, ng_sb[:], ALU.add, ALU.mult, [PB_[4], B_cst], [B_col])
        B_gate = Buf("gate")
        for n in range(4):
            MM(psb[n][:, :], ones_f[0:1, :], modrow[0:1, 2 * D + n * 512: 2 * D + (n + 1) * 512], True, True,
               [B_mod, B_cst], [PB_[n]])
            CP("act", gate_bc[:, n * 512:(n + 1) * 512], psb[n][:, :], [PB_[n]], [B_gate])
        release(m_persist)
        sch.barrier()
        if stop == "mod":
            return finish([("A", A_col[:], [128, KC], F32), ("sh", sh_col[:], [128, KC], F32), ("gate", gate_bc[:], [128, D], F32)])

        xt = [alloc([128, KC * 512], F32, "xt") for _ in range(3)]
        sq = [alloc([128, KC * 512], BF16, "sq") for _ in range(2)]
        sd = [alloc([128, 512], F32, "sd") for _ in range(2)]
        rs = [alloc([128, 512], F32, "rs") for _ in range(2)]
        hts = [alloc([128, KC * 512], BF16, "hts") for _ in range(2)]
        B_xt = [Buf("xt0"), Buf("xt1"), Buf("xt2")]
        B_sq = [Buf("sq0"), Buf("sq1")]
        B_sd = [Buf("sd0"), Buf("sd1")]
        B_rs = [Buf("rs0"), Buf("rs1")]
        B_hts = [Buf("hts0"), Buf("hts1")]
        B_hT = [Buf("hT%d" % i) for i in range(NT)]

        def p0_load(i):
            DMA(xt[i % 3][:], xT[i], (), [B_xt[i % 3]], "ld_xt%d" % (i % 3))

        def p0_a(i):
            b = i % 2
            for hf in range(2):
                cs = slice(hf * 8 * 512, (hf + 1) * 8 * 512)
                ACT(sq[b][:, cs], xt[i % 3][:, cs], AF.Square, [B_xt[i % 3]], [B_sq[b]])
            for k in range(KC):
                MM(psb[b][:, :], ones_b[:], sq[b][:, k * 512:(k + 1) * 512], k == 0, k == KC - 1, [B_sq[b], B_cst], [PB_[b]])

        def p0_a2(i):
            b = i % 2
            ACT(sd[b][:], psb[b][:, :], AF.Sqrt, [PB_[b]], [B_sd[b]], scale=1.0 / D, bias=EPS)
            RECIP(rs[b][:], sd[b][:], [B_sd[b]], [B_rs[b]])
            x3 = xt[i % 3][:].rearrange("p (k t) -> p k t", k=KC)
            TT("dve", x3, x3, bc_mid(rs[b], 512, KC), ALU.mult, [B_xt[i % 3], B_rs[b]], [B_xt[i % 3]])

        def p0_b(i):
            b = i % 2
            for k in range(KC):
                if k % 8 in (2, 5, 7):
                    TS("dve", hts[b][:, k * 512:(k + 1) * 512], xt[i % 3][:, k * 512:(k + 1) * 512], A_col[:, k:k + 1], sh_col[:, k:k + 1],
                       ALU.mult, ALU.add, [B_xt[i % 3], B_col], [B_hts[b]])
                else:
                    ACT(hts[b][:, k * 512:(k + 1) * 512], xt[i % 3][:, k * 512:(k + 1) * 512], AF.Identity, [B_xt[i % 3], B_col], [B_hts[b]],
                        scale=A_col[:, k:k + 1], bias=sh_col[:, k:k + 1])
            DMA(hT_d[i], hts[b][:], [B_hts[b]], [B_hT[i]], "st_hts%d" % b)

        p0_load(0)
        p0_load(1)
        p0_load(2)
        p0_a(0)
        p0_a2(0)
        for i in range(NT):
            if i + 1 < NT:
                p0_a(i + 1)
            p0_b(i)
            if i + 1 < NT:
                p0_a2(i + 1)
            if i + 3 < NT:
                p0_load(i + 3)
        release(m_persist)
        sch.barrier()
        if stop == "p0":
            return finish([("hT", hT_d.ap(), [NT, 128, KC * 512], BF16)])

    else:
        B_hT = [Buf('hT%d' % i) for i in range(NT)]
        hT_d = din('hT_in', [NT, 128, KC * 512], BF16)
    B_hTo = Buf("hTo")
    for t in range(QT):
        reg = nc.gpsimd.alloc_register("rrh%d" % t)

        def f(e, reg=reg, idx=4 + t, t=t):
            e.reg_load(reg, ri[0:1, idx:idx + 1])
            return e.dma_start(out=hTo_d[t], in_=bass.AP(hT_d, reg, [[KC * 512, 128], [1, KC * 512]]))
        sch.op("pool", f, [B_hT[i] for i in range(NT)] + [B_ri], [B_hTo], dkey="cp_h")
    hto_all, hto_off = alloc_top([128, QT * KC * 512], BF16, "htoall")
    B_htoa = [Buf("htoa%d" % t) for t in range(QT)]
    B_ybuf = [[Buf("ybuf%d_%d" % (q, f)) for f in range(4)] for q in range(4)]
    B_yallp = [Buf("yall%d" % i) for i in range(16)]
    groups = [[0, 1, 2, 3], [4, 5, 6, 7]]

    def emit_gather(q, f4):
        i_ = q * 4 + f4

        def emit_cc(e):
            return e.collective_compute("AllGather", ALU.bypass, replica_groups=groups,
                                        ins=[ybuf_d[q * 512 + f4 * 128: q * 512 + (f4 + 1) * 128, :]],
                                        outs=[yall_d[i_ * 512:(i_ + 1) * 512, :]])
        sch.op("pool", emit_cc, [B_ybuf[q][f4]], [B_yallp[i_]], dkey="cc", dinc=1)
    if mode not in ('pb_only', 'ex_only'):
        wA = alloc([128, KC * 1024], BF16, "wA")
        B_wA = Buf("wA")
        for k in range(KC):
            DMA(wA[:, k * 1024:(k + 1) * 1024], w1[k * 128:(k + 1) * 128, 0:1024], (), [B_wA], "ld_wA", eng="pool")
        abias = alloc([128, 6 * 256], F32, "abias")
        B_ab = Buf("abias")
        DMA(abias[:], abias_d.ap(), (), [B_ab], "ld_ab")
        ht = [alloc([128, KC * 512], BF16, "ht") for _ in range(2)]
        B_ht = [Buf("ht0"), Buf("ht1")]
        qT = alloc([128, 2 * 2048], BF16, "qT")
        kT = alloc([128, 2 * 4096], BF16, "kT")
        vT = alloc([128, 2 * 2048], BF16, "vT")
        zs = alloc([128, 2 * 2048], BF16, "zs")
        B_q = [[Buf("q") for _ in range(4)] for _ in range(2)]
        B_k = [[[Buf("k") for _ in range(4)] for _ in range(2)] for _ in range(2)]
        B_v = [[Buf("v") for _ in range(4)] for _ in range(2)]
        B_z = [[Buf("z") for _ in range(4)] for _ in range(2)]
        NSLOT = {1: 3, 4: 8, 16: 32}
        Vv = {d: alloc([128, 2 * NSLOT[d] * 128], BF16, "Vv%d" % d) for d in (1, 4, 16)}
        B_Vv = {d: [[Buf("vv") for _ in range(NSLOT[d])] for _ in range(2)] for d in (1, 4, 16)}
        accn = alloc([128, 2048], F32, "accn")
        accd = alloc([128, 2048], F32, "accd")
        B_an = [Buf("an") for _ in range(4)]
        B_ad = [Buf("ad") for _ in range(4)]
        sbs = [alloc([128, 256], F32, "sbs") for _ in range(2)]
        B_sbs = [Buf("sbs0"), Buf("sbs1")]
        pTs = [alloc([128, 256], BF16, "pT") for _ in range(3)]
        B_pT = [Buf("pT%d" % i) for i in range(3)]
        yst = [alloc([128, 2048], BF16, "yst") for _ in range(2)]
        B_yst = [Buf("yst0"), Buf("yst1")]

        def pa_load(i):
            DMA(ht[i % 2][:], hT_d[i], [B_hT[i]], [B_ht[i % 2]], "ld_ht%d" % (i % 2))

        psS = [psb[3][:, 0:256], psb[4][:, 0:256], psb[5][:, 0:256]]
        B_psS = [PB_[3], PB_[4], PB_[5]]
        psN = [psb[6][:, 0:128], psb[7][:, 0:128]]
        psD = [psb[6][:, 128:256], psb[7][:, 128:256]]
        B_psN = [PB_[6], PB_[7]]
        B_psD = [PB_[6], PB_[7]]
        psT = [psb[0][:].bitcast(BF16)[:, 0:128], psb[1][:].bitcast(BF16)[:, 0:128]]
        B_psT = [PB_[0], PB_[1]]
        cnt = {"ip": 0, "s": 0, "n": 0, "t": 0, "p": 0, "sb": 0}

        def pa_inproj(i):
            u, m = i // 4, i % 4
            slot = u % 2
            b = i % 2
            for c in range(8):
                pb = cnt["ip"] % 3
                cnt["ip"] += 1
                for k in range(KC):
                    MM(psb[pb][:, :], wA[:, k * 1024 + c * 128: k * 1024 + (c + 1) * 128], ht[b][:, k * 512:(k + 1) * 512],
                       k == 0, k == KC - 1, [B_wA, B_ht[b]], [PB_[pb]])
                h = c % 2
                kind = c // 2
                if kind == 0:
                    CP("dve", qT[:, h * 2048 + m * 512: h * 2048 + (m + 1) * 512], psb[pb][:, :], [PB_[pb]], [B_q[h][m]])
                elif kind == 1:
                    o = h * 4096 + slot * 2048 + m * 512
                    CP("dve", kT[:, o:o + 512], psb[pb][:, :], [PB_[pb]], [B_k[h][slot][m]])
                elif kind == 2:
                    CP("act", vT[:, h * 2048 + m * 512: h * 2048 + (m + 1) * 512], psb[pb][:, :], [PB_[pb]], [B_v[h][m]])
                else:
                    ACT(zs[:, h * 2048 + m * 512: h * 2048 + (m + 1) * 512], psb[pb][:, :], AF.Silu, [PB_[pb]], [B_z[h][m]])

        def blocks_for(u):
            L = []
            for mm in range(16):
                g = 16 * u + mm
                prev = None
                if g > 0:
                    prev = (u % 2, 128 * (mm - 1)) if mm > 0 else ((u - 1) % 2, 1920)
                L.append((0, 1, 128 * mm, 1, [mm // 4], prev, g % 3, (g - 1) % 3))
            for m4 in range(4):
                for r in range(4):
                    g = 4 * u + m4
                    prev = None
                    if g > 0:
                        prev = (u % 2, 512 * (m4 - 1) + r) if m4 > 0 else ((u - 1) % 2, 1536 + r)
                    L.append((1, 4, 512 * m4 + r, 4, [m4], prev, r * 2 + g % 2, r * 2 + (g - 1) % 2))
            for r in range(16):
                prev = ((u - 1) % 2, r) if u > 0 else None
                L.append((2, 16, r, 16, [0, 1, 2, 3], prev, r * 2 + u % 2, r * 2 + (u - 1) % 2))
            return L

        def pa_attn(u, h):
            if sub < 1:
                return
            slot = u % 2
            blks = blocks_for(u)
            pend = []

            def stage1(bi):
                p, d, c0, st, ms, prev, vc, vp = blks[bi]
                qcols = slice(h * 2048 + c0, h * 2048 + c0 + 127 * st + 1, st)
                si = cnt["s"] % 3
                cnt["s"] += 1
                ti = cnt["t"] % 2
                cnt["t"] += 1
                TR(psT[ti], vT[:, qcols], ident_b[:], [B_v[h][m] for m in ms] + [B_cst], [B_psT[ti]])
                ACT(Vv[d][:, (h * NSLOT[d] + vc) * 128:(h * NSLOT[d] + vc + 1) * 128], psT[ti], AF.Identity, [B_psT[ti]], [B_Vv[d][h][vc]])
                kc0 = h * 4096 + slot * 2048 + c0
                lo = 0
                if sub < 1.2:
                    return (bi, 0, 0)
                if prev is not None:
                    ps_, pc0 = prev
                    pk0 = h * 4096 + ps_ * 2048 + pc0
                    pm = sorted(set([(pc0 + j * st) // 512 for j in (0, 127)])) if st < 16 else [0, 1, 2, 3]
                    MM(psS[si][:, 0:128], kT[:, pk0: pk0 + 127 * st + 1: st], qT[:, qcols], True, True,
                       [B_k[h][ps_][m] for m in pm] + [B_q[h][m] for m in ms], [B_psS[si]])
                else:
                    lo = 128
                MM(psS[si][:, 128:256], kT[:, kc0: kc0 + 127 * st + 1: st], qT[:, qcols], True, True,
                   [B_k[h][slot][m] for m in ms] + [B_q[h][m] for m in ms], [B_psS[si]])
                if sub < 1.4:
                    return (bi, 0, 0)
                sbi = cnt["sb"] % 2
                cnt["sb"] += 1
                ab0 = (p * 2 + h) * 256
                STT(sbs[sbi][:, lo:256], psS[si][:, lo:256], 128.0 ** -0.5, abias[:, ab0 + lo: ab0 + 256], ALU.mult, ALU.add,
                    [B_psS[si], B_ab], [B_sbs[sbi]])
                if sub < 1.6:
                    return (bi, 0, 0)
                pi = cnt["p"] % 3
                cnt["p"] += 1
                ACT(pTs[pi][:, lo:256], sbs[sbi][:, lo:256], AF.Exp, [B_sbs[sbi]], [B_pT[pi]])
                return (bi, pi, lo)

            def stage2(info):
                bi, pi, lo = info
                p, d, c0, st, ms, prev, vc, vp = blks[bi]
                ni = cnt["n"] % 2
                cnt["n"] += 1
                vcur = Vv[d][:, (h * NSLOT[d] + vc) * 128:(h * NSLOT[d] + vc + 1) * 128]
                vprev = Vv[d][:, (h * NSLOT[d] + vp) * 128:(h * NSLOT[d] + vp + 1) * 128]
                if lo == 0:
                    MM(psN[ni], vprev, pTs[pi][:, 0:128], True, False, [B_Vv[d][h][vp], B_pT[pi]], [B_psN[ni]])
                    MM(psN[ni], vcur, pTs[pi][:, 128:256], False, True, [B_Vv[d][h][vc], B_pT[pi]], [B_psN[ni]])
                    MM(psD[ni], ones_b[:], pTs[pi][:, 0:128], True, False, [B_cst, B_pT[pi]], [B_psD[ni]])
                    MM(psD[ni], ones_b[:], pTs[pi][:, 128:256], False, True, [B_cst, B_pT[pi]], [B_psD[ni]])
                else:
                    MM(psN[ni], vcur, pTs[pi][:, 128:256], True, True, [B_Vv[d][h][vc], B_pT[pi]], [B_psN[ni]])
                    MM(psD[ni], ones_b[:], pTs[pi][:, 128:256], True, True, [B_cst, B_pT[pi]], [B_psD[ni]])
                ocols = slice(c0, c0 + 127 * st + 1, st)
                if p == 0:
                    CP("act", accn[:, ocols], psN[ni], [B_psN[ni]], [B_an[m] for m in ms])
                    CP("act", accd[:, ocols], psD[ni], [B_psD[ni]], [B_ad[m] for m in ms])
                else:
                    TT("dve", accn[:, ocols], psN[ni], accn[:, ocols], ALU.add, [B_psN[ni]] + [B_an[m] for m in ms], [B_an[m] for m in ms])
                    TT("dve", accd[:, ocols], psD[ni], accd[:, ocols], ALU.add, [B_psD[ni]] + [B_ad[m] for m in ms], [B_ad[m] for m in ms])

            for bi in range(len(blks)):
                pend.append(stage1(bi))
                if len(pend) > 1:
                    x_ = pend.pop(0)
                    if sub >= 2:
                        stage2(x_)
            while pend:
                x_ = pend.pop(0)
                if sub >= 2:
                    stage2(x_)
            if sub < 3:
                return
            yb = (u * 2 + h) % 2
            for m in range(4):
                cs = slice(m * 512, (m + 1) * 512)
                ACT(accd[:, cs], accd[:, cs], AF.Ln, [B_ad[m]], [B_ad[m]])
                ACT(accd[:, cs], accd[:, cs], AF.Exp, [B_ad[m]], [B_ad[m]], scale=-1.0)
                TT("pool", accn[:, cs], accn[:, cs], accd[:, cs], ALU.mult, [B_an[m], B_ad[m]], [B_an[m]])
                TT("pool", yst[yb][:, cs], accn[:, cs], zs[:, h * 2048 + m * 512: h * 2048 + (m + 1) * 512], ALU.mult,
                   [B_an[m], B_z[h][m]], [B_yst[yb]])
            t0 = u * 2048
            while t0 < (u + 1) * 2048:
                q = t0 // SQ
                n = min(SQ - (t0 % SQ), (u + 1) * 2048 - t0)
                DMA(ybuf_d[q * 512 + h * 128: q * 512 + (h + 1) * 128, (t0 % SQ):(t0 % SQ) + n],
                    yst[yb][:, t0 - u * 2048: t0 - u * 2048 + n], [B_yst[yb]], [B_ybuf[q][h]], "st_yst%d" % yb)
                t0 += n

        pa_load(0)
        for u in range(NU):
            for m in range(4):
                i = 4 * u + m
                if i + 1 < NT:
                    pa_load(i + 1)
                pa_inproj(i)
            for h in range(2):
                pa_attn(u, h)
        release(m_persist)
        sch.barrier()
        if stop == "pa":
            if mode == "pa_only":
                return finish([("ybuf", ybuf_d.ap(), [2048, SQ], BF16), ("qT", qT[:], [128, 4096], BF16), ("kT", kT[:], [128, 8192], BF16),
                               ("accn", accn[:], [128, 2048], F32), ("accd", accd[:], [128, 2048], F32), ("pT", pTs[0][:], [128, 256], BF16)])
            return finish([("ybuf", ybuf_d.ap(), [2048, SQ], BF16), ("hT", hT_d.ap(), [NT, 128, KC * 512], BF16)])

    if mode != 'ex_only':
        wB = alloc([128, KC * 1282], BF16, "wB")
        B_wB = Buf("wB")
        for k in range(KC):
            DMA(wB[:, k * 1282:(k + 1) * 1282], w1[k * 128:(k + 1) * 128, 1024:2306], (), [B_wB], "ld_wB", eng="pool")
        ht = [alloc([128, KC * 512], BF16, "htb") for _ in range(2)]
        B_ht = [Buf("htb0"), Buf("htb1")]
        for t in range(QT):
            DMA(hto_all[:, t * KC * 512:(t + 1) * KC * 512], hTo_d[t], [B_hTo], [B_htoa[t]], "ld_htoa%d" % t)
        smalls = alloc([128, 16 + 4 + 2 + 256], F32, "smalls")
        B_sm = Buf("smalls")
        DMA(smalls[:, 0:16], cw.ap(), (), [B_sm], "ld_sm")
        DMA(smalls[:, 16:20], cb.ap(), (), [B_sm], "ld_sm")
        DMA(smalls[:, 20:22], bif.ap(), (), [B_sm], "ld_sm")
        DMA(smalls[:, 22:278], mg.ap(), (), [B_sm], "ld_sm")
        xq = [alloc([128, 515], F32, "xq") for _ in range(4)]
        B_xq = [Buf("xq%d" % i) for i in range(4)]
        cacc = [alloc([128, 512], F32, "cacc") for _ in range(2)]
        B_cacc = [Buf("cacc0"), Buf("cacc1")]
        csig = [alloc([128, 512], F32, "csig") for _ in range(2)]
        B_csig = [Buf("csig0"), Buf("csig1")]
        qk = [[alloc([128, 512], BF16, "qk") for _ in range(4)] for _ in range(2)]
        B_qk = [[Buf("qk") for _ in range(4)] for _ in range(2)]
        vaug = [alloc([128, 257], BF16, "vaug") for _ in range(2)]
        B_vaug = [Buf("vaug0"), Buf("vaug1")]
        gsm = [alloc([128, 16], F32, "gsm") for _ in range(2)]
        B_gsm = [Buf("gsm0"), Buf("gsm1")]
        sgo = [alloc([128, 512], F32, "sgo") for _ in range(2)]
        szm = [alloc([128, 256], BF16, "szm") for _ in range(2)]
        B_sgo = [Buf("sgo0"), Buf("sgo1")]
        B_szm = [Buf("szm0"), Buf("szm1")]
        pTm = [alloc([128, 128], BF16, "pTm") for _ in range(2)]
        B_pTm = [Buf("pTm0"), Buf("pTm1")]
        kw = [alloc([128, 256], BF16, "kw") for _ in range(2)]
        B_kw = [Buf("kw0"), Buf("kw1")]
        Cst = alloc([128, 2 * 257], F32, "Cst")
        Cb = [alloc([128, 2 * 257], BF16, "Cb") for _ in range(2)]
        B_C = Buf("C")
        B_Cb = [Buf("Cb0"), Buf("Cb1")]
        hh = [alloc([128, 256], F32, "hh") for _ in range(2)]
        B_hh = [Buf("hh0"), Buf("hh1")]
        lnst = [alloc([128, 16], F32, "lnst") for _ in range(2)]
        B_lnst = [Buf("lnst0"), Buf("lnst1")]
        ym = [alloc([128, 256], BF16, "ym") for _ in range(2)]
        B_ym = [Buf("ym0"), Buf("ym1")]
        ystm = [alloc([128, 2 * 512], BF16, "ystm") for _ in range(2)]
        B_ystm = [Buf("ystm0"), Buf("ystm1")]
        for i in range(4):
            MEMSET("dve", xq[i][:, 0:3], 0.0, [B_xq[i]])
        for i in range(2):
            MEMSET("dve", vaug[i][:, 256:257], 1.0, [B_vaug[i]])
        MEMSET("dve", Cst[:], 0.0, [B_C])

        assert state["off"] < hto_off, (state["off"], hto_off)

        def pb_load(i):
            DMA(ht[i % 2][:], hT_d[i], [B_hT[i]], [B_ht[i % 2]], "ld_htb%d" % (i % 2))

        psSm = psb[4][:, 0:128]
        psG = psb[2][:, 300:302]
        psUn = psb[4][:, 130:132]
        B_psSm, B_psG, B_psUn = PB_[4], PB_[2], PB_[4]
        psH = psb[5][:, 0:257]
        B_psH = PB_[5]
        psU = psb[6][:, 0:512]
        B_psU = PB_[6]
        ps7 = psb[7][:].bitcast(BF16)
        psK = ps7[:, 0:256]
        psY = ps7[:, 256:512]
        B_psK, B_psY = PB_[7], PB_[7]

        def pb_inproj_feat(i):
            b = i % 2
            for ch in range(4):
                pbk = ch % 2
                for k in range(KC):
                    MM(psb[pbk][:, :], wB[:, k * 1282 + ch * 128: k * 1282 + (ch + 1) * 128], ht[b][:, k * 512:(k + 1) * 512],
                       k == 0, k == KC - 1, [B_wB, B_ht[b]], [PB_[pbk]])
                if i > 0:
                    CP("dve", xq[ch][:, 0:3], xq[ch][:, 512:515], [B_xq[ch]], [B_xq[ch]])
                CP("act", xq[ch][:, 3:515], psb[pbk][:, :], [PB_[pbk]], [B_xq[ch]])
                ca = ch % 2
                TS("dve", cacc[ca][:], xq[ch][:, 3:515], smalls[:, ch * 4 + 3: ch * 4 + 4], smalls[:, 16 + ch:17 + ch], ALU.mult, ALU.add,
                   [B_xq[ch], B_sm], [B_cacc[ca]])
                for j in range(3):
                    STT(cacc[ca][:], xq[ch][:, j:j + 512], smalls[:, ch * 4 + j: ch * 4 + j + 1], cacc[ca][:], ALU.mult, ALU.add,
                        [B_xq[ch], B_sm, B_cacc[ca]], [B_cacc[ca]])
                if OPT_CONV:
                    ACT(csig[ca][:], cacc[ca][:], AF.Sigmoid, [B_cacc[ca]], [B_csig[ca]])
                    TT("pool", qk[b][ch][:], cacc[ca][:], csig[ca][:], ALU.mult, [B_cacc[ca], B_csig[ca]], [B_qk[b][ch]])
                else:
                    ACT(qk[b][ch][:], cacc[ca][:], AF.Silu, [B_cacc[ca]], [B_qk[b][ch]])

        def pb_T(i, s):
            c = 4 * i + s
            b = i % 2
            cb_ = c % 2
            tok = slice(s * 128, (s + 1) * 128)
            for k in range(KC):
                MM(psb[2][:, 0:258], ht[b][:, k * 512 + s * 128: k * 512 + (s + 1) * 128], wB[:, k * 1282 + 512: k * 1282 + 770],
                   k == 0, k == KC - 1, [B_wB, B_ht[b]], [PB_[2]])
            for k in range(KC):
                MM(psb[3][:, :], ht[b][:, k * 512 + s * 128: k * 512 + (s + 1) * 128], wB[:, k * 1282 + 770: k * 1282 + 1282],
                   k == 0, k == KC - 1, [B_wB, B_ht[b]], [PB_[3]])
            CP("act", vaug[cb_][:, 0:256], psb[2][:, 0:256], [PB_[2]], [B_vaug[cb_]])
            g = gsm[cb_]
            Bg = B_gsm[cb_]
            TT("dve", g[:, 0:2], psb[2][:, 256:258], smalls[:, 20:22], ALU.add, [PB_[2], B_sm], [Bg])
            ACT(g[:, 2:3], g[:, 1:2], AF.Exp, [Bg], [Bg], scale=-1.0)
            ACT(g[:, 3:4], g[:, 2:3], AF.Ln, [Bg], [Bg], bias=1.0)
            MM(psG[:, 0:1], tri_f[:], g[:, 3:4], True, True, [Bg, B_cst], [B_psG])
            MM(psG[:, 1:2], ones_f[:], g[:, 3:4], True, True, [Bg, B_cst], [B_psG])
            TT("dve", g[:, 4:5], g[:, 0:1], psG[:, 0:1], ALU.add, [Bg, B_psG], [Bg])
            ACT(g[:, 5:6], g[:, 4:5], AF.Exp, [Bg], [Bg], bias=-LN16)
            ACT(g[:, 6:7], psG[:, 0:1], AF.Exp, [B_psG], [Bg], scale=-1.0)
            TT("dve", g[:, 7:8], g[:, 4:5], psG[:, 1:2], ALU.subtract, [Bg, B_psG], [Bg])
            ACT(g[:, 8:9], g[:, 7:8], AF.Exp, [Bg], [Bg], bias=-LN16)
            ACT(g[:, 9:10], psG[:, 1:2], AF.Exp, [B_psG], [Bg], scale=-1.0)
            if OPT_SIG:
                ACT(sgo[cb_][:], psb[3][:, 0:512], AF.Sigmoid, [PB_[3]], [B_sgo[cb_]])
                TT("dve", szm[cb_][:], psb[3][:, 256:512], sgo[cb_][:, 256:512], ALU.mult, [PB_[3], B_sgo[cb_]], [B_szm[cb_]])
            else:
                ACT(sgo[cb_][:, 0:256], psb[3][:, 0:256], AF.Sigmoid, [PB_[3]], [B_sgo[cb_]])
                ACT(szm[cb_][:], psb[3][:, 256:512], AF.Silu, [PB_[3]], [B_szm[cb_]])

        def pb_M(i, s):
            c = 4 * i + s
            b = i % 2
            cb_ = c % 2
            tok = slice(s * 128, (s + 1) * 128)
            g = gsm[cb_]
            Bg = B_gsm[cb_]
            for e2 in range(2):
                MM(psSm, qk[b][2 + e2][:, tok], qk[b][e2][:, tok], e2 == 0, e2 == 1, [B_qk[b][2 + e2], B_qk[b][e2]], [B_psSm])
            STT(pTm[cb_][:], psSm, g[:, 5:6], tri_f[:], ALU.mult, ALU.mult, [B_psSm, Bg, B_cst], [B_pTm[cb_]])
            for e2 in range(2):
                TR(psK[:, e2 * 128:(e2 + 1) * 128], qk[b][2 + e2][:, tok], ident_b[:], [B_qk[b][2 + e2], B_cst], [B_psK])
            ACT(kw[cb_][:], psK, AF.Copy, [B_psK, Bg], [B_kw[cb_]], scale=g[:, 8:9])
            for e2 in range(2):
                MM(psU[:, e2 * 256:(e2 + 1) * 256], kw[cb_][:, e2 * 128:(e2 + 1) * 128], vaug[cb_][:, 0:256], True, True,
                   [B_kw[cb_], B_vaug[cb_]], [B_psU])
            for e2 in range(2):
                MM(psUn[:, e2:e2 + 1], kw[cb_][:, e2 * 128:(e2 + 1) * 128], vaug[cb_][:, 256:257], True, True,
                   [B_kw[cb_], B_vaug[cb_]], [B_psUn])
            cbi = c % 2
            MM(psH, pTm[cb_][:], vaug[cb_][:], True, c == 0, [B_pTm[cb_], B_vaug[cb_]], [B_psH])
            if c > 0:
                for e2 in range(2):
                    MM(psH, qk[b][e2][:, tok], Cb[cbi][:, e2 * 257:(e2 + 1) * 257], False, e2 == 1,
                       [B_qk[b][e2], B_Cb[cbi]], [B_psH])
            C3 = Cst[:].rearrange("p (e f) -> p e f", e=2)
            STT(C3[:, :, 0:256], C3[:, :, 0:256], g[:, 9:10], psU.rearrange("p (e f) -> p e f", e=2), ALU.mult, ALU.add,
                [B_C, Bg, B_psU], [B_C])
            STT(C3[:, :, 256], C3[:, :, 256], g[:, 9:10], psUn, ALU.mult, ALU.add, [B_C, Bg, B_psUn], [B_C])
            ACT(Cb[1 - cbi][:], Cst[:], AF.Identity, [B_C], [B_Cb[1 - cbi]])
            ACT(g[:, 10:11], psH[:, 256:257], AF.Abs, [B_psH, Bg], [Bg], scale=g[:, 6:7])
            TS("dve", g[:, 11:12], g[:, 10:11], 1.0, None, ALU.max, None, [Bg], [Bg])
            RECIP(g[:, 12:13], g[:, 11:12], [Bg], [Bg])
            TT("dve", g[:, 12:13], g[:, 12:13], g[:, 6:7], ALU.mult, [Bg], [Bg])
            STT(hh[cb_][:], psH[:, 0:256], g[:, 12:13], sgo[cb_][:, 0:256], ALU.mult, ALU.mult, [B_psH, Bg, B_sgo[cb_]], [B_hh[cb_]])
            ls = lnst[cb_]
            Bl = B_lnst[cb_]
            sch.op("dve", lambda e: e.bn_stats(out=ls[:, 0:6], in_=hh[cb_][:]), [B_hh[cb_]], [Bl])
            sch.op("dve", lambda e: e.bn_aggr(out=ls[:, 6:8], in_=ls[:, 0:6]), [Bl], [Bl])
            if OPT_LN:
                ACT(ls[:, 8:9], ls[:, 7:8], AF.Ln, [Bl], [Bl], bias=EPS)
                ACT(ls[:, 9:10], ls[:, 8:9], AF.Exp, [Bl], [Bl], scale=-0.5)
            else:
                ACT(ls[:, 8:9], ls[:, 7:8], AF.Sqrt, [Bl], [Bl], bias=EPS)
                RECIP(ls[:, 9:10], ls[:, 8:9], [Bl], [Bl])
            TS("dve", hh[cb_][:], hh[cb_][:], ls[:, 6:7], ls[:, 9:10], ALU.subtract, ALU.mult, [B_hh[cb_], Bl], [B_hh[cb_]])
            TT("dve", hh[cb_][:], hh[cb_][:], smalls[:, 22:278], ALU.mult, [B_hh[cb_], B_sm], [B_hh[cb_]])
            TT("dve", ym[cb_][:], hh[cb_][:], szm[cb_][:], ALU.mult, [B_hh[cb_], B_szm[cb_]], [B_ym[cb_]])

        def pb_M_tail(i, s):
            c = 4 * i + s
            b = i % 2
            cb_ = c % 2
            for f2 in range(2):
                TR(psY[:, f2 * 128:(f2 + 1) * 128], ym[cb_][:, f2 * 128:(f2 + 1) * 128], ident_b[:], [B_ym[cb_], B_cst], [B_psY])
            y3 = ystm[b][:].rearrange("p (f t) -> p f t", f=2)[:, :, s * 128:(s + 1) * 128]
            ACT(y3, psY.rearrange("p (f t) -> p f t", f=2), AF.Identity, [B_psY], [B_ystm[b]])
            if s == 3:
                t0 = i * 512
                q = t0 // SQ
                for f2 in range(2):
                    DMA(ybuf_d[q * 512 + 256 + f2 * 128: q * 512 + 256 + (f2 + 1) * 128, (t0 % SQ):(t0 % SQ) + 512],
                        ystm[b][:, f2 * 512:(f2 + 1) * 512], [B_ystm[b]], [B_ybuf[q][2 + f2]], "st_ystm%d_%d" % (b, f2))
                if (i + 1) % QT == 0:
                    pend_g.append(q)

        pend_g = []
        pb_load(0)
        for q in range(4):
            for f4 in range(2):
                emit_gather(q, f4)
        if NT > 1:
            pb_load(1)
        chunks = [(i, s) for i in range(NT) for s in range(4)]
        pb_inproj_feat(0)
        pb_T(0, 0)
        for ci, (i, s) in enumerate(chunks):
            if ci + 1 < len(chunks):
                i2, s2 = chunks[ci + 1]
                if s2 == 0:
                    pb_inproj_feat(i2)
                pb_T(i2, s2)
                if s2 == 3 and i2 + 2 < NT:
                    pb_load(i2 + 2)
            pb_M(i, s)
            if ci > 0:
                pb_M_tail(*chunks[ci - 1])
            if s == 1 and pend_g:
                q_ = pend_g.pop(0)
                emit_gather(q_, 2)
                emit_gather(q_, 3)
        pb_M_tail(*chunks[-1])
        while pend_g:
            q_ = pend_g.pop(0)
            emit_gather(q_, 2)
            emit_gather(q_, 3)
        release(m_persist)
        sch.barrier()
        if stop == "pb":
            return finish([("ybuf", ybuf_d.ap(), [2048, SQ], BF16), ("C", Cst[:], [128, 514], F32), ("gsm", gsm[0][:], [128, 16], F32),
                           ("hh", hh[0][:], [128, 256], F32), ("qk", qk[1][0][:], [128, 512], BF16)])

    if mode == 'ex_only':
        for q in range(4):
            for f4 in range(4):
                emit_gather(q, f4)
    B_yall = Buf("yall")
    B_ymine = Buf("ymine")
    allyb = [B_ybuf[q][f] for q in range(4) for f in range(4)]
    groups = [[0, 1, 2, 3], [4, 5, 6, 7]]
    yregs = [nc.gpsimd.alloc_register("rry%d" % f4) for f4 in range(4)]
    sch.barrier()
    if stop == "ex":
        return finish([("hTo", hTo_d.ap(), [QT, 128, KC * 512], BF16)])

    B_mg = [[Buf("mg%d_%d" % (t, sl)) for sl in range(2)] for t in range(QT)]
    yin_all = alloc([128, 16 * SQ], BF16, "yinall")
    B_yina = Buf("yina")
    assert state["off"] + 40000 < hto_off
    for f4 in range(4):
        a_, c_ = f4 // 2, f4 % 2

        def f(e, reg=yregs[f4], idx=f4, a_=a_, c_=c_):
            e.reg_load(reg, ri[0:1, idx:idx + 1])
            y3 = yin_all[:].rearrange("p (c t) -> p c t", c=16)
            dst = y3[:, a_ * 8 + c_: a_ * 8 + c_ + 7: 2, :]
            return e.dma_start(out=dst, in_=bass.AP(yall_d, reg, [[SQ, 128], [128 * SQ, 4], [1, SQ]]))
        sch.op("pool", f, B_yallp + [B_ri], [B_yina], dkey="ld_yina")
    wgj = [alloc([128, 2 * KC * 128], BF16, "wgj") for _ in range(2)]
    wpj = [alloc([128, 2 * 8 * 128], BF16, "wpj") for _ in range(2)]
    B_wgj = [Buf("wgj0"), Buf("wgj1")]
    B_wpj = [Buf("wpj0"), Buf("wpj1")]
    sga = [alloc([128, 512], F32, "sga") for _ in range(2)]
    sgb = [alloc([128, 512], F32, "sgb") for _ in range(2)]
    mo = [alloc([128, 512], BF16, "mo") for _ in range(2)]
    B_sga = [Buf("sga0"), Buf("sga1")]
    B_sgb = [Buf("sgb0"), Buf("sgb1")]
    B_mo = [Buf("mo0"), Buf("mo1")]

    def p2_wload(j):
        b = j % 2
        for a in range(2):
            DMA(wgj[b][:, a * KC * 128:(a + 1) * KC * 128], w_gj[j, :, a * KC * 128:(a + 1) * KC * 128], (), [B_wgj[b]],
                "ld_wgj%d" % b, eng="pool")
        DMA(wpj[b][:], w_pj[j], (), [B_wpj[b]], "ld_wpj%d" % b, eng="pool")

    p2_wload(0)
    it = 0
    for j in range(KC):
        b = j % 2
        if j + 1 < KC:
            p2_wload(j + 1)
        for t in range(QT):
            pb0 = (it % 2) * 4
            si = it % 2
            it += 1
            for a in range(2):
                for k in range(KC):
                    MM(psb[pb0 + a][:, :], wgj[b][:, (a * KC + k) * 128:(a * KC + k + 1) * 128],
                       hto_all[:, (t * KC + k) * 512:(t * KC + k + 1) * 512], k == 0, k == KC - 1, [B_wgj[b], B_htoa[t]], [PB_[pb0 + a]])
            for a in range(2):
                for k in range(8):
                    MM(psb[pb0 + 2 + a][:, :], wpj[b][:, (a * 8 + k) * 128:(a * 8 + k + 1) * 128],
                       yin_all[:, (a * 8 + k) * SQ + t * 512:(a * 8 + k) * SQ + (t + 1) * 512], k == 0, k == 7, [B_wpj[b], B_yina], [PB_[pb0 + 2 + a]])
            ACT(sga[si][:], psb[pb0][:, :], AF.Sigmoid, [PB_[pb0]], [B_sga[si]])
            ACT(sgb[si][:], psb[pb0 + 1][:, :], AF.Sigmoid, [PB_[pb0 + 1]], [B_sgb[si]])
            TT("dve", sga[si][:], psb[pb0 + 2][:, :], sga[si][:], ALU.mult, [PB_[pb0 + 2], B_sga[si]], [B_sga[si]])
            TT("dve", sgb[si][:], psb[pb0 + 3][:, :], sgb[si][:], ALU.mult, [PB_[pb0 + 3], B_sgb[si]], [B_sgb[si]])
            TT("pool", mo[si][:], sga[si][:], sgb[si][:], ALU.add, [B_sga[si], B_sgb[si]], [B_mo[si]])
            DMA(mgT_d[t][:, j * 512:(j + 1) * 512], mo[si][:], [B_mo[si]], [B_mg[t][si]], "st_mo%d" % si)
    release(m_persist)
    sch.barrier()
    if stop == "p2":
        return finish([("mgT", mgT_d.ap(), [QT, 128, KC * 512], BF16)])

    wO = alloc([128, KC * D], BF16, "wO")
    B_wO = Buf("wO")
    for k in range(KC):
        DMA(wO[:, k * D:(k + 1) * D], w_out[k * 128:(k + 1) * 128, :], (), [B_wO], "ld_wO", eng="pool")
    fgs = alloc([128, D], F32, "fgs")
    B_fg = Buf("fg")
    DMA(fgs[:], fg.ap(), (), [B_fg], "ld_fg")
    xin = [alloc([128, D], F32, "xin") for _ in range(2)]
    xn = [alloc([128, D], F32, "xn") for _ in range(2)]
    mti = [alloc([128, KC * 512], BF16, "mti") for _ in range(2)]
    B_mti = [Buf("mti0"), Buf("mti1")]
    st2 = [alloc([128, 40], F32, "st2") for _ in range(2)]
    B_xin = [Buf("xin0"), Buf("xin1")]
    B_xn = [Buf("xn0"), Buf("xn1")]
    B_st2 = [Buf("st20"), Buf("st21")]
    NTT = SQ // 128

    def p2c_load(tt):
        DMA(xin[tt % 2][:], xtok[tt * 128:(tt + 1) * 128, :], (), [B_xin[tt % 2]], "ld_xin%d" % (tt % 2))
        if tt % 4 == 0:
            t_ = tt // 4
            DMA(mti[t_ % 2][:], mgT_d[t_], B_mg[t_], [B_mti[t_ % 2]], "ld_mti%d" % (t_ % 2))

    p2c_load(0)
    for tt in range(NTT):
        b = tt % 2
        if tt + 1 < NTT:
            p2c_load(tt + 1)
        t, sub = tt // 4, tt % 4
        for n in range(4):
            pbk = (tt % 2) * 4 + n
            for j in range(KC):
                MM(psb[pbk][:, :], mti[t % 2][:, j * 512 + sub * 128: j * 512 + (sub + 1) * 128], wO[:, j * D + n * 512: j * D + (n + 1) * 512],
                   j == 0, j == KC - 1, [B_wO, B_mti[t % 2]], [PB_[pbk]])
            cs = slice(n * 512, (n + 1) * 512)
            TT("dve", xn[b][:, cs], psb[pbk][:, :], gate_bc[:, cs], ALU.mult, [PB_[pbk], B_gate], [B_xn[b]])
            TT("pool", xn[b][:, cs], xn[b][:, cs], xin[b][:, cs], ALU.add, [B_xn[b], B_xin[b]], [B_xn[b]])
        s2 = st2[b]
        for n in range(4):
            sch.op("dve", lambda e, b=b, s2=s2, n=n: e.bn_stats(out=s2[:, 8 + n * 6: 8 + (n + 1) * 6], in_=xn[b][:, n * 512:(n + 1) * 512]),
                   [B_xn[b]], [B_st2[b]])
        sch.op("dve", lambda e, s2=s2: e.bn_aggr(out=s2[:, 4:6], in_=s2[:, 8:32]), [B_st2[b]], [B_st2[b]])
        STT(s2[:, 0:1], s2[:, 4:5], s2[:, 4:5], s2[:, 5:6], ALU.mult, ALU.add, [B_st2[b]], [B_st2[b]])
        ACT(s2[:, 1:2], s2[:, 0:1], AF.Sqrt, [B_st2[b]], [B_st2[b]], bias=EPS)
        RECIP(s2[:, 2:3], s2[:, 1:2], [B_st2[b]], [B_st2[b]])
        STT(xn[b][:], xn[b][:], s2[:, 2:3], fgs[:], ALU.mult, ALU.mult, [B_xn[b], B_st2[b], B_fg], [B_xn[b]])
        DMA(out_d[tt * 128:(tt + 1) * 128, :], xn[b][:], [B_xn[b]], [Buf("o")], "st_out%d" % b)
    sch.barrier()
    sch.emit()
    return nc, dbg_out


def _regload(e, reg, ap):
    return e.reg_load(reg, ap)


def _dyn(t, reg, const_off, pattern):
    return bass.AP(t, reg + const_off, pattern)


def _prep_inputs(S, x, c, norm_gain, w_ada, b_ada, w_in, b_gate_if, conv_w, conv_b, mlstm_norm_gain,
                 w_proj_attn, w_proj_mlstm, w_out, final_gain):
    f = np.float32
    NS = S // 128
    SQ = S // 4
    x = np.asarray(x, f)
    w_in0 = np.asarray(w_in, f)[0]
    ident = np.eye(128, dtype=f)
    tri = np.triu(np.ones((128, 128), f))
    ones = np.ones((128, 128), f)
    cst = np.ascontiguousarray(np.concatenate([ident, tri, ones], axis=1))
    xTs = []
    for b in range(2):
        a = x[b].reshape(NS // 4, 512, KC, 128).transpose(0, 3, 2, 1)
        xTs.append(np.ascontiguousarray(a).reshape(NS // 4, 128, KC * 512))
    w_ada0 = np.ascontiguousarray(np.asarray(w_ada, f)[0])
    b_row = np.ascontiguousarray(np.asarray(b_ada, f)[0].reshape(1, -1))
    ngc = np.ascontiguousarray(np.asarray(norm_gain, f)[0].reshape(KC, 128).T)
    wg4 = w_in0[:, O_GA:O_GA + 2 * D].reshape(KC, 128, 2, KC, 128)
    w_gj = np.ascontiguousarray(wg4.transpose(3, 1, 2, 0, 4)).reshape(KC, 128, 2 * KC * 128)
    wp4 = np.stack([np.asarray(w_proj_attn, f)[0], np.asarray(w_proj_mlstm, f)[0]], axis=0).reshape(2, 8, 128, KC, 128)
    w_pj = np.ascontiguousarray(wp4.transpose(3, 2, 0, 1, 4)).reshape(KC, 128, 2 * 8 * 128)
    w_o = np.ascontiguousarray(np.asarray(w_out, f)[0])
    fgb = np.ascontiguousarray(np.broadcast_to(np.asarray(final_gain, f)[None, :], (128, D)))
    cwf = np.asarray(conv_w, f)[0]
    cbf = np.asarray(conv_b, f)[0]
    bg = np.asarray(b_gate_if, f)[0]
    mgf = np.asarray(mlstm_norm_gain, f)[0]
    ki = np.arange(128)[:, None]
    qi = np.arange(128)[None, :]
    in_maps = []
    for core in range(8):
        b, g = core // 4, core % 4
        cols = []
        for off in (O_QA, O_KA, O_VA, O_ZA):
            for lh in range(2):
                h = 2 * g + lh
                cols.append(np.arange(off + h * 128, off + (h + 1) * 128))
        cols.append(np.arange(O_QM + g * 256, O_QM + (g + 1) * 256))
        cols.append(np.arange(O_KM + g * 256, O_KM + (g + 1) * 256))
        cols.append(np.arange(O_VM + g * 256, O_VM + (g + 1) * 256))
        cols.append(np.array([O_IF + g, O_IF + 4 + g]))
        cols.append(np.arange(O_OM + g * 256, O_OM + (g + 1) * 256))
        cols.append(np.arange(O_ZM + g * 256, O_ZM + (g + 1) * 256))
        cols = np.concatenate(cols)
        assert cols.size == 2306
        w1 = np.ascontiguousarray(w_in0[:, cols])
        cwc = np.zeros((128, 16), f)
        cbc = np.zeros((128, 4), f)
        for ch in range(4):
            base = (0 if ch < 2 else 1024) + g * 256 + (ch % 2) * 128
            cwc[:, ch * 4:(ch + 1) * 4] = cwf[:, base:base + 128].T
            cbc[:, ch] = cbf[base:base + 128]
        bifc = np.ascontiguousarray(np.broadcast_to(np.array([bg[g], bg[4 + g]], f)[None, :], (128, 2)))
        mgc = np.ascontiguousarray(np.broadcast_to(mgf[g * 256:(g + 1) * 256][None, :], (128, 256)))
        ab = np.zeros((128, 6, 256), f)
        for p, d in enumerate((1, 4, 16)):
            for lh in range(2):
                slope = 2.0 ** (-(2 * g + lh + 1))
                for half, shift in ((0, 128), (1, 0)):
                    delta = qi - ki + shift
                    valid = (delta >= 0) & (delta <= 128)
                    ab[:, p * 2 + lh, half * 128:(half + 1) * 128] = np.where(valid, -slope * d * delta, NEG)
        in_maps.append({
            "xT": xTs[b],
            "xtok": np.ascontiguousarray(x[b, g * SQ:(g + 1) * SQ, :]),
            "ccol": np.ascontiguousarray(np.asarray(c, f)[b].reshape(KC, 128).T),
            "w_ada": w_ada0, "b_row": b_row, "ng": ngc, "w1": w1, "w_gj": w_gj, "w_pj": w_pj,
            "bif": bifc, "cw": cwc, "cb": cbc, "mg": mgc,
            "w_out": w_o, "fg": fgb, "cst": cst,
            "abias": np.ascontiguousarray(ab.reshape(128, 6 * 256)),
            "roff": np.array([[(g * 4 + f4) * 512 * SQ for f4 in range(4)]
                              + [(g * (SQ // 512) + t) * 128 * KC * 512 for t in range(SQ // 512)]], dtype=np.int32),
        })
    return in_maps


_STOP = None


def kernel(x, c, norm_gain, w_ada, b_ada, w_in, b_gate_if, conv_w, conv_b, mlstm_norm_gain,
           w_proj_attn, w_proj_mlstm, w_out, final_gain):
    S = int(np.asarray(x).shape[1])
    in_maps = _prep_inputs(S, x, c, norm_gain, w_ada, b_ada, w_in, b_gate_if, conv_w, conv_b, mlstm_norm_gain,
                           w_proj_attn, w_proj_mlstm, w_out, final_gain)
    nc, _ = build(S, stop=_STOP)
    res = run_bass_kernel_spmd(nc, in_maps, core_ids=list(range(8)))
    if _STOP is not None:
        return res.results
    SQ = S // 4
    out = np.zeros((2, S, D), np.float32)
    for core in range(8):
        b, g = core // 4, core % 4
        out[b, g * SQ:(g + 1) * SQ, :] = res.results[core]["out"]
    return out
```

```python
import numpy as np
import concourse.bass as bass
import concourse.mybir as mybir
from concourse.bass_utils import run_bass_kernel_spmd

F32 = mybir.dt.float32
BF16 = mybir.dt.bfloat16
I32 = mybir.dt.int32
AF = mybir.ActivationFunctionType
ALU = mybir.AluOpType

D = 2048
KC = 16
EPS = 1e-6
SEQ = 8192
import os as _os
OPT_CONV = _os.environ.get("OPT_CONV", "0") == "1"
OPT_SIG = _os.environ.get("OPT_SIG", "1") == "1"
OPT_LN = _os.environ.get("OPT_LN", "1") == "1"
NEG = -30000.0
LN16 = 2.772588722239781
O_QA, O_KA, O_VA, O_ZA, O_QM, O_KM, O_VM, O_OM, O_ZM, O_IF, O_GA, O_GB = (
    0, 1024, 2048, 3072, 4096, 5120, 6144, 7168, 8192, 9216, 9224, 11272)


class Buf:
    __slots__ = ("name", "w", "r", "excl")

    def __init__(self, name, excl=False):
        self.name = name
        self.w = None
        self.r = {}
        self.excl = excl


class Sched:
    ENG = ("pe", "act", "dve", "pool", "sp")

    def __init__(self, nc):
        self.nc = nc
        self.prog = {e: [] for e in self.ENG}
        self.cnt = {}
        self.known = {e: {} for e in self.ENG}
        self.sems = {}

    def op(self, eng, fn, reads=(), writes=(), dkey=None, dinc=16):
        deps = {}

        def add(tok):
            if tok is None:
                return
            k, v = tok
            if deps.get(k, 0) < v:
                deps[k] = v

        ex = [b for b in reads if b.excl]
        if ex:
            writes = list(writes) + ex
        for b in reads:
            add(b.w)
        for b in writes:
            add(b.w)
            for k, v in b.r.items():
                add((k, v))
        waits = []
        kn = self.known[eng]
        for k, v in deps.items():
            if eng == "pe" and k == "pe":
                continue
            if kn.get(k, 0) >= v:
                continue
            kn[k] = v
            waits.append((k, v))
        if dkey is None:
            key, inc = eng, 1
        else:
            key, inc = dkey, dinc
        self.cnt[key] = self.cnt.get(key, 0) + inc
        tok = (key, self.cnt[key])
        self.prog[eng].append((waits, fn, key, inc))
        for b in reads:
            if b.r.get(key, 0) < tok[1]:
                b.r[key] = tok[1]
        for b in writes:
            b.w = tok
            b.r = {}
        return tok

    def barrier(self):
        for e in self.ENG:
            waits = []
            for k, v in self.cnt.items():
                if (k == e and e == "pe") or k == "cc":
                    continue
                if self.known[e].get(k, 0) >= v:
                    continue
                self.known[e][k] = v
                waits.append((k, v))
            if waits:
                self.prog[e].append((waits, None, None, 0))

    def emit(self):
        nc = self.nc
        keys = list(self.cnt.keys())
        for k in keys:
            self.sems[k] = nc.alloc_semaphore("s_" + k)
        engmap = {"pe": "tensor", "act": "scalar", "dve": "vector", "pool": "gpsimd", "sp": "sync"}
        with nc.Block() as block:
            for e in self.ENG:
                prog = self.prog[e]

                def body(eng, prog=prog):
                    for waits, fn, key, inc in prog:
                        for k, v in waits:
                            eng.wait_ge(self.sems[k], v)
                        if fn is not None:
                            ins = fn(eng)
                            ins.then_inc(self.sems[key], inc)

                getattr(block, engmap[e])(body)


def build(S=SEQ, stop=None, mode=None, sub=99):
    assert S % 2048 == 0
    NT = S // 512
    NS = S // 128
    NU = S // 2048
    SQ = S // 4
    QT = SQ // 512
    nc = bass.Bass("TRN2", target_bir_lowering=False)
    sch = Sched(nc)

    def din(name, shape, dt=F32):
        if mode in ("pa_only", "pb_only", "ex_only") and name not in ("w1", "cst", "abias", "roff", "hT_in", "ccol", "ng", "bif", "cw", "cb", "mg"):
            shape = [1, 1]
        return nc.dram_tensor(name, list(shape), dt, kind="ExternalInput")

    xT = din("xT", [NT, 128, KC * 512])
    xtok = din("xtok", [SQ, D])
    ccol = din("ccol", [128, KC])
    w_ada = din("w_ada", [D, 3 * D])
    b_row = din("b_row", [1, 3 * D])
    ng = din("ng", [128, KC])
    w1 = din("w1", [D, 2306])
    w_gj = din("w_gj", [KC, 128, 2 * KC * 128])
    w_pj = din("w_pj", [KC, 128, 2 * 8 * 128])
    bif = din("bif", [128, 2])
    cw = din("cw", [128, 16])
    cb = din("cb", [128, 4])
    mg = din("mg", [128, 256])
    w_out = din("w_out", [D, D])
    fg = din("fg", [128, D])
    cst = din("cst", [128, 384])
    abias_d = din("abias", [128, 6 * 256])
    roff = din("roff", [1, 4 + QT], I32)
    out_d = nc.dram_tensor("out", [SQ, D], F32, kind="ExternalOutput")

    hT_d = nc.dram_tensor("hT_s", [NT, 128, KC * 512], BF16)
    hTo_d = nc.dram_tensor("hTo_s", [QT, 128, KC * 512], BF16)
    ybuf_d = nc.dram_tensor("ybuf_s", [4 * 512, SQ], BF16)
    yall_d = nc.dram_tensor("yall_s", [4 * 4 * 512, SQ], BF16)
    ymine_d = nc.dram_tensor("ymine_s", [4 * 512, SQ], BF16)
    maT_d = nc.dram_tensor("maT_s", [QT, 128, KC * 512], BF16)
    mgT_d = nc.dram_tensor("mgT_s", [QT, 128, KC * 512], BF16)
    dbg_out = {}

    SB_LO = 16512
    SB_HI = 229344
    state = {"off": SB_LO, "n": 0}

    def alloc(shape, dt, name=None):
        nbytes = int(np.prod(shape[1:])) * (4 if dt in (F32, I32) else 2)
        off = (state["off"] + 63) // 64 * 64
        assert off + nbytes <= SB_HI, ("SBUF overflow", name, off, nbytes)
        state["off"] = off + nbytes
        state["n"] += 1
        return nc.alloc_sbuf_tensor_at("%s_%d" % (name or "t", state["n"]), list(shape), dt, offset=off)

    def alloc_top(shape, dt, name):
        nbytes = int(np.prod(shape[1:])) * (4 if dt in (F32, I32) else 2)
        off = (SB_HI - nbytes) // 64 * 64
        state["n"] += 1
        return nc.alloc_sbuf_tensor_at("%s_%d" % (name, state["n"]), list(shape), dt, offset=off), off

    def mark():
        return state["off"]

    def release(m):
        state["off"] = m

    psb = [nc.alloc_psum_tensor("ps%d" % i, [128, 512], F32) for i in range(8)]
    PB_ = [Buf("psum%d" % i, excl=True) for i in range(8)]

    def MM(out, lhsT, rhs, st, sp, R, W):
        return sch.op("pe", lambda e: e.matmul(out, lhsT=lhsT, rhs=rhs, start=st, stop=sp), R, W)

    def TR(out, in_, ident, R, W):
        return sch.op("pe", lambda e: e.transpose(out=out, in_=in_, identity=ident), R, W)

    def ACT(out, in_, func, R, W, scale=None, bias=None):
        def f(e):
            kw = {}
            if scale is not None:
                kw["scale"] = scale
            if bias is not None:
                kw["bias"] = bias
            return e.activation(out=out, in_=in_, func=func, **kw)
        return sch.op("act", f, R, W)

    def TT(eng, out, in0, in1, op, R, W):
        return sch.op(eng, lambda e: e.tensor_tensor(out=out, in0=in0, in1=in1, op=op), R, W)

    def TS(eng, out, in0, s1, s2, op0, op1, R, W):
        if op1 is None:
            return sch.op(eng, lambda e: e.tensor_scalar(out=out, in0=in0, scalar1=s1, scalar2=None, op0=op0), R, W)
        return sch.op(eng, lambda e: e.tensor_scalar(out=out, in0=in0, scalar1=s1, scalar2=s2, op0=op0, op1=op1), R, W)

    def STT(out, in0, scalar, in1, op0, op1, R, W):
        return sch.op("dve", lambda e: e.scalar_tensor_tensor(out=out, in0=in0, scalar=scalar, in1=in1, op0=op0, op1=op1), R, W)

    def CP(eng, out, in_, R, W):
        if eng == "act":
            return sch.op("act", lambda e: e.copy(out=out, in_=in_), R, W)
        return sch.op(eng, lambda e: e.tensor_copy(out=out, in_=in_), R, W)

    def RECIP(out, in_, R, W):
        return sch.op("dve", lambda e: e.reciprocal(out=out, in_=in_), R, W)

    def MEMSET(eng, ap, val, W):
        return sch.op(eng, lambda e: e.memset(ap, val), (), W)

    def DMA(out, in_, R, W, key, eng="sp", **kw):
        return sch.op(eng, lambda e: e.dma_start(out=out, in_=in_, **kw), R, W, dkey=key)

    def bc_mid(t, n, reps, off=0):
        a = t[:, off:off + n]
        return bass.AP(a.tensor, a.offset, [list(a.ap[0]), [0, reps], [1, n]])

    def finish(dumps):
        for name, src_ap, shape, dt in dumps:
            o = nc.dram_tensor("dbg_" + name, list(shape), dt, kind="ExternalOutput")
            DMA(o.ap(), src_ap, [], [Buf("dbg")], "dbg_" + name)
        sch.barrier()
        sch.emit()
        return nc, None

    ident_f = alloc([128, 128], F32, "identf")
    tri_f = alloc([128, 128], F32, "trif")
    ones_f = alloc([128, 128], F32, "onesf")
    ident_b = alloc([128, 128], BF16, "identb")
    ones_b = alloc([128, 128], BF16, "onesb")
    cst_sb = alloc([128, 384], F32, "cst")
    gate_bc = alloc([128, D], F32, "gatebc")
    c_f = alloc([128, KC], F32, "cf")
    c_b = alloc([128, KC], BF16, "cb16")
    ng_sb = alloc([128, KC], F32, "ng")
    A_col = alloc([128, KC], F32, "Acol")
    sh_col = alloc([128, KC], F32, "shcol")
    ri = alloc([1, 4 + QT], I32, "ri")
    B_cst = Buf("cst")
    B_mod = Buf("modrow")
    B_col = Buf("cols")
    B_ri = Buf("ri")

    DMA(cst_sb[:], cst.ap(), (), [B_cst], "ld_cst")
    DMA(c_f[:], ccol.ap(), (), [B_cst], "ld_cst")
    DMA(ng_sb[:], ng.ap(), (), [B_cst], "ld_cst")
    DMA(ri[:], roff.ap(), (), [B_ri], "ld_ri")
    CP("dve", ident_f[:], cst_sb[:, 0:128], [B_cst], [B_cst])
    CP("dve", tri_f[:], cst_sb[:, 128:256], [B_cst], [B_cst])
    CP("dve", ones_f[:], cst_sb[:, 256:384], [B_cst], [B_cst])
    CP("dve", ident_b[:], cst_sb[:, 0:128], [B_cst], [B_cst])
    CP("dve", ones_b[:], cst_sb[:, 256:384], [B_cst], [B_cst])
    CP("dve", c_b[:], c_f[:], [B_cst], [B_cst])

    m_persist = mark()

    if mode not in ('pa_only', 'pb_only', 'ex_only'):
        modrow = alloc([1, 3 * D], F32, "modrow")
        brow = alloc([1, 3 * D], F32, "brow")
        B_brow = Buf("brow")
        DMA(brow[:], b_row.ap(), (), [B_brow], "ld_brow")
        wa = [alloc([128, 2048], BF16, "wa") for _ in range(2)]
        B_wa = [Buf("wa0"), Buf("wa1")]
        n_wa = 0
        for r in range(3):
            for k in range(KC):
                b = n_wa % 2
                n_wa += 1
                DMA(wa[b][:], w_ada[k * 128:(k + 1) * 128, r * 2048:(r + 1) * 2048], (), [B_wa[b]],
                    "ld_wa%d" % b, eng="pool")
                for n in range(4):
                    MM(psb[n][0:1, :], c_b[:, k:k + 1], wa[b][:, n * 512:(n + 1) * 512], k == 0, k == KC - 1,
                       [B_wa[b], B_cst], [PB_[n]])
            for n in range(4):
                TT("dve", modrow[0:1, r * 2048 + n * 512: r * 2048 + (n + 1) * 512], psb[n][0:1, :],
                   brow[0:1, r * 2048 + n * 512: r * 2048 + (n + 1) * 512], ALU.add, [PB_[n], B_brow], [B_mod])
        for j in range(32):
            TR(psb[4][:, j:j + 1], modrow[0:1, j * 128:(j + 1) * 128], ident_f[0:1, 0:1], [B_mod, B_cst], [PB_[4]])
        CP("dve", sh_col[:], psb[4][:, 0:16], [PB_[4]], [B_col])
        STT(A_col[:], psb[4][:, 16:32], 1.0, ng_sb[:], ALU.add, ALU.mult, [PB_[4], B_cst], [B_col])
        B_gate = Buf("gate")
        for n in range(4):
            MM(psb[n][:, :], ones_f[0:1, :], modrow[0:1, 2 * D + n * 512: 2 * D + (n + 1) * 512], True, True,
               [B_mod, B_cst], [PB_[n]])
            CP("act", gate_bc[:, n * 512:(n + 1) * 512], psb[n][:, :], [PB_[n]], [B_gate])
        release(m_persist)
        sch.barrier()
        if stop == "mod":
            return finish([("A", A_col[:], [128, KC], F32), ("sh", sh_col[:], [128, KC], F32), ("gate", gate_bc[:], [128, D], F32)])

        xt = [alloc([128, KC * 512], F32, "xt") for _ in range(3)]
        sq = [alloc([128, KC * 512], BF16, "sq") for _ in range(2)]
        sd = [alloc([128, 512], F32, "sd") for _ in range(2)]
        rs = [alloc([128, 512], F32, "rs") for _ in range(2)]
        hts = [alloc([128, KC * 512], BF16, "hts") for _ in range(2)]
        B_xt = [Buf("xt0"), Buf("xt1"), Buf("xt2")]
        B_xtp = [Buf("xtp0"), Buf("xtp1"), Buf("xtp2")]
        B_sq = [Buf("sq0"), Buf("sq1")]
        B_sd = [Buf("sd0"), Buf("sd1")]
        B_rs = [Buf("rs0"), Buf("rs1")]
        B_hts = [Buf("hts0"), Buf("hts1")]
        B_hT = [Buf("hT%d" % i) for i in range(NT)]

        def p0_load(i):
            DMA(xt[i % 3][:], xT[i], (), [B_xt[i % 3], B_xtp[i % 3]], "ld_xt%d" % (i % 3))

        def p0_a(i):
            b = i % 2
            for hf in range(2):
                cs = slice(hf * 8 * 512, (hf + 1) * 8 * 512)
                ACT(sq[b][:, cs], xt[i % 3][:, cs], AF.Square, [B_xt[i % 3], B_xtp[i % 3]], [B_sq[b]])
            for k in range(KC):
                MM(psb[b][:, :], ones_b[:], sq[b][:, k * 512:(k + 1) * 512], k == 0, k == KC - 1, [B_sq[b], B_cst], [PB_[b]])

        def p0_a2(i):
            b = i % 2
            ACT(sd[b][:], psb[b][:, :], AF.Ln, [PB_[b]], [B_sd[b]], scale=1.0 / D, bias=EPS)
            ACT(rs[b][:], sd[b][:], AF.Exp, [B_sd[b]], [B_rs[b]], scale=-0.5)
            x3 = xt[i % 3][:].rearrange("p (k t) -> p k t", k=KC)
            KP = 12
            TT("dve", x3[:, 0:KP, :], x3[:, 0:KP, :], bc_mid(rs[b], 512, KP), ALU.mult, [B_xt[i % 3], B_rs[b]], [B_xt[i % 3]])
            TT("pool", x3[:, KP:KC, :], x3[:, KP:KC, :], bc_mid(rs[b], 512, KC - KP), ALU.mult, [B_xtp[i % 3], B_rs[b]], [B_xtp[i % 3]])

        def p0_b(i):
            b = i % 2
            for k in range(KC):
                if k % 8 in (2, 5, 7):
                    TS("dve", hts[b][:, k * 512:(k + 1) * 512], xt[i % 3][:, k * 512:(k + 1) * 512], A_col[:, k:k + 1], sh_col[:, k:k + 1],
                       ALU.mult, ALU.add, [B_xt[i % 3] if k < 12 else B_xtp[i % 3], B_col], [B_hts[b]])
                else:
                    ACT(hts[b][:, k * 512:(k + 1) * 512], xt[i % 3][:, k * 512:(k + 1) * 512], AF.Identity,
                        [B_xt[i % 3] if k < 12 else B_xtp[i % 3], B_col], [B_hts[b]], scale=A_col[:, k:k + 1], bias=sh_col[:, k:k + 1])
            DMA(hT_d[i], hts[b][:], [B_hts[b]], [B_hT[i]], "st_hts%d" % b)

        p0_load(0)
        p0_load(1)
        p0_load(2)
        p0_a(0)
        p0_a2(0)
        for i in range(NT):
            if i + 1 < NT:
                p0_a(i + 1)
            p0_b(i)
            if i + 1 < NT:
                p0_a2(i + 1)
            if i + 3 < NT:
                p0_load(i + 3)
        release(m_persist)
        sch.barrier()
        if stop == "p0":
            return finish([("hT", hT_d.ap(), [NT, 128, KC * 512], BF16)])

    else:
        B_hT = [Buf('hT%d' % i) for i in range(NT)]
        hT_d = din('hT_in', [NT, 128, KC * 512], BF16)
    B_hTo = Buf("hTo")
    for t in range(QT):
        reg = nc.gpsimd.alloc_register("rrh%d" % t)

        def f(e, reg=reg, idx=4 + t, t=t):
            e.reg_load(reg, ri[0:1, idx:idx + 1])
            return e.dma_start(out=hTo_d[t], in_=bass.AP(hT_d, reg, [[KC * 512, 128], [1, KC * 512]]))
        sch.op("pool", f, [B_hT[i] for i in range(NT)] + [B_ri], [B_hTo], dkey="cp_h")
    hto_all, hto_off = alloc_top([128, QT * KC * 512], BF16, "htoall")
    B_htoa = [Buf("htoa%d" % t) for t in range(QT)]
    B_ybuf = [[Buf("ybuf%d_%d" % (q, f)) for f in range(4)] for q in range(4)]
    B_yallp = [Buf("yall%d" % i) for i in range(16)]
    groups = [[0, 1, 2, 3], [4, 5, 6, 7]]

    def emit_gather(q, f4):
        i_ = q * 4 + f4

        def emit_cc(e):
            return e.collective_compute("AllGather", ALU.bypass, replica_groups=groups,
                                        ins=[ybuf_d[q * 512 + f4 * 128: q * 512 + (f4 + 1) * 128, :]],
                                        outs=[yall_d[i_ * 512:(i_ + 1) * 512, :]])
        sch.op("pool", emit_cc, [B_ybuf[q][f4]], [B_yallp[i_]], dkey="cc", dinc=1)
    if mode not in ('pb_only', 'ex_only'):
        wA = alloc([128, KC * 1024], BF16, "wA")
        B_wA = Buf("wA")
        for k in range(KC):
            DMA(wA[:, k * 1024:(k + 1) * 1024], w1[k * 128:(k + 1) * 128, 0:1024], (), [B_wA], "ld_wA", eng="pool")
        abias = alloc([128, 6 * 256], F32, "abias")
        B_ab = Buf("abias")
        DMA(abias[:], abias_d.ap(), (), [B_ab], "ld_ab")
        ht = [alloc([128, KC * 512], BF16, "ht") for _ in range(2)]
        B_ht = [Buf("ht0"), Buf("ht1")]
        qT = alloc([128, 2 * 2048], BF16, "qT")
        kT = alloc([128, 2 * 4096], BF16, "kT")
        vT = alloc([128, 2 * 2048], BF16, "vT")
        zs = alloc([128, 2 * 2048], BF16, "zs")
        B_q = [[Buf("q") for _ in range(4)] for _ in range(2)]
        B_k = [[[Buf("k") for _ in range(4)] for _ in range(2)] for _ in range(2)]
        B_v = [[Buf("v") for _ in range(4)] for _ in range(2)]
        B_z = [[Buf("z") for _ in range(4)] for _ in range(2)]
        NSLOT = {1: 3, 4: 8, 16: 32}
        Vv = {d: alloc([128, 2 * NSLOT[d] * 128], BF16, "Vv%d" % d) for d in (1, 4, 16)}
        B_Vv = {d: [[Buf("vv") for _ in range(NSLOT[d])] for _ in range(2)] for d in (1, 4, 16)}
        accn = alloc([128, 2048], F32, "accn")
        accd = alloc([128, 2048], F32, "accd")
        B_an = [Buf("an") for _ in range(4)]
        B_ad = [Buf("ad") for _ in range(4)]
        sbs = [alloc([128, 256], F32, "sbs") for _ in range(2)]
        B_sbs = [Buf("sbs0"), Buf("sbs1")]
        pTs = [alloc([128, 256], BF16, "pT") for _ in range(3)]
        B_pT = [Buf("pT%d" % i) for i in range(3)]
        yst = [alloc([128, 2048], BF16, "yst") for _ in range(2)]
        B_yst = [Buf("yst0"), Buf("yst1")]

        def pa_load(i):
            DMA(ht[i % 2][:], hT_d[i], [B_hT[i]], [B_ht[i % 2]], "ld_ht%d" % (i % 2))

        psS = [psb[3][:, 0:256], psb[4][:, 0:256], psb[5][:, 0:256]]
        B_psS = [PB_[3], PB_[4], PB_[5]]
        psN = [psb[6][:, 0:128], psb[7][:, 0:128]]
        psD = [psb[6][:, 128:256], psb[7][:, 128:256]]
        B_psN = [PB_[6], PB_[7]]
        B_psD = [PB_[6], PB_[7]]
        psT = [psb[0][:].bitcast(BF16)[:, 0:128], psb[1][:].bitcast(BF16)[:, 0:128]]
        B_psT = [PB_[0], PB_[1]]
        cnt = {"ip": 0, "s": 0, "n": 0, "t": 0, "p": 0, "sb": 0}

        def pa_inproj(i):
            u, m = i // 4, i % 4
            slot = u % 2
            b = i % 2
            for c in range(8):
                pb = cnt["ip"] % 3
                cnt["ip"] += 1
                for k in range(KC):
                    MM(psb[pb][:, :], wA[:, k * 1024 + c * 128: k * 1024 + (c + 1) * 128], ht[b][:, k * 512:(k + 1) * 512],
                       k == 0, k == KC - 1, [B_wA, B_ht[b]], [PB_[pb]])
                h = c % 2
                kind = c // 2
                if kind == 0:
                    CP("dve", qT[:, h * 2048 + m * 512: h * 2048 + (m + 1) * 512], psb[pb][:, :], [PB_[pb]], [B_q[h][m]])
                elif kind == 1:
                    o = h * 4096 + slot * 2048 + m * 512
                    CP("dve", kT[:, o:o + 512], psb[pb][:, :], [PB_[pb]], [B_k[h][slot][m]])
                elif kind == 2:
                    CP("act", vT[:, h * 2048 + m * 512: h * 2048 + (m + 1) * 512], psb[pb][:, :], [PB_[pb]], [B_v[h][m]])
                else:
                    ACT(zs[:, h * 2048 + m * 512: h * 2048 + (m + 1) * 512], psb[pb][:, :], AF.Silu, [PB_[pb]], [B_z[h][m]])

        def blocks_for(u):
            L = []
            for mm in range(16):
                g = 16 * u + mm
                prev = None
                if g > 0:
                    prev = (u % 2, 128 * (mm - 1)) if mm > 0 else ((u - 1) % 2, 1920)
                L.append((0, 1, 128 * mm, 1, [mm // 4], prev, g % 3, (g - 1) % 3))
            for m4 in range(4):
                for r in range(4):
                    g = 4 * u + m4
                    prev = None
                    if g > 0:
                        prev = (u % 2, 512 * (m4 - 1) + r) if m4 > 0 else ((u - 1) % 2, 1536 + r)
                    L.append((1, 4, 512 * m4 + r, 4, [m4], prev, r * 2 + g % 2, r * 2 + (g - 1) % 2))
            for r in range(16):
                prev = ((u - 1) % 2, r) if u > 0 else None
                L.append((2, 16, r, 16, [0, 1, 2, 3], prev, r * 2 + u % 2, r * 2 + (u - 1) % 2))
            return L

        def pa_attn(u, h):
            if sub < 1:
                return
            slot = u % 2
            blks = blocks_for(u)
            pend = []

            def stage1(bi):
                p, d, c0, st, ms, prev, vc, vp = blks[bi]
                qcols = slice(h * 2048 + c0, h * 2048 + c0 + 127 * st + 1, st)
                si = cnt["s"] % 3
                cnt["s"] += 1
                ti = cnt["t"] % 2
                cnt["t"] += 1
                TR(psT[ti], vT[:, qcols], ident_b[:], [B_v[h][m] for m in ms] + [B_cst], [B_psT[ti]])
                ACT(Vv[d][:, (h * NSLOT[d] + vc) * 128:(h * NSLOT[d] + vc + 1) * 128], psT[ti], AF.Identity, [B_psT[ti]], [B_Vv[d][h][vc]])
                kc0 = h * 4096 + slot * 2048 + c0
                lo = 0
                if sub < 1.2:
                    return (bi, 0, 0)
                if prev is not None:
                    ps_, pc0 = prev
                    pk0 = h * 4096 + ps_ * 2048 + pc0
                    pm = sorted(set([(pc0 + j * st) // 512 for j in (0, 127)])) if st < 16 else [0, 1, 2, 3]
                    MM(psS[si][:, 0:128], kT[:, pk0: pk0 + 127 * st + 1: st], qT[:, qcols], True, True,
                       [B_k[h][ps_][m] for m in pm] + [B_q[h][m] for m in ms], [B_psS[si]])
                else:
                    lo = 128
                MM(psS[si][:, 128:256], kT[:, kc0: kc0 + 127 * st + 1: st], qT[:, qcols], True, True,
                   [B_k[h][slot][m] for m in ms] + [B_q[h][m] for m in ms], [B_psS[si]])
                if sub < 1.4:
                    return (bi, 0, 0)
                sbi = cnt["sb"] % 2
                cnt["sb"] += 1
                ab0 = (p * 2 + h) * 256
                STT(sbs[sbi][:, lo:256], psS[si][:, lo:256], 128.0 ** -0.5, abias[:, ab0 + lo: ab0 + 256], ALU.mult, ALU.add,
                    [B_psS[si], B_ab], [B_sbs[sbi]])
                if sub < 1.6:
                    return (bi, 0, 0)
                pi = cnt["p"] % 3
                cnt["p"] += 1
                ACT(pTs[pi][:, lo:256], sbs[sbi][:, lo:256], AF.Exp, [B_sbs[sbi]], [B_pT[pi]])
                return (bi, pi, lo)

            def stage2(info):
                bi, pi, lo = info
                p, d, c0, st, ms, prev, vc, vp = blks[bi]
                ni = cnt["n"] % 2
                cnt["n"] += 1
                vcur = Vv[d][:, (h * NSLOT[d] + vc) * 128:(h * NSLOT[d] + vc + 1) * 128]
                vprev = Vv[d][:, (h * NSLOT[d] + vp) * 128:(h * NSLOT[d] + vp + 1) * 128]
                if lo == 0:
                    MM(psN[ni], vprev, pTs[pi][:, 0:128], True, False, [B_Vv[d][h][vp], B_pT[pi]], [B_psN[ni]])
                    MM(psN[ni], vcur, pTs[pi][:, 128:256], False, True, [B_Vv[d][h][vc], B_pT[pi]], [B_psN[ni]])
                    MM(psD[ni], ones_b[:], pTs[pi][:, 0:128], True, False, [B_cst, B_pT[pi]], [B_psD[ni]])
                    MM(psD[ni], ones_b[:], pTs[pi][:, 128:256], False, True, [B_cst, B_pT[pi]], [B_psD[ni]])
                else:
                    MM(psN[ni], vcur, pTs[pi][:, 128:256], True, True, [B_Vv[d][h][vc], B_pT[pi]], [B_psN[ni]])
                    MM(psD[ni], ones_b[:], pTs[pi][:, 128:256], True, True, [B_cst, B_pT[pi]], [B_psD[ni]])
                ocols = slice(c0, c0 + 127 * st + 1, st)
                if p == 0:
                    CP("act", accn[:, ocols], psN[ni], [B_psN[ni]], [B_an[m] for m in ms])
                    CP("act", accd[:, ocols], psD[ni], [B_psD[ni]], [B_ad[m] for m in ms])
                else:
                    TT("dve", accn[:, ocols], psN[ni], accn[:, ocols], ALU.add, [B_psN[ni]] + [B_an[m] for m in ms], [B_an[m] for m in ms])
                    TT("dve", accd[:, ocols], psD[ni], accd[:, ocols], ALU.add, [B_psD[ni]] + [B_ad[m] for m in ms], [B_ad[m] for m in ms])

            for bi in range(len(blks)):
                pend.append(stage1(bi))
                if len(pend) > 1:
                    x_ = pend.pop(0)
                    if sub >= 2:
                        stage2(x_)
            while pend:
                x_ = pend.pop(0)
                if sub >= 2:
                    stage2(x_)
            if sub < 3:
                return
            yb = (u * 2 + h) % 2
            for m in range(4):
                cs = slice(m * 512, (m + 1) * 512)
                ACT(accd[:, cs], accd[:, cs], AF.Ln, [B_ad[m]], [B_ad[m]])
                ACT(accd[:, cs], accd[:, cs], AF.Exp, [B_ad[m]], [B_ad[m]], scale=-1.0)
                TT("pool", accn[:, cs], accn[:, cs], accd[:, cs], ALU.mult, [B_an[m], B_ad[m]], [B_an[m]])
                TT("pool", yst[yb][:, cs], accn[:, cs], zs[:, h * 2048 + m * 512: h * 2048 + (m + 1) * 512], ALU.mult,
                   [B_an[m], B_z[h][m]], [B_yst[yb]])
            t0 = u * 2048
            while t0 < (u + 1) * 2048:
                q = t0 // SQ
                n = min(SQ - (t0 % SQ), (u + 1) * 2048 - t0)
                DMA(ybuf_d[q * 512 + h * 128: q * 512 + (h + 1) * 128, (t0 % SQ):(t0 % SQ) + n],
                    yst[yb][:, t0 - u * 2048: t0 - u * 2048 + n], [B_yst[yb]], [B_ybuf[q][h]], "st_yst%d" % yb)
                t0 += n

        pa_load(0)
        for u in range(NU):
            for m in range(4):
                i = 4 * u + m
                if i + 1 < NT:
                    pa_load(i + 1)
                pa_inproj(i)
            for h in range(2):
                pa_attn(u, h)
        release(m_persist)
        sch.barrier()
        if stop == "pa":
            if mode == "pa_only":
                return finish([("ybuf", ybuf_d.ap(), [2048, SQ], BF16), ("qT", qT[:], [128, 4096], BF16), ("kT", kT[:], [128, 8192], BF16),
                               ("accn", accn[:], [128, 2048], F32), ("accd", accd[:], [128, 2048], F32), ("pT", pTs[0][:], [128, 256], BF16)])
            return finish([("ybuf", ybuf_d.ap(), [2048, SQ], BF16), ("hT", hT_d.ap(), [NT, 128, KC * 512], BF16)])

    if mode != 'ex_only':
        wB = alloc([128, KC * 1282], BF16, "wB")
        B_wB = Buf("wB")
        for k in range(KC):
            DMA(wB[:, k * 1282:(k + 1) * 1282], w1[k * 128:(k + 1) * 128, 1024:2306], (), [B_wB], "ld_wB", eng="pool")
        ht = [alloc([128, KC * 512], BF16, "htb") for _ in range(2)]
        B_ht = [Buf("htb0"), Buf("htb1")]
        for t in range(QT):
            DMA(hto_all[:, t * KC * 512:(t + 1) * KC * 512], hTo_d[t], [B_hTo], [B_htoa[t]], "ld_htoa%d" % t)
        smalls = alloc([128, 16 + 4 + 2 + 256], F32, "smalls")
        B_sm = Buf("smalls")
        DMA(smalls[:, 0:16], cw.ap(), (), [B_sm], "ld_sm")
        DMA(smalls[:, 16:20], cb.ap(), (), [B_sm], "ld_sm")
        DMA(smalls[:, 20:22], bif.ap(), (), [B_sm], "ld_sm")
        DMA(smalls[:, 22:278], mg.ap(), (), [B_sm], "ld_sm")
        xq = [alloc([128, 515], F32, "xq") for _ in range(4)]
        B_xq = [Buf("xq%d" % i) for i in range(4)]
        cacc = [alloc([128, 512], F32, "cacc") for _ in range(2)]
        B_cacc = [Buf("cacc0"), Buf("cacc1")]
        csig = [alloc([128, 512], F32, "csig") for _ in range(2)]
        B_csig = [Buf("csig0"), Buf("csig1")]
        qk = [[alloc([128, 512], BF16, "qk") for _ in range(4)] for _ in range(2)]
        B_qk = [[Buf("qk") for _ in range(4)] for _ in range(2)]
        vaug = [alloc([128, 257], BF16, "vaug") for _ in range(2)]
        B_vaug = [Buf("vaug0"), Buf("vaug1")]
        gsm = [alloc([128, 16], F32, "gsm") for _ in range(2)]
        B_gsm = [Buf("gsm0"), Buf("gsm1")]
        sgo = [alloc([128, 512], F32, "sgo") for _ in range(2)]
        szm = [alloc([128, 256], BF16, "szm") for _ in range(2)]
        B_sgo = [Buf("sgo0"), Buf("sgo1")]
        B_szm = [Buf("szm0"), Buf("szm1")]
        pTm = [alloc([128, 128], BF16, "pTm") for _ in range(2)]
        B_pTm = [Buf("pTm0"), Buf("pTm1")]
        kw = [alloc([128, 256], BF16, "kw") for _ in range(2)]
        B_kw = [Buf("kw0"), Buf("kw1")]
        Cst = alloc([128, 2 * 257], F32, "Cst")
        Cb = [alloc([128, 2 * 257], BF16, "Cb") for _ in range(2)]
        B_C = Buf("C")
        B_Cb = [Buf("Cb0"), Buf("Cb1")]
        hh = [alloc([128, 256], F32, "hh") for _ in range(2)]
        B_hh = [Buf("hh0"), Buf("hh1")]
        lnst = [alloc([128, 16], F32, "lnst") for _ in range(2)]
        B_lnst = [Buf("lnst0"), Buf("lnst1")]
        ym = [alloc([128, 256], BF16, "ym") for _ in range(2)]
        B_ym = [Buf("ym0"), Buf("ym1")]
        ystm = [alloc([128, 2 * 512], BF16, "ystm") for _ in range(2)]
        B_ystm = [Buf("ystm0"), Buf("ystm1")]
        for i in range(4):
            MEMSET("dve", xq[i][:, 0:3], 0.0, [B_xq[i]])
        for i in range(2):
            MEMSET("dve", vaug[i][:, 256:257], 1.0, [B_vaug[i]])
        MEMSET("dve", Cst[:], 0.0, [B_C])

        assert state["off"] < hto_off, (state["off"], hto_off)

        def pb_load(i):
            DMA(ht[i % 2][:], hT_d[i], [B_hT[i]], [B_ht[i % 2]], "ld_htb%d" % (i % 2))

        psSm = psb[4][:, 0:128]
        psG = psb[2][:, 300:302]
        psUn = psb[4][:, 130:132]
        B_psSm, B_psG, B_psUn = PB_[4], PB_[2], PB_[4]
        psH = psb[5][:, 0:257]
        B_psH = PB_[5]
        psU = psb[6][:, 0:512]
        B_psU = PB_[6]
        ps7 = psb[7][:].bitcast(BF16)
        psK = ps7[:, 0:256]
        psY = ps7[:, 256:512]
        B_psK, B_psY = PB_[7], PB_[7]

        def pb_inproj_feat(i):
            b = i % 2
            for ch in range(4):
                pbk = ch % 2
                for k in range(KC):
                    MM(psb[pbk][:, :], wB[:, k * 1282 + ch * 128: k * 1282 + (ch + 1) * 128], ht[b][:, k * 512:(k + 1) * 512],
                       k == 0, k == KC - 1, [B_wB, B_ht[b]], [PB_[pbk]])
                if i > 0:
                    CP("dve", xq[ch][:, 0:3], xq[ch][:, 512:515], [B_xq[ch]], [B_xq[ch]])
                CP("act", xq[ch][:, 3:515], psb[pbk][:, :], [PB_[pbk]], [B_xq[ch]])
                ca = ch % 2
                TS("dve", cacc[ca][:], xq[ch][:, 3:515], smalls[:, ch * 4 + 3: ch * 4 + 4], smalls[:, 16 + ch:17 + ch], ALU.mult, ALU.add,
                   [B_xq[ch], B_sm], [B_cacc[ca]])
                for j in range(3):
                    STT(cacc[ca][:], xq[ch][:, j:j + 512], smalls[:, ch * 4 + j: ch * 4 + j + 1], cacc[ca][:], ALU.mult, ALU.add,
                        [B_xq[ch], B_sm, B_cacc[ca]], [B_cacc[ca]])
                if OPT_CONV:
                    ACT(csig[ca][:], cacc[ca][:], AF.Sigmoid, [B_cacc[ca]], [B_csig[ca]])
                    TT("pool", qk[b][ch][:], cacc[ca][:], csig[ca][:], ALU.mult, [B_cacc[ca], B_csig[ca]], [B_qk[b][ch]])
                else:
                    ACT(qk[b][ch][:], cacc[ca][:], AF.Silu, [B_cacc[ca]], [B_qk[b][ch]])

        def pb_T(i, s):
            c = 4 * i + s
            b = i % 2
            cb_ = c % 2
            tok = slice(s * 128, (s + 1) * 128)
            for k in range(KC):
                MM(psb[2][:, 0:258], ht[b][:, k * 512 + s * 128: k * 512 + (s + 1) * 128], wB[:, k * 1282 + 512: k * 1282 + 770],
                   k == 0, k == KC - 1, [B_wB, B_ht[b]], [PB_[2]])
            for k in range(KC):
                MM(psb[3][:, :], ht[b][:, k * 512 + s * 128: k * 512 + (s + 1) * 128], wB[:, k * 1282 + 770: k * 1282 + 1282],
                   k == 0, k == KC - 1, [B_wB, B_ht[b]], [PB_[3]])
            CP("act", vaug[cb_][:, 0:256], psb[2][:, 0:256], [PB_[2]], [B_vaug[cb_]])
            g = gsm[cb_]
            Bg = B_gsm[cb_]
            TT("dve", g[:, 0:2], psb[2][:, 256:258], smalls[:, 20:22], ALU.add, [PB_[2], B_sm], [Bg])
            ACT(g[:, 2:3], g[:, 1:2], AF.Exp, [Bg], [Bg], scale=-1.0)
            ACT(g[:, 3:4], g[:, 2:3], AF.Ln, [Bg], [Bg], bias=1.0)
            MM(psG[:, 0:1], tri_f[:], g[:, 3:4], True, True, [Bg, B_cst], [B_psG])
            MM(psG[:, 1:2], ones_f[:], g[:, 3:4], True, True, [Bg, B_cst], [B_psG])
            TT("dve", g[:, 4:5], g[:, 0:1], psG[:, 0:1], ALU.add, [Bg, B_psG], [Bg])
            ACT(g[:, 5:6], g[:, 4:5], AF.Exp, [Bg], [Bg], bias=-LN16)
            ACT(g[:, 6:7], psG[:, 0:1], AF.Exp, [B_psG], [Bg], scale=-1.0)
            TT("dve", g[:, 7:8], g[:, 4:5], psG[:, 1:2], ALU.subtract, [Bg, B_psG], [Bg])
            ACT(g[:, 8:9], g[:, 7:8], AF.Exp, [Bg], [Bg], bias=-LN16)
            ACT(g[:, 9:10], psG[:, 1:2], AF.Exp, [B_psG], [Bg], scale=-1.0)
            if OPT_SIG:
                ACT(sgo[cb_][:], psb[3][:, 0:512], AF.Sigmoid, [PB_[3]], [B_sgo[cb_]])
                TT("dve", szm[cb_][:], psb[3][:, 256:512], sgo[cb_][:, 256:512], ALU.mult, [PB_[3], B_sgo[cb_]], [B_szm[cb_]])
            else:
                ACT(sgo[cb_][:, 0:256], psb[3][:, 0:256], AF.Sigmoid, [PB_[3]], [B_sgo[cb_]])
                ACT(szm[cb_][:], psb[3][:, 256:512], AF.Silu, [PB_[3]], [B_szm[cb_]])

        def pb_M(i, s):
            c = 4 * i + s
            b = i % 2
            cb_ = c % 2
            tok = slice(s * 128, (s + 1) * 128)
            g = gsm[cb_]
            Bg = B_gsm[cb_]
            for e2 in range(2):
                MM(psSm, qk[b][2 + e2][:, tok], qk[b][e2][:, tok], e2 == 0, e2 == 1, [B_qk[b][2 + e2], B_qk[b][e2]], [B_psSm])
            STT(pTm[cb_][:], psSm, g[:, 5:6], tri_f[:], ALU.mult, ALU.mult, [B_psSm, Bg, B_cst], [B_pTm[cb_]])
            for e2 in range(2):
                TR(psK[:, e2 * 128:(e2 + 1) * 128], qk[b][2 + e2][:, tok], ident_b[:], [B_qk[b][2 + e2], B_cst], [B_psK])
            ACT(kw[cb_][:], psK, AF.Copy, [B_psK, Bg], [B_kw[cb_]], scale=g[:, 8:9])
            for e2 in range(2):
                MM(psU[:, e2 * 256:(e2 + 1) * 256], kw[cb_][:, e2 * 128:(e2 + 1) * 128], vaug[cb_][:, 0:256], True, True,
                   [B_kw[cb_], B_vaug[cb_]], [B_psU])
            for e2 in range(2):
                MM(psUn[:, e2:e2 + 1], kw[cb_][:, e2 * 128:(e2 + 1) * 128], vaug[cb_][:, 256:257], True, True,
                   [B_kw[cb_], B_vaug[cb_]], [B_psUn])
            cbi = c % 2
            MM(psH, pTm[cb_][:], vaug[cb_][:], True, c == 0, [B_pTm[cb_], B_vaug[cb_]], [B_psH])
            if c > 0:
                for e2 in range(2):
                    MM(psH, qk[b][e2][:, tok], Cb[cbi][:, e2 * 257:(e2 + 1) * 257], False, e2 == 1,
                       [B_qk[b][e2], B_Cb[cbi]], [B_psH])
            C3 = Cst[:].rearrange("p (e f) -> p e f", e=2)
            STT(C3[:, :, 0:256], C3[:, :, 0:256], g[:, 9:10], psU.rearrange("p (e f) -> p e f", e=2), ALU.mult, ALU.add,
                [B_C, Bg, B_psU], [B_C])
            STT(C3[:, :, 256], C3[:, :, 256], g[:, 9:10], psUn, ALU.mult, ALU.add, [B_C, Bg, B_psUn], [B_C])
            ACT(Cb[1 - cbi][:], Cst[:], AF.Identity, [B_C], [B_Cb[1 - cbi]])
            ACT(g[:, 10:11], psH[:, 256:257], AF.Abs, [B_psH, Bg], [Bg], scale=g[:, 6:7])
            TS("dve", g[:, 11:12], g[:, 10:11], 1.0, None, ALU.max, None, [Bg], [Bg])
            RECIP(g[:, 12:13], g[:, 11:12], [Bg], [Bg])
            TT("dve", g[:, 12:13], g[:, 12:13], g[:, 6:7], ALU.mult, [Bg], [Bg])
            STT(hh[cb_][:], psH[:, 0:256], g[:, 12:13], sgo[cb_][:, 0:256], ALU.mult, ALU.mult, [B_psH, Bg, B_sgo[cb_]], [B_hh[cb_]])
            ls = lnst[cb_]
            Bl = B_lnst[cb_]
            sch.op("dve", lambda e: e.bn_stats(out=ls[:, 0:6], in_=hh[cb_][:]), [B_hh[cb_]], [Bl])
            sch.op("dve", lambda e: e.bn_aggr(out=ls[:, 6:8], in_=ls[:, 0:6]), [Bl], [Bl])
            if OPT_LN:
                ACT(ls[:, 8:9], ls[:, 7:8], AF.Ln, [Bl], [Bl], bias=EPS)
                ACT(ls[:, 9:10], ls[:, 8:9], AF.Exp, [Bl], [Bl], scale=-0.5)
            else:
                ACT(ls[:, 8:9], ls[:, 7:8], AF.Sqrt, [Bl], [Bl], bias=EPS)
                RECIP(ls[:, 9:10], ls[:, 8:9], [Bl], [Bl])
            TS("dve", hh[cb_][:], hh[cb_][:], ls[:, 6:7], ls[:, 9:10], ALU.subtract, ALU.mult, [B_hh[cb_], Bl], [B_hh[cb_]])
            TT("dve", hh[cb_][:], hh[cb_][:], smalls[:, 22:278], ALU.mult, [B_hh[cb_], B_sm], [B_hh[cb_]])
            TT("dve", ym[cb_][:], hh[cb_][:], szm[cb_][:], ALU.mult, [B_hh[cb_], B_szm[cb_]], [B_ym[cb_]])

        def pb_M_tail(i, s):
            c = 4 * i + s
            b = i % 2
            cb_ = c % 2
            for f2 in range(2):
                TR(psY[:, f2 * 128:(f2 + 1) * 128], ym[cb_][:, f2 * 128:(f2 + 1) * 128], ident_b[:], [B_ym[cb_], B_cst], [B_psY])
            y3 = ystm[b][:].rearrange("p (f t) -> p f t", f=2)[:, :, s * 128:(s + 1) * 128]
            ACT(y3, psY.rearrange("p (f t) -> p f t", f=2), AF.Identity, [B_psY], [B_ystm[b]])
            if s == 3:
                t0 = i * 512
                q = t0 // SQ
                for f2 in range(2):
                    DMA(ybuf_d[q * 512 + 256 + f2 * 128: q * 512 + 256 + (f2 + 1) * 128, (t0 % SQ):(t0 % SQ) + 512],
                        ystm[b][:, f2 * 512:(f2 + 1) * 512], [B_ystm[b]], [B_ybuf[q][2 + f2]], "st_ystm%d_%d" % (b, f2))
                if (i + 1) % QT == 0:
                    pend_g.append(q)

        pend_g = []
        pb_load(0)
        for q in range(4):
            for f4 in range(2):
                emit_gather(q, f4)
        if NT > 1:
            pb_load(1)
        chunks = [(i, s) for i in range(NT) for s in range(4)]
        pb_inproj_feat(0)
        pb_T(0, 0)
        for ci, (i, s) in enumerate(chunks):
            if ci + 1 < len(chunks):
                i2, s2 = chunks[ci + 1]
                if s2 == 0:
                    pb_inproj_feat(i2)
                pb_T(i2, s2)
                if s2 == 3 and i2 + 2 < NT:
                    pb_load(i2 + 2)
            pb_M(i, s)
            if ci > 0:
                pb_M_tail(*chunks[ci - 1])
            if s == 1 and pend_g:
                q_ = pend_g.pop(0)
                emit_gather(q_, 2)
                emit_gather(q_, 3)
        pb_M_tail(*chunks[-1])
        while pend_g:
            q_ = pend_g.pop(0)
            emit_gather(q_, 2)
            emit_gather(q_, 3)
        release(m_persist)
        sch.barrier()
        if stop == "pb":
            return finish([("ybuf", ybuf_d.ap(), [2048, SQ], BF16), ("C", Cst[:], [128, 514], F32), ("gsm", gsm[0][:], [128, 16], F32),
                           ("hh", hh[0][:], [128, 256], F32), ("qk", qk[1][0][:], [128, 512], BF16)])

    if mode == 'ex_only':
        for q in range(4):
            for f4 in range(4):
                emit_gather(q, f4)
    B_yall = Buf("yall")
    B_ymine = Buf("ymine")
    allyb = [B_ybuf[q][f] for q in range(4) for f in range(4)]
    groups = [[0, 1, 2, 3], [4, 5, 6, 7]]
    yregs = [nc.gpsimd.alloc_register("rry%d" % f4) for f4 in range(4)]
    sch.barrier()
    if stop == "ex":
        return finish([("hTo", hTo_d.ap(), [QT, 128, KC * 512], BF16)])

    B_mg = [[Buf("mg%d_%d" % (t, sl)) for sl in range(2)] for t in range(QT)]
    yin_all = alloc([128, 16 * SQ], BF16, "yinall")
    B_yina = Buf("yina")
    assert state["off"] + 40000 < hto_off
    for f4 in range(4):
        a_, c_ = f4 // 2, f4 % 2

        def f(e, reg=yregs[f4], idx=f4, a_=a_, c_=c_):
            e.reg_load(reg, ri[0:1, idx:idx + 1])
            y3 = yin_all[:].rearrange("p (c t) -> p c t", c=16)
            dst = y3[:, a_ * 8 + c_: a_ * 8 + c_ + 7: 2, :]
            return e.dma_start(out=dst, in_=bass.AP(yall_d, reg, [[SQ, 128], [128 * SQ, 4], [1, SQ]]))
        sch.op("pool", f, B_yallp + [B_ri], [B_yina], dkey="ld_yina")
    wgj = [alloc([128, 2 * KC * 128], BF16, "wgj") for _ in range(2)]
    wpj = [alloc([128, 2 * 8 * 128], BF16, "wpj") for _ in range(2)]
    B_wgj = [Buf("wgj0"), Buf("wgj1")]
    B_wpj = [Buf("wpj0"), Buf("wpj1")]
    sga = [alloc([128, 512], F32, "sga") for _ in range(2)]
    sgb = [alloc([128, 512], F32, "sgb") for _ in range(2)]
    mo = [alloc([128, 512], BF16, "mo") for _ in range(2)]
    B_sga = [Buf("sga0"), Buf("sga1")]
    B_sgb = [Buf("sgb0"), Buf("sgb1")]
    B_mo = [Buf("mo0"), Buf("mo1")]

    def p2_wload(j):
        b = j % 2
        for a in range(2):
            DMA(wgj[b][:, a * KC * 128:(a + 1) * KC * 128], w_gj[j, :, a * KC * 128:(a + 1) * KC * 128], (), [B_wgj[b]],
                "ld_wgj%d" % b, eng="pool")
        DMA(wpj[b][:], w_pj[j], (), [B_wpj[b]], "ld_wpj%d" % b, eng="pool")

    p2_wload(0)
    it = 0
    for j in range(KC):
        b = j % 2
        if j + 1 < KC:
            p2_wload(j + 1)
        for t in range(QT):
            pb0 = (it % 2) * 4
            si = it % 2
            it += 1
            for a in range(2):
                for k in range(KC):
                    MM(psb[pb0 + a][:, :], wgj[b][:, (a * KC + k) * 128:(a * KC + k + 1) * 128],
                       hto_all[:, (t * KC + k) * 512:(t * KC + k + 1) * 512], k == 0, k == KC - 1, [B_wgj[b], B_htoa[t]], [PB_[pb0 + a]])
            for a in range(2):
                for k in range(8):
                    MM(psb[pb0 + 2 + a][:, :], wpj[b][:, (a * 8 + k) * 128:(a * 8 + k + 1) * 128],
                       yin_all[:, (a * 8 + k) * SQ + t * 512:(a * 8 + k) * SQ + (t + 1) * 512], k == 0, k == 7, [B_wpj[b], B_yina], [PB_[pb0 + 2 + a]])
            ACT(sga[si][:], psb[pb0][:, :], AF.Sigmoid, [PB_[pb0]], [B_sga[si]])
            ACT(sgb[si][:], psb[pb0 + 1][:, :], AF.Sigmoid, [PB_[pb0 + 1]], [B_sgb[si]])
            TT("dve", sga[si][:], psb[pb0 + 2][:, :], sga[si][:], ALU.mult, [PB_[pb0 + 2], B_sga[si]], [B_sga[si]])
            TT("dve", sgb[si][:], psb[pb0 + 3][:, :], sgb[si][:], ALU.mult, [PB_[pb0 + 3], B_sgb[si]], [B_sgb[si]])
            TT("pool", mo[si][:], sga[si][:], sgb[si][:], ALU.add, [B_sga[si], B_sgb[si]], [B_mo[si]])
            DMA(mgT_d[t][:, j * 512:(j + 1) * 512], mo[si][:], [B_mo[si]], [B_mg[t][si]], "st_mo%d" % si)
    release(m_persist)
    sch.barrier()
    if stop == "p2":
        return finish([("mgT", mgT_d.ap(), [QT, 128, KC * 512], BF16)])

    wO = alloc([128, KC * D], BF16, "wO")
    B_wO = Buf("wO")
    for k in range(KC):
        DMA(wO[:, k * D:(k + 1) * D], w_out[k * 128:(k + 1) * 128, :], (), [B_wO], "ld_wO", eng="pool")
    fgs = alloc([128, D], F32, "fgs")
    B_fg = Buf("fg")
    DMA(fgs[:], fg.ap(), (), [B_fg], "ld_fg")
    xin = [alloc([128, D], F32, "xin") for _ in range(2)]
    xn = [alloc([128, D], F32, "xn") for _ in range(2)]
    mti = [alloc([128, KC * 512], BF16, "mti") for _ in range(2)]
    B_mti = [Buf("mti0"), Buf("mti1")]
    st2 = [alloc([128, 40], F32, "st2") for _ in range(2)]
    B_xin = [Buf("xin0"), Buf("xin1")]
    B_xn = [Buf("xn0"), Buf("xn1")]
    B_st2 = [Buf("st20"), Buf("st21")]
    NTT = SQ // 128

    def p2c_load(tt):
        DMA(xin[tt % 2][:], xtok[tt * 128:(tt + 1) * 128, :], (), [B_xin[tt % 2]], "ld_xin%d" % (tt % 2))
        if tt % 4 == 0:
            t_ = tt // 4
            DMA(mti[t_ % 2][:], mgT_d[t_], B_mg[t_], [B_mti[t_ % 2]], "ld_mti%d" % (t_ % 2))

    p2c_load(0)
    for tt in range(NTT):
        b = tt % 2
        if tt + 1 < NTT:
            p2c_load(tt + 1)
        t, sub = tt // 4, tt % 4
        for n in range(4):
            pbk = (tt % 2) * 4 + n
            for j in range(KC):
                MM(psb[pbk][:, :], mti[t % 2][:, j * 512 + sub * 128: j * 512 + (sub + 1) * 128], wO[:, j * D + n * 512: j * D + (n + 1) * 512],
                   j == 0, j == KC - 1, [B_wO, B_mti[t % 2]], [PB_[pbk]])
            cs = slice(n * 512, (n + 1) * 512)
            TT("dve", xn[b][:, cs], psb[pbk][:, :], gate_bc[:, cs], ALU.mult, [PB_[pbk], B_gate], [B_xn[b]])
            TT("pool", xn[b][:, cs], xn[b][:, cs], xin[b][:, cs], ALU.add, [B_xn[b], B_xin[b]], [B_xn[b]])
        s2 = st2[b]
        for n in range(4):
            sch.op("dve", lambda e, b=b, s2=s2, n=n: e.bn_stats(out=s2[:, 8 + n * 6: 8 + (n + 1) * 6], in_=xn[b][:, n * 512:(n + 1) * 512]),
                   [B_xn[b]], [B_st2[b]])
        sch.op("dve", lambda e, s2=s2: e.bn_aggr(out=s2[:, 4:6], in_=s2[:, 8:32]), [B_st2[b]], [B_st2[b]])
        STT(s2[:, 0:1], s2[:, 4:5], s2[:, 4:5], s2[:, 5:6], ALU.mult, ALU.add, [B_st2[b]], [B_st2[b]])
        ACT(s2[:, 1:2], s2[:, 0:1], AF.Sqrt, [B_st2[b]], [B_st2[b]], bias=EPS)
        RECIP(s2[:, 2:3], s2[:, 1:2], [B_st2[b]], [B_st2[b]])
        STT(xn[b][:], xn[b][:], s2[:, 2:3], fgs[:], ALU.mult, ALU.mult, [B_xn[b], B_st2[b], B_fg], [B_xn[b]])
        DMA(out_d[tt * 128:(tt + 1) * 128, :], xn[b][:], [B_xn[b]], [Buf("o")], "st_out%d" % b)
    sch.barrier()
    sch.emit()
    return nc, dbg_out


def _regload(e, reg, ap):
    return e.reg_load(reg, ap)


def _dyn(t, reg, const_off, pattern):
    return bass.AP(t, reg + const_off, pattern)


def _prep_inputs(S, x, c, norm_gain, w_ada, b_ada, w_in, b_gate_if, conv_w, conv_b, mlstm_norm_gain,
                 w_proj_attn, w_proj_mlstm, w_out, final_gain):
    f = np.float32
    NS = S // 128
    SQ = S // 4
    x = np.asarray(x, f)
    w_in0 = np.asarray(w_in, f)[0]
    ident = np.eye(128, dtype=f)
    tri = np.triu(np.ones((128, 128), f))
    ones = np.ones((128, 128), f)
    cst = np.ascontiguousarray(np.concatenate([ident, tri, ones], axis=1))
    xTs = []
    for b in range(2):
        a = x[b].reshape(NS // 4, 512, KC, 128).transpose(0, 3, 2, 1)
        xTs.append(np.ascontiguousarray(a).reshape(NS // 4, 128, KC * 512))
    w_ada0 = np.ascontiguousarray(np.asarray(w_ada, f)[0])
    b_row = np.ascontiguousarray(np.asarray(b_ada, f)[0].reshape(1, -1))
    ngc = np.ascontiguousarray(np.asarray(norm_gain, f)[0].reshape(KC, 128).T)
    wg4 = w_in0[:, O_GA:O_GA + 2 * D].reshape(KC, 128, 2, KC, 128)
    w_gj = np.ascontiguousarray(wg4.transpose(3, 1, 2, 0, 4)).reshape(KC, 128, 2 * KC * 128)
    wp4 = np.stack([np.asarray(w_proj_attn, f)[0], np.asarray(w_proj_mlstm, f)[0]], axis=0).reshape(2, 8, 128, KC, 128)
    w_pj = np.ascontiguousarray(wp4.transpose(3, 2, 0, 1, 4)).reshape(KC, 128, 2 * 8 * 128)
    w_o = np.ascontiguousarray(np.asarray(w_out, f)[0])
    fgb = np.ascontiguousarray(np.broadcast_to(np.asarray(final_gain, f)[None, :], (128, D)))
    cwf = np.asarray(conv_w, f)[0]
    cbf = np.asarray(conv_b, f)[0]
    bg = np.asarray(b_gate_if, f)[0]
    mgf = np.asarray(mlstm_norm_gain, f)[0]
    ki = np.arange(128)[:, None]
    qi = np.arange(128)[None, :]
    in_maps = []
    for core in range(8):
        b, g = core // 4, core % 4
        cols = []
        for off in (O_QA, O_KA, O_VA, O_ZA):
            for lh in range(2):
                h = 2 * g + lh
                cols.append(np.arange(off + h * 128, off + (h + 1) * 128))
        cols.append(np.arange(O_QM + g * 256, O_QM + (g + 1) * 256))
        cols.append(np.arange(O_KM + g * 256, O_KM + (g + 1) * 256))
        cols.append(np.arange(O_VM + g * 256, O_VM + (g + 1) * 256))
        cols.append(np.array([O_IF + g, O_IF + 4 + g]))
        cols.append(np.arange(O_OM + g * 256, O_OM + (g + 1) * 256))
        cols.append(np.arange(O_ZM + g * 256, O_ZM + (g + 1) * 256))
        cols = np.concatenate(cols)
        assert cols.size == 2306
        w1 = np.ascontiguousarray(w_in0[:, cols])
        cwc = np.zeros((128, 16), f)
        cbc = np.zeros((128, 4), f)
        for ch in range(4):
            base = (0 if ch < 2 else 1024) + g * 256 + (ch % 2) * 128
            cwc[:, ch * 4:(ch + 1) * 4] = cwf[:, base:base + 128].T
            cbc[:, ch] = cbf[base:base + 128]
        bifc = np.ascontiguousarray(np.broadcast_to(np.array([bg[g], bg[4 + g]], f)[None, :], (128, 2)))
        mgc = np.ascontiguousarray(np.broadcast_to(mgf[g * 256:(g + 1) * 256][None, :], (128, 256)))
        ab = np.zeros((128, 6, 256), f)
        for p, d in enumerate((1, 4, 16)):
            for lh in range(2):
                slope = 2.0 ** (-(2 * g + lh + 1))
                for half, shift in ((0, 128), (1, 0)):
                    delta = qi - ki + shift
                    valid = (delta >= 0) & (delta <= 128)
                    ab[:, p * 2 + lh, half * 128:(half + 1) * 128] = np.where(valid, -slope * d * delta, NEG)
        in_maps.append({
            "xT": xTs[b],
            "xtok": np.ascontiguousarray(x[b, g * SQ:(g + 1) * SQ, :]),
            "ccol": np.ascontiguousarray(np.asarray(c, f)[b].reshape(KC, 128).T),
            "w_ada": w_ada0, "b_row": b_row, "ng": ngc, "w1": w1, "w_gj": w_gj, "w_pj": w_pj,
            "bif": bifc, "cw": cwc, "cb": cbc, "mg": mgc,
            "w_out": w_o, "fg": fgb, "cst": cst,
            "abias": np.ascontiguousarray(ab.reshape(128, 6 * 256)),
            "roff": np.array([[(g * 4 + f4) * 512 * SQ for f4 in range(4)]
                              + [(g * (SQ // 512) + t) * 128 * KC * 512 for t in range(SQ // 512)]], dtype=np.int32),
        })
    return in_maps


_STOP = None


def kernel(x, c, norm_gain, w_ada, b_ada, w_in, b_gate_if, conv_w, conv_b, mlstm_norm_gain,
           w_proj_attn, w_proj_mlstm, w_out, final_gain):
    S = int(np.asarray(x).shape[1])
    in_maps = _prep_inputs(S, x, c, norm_gain, w_ada, b_ada, w_in, b_gate_if, conv_w, conv_b, mlstm_norm_gain,
                           w_proj_attn, w_proj_mlstm, w_out, final_gain)
    nc, _ = build(S, stop=_STOP)
    res = run_bass_kernel_spmd(nc, in_maps, core_ids=list(range(8)))
    if _STOP is not None:
        return res.results
    SQ = S // 4
    out = np.zeros((2, S, D), np.float32)
    for core in range(8):
        b, g = core // 4, core % 4
        out[b, g * SQ:(g + 1) * SQ, :] = res.results[core]["out"]
    return out
```
